# Optimizing a Trainium2 kernel written in Bass

```python
import math
import jax
import jax.numpy as jnp
from jax import lax
import numpy as np

D_MODEL = 1024
BATCH = 16
SEQ = 4096
DEPTH = 2
DEC_BATCH = 2
DEC_SEQ = 16384
PAST_LEN = 128

HEAD_DIM = 64
A_Q_HEADS = 8
A_KV_HEADS = 2
A_GROUP = A_Q_HEADS // A_KV_HEADS
A_HALF_WINDOW = 128
B_PAIRS = ((128, 1), (512, 4), (2048, 16))
N_DIL = len(B_PAIRS)
B_HEADS = 4
C_HEADS = 8
GRID_W = 64
NA_KH = 8
NA_KW = 16
D_HEADS = 4
D_HEAD_DIM = 128
D_CHUNK = 64
T5_BUCKETS = 32
T5_MAX_DISTANCE = 1024
T5_HEADS = A_Q_HEADS + N_DIL * B_HEADS
ALPHA = (2.0 * DEPTH) ** 0.25
BETA = (8.0 * DEPTH) ** -0.25
N_EVEN = (DEPTH + 1) // 2
N_ODD = DEPTH // 2
LN_EPS = 1e-5
RMS_EPS = 1e-6

A_Q_W = A_Q_HEADS * HEAD_DIM
A_KV_W = A_KV_HEADS * HEAD_DIM
B_W = B_HEADS * HEAD_DIM
B_QKV_W = N_DIL * B_W
EVEN_SPLITS = (A_Q_W, A_KV_W, A_KV_W, B_QKV_W, B_QKV_W, B_QKV_W, A_Q_W, B_W)
EVEN_IN = sum(EVEN_SPLITS)
EVEN_OUT = A_Q_W + B_W
C_W = C_HEADS * HEAD_DIM
D_W = D_HEADS * D_HEAD_DIM
ODD_SPLITS = (C_W, C_W, C_W, D_W, D_W, D_W, D_W, C_W, D_W)
ODD_IN = sum(ODD_SPLITS)
ODD_OUT = C_W + D_W

kernel_name = "hybrid_bidir_encoder_swa_dilated_na_hgrn2"


def _split(h, sizes):
    return jnp.split(h, [int(c) for c in np.cumsum(sizes)[:-1]], axis=-1)


def t5_buckets(rel):
    nb = T5_BUCKETS // 2
    ret = (rel > 0).astype(np.int64) * nb
    n = np.abs(rel)
    max_exact = nb // 2
    large = max_exact + (np.log(np.maximum(n, 1) / max_exact) / math.log(T5_MAX_DISTANCE / max_exact) * (nb - max_exact)).astype(np.int64)
    large = np.minimum(large, nb - 1)
    return ret + np.where(n < max_exact, n, large)


def layer_norm(x, g, b):
    xf = x.astype(jnp.float32)
    mu = xf.mean(-1, keepdims=True)
    var = jnp.square(xf - mu).mean(-1, keepdims=True)
    return ((xf - mu) * lax.rsqrt(var + LN_EPS) * g.astype(jnp.float32) + b.astype(jnp.float32)).astype(x.dtype)


def banded_attention(q, k, v, bias, sink):
    nb, L, hk, g, hd = q.shape
    W = bias.shape[-2]
    n = -(-L // W)
    pad = n * W - L
    qb = jnp.pad(q, ((0, 0), (0, pad), (0, 0), (0, 0), (0, 0))).reshape(nb, n, W, hk, g, hd)

    def windows(t):
        tp = jnp.pad(t, ((0, 0), (W, pad + W), (0, 0), (0, 0))).reshape(nb, n + 2, W, hk, hd)
        return jnp.concatenate([tp[:, :-2], tp[:, 1:-1], tp[:, 2:]], axis=2)

    kw, vw = windows(k), windows(v)
    qpos = np.arange(n)[:, None, None] * W + np.arange(W)[None, :, None]
    kpos = np.arange(n)[:, None, None] * W + np.arange(3 * W)[None, None, :] - W
    valid = (kpos >= 0) & (kpos < L) & (np.abs(kpos - qpos) <= W)
    s = jnp.einsum('bnqhgd,bnkhd->bnhgqk', qb, kw).astype(jnp.float32) * (hd ** -0.5) + bias
    s = jnp.where(valid[None, :, None, None], s, -jnp.inf)
    m = s.max(-1)
    if sink is not None:
        m = jnp.maximum(m, sink[:, :, None])
    p = jnp.exp(s - m[..., None])
    denom = p.sum(-1)
    if sink is not None:
        denom = denom + jnp.exp(sink[:, :, None] - m)
    o = jnp.einsum('bnhgqk,bnkhd->bnqhgd', p, vw.astype(jnp.float32))
    denom_q = jnp.moveaxis(denom, -1, 2)
    o = (o / denom_q[..., None]).reshape(nb, n * W, hk, g, hd)[:, :L]
    lse = (jnp.moveaxis(m, -1, 2) + jnp.log(denom_q)).reshape(nb, n * W, hk, g)[:, :L]
    return o, lse


def _to_dilated(t, d):
    nb, T = t.shape[:2]
    t = t.reshape(nb, T // d, d, *t.shape[2:])
    return jnp.moveaxis(t, 2, 1).reshape(nb * d, T // d, *t.shape[3:])


def _from_dilated(t, nb, d):
    ld = t.shape[1]
    t = t.reshape(nb, d, ld, *t.shape[2:])
    return jnp.moveaxis(t, 1, 2).reshape(nb, ld * d, *t.shape[3:])


def neighbourhood_attention(q, k, v, rpb):
    nb, T, H, hd = q.shape
    rows = T // GRID_W
    kh = min(NA_KH, rows)
    qg = q.reshape(nb, rows, GRID_W, H, hd)
    kg = k.reshape(nb, rows, GRID_W, H, hd)
    vg = v.reshape(nb, rows, GRID_W, H, hd)
    qc = np.arange(GRID_W)
    sc = np.clip(qc - NA_KW // 2, 0, GRID_W - NA_KW)
    col_mask = (qc[None, :] >= sc[:, None]) & (qc[None, :] < sc[:, None] + NA_KW)
    col_idx = np.clip(qc[None, :] - qc[:, None] + NA_KW - 1, 0, 2 * NA_KW - 2)
    rpb = rpb.astype(jnp.float32)

    def row_block(r):
        sr = jnp.clip(r - kh // 2, 0, rows - kh)
        ks = lax.dynamic_slice_in_dim(kg, sr, kh, axis=1)
        vs = lax.dynamic_slice_in_dim(vg, sr, kh, axis=1)
        qr = lax.dynamic_index_in_dim(qg, r, axis=1, keepdims=False)
        dr = sr + jnp.arange(kh) - r
        bias = jnp.take(rpb, dr + NA_KH - 1, axis=1)[:, :, col_idx]
        s = jnp.einsum('bqhd,bikhd->bhqik', qr, ks).astype(jnp.float32) * (hd ** -0.5) + jnp.transpose(bias, (0, 2, 1, 3))
        s = jnp.where(col_mask[None, None, :, None, :], s, -jnp.inf)
        p = jax.nn.softmax(s.reshape(nb, H, GRID_W, kh * GRID_W), axis=-1).reshape(s.shape)
        return jnp.einsum('bhqik,bikhd->bqhd', p, vs.astype(jnp.float32))

    out = lax.map(row_block, jnp.arange(rows))
    return jnp.moveaxis(out, 0, 1).reshape(nb, T, H, hd)


def gla_chunk_scan(q, k, v, log_f):
    nb, T, H, dk = q.shape
    dv = v.shape[-1]
    C = D_CHUNK
    n = T // C

    def chunks(t):
        return t.reshape(nb, n, C, H, t.shape[-1]).transpose(1, 0, 3, 2, 4)

    b = jnp.cumsum(chunks(log_f), axis=3)
    causal = jnp.tril(jnp.ones((C, C), dtype=bool))

    def step(S, inp):
        qc, kc, vc, bc = inp
        blast = bc[:, :, -1:, :]
        inter = jnp.einsum('bhtk,bhkv->bhtv', qc * jnp.exp(bc), S)
        decay = jnp.exp(jnp.where(causal[:, :, None], bc[:, :, :, None, :] - bc[:, :, None, :, :], -jnp.inf))
        att = jnp.einsum('bhtsk,bhsk->bhts', qc[:, :, :, None, :] * decay, kc)
        o = inter + jnp.einsum('bhts,bhsv->bhtv', att, vc)
        S = jnp.exp(blast[:, :, 0, :, None]) * S + jnp.einsum('bhsk,bhsv->bhkv', kc * jnp.exp(blast - bc), vc)
        return S, o

    S0 = jnp.zeros((nb, H, dk, dv), jnp.float32)
    _, o = lax.scan(step, S0, (chunks(q), chunks(k), chunks(v), b))
    return o.transpose(1, 0, 3, 2, 4).reshape(nb, T, H, dv)


def hgrn2_direction(q, i, z, lb, reverse):
    f = (lb + (1.0 - lb) * jax.nn.sigmoid(z.astype(jnp.float32))).reshape(q.shape)
    k = 1.0 - f
    log_f = jnp.log(f)
    if reverse:
        q, k, i, log_f = (jnp.flip(t, axis=1) for t in (q, k, i, log_f))
    o = gla_chunk_scan(q, k, i, log_f)
    return jnp.flip(o, axis=1) if reverse else o


def even_layer(x, w_in, sink, w_out, t5_table):
    nb, T, _ = x.shape
    qa, ka, va, qb, kb, vb, ga, gb = _split(x @ w_in, EVEN_SPLITS)
    t5 = t5_table.astype(jnp.float32)
    W = A_HALF_WINDOW
    rel_a = np.arange(3 * W)[None, :] - W - np.arange(W)[:, None]
    bias_a = jnp.transpose(t5[:, :A_Q_HEADS][t5_buckets(rel_a)], (2, 0, 1)).reshape(A_KV_HEADS, A_GROUP, W, 3 * W)
    oa, _ = banded_attention(qa.reshape(nb, T, A_KV_HEADS, A_GROUP, HEAD_DIM), ka.reshape(nb, T, A_KV_HEADS, HEAD_DIM),
                             va.reshape(nb, T, A_KV_HEADS, HEAD_DIM), bias_a,
                             sink.astype(jnp.float32).reshape(A_KV_HEADS, A_GROUP))
    oa = oa.reshape(nb, T, A_Q_W)
    qb = qb.reshape(nb, T, N_DIL, B_HEADS, HEAD_DIM)
    kb = kb.reshape(nb, T, N_DIL, B_HEADS, HEAD_DIM)
    vb = vb.reshape(nb, T, N_DIL, B_HEADS, HEAD_DIM)
    outs, lses = [], []
    for g, (win, d) in enumerate(B_PAIRS):
        half = win // (2 * d)
        rel = (np.arange(3 * half)[None, :] - half - np.arange(half)[:, None]) * d
        cols = t5[:, A_Q_HEADS + g * B_HEADS: A_Q_HEADS + (g + 1) * B_HEADS]
        bias = jnp.transpose(cols[t5_buckets(rel)], (2, 0, 1))[:, None]
        o, lse = banded_attention(_to_dilated(qb[:, :, g, :, None, :], d), _to_dilated(kb[:, :, g], d),
                                  _to_dilated(vb[:, :, g], d), bias, None)
        outs.append(_from_dilated(o[:, :, :, 0], nb, d))
        lses.append(_from_dilated(lse[:, :, :, 0], nb, d))
    wts = jax.nn.softmax(jnp.stack(lses), axis=0)
    ob = jnp.einsum('gbth,gbthd->bthd', wts, jnp.stack(outs)).reshape(nb, T, B_W)
    y = jnp.concatenate([oa * jax.nn.silu(ga.astype(jnp.float32)), ob * jax.nn.silu(gb.astype(jnp.float32))], axis=-1)
    return y.astype(x.dtype) @ w_out


def odd_layer(x, w_in, rpb, lb, gnorm, w_out):
    nb, T, _ = x.shape
    qc, kc, vc, qd, idd, zf, zb, gc, gd = _split(x @ w_in, ODD_SPLITS)
    oc = neighbourhood_attention(qc.reshape(nb, T, C_HEADS, HEAD_DIM), kc.reshape(nb, T, C_HEADS, HEAD_DIM),
                                 vc.reshape(nb, T, C_HEADS, HEAD_DIM), rpb).reshape(nb, T, C_W)
    qh = qd.astype(jnp.float32).reshape(nb, T, D_HEADS, D_HEAD_DIM)
    ih = idd.astype(jnp.float32).reshape(nb, T, D_HEADS, D_HEAD_DIM)
    od = hgrn2_direction(qh, ih, zf, lb[0], False) + hgrn2_direction(qh, ih, zb, lb[1], True)
    od = od * lax.rsqrt(jnp.mean(jnp.square(od), axis=-1, keepdims=True) + RMS_EPS)
    od = od.reshape(nb, T, D_W) * gnorm.astype(jnp.float32)
    y = jnp.concatenate([oc * jax.nn.silu(gc.astype(jnp.float32)), od * jax.nn.silu(gd.astype(jnp.float32))], axis=-1)
    return y.astype(x.dtype) @ w_out


def trunk(x, t5_table, w_in_even, sink_a, w_out_even, w_in_odd, rpb_c, lb_d, gnorm_d, w_out_odd, ln_g, ln_b):
    lbs = jnp.cumsum(jax.nn.softmax(lb_d.astype(jnp.float32), axis=1), axis=1)
    lbs = lbs - lbs[:, :1]
    for l in range(DEPTH):
        if l % 2 == 0:
            j = l // 2
            y = even_layer(x, w_in_even[j], sink_a[j], w_out_even[j], t5_table)
        else:
            j = l // 2
            y = odd_layer(x, w_in_odd[j], rpb_c[j], lbs[:, l], gnorm_d[j], w_out_odd[j])
        x = layer_norm(ALPHA * x + y, ln_g[l], ln_b[l])
    return x


def setup_inputs(seed: int = 0) -> dict:
    key = jax.random.key(seed)
    ks = jax.random.split(key, 13)
    nrm = jax.random.normal
    return {
        "x_prompt": nrm(ks[0], (BATCH, SEQ, D_MODEL), jnp.float32),
        "x_sample": nrm(ks[1], (DEC_BATCH, DEC_SEQ, D_MODEL), jnp.float32),
        "t5_table": 0.2 * nrm(ks[2], (T5_BUCKETS, T5_HEADS), jnp.float32),
        "w_in_even": nrm(ks[3], (N_EVEN, D_MODEL, EVEN_IN), jnp.float32) * D_MODEL ** -0.5,
        "sink_a": 0.5 * nrm(ks[4], (N_EVEN, A_Q_HEADS), jnp.float32),
        "w_out_even": nrm(ks[5], (N_EVEN, EVEN_OUT, D_MODEL), jnp.float32) * (EVEN_OUT ** -0.5 * BETA),
        "w_in_odd": nrm(ks[6], (N_ODD, D_MODEL, ODD_IN), jnp.float32) * D_MODEL ** -0.5,
        "rpb_c": 0.2 * nrm(ks[7], (N_ODD, C_HEADS, 2 * NA_KH - 1, 2 * NA_KW - 1), jnp.float32),
        "lb_d": 0.5 * nrm(ks[8], (2, DEPTH, D_W), jnp.float32),
        "gnorm_d": 1.0 + 0.01 * nrm(ks[9], (N_ODD, D_W), jnp.float32),
        "w_out_odd": nrm(ks[10], (N_ODD, ODD_OUT, D_MODEL), jnp.float32) * (ODD_OUT ** -0.5 * BETA),
        "ln_g": 1.0 + 0.01 * nrm(ks[11], (DEPTH, D_MODEL), jnp.float32),
        "ln_b": 0.01 * nrm(ks[12], (DEPTH, D_MODEL), jnp.float32),
    }


def reference(x_prompt, x_sample, t5_table, w_in_even, sink_a, w_out_even, w_in_odd, rpb_c, lb_d, gnorm_d, w_out_odd, ln_g, ln_b):
    y_prompt = trunk(x_prompt, t5_table, w_in_even, sink_a, w_out_even, w_in_odd, rpb_c, lb_d, gnorm_d, w_out_odd, ln_g, ln_b)
    y_sample = trunk(x_sample, t5_table, w_in_even, sink_a, w_out_even, w_in_odd, rpb_c, lb_d, gnorm_d, w_out_odd, ln_g, ln_b)
    return (y_prompt, y_sample)
```

```python
import math
from contextlib import ExitStack

import numpy as np
import concourse.bass as bass
import concourse.mybir as mybir
from concourse.bass_utils import run_bass_kernel_spmd

F32 = mybir.dt.float32
BF16 = mybir.dt.bfloat16
ALU = mybir.AluOpType
AF = mybir.ActivationFunctionType
AX = mybir.AxisListType

D_MODEL = 1024
DEPTH = 2
ALPHA = (2.0 * DEPTH) ** 0.25
LN_EPS = 1e-5
RMS_EPS = 1e-6
NEG = -30000.0

QA, KA, VA, QB, KB, VB, GA, GB, EVEN_IN = 0, 512, 640, 768, 1536, 2304, 3072, 3584, 3840
QC, KC, VC, QD, IDD, ZF, ZB, GC, GD, ODD_IN = 0, 512, 1024, 1536, 2048, 2560, 3072, 3584, 4096, 4608

H0_VA, H0_VB, H0_GA, H0_GB, H0_ROW = 0, 130, 910, 1422, 1680
H1_VC, H1_ID, H1_GC, H1_GD, H1_ROW = 0, 520, 1032, 1544, 2056
OA_ROW = 1300


class Op:
    __slots__ = ("eng", "fn", "deps", "dma_sem", "need_inc", "val")

    def __init__(self, eng, fn):
        self.eng = eng
        self.fn = fn
        self.deps = []
        self.dma_sem = None
        self.need_inc = False
        self.val = 0


class Sched:
    ENGS = ("pe", "act", "dve", "pool", "sp")
    SAME_ENG_RAW = ("act", "dve", "pool")

    def __init__(self, nc, stack):
        self.nc = nc
        self.stack = stack
        self.ops = {e: [] for e in self.ENGS}
        self.sem = {e: stack.enter_context(nc.semaphore("sem_" + e)) for e in self.ENGS}
        self.lastw = {}
        self.readers = {}
        self.dsem = {}
        self.dcnt = {}
        self.key2idx = {}
        self.next_idx = {}

    def op(self, eng, fn, reads=(), writes=(), dma=None):
        o = Op(eng, fn)
        deps = []
        for k in reads:
            w = self.lastw.get(k)
            if w is not None:
                deps.append((w, True))
        for k in writes:
            w = self.lastw.get(k)
            if w is not None:
                deps.append((w, False))
            for r in self.readers.get(k, ()):
                deps.append((r, False))
        for (d, raw) in deps:
            if d.dma_sem is not None:
                o.deps.append(("dma", d.dma_sem, self.dcnt[d.dma_sem]))
                continue
            if d.eng == eng and dma is None:
                if eng not in self.SAME_ENG_RAW:
                    continue
            d.need_inc = True
            o.deps.append(("op", d))
        if dma is not None:
            if (eng, dma) not in self.key2idx:
                n_ = self.next_idx.get(eng, 0)
                self.next_idx[eng] = n_ + 1
                idx = (eng, n_)
                self.key2idx[(eng, dma)] = idx
                if idx not in self.dsem:
                    self.dsem[idx] = self.stack.enter_context(self.nc.semaphore("dsem_%s%d" % idx))
                    self.dcnt[idx] = 0
            idx = self.key2idx[(eng, dma)]
            self.dcnt[idx] += 16
            o.dma_sem = idx
        for k in reads:
            lst = self.readers.setdefault(k, [])
            if dma is None:
                for idx in range(len(lst)):
                    if lst[idx].dma_sem is None and lst[idx].eng == eng:
                        lst[idx] = o
                        break
                else:
                    lst.append(o)
            else:
                lst.append(o)
        for k in writes:
            self.lastw[k] = o
            self.readers[k] = []
        self.ops[eng].append(o)
        return o

    def I(self, eng, name, *a, reads=(), writes=(), dma=None, **kw):
        return self.op(eng, (name, a, kw), reads=reads, writes=writes, dma=dma)

    def barrier(self):
        lasts = {}
        for e in self.ENGS:
            for o in reversed(self.ops[e]):
                if o.fn is not None and o.dma_sem is None:
                    lasts[e] = o
                    break
        for e in self.ENGS:
            o = Op(e, None)
            for f, l in lasts.items():
                if f != e:
                    l.need_inc = True
                    o.deps.append(("op", l))
            for k in self.dsem:
                o.deps.append(("dma", k, self.dcnt[k]))
            self.ops[e].append(o)
        self.lastw = {}
        self.readers = {}
        self.key2idx = {}
        self.next_idx = {}

    def emit(self, block):
        for e in self.ENGS:
            c = 0
            for o in self.ops[e]:
                if o.need_inc:
                    c += 1
                    o.val = c
        sem = self.sem
        dsem = self.dsem

        def run(engname, engine):
            seen = {}
            for o in self.ops[engname]:
                need = {}
                for d in o.deps:
                    if d[0] == "dma":
                        k = ("d", d[1])
                        v = d[2]
                        s = dsem[d[1]]
                    else:
                        k = ("e", d[1].eng)
                        v = d[1].val
                        s = sem[d[1].eng]
                    if v > seen.get(k, 0) and v > need.get(k, (0, None))[0]:
                        need[k] = (v, s)
                for k, (v, s) in need.items():
                    engine.wait_ge(s, v)
                    seen[k] = v
                if o.fn is None:
                    continue
                name, a, kw = o.fn
                ins = getattr(engine, name)(*a, **kw)
                if o.dma_sem is not None:
                    ins.then_inc(dsem[o.dma_sem], 16)
                elif o.need_inc:
                    ins.then_inc(sem[engname], 1)

        block.tensor(lambda e: run("pe", e))
        block.scalar(lambda e: run("act", e))
        block.vector(lambda e: run("dve", e))
        block.gpsimd(lambda e: run("pool", e))
        block.sync(lambda e: run("sp", e))


class Ring:
    def __init__(self, nc, st, name, n, shape, dtype, psum=False):
        self.name = name
        self.n = n
        self.t = []
        for i in range(n):
            if psum:
                self.t.append(st.enter_context(nc.psum_tensor("%s%d" % (name, i), shape, dtype)))
            else:
                self.t.append(st.enter_context(nc.sbuf_tensor("%s%d" % (name, i), shape, dtype)))
        self.i = 0

    def next(self):
        i = self.i % self.n
        self.i += 1
        return self.t[i], (self.name, i)

    def at(self, i):
        i = i % self.n
        return self.t[i], (self.name, i)

    def can_next(self):
        if not hasattr(self, "busy"):
            self.busy = [False] * self.n
        return not self.busy[self.i % self.n]

    def next_hold(self):
        if not hasattr(self, "busy"):
            self.busy = [False] * self.n
        i = self.i % self.n
        assert not self.busy[i], self.name
        self.busy[i] = True
        self.i += 1
        return self.t[i], (self.name, i), i

    def release(self, i):
        self.busy[i] = False


def _t5_buckets(rel):
    nb = 16
    ret = (rel > 0).astype(np.int64) * nb
    n = np.abs(rel)
    max_exact = nb // 2
    large = max_exact + (np.log(np.maximum(n, 1) / max_exact) / math.log(1024 / max_exact) * (nb - max_exact)).astype(np.int64)
    large = np.minimum(large, nb - 1)
    return ret + np.where(n < max_exact, n, large)


def _tables_l0(t5_table):
    k = np.arange(128)[:, None, None]
    j = np.arange(3)[None, :, None]
    q = np.arange(128)[None, None, :]
    rel = (j - 1) * 128 + k - q
    out = np.full((5, 128, 3, 4, 128), NEG, np.float32)
    specs = [(128, 1, [0, 1, 2, 3]), (128, 1, [4, 5, 6, 7]), (64, 1, [8, 9, 10, 11]), (64, 4, [12, 13, 14, 15]), (64, 16, [16, 17, 18, 19])]
    for s, (hw, d, heads) in enumerate(specs):
        valid = np.abs(rel) <= hw
        bk = _t5_buckets(rel * d)
        for hi, h in enumerate(heads):
            pos = ((hi % 2) * 2 + hi // 2) if s < 2 else hi
            out[s, :, :, pos, :] = np.where(valid, t5_table[bk, h], NEG)
    return out.reshape(5, 128, 1536)


def _tables_na(rpb):
    rpb = rpb.reshape(8, 15, 31)
    k = np.arange(128)[:, None]
    q = np.arange(128)[None, :]
    krl, kc = k // 64, k % 64
    qrl, qc = q // 64, q % 64
    sc = np.clip(qc - 8, 0, 48)
    colmask = (kc >= sc) & (kc < sc + 16)
    colidx = np.clip(kc - qc + 15, 0, 30)
    out = np.full((4, 128, 9, 2, 128), NEG, np.float32)
    for b in range(9):
        off = b - 3 if b < 7 else (-2 if b == 7 else 2)
        dr = 2 * off + krl - qrl
        valid = colmask & (np.abs(dr) <= 7)
        if b >= 7:
            valid = valid & (dr >= -4) & (dr <= 3)
        dri = np.clip(dr + 7, 0, 14)
        for h in range(8):
            vals = rpb[h][dri, colidx]
            out[h // 2, :, b, h % 2, :] = np.where(valid, vals, NEG)
    return out.reshape(4, 128, 9 * 256)


def na_neighbors(u, U):
    if U <= 4:
        return [(m, m - u + 3) for m in range(U)]
    if u == 0 or u == 1:
        return [(m, m - u + 3) for m in range(0, 4)]
    if u == U - 1 or u == U - 2:
        return [(m, m - u + 3) for m in range(U - 4, U)]
    res = []
    for off in (-2, -1, 0, 1, 2):
        m = u + off
        blk = 7 if off == -2 else (8 if off == 2 else off + 3)
        res.append((m, blk))
    return res


def build(seq_lens, debug=False, stop_after=99, lvl=9, skipA=False):
    nc = bass.Bass("TRN2", target_bir_lowering=False)
    TT = sum(seq_lens)
    seq_off = [sum(seq_lens[:i]) for i in range(len(seq_lens))]
    okind = "ExternalOutput" if debug else "Internal"

    def dram(name, shape, dt, kind):
        return nc.dram_tensor(name, shape, dt, kind=kind).ap()

    x_in = dram("x", [TT, D_MODEL], F32, "ExternalInput")
    y_out = dram("y", [TT, D_MODEL], F32, "ExternalOutput")
    w_in_even = dram("w_in_even", [D_MODEL, EVEN_IN], F32, "ExternalInput")
    w_out_even = dram("w_out_even", [768, D_MODEL], F32, "ExternalInput")
    w_in_odd = dram("w_in_odd", [D_MODEL, ODD_IN], F32, "ExternalInput")
    w_out_odd = dram("w_out_odd", [1024, D_MODEL], F32, "ExternalInput")
    tab0_in = dram("tab0", [5, 128, 1536], F32, "ExternalInput")
    tabn_in = dram("tabn", [4, 128, 2304], F32, "ExternalInput")
    sink_in = dram("sink", [128, 8], F32, "ExternalInput")
    lb_in = dram("lb", [128, 16], F32, "ExternalInput")
    gn_in = dram("gnorm", [128, 512], F32, "ExternalInput")
    lng_in = dram("lng", [128, 2048], F32, "ExternalInput")
    lnb_in = dram("lnb", [128, 2048], F32, "ExternalInput")
    cst_in = dram("cst", [128, 1280], F32, "ExternalInput")

    h0T = dram("h0T", [2176, TT], BF16, okind)
    h0tm = dram("h0tm", [TT, H0_ROW], BF16, okind)
    oatt = dram("oatt", [TT, OA_ROW], F32, okind)
    x1s = dram("x1s", [TT, D_MODEL], F32, okind)
    h1Tb = dram("h1Tb", [1024, TT], BF16, okind)
    h1Tf = dram("h1Tf", [1536, TT], F32, okind)
    h1tm = dram("h1tm", [TT, H1_ROW], BF16, okind)
    ocs = dram("ocs", [TT, 520], F32, okind)
    odf = dram("odf", [TT, 512], F32, okind)
    odb = dram("odb", [TT, 512], F32, okind)

    with ExitStack() as top:
        S = Sched(nc, top)
        sb = lambda st, name, shape, dt: st.enter_context(nc.sbuf_tensor(name, shape, dt))
        cyc = {"i": 0}

        def rr(engs):
            cyc["i"] += 1
            return engs[cyc["i"] % len(engs)]

        def copy_op(eng, out, in_, reads, writes):
            if eng == "act":
                S.I("act", "copy", out=out, in_=in_, reads=reads, writes=writes)
            else:
                S.I(eng, "tensor_copy", out=out, in_=in_, reads=reads, writes=writes)

        cstf = sb(top, "cstf", [128, 1280], F32)
        ident = sb(top, "ident", [128, 128], BF16)
        S.I("sp", "dma_start", out=cstf[:], in_=cst_in, writes=["cstf"], dma="cstf")
        S.I("dve", "tensor_copy", out=ident[:], in_=cstf[:, 0:128], reads=["cstf"], writes=["ident"])
        trilf = cstf[0:64, 128:384]
        trilb = cstf[0:64, 384:640]
        scanmask = cstf[:, 640:1152]

        def load_weight(dst, src, KCn, F, stage_ring, pieces=None):
            if pieces is None:
                pieces = [(c0, min(1024, F - c0), c0) for c0 in range(0, F, 1024)]
            for kc in range(KCn):
                for (c0, cw, d0) in pieces:
                    stg, sk = stage_ring.next()
                    S.I("sp", "dma_start", out=stg[:, 0:cw], in_=src[kc * 128:(kc + 1) * 128, c0:c0 + cw], writes=[sk], dma=sk)
                    copy_op(rr(["act", "dve"]), dst[:, kc, d0:d0 + cw], stg[:, 0:cw], [sk], [("w", id(dst), kc, d0)])

        def make_xT(xs, xs_k, xb_ring, pT_ring, xT, xT_k, tcol):
            xb, xb_k = xb_ring.next()
            copy_op(rr(["dve", "act"]), xb[:], xs[:], [xs_k], [xb_k])
            pT, pT_k = pT_ring.next()
            for c in range(8):
                S.I("pe", "transpose", out=pT[:, c * 128:(c + 1) * 128], in_=xb[:, c * 128:(c + 1) * 128], identity=ident[:],
                    reads=[xb_k, "ident"], writes=[pT_k])
            copy_op(rr(["act", "dve"]), xT[:, :, tcol:tcol + 128], pT[:].rearrange("p (c t) -> p c t", c=8), [pT_k], [xT_k])

        def project(xT, xT_k, wb, acc_r, fm_r, fm_list, tm_r, tm_groups, t0, tm_dst, ntt=4, tts=None):
            ncols = ntt * 128
            for (f0, dst, ring) in fm_list:
                acc, acc_k = acc_r.next()
                for c in range(8):
                    S.I("pe", "matmul", acc[:, 0:ncols], lhsT=wb[:, c, f0:f0 + 128], rhs=xT[:, c, 0:ncols], start=(c == 0), stop=(c == 7),
                        reads=[xT_k], writes=[acc_k])
                fm, fm_k = ring.next()
                copy_op(rr(["act", "dve"]), fm[:, 0:ncols], acc[:, 0:ncols], [acc_k], [fm_k])
                S.I("pool", "dma_start", out=dst, in_=fm[:, 0:ncols], reads=[fm_k], dma=fm_k)
            for tt in (range(ntt) if tts is None else tts):
                tm, tm_k = tm_r.next()
                for grp in tm_groups:
                    if len(grp) == 5:
                        wc, width, sc, nh, func = grp
                        plist = [(0, width, sc, nh, func)]
                    else:
                        wc, width, plist = grp
                    acc, acc_k = acc_r.next()
                    for c in range(8):
                        S.I("pe", "matmul", acc[:, 0:width], lhsT=xT[:, c, tt * 128:(tt + 1) * 128], rhs=wb[:, c, wc:wc + width], start=(c == 0), stop=(c == 7),
                            reads=[xT_k], writes=[acc_k])
                    for (o, w, sc, nh, func) in plist:
                        if nh > 0:
                            copy_op(rr(["act", "dve"]), tm[:, sc:sc + nh * 65].rearrange("p (h d) -> p h d", d=65)[:, :, 0:64],
                                    acc[:, o:o + w].rearrange("p (h d) -> p h d", d=64), [acc_k], [tm_k])
                        elif func is not None:
                            S.I("act", "activation", out=tm[:, sc:sc + w], in_=acc[:, o:o + w], func=func, reads=[acc_k], writes=[tm_k])
                        else:
                            copy_op(rr(["act", "dve"]), tm[:, sc:sc + w], acc[:, o:o + w], [acc_k], [tm_k])
                r0 = t0 + tt * 128
                S.I("pool", "dma_start", out=tm_dst[r0:r0 + 128, :], in_=tm[:], reads=[tm_k], dma=tm_k)

        with ExitStack() as st:
            wb = sb(st, "wb0", [128, 8, EVEN_IN], BF16)
            stage = Ring(nc, st, "wstg", 3, [128, 1024], F32)
            load_weight(wb, w_in_even, 8, EVEN_IN, stage,
                        pieces=[(0, 640, 0), (768, 1024, 640), (1792, 512, 1664), (640, 128, 2176), (2304, 1024, 2304), (3328, 512, 3328)])
            xs_r = Ring(nc, st, "xs", 4, [128, 1024], F32)
            xb_r = Ring(nc, st, "xb", 2, [128, 1024], BF16)
            pT_r = Ring(nc, st, "pT", 2, [128, 1024], BF16, psum=True)
            xT_r = Ring(nc, st, "xT", 2, [128, 8, 512], BF16)
            acc_r = Ring(nc, st, "acc", 4, [128, 512], F32, psum=True)
            fm_r = Ring(nc, st, "fm", 4, [128, 512], BF16)
            tm_r = Ring(nc, st, "tm", 2, [128, H0_ROW], BF16)
            for i in range(2):
                t, k = tm_r.at(i)
                S.I("pool", "memset", t[:], 1.0, writes=[k])
            fm_feats = [i * 128 for i in range(17)]
            tm_groups = [
                (2176, 512, [(0, 128, H0_VA, 2, None), (128, 384, H0_VB, 6, None)]),
                (2688, 512, [(0, 384, H0_VB + 6 * 65, 6, None), (384, 128, H0_GA, 0, AF.Silu)]),
                (3200, 512, [(0, 384, H0_GA + 128, 0, AF.Silu), (384, 128, H0_GB, 0, AF.Silu)]),
                (3712, 128, [(0, 128, H0_GB + 128, 0, AF.Silu)]),
            ]
            for b in range(TT // 512):
                t0 = b * 512
                xT, xT_k = xT_r.next()
                for tt in range(4):
                    xs, xs_k = xs_r.next()
                    r0 = t0 + tt * 128
                    S.I("sp", "dma_start", out=xs[:], in_=x_in[r0:r0 + 128, :], writes=[xs_k], dma=xs_k)
                    make_xT(xs, xs_k, xb_r, pT_r, xT, xT_k, tt * 128)
                fm_list = [(f0, h0T[fi * 128:(fi + 1) * 128, t0:t0 + 512], fm_r) for fi, f0 in enumerate(fm_feats)]
                project(xT, xT_k, wb, acc_r, fm_r, fm_list, tm_r, tm_groups, t0, h0tm)
        S.barrier()

        h0tm_h = h0tm.tensor
        oatt_h = oatt.tensor

        def attn_group(st, E0, gname, d, quadsets, qrow, krow, vcol, VW, ocol, is_a):
            W = max(512, 128 * d)
            tpw = W // (128 * d)
            CQ = 4 if is_a else 2
            nring = 4 if d < 16 else 3
            Qr = Ring(nc, st, gname + "Q", 2, [128, CQ, W], BF16)
            Kr = Ring(nc, st, gname + "K", nring, [128, 4, W], BF16)
            Vr = Ring(nc, st, gname + "V", nring, [128, tpw * d, VW], BF16)
            psr = Ring(nc, st, gname + "ps", 2, [128, 1536], F32, psum=True)
            por = Ring(nc, st, gname + "po", 2, [128, 512], F32, psum=True)
            esr = Ring(nc, st, gname + "es", 2, [128, 1536], BF16)
            ptr = Ring(nc, st, gname + "pt", 2, [128, 1536], BF16)
            osr = Ring(nc, st, gname + "os", 3, [128, 260], F32)
            for i in range(nring):
                Kw, Kk = Kr.at(i)
                S.I("pool", "memset", Kw[:], 0.0, writes=[Kk])
            for si, T in enumerate(seq_lens):
                so = seq_off[si]
                NW = T // W
                NT = T // (128 * d)

                def load_kv(wn):
                    Kw, Kk = Kr.at(wn)
                    Vw, Vk = Vr.at(wn)
                    c0 = so + wn * W
                    for v in range(4):
                        if is_a:
                            g, half = v // 2, v % 2
                            src = h0T[krow + g * 64:krow + (g + 1) * 64, c0:c0 + W]
                        else:
                            half = v % 2
                            src = h0T[krow + v * 64:krow + (v + 1) * 64, c0:c0 + W]
                        S.I("sp", "dma_start", out=Kw[half * 64:(half + 1) * 64, v, :], in_=src, writes=[Kk], dma=Kk)
                    if d == 1:
                        src = bass.AP(h0tm_h, c0 * H0_ROW + vcol, [[H0_ROW, 128], [128 * H0_ROW, tpw], [1, VW]])
                    else:
                        src = bass.AP(h0tm_h, c0 * H0_ROW + vcol, [[d * H0_ROW, 128], [H0_ROW, d], [1, VW]])
                    S.I("sp", "dma_start", out=Vw[:], in_=src, writes=[Vk], dma=Vk)

                def stage2(n, r, js, qi, pt, ptk, tt, so=so, NT=NT):
                    po, pok = por.next()
                    for hh in range(4):
                        for ji, j in enumerate(js):
                            n2 = n + j - 1
                            Vw, Vk = Vr.at(n2 // tpw)
                            t2 = n2 % tpw
                            vi = t2 if d == 1 else r
                            vc0 = (qi * 65) if is_a else hh * 65
                            pos = ((hh % 2) * 2 + hh // 2) if is_a else hh
                            o0 = (j * 4 + pos) * 128
                            S.I("pe", "matmul", po[:, hh * 65:hh * 65 + 65], lhsT=pt[:, o0:o0 + 128], rhs=Vw[:, vi, vc0:vc0 + 65], start=(ji == 0), stop=(ji == len(js) - 1),
                                reads=[ptk, Vk], writes=[pok])
                    osb, osk = osr.next()
                    copy_op("dve", osb[:], po[:, 0:260], [pok], [osk])
                    row0 = so + n * 128 * d + r
                    dst = bass.AP(oatt_h, row0 * OA_ROW + ocol + (qi * 260 if is_a else 0), [[d * OA_ROW, 128], [1, 260]])
                    S.I("pool", "dma_start", out=dst, in_=osb[:], reads=[osk], dma=osk)

                load_kv(0)
                for wn in range(NW):
                    Qw, Qk = Qr.at(wn)
                    c0 = so + wn * W
                    S.I("sp", "dma_start", out=Qw[:], in_=h0T[qrow:qrow + CQ * 128, c0:c0 + W].rearrange("(c p) t -> p c t", p=128), writes=[Qk], dma=Qk)
                    if wn + 1 < NW:
                        load_kv(wn + 1)
                    pending = None
                    for tt in range(tpw):
                        for r in range(d):
                            n = wn * tpw + tt
                            js = [j for j in range(3) if 0 <= n + j - 1 < NT]
                            qcols = slice(tt * 128 * d + r, tt * 128 * d + r + 127 * d + 1, d)
                            for qi, qs in enumerate(quadsets):
                                ps, psk = psr.next()
                                for j in js:
                                    n2 = n + j - 1
                                    Kw, Kk = Kr.at(n2 // tpw)
                                    t2 = n2 % tpw
                                    kcols = slice(t2 * 128 * d + r, t2 * 128 * d + r + 127 * d + 1, d)
                                    if is_a:
                                        for half in range(2):
                                            out_ap = ps[:, (j * 4 + half * 2) * 128:(j * 4 + half * 2 + 2) * 128]
                                            S.I("pe", "matmul", out_ap, lhsT=Kw[:, qi * 2 + half, kcols], rhs=Qw[:, qi * 2:qi * 2 + 2, qcols], start=True, stop=True,
                                                reads=[Kk, Qk], writes=[psk])
                                    else:
                                        for hh in range(4):
                                            o0 = (j * 4 + hh) * 128
                                            S.I("pe", "matmul", ps[:, o0:o0 + 128], lhsT=Kw[:, hh, kcols], rhs=Qw[:, hh // 2, qcols], start=True, stop=True, reads=[Kk, Qk], writes=[psk])
                                es, esk = esr.next()
                                pt, ptk = ptr.next()
                                for j in js:
                                    S.I("act", "activation", out=es[:, j * 512:(j + 1) * 512], in_=ps[:, j * 512:(j + 1) * 512], func=AF.Exp, scale=0.125,
                                        reads=[psk], writes=[esk])
                                for j in js:
                                    S.I("dve", "tensor_tensor", out=pt[:, j * 512:(j + 1) * 512], in0=es[:, j * 512:(j + 1) * 512], in1=E0[:, qs, j * 512:(j + 1) * 512], op=ALU.mult,
                                        reads=[esk, "E0"], writes=[ptk])
                                unit = (n, r, js, qi, pt, ptk, tt)
                                if pending is not None:
                                    stage2(*pending)
                                pending = unit
                    if pending is not None:
                        stage2(*pending)
                        pending = None

        if stop_after >= 2:
            with ExitStack() as st:
                E0 = sb(st, "E0", [128, 5, 1536], BF16)
                tstage = Ring(nc, st, "tstg", 2, [128, 1536], F32)
                for qs in range(5):
                    stg, sk = tstage.next()
                    S.I("sp", "dma_start", out=stg[:], in_=tab0_in[qs], writes=[sk], dma=sk)
                    S.I("act", "activation", out=E0[:, qs, :], in_=stg[:], func=AF.Exp, reads=[sk], writes=["E0"])
                with ExitStack() as st2:
                    if not skipA:
                        attn_group(st2, E0, "A", 1, [0, 1], 0, 512, H0_VA, 130, 0, True)
                S.barrier()
                for gi, d in enumerate((1, 4, 16)):
                    if stop_after < 3 + gi:
                        continue
                    with ExitStack() as st2:
                        attn_group(st2, E0, "B%d" % gi, d, [2 + gi], 640 + gi * 256, 1408 + gi * 256, H0_VB + gi * 260, 260, 520 + gi * 260, False)
                    S.barrier()

        def run_skewed(gens):
            active = []
            it = iter(gens)
            done = False
            while True:
                if not done:
                    try:
                        active.append(next(it))
                    except StopIteration:
                        done = True
                if done and not active:
                    break
                for g in list(active):
                    try:
                        next(g)
                    except StopIteration:
                        active.remove(g)

        def layer_norm_gen(zps, zk, xres, xres_k, lng, lnb, v_r, st_r, junk, x1_r, dst):
            v, vk = v_r.next()
            for hf in range(2):
                S.I("dve", "scalar_tensor_tensor", out=v[:, hf * 512:(hf + 1) * 512], in0=xres[:, hf * 512:(hf + 1) * 512], scalar=ALPHA, in1=zps[hf][:],
                    op0=ALU.mult, op1=ALU.add, reads=[xres_k, zk[hf]], writes=[vk])
            stt, stk = st_r.next()
            S.I("dve", "tensor_reduce", out=stt[:, 0:1], in_=v[:], axis=AX.X, op=ALU.add, reads=[vk], writes=[(stk, 0)])
            S.I("act", "activation", out=junk[:], in_=v[:], func=AF.Square, accum_out=stt[:, 1:2], reads=[vk], writes=["junk", (stk, 1)])
            yield
            S.I("dve", "tensor_scalar", out=stt[:, 2:3], in0=stt[:, 0:1], scalar1=1.0 / 1024, scalar2=None, op0=ALU.mult, reads=[(stk, 0)], writes=[(stk, 2)])
            S.I("dve", "tensor_tensor", out=stt[:, 3:4], in0=stt[:, 2:3], in1=stt[:, 2:3], op=ALU.mult, reads=[(stk, 2)], writes=[(stk, 3)])
            yield
            S.I("dve", "scalar_tensor_tensor", out=stt[:, 4:5], in0=stt[:, 1:2], scalar=1.0 / 1024, in1=stt[:, 3:4], op0=ALU.mult, op1=ALU.subtract,
                reads=[(stk, 1), (stk, 3)], writes=[(stk, 4)])
            yield
            S.I("dve", "tensor_scalar", out=stt[:, 4:5], in0=stt[:, 4:5], scalar1=LN_EPS, scalar2=None, op0=ALU.add, reads=[(stk, 4)], writes=[(stk, 4)])
            S.I("act", "activation", out=stt[:, 6:7], in_=stt[:, 4:5], func=AF.Sqrt, reads=[(stk, 4)], writes=[(stk, 6)])
            yield
            S.I("dve", "reciprocal", out=stt[:, 5:6], in_=stt[:, 6:7], reads=[(stk, 6)], writes=[(stk, 5)])
            S.I("dve", "scalar_tensor_tensor", out=v[:], in0=v[:], scalar=stt[:, 2:3], in1=lng, op0=ALU.subtract, op1=ALU.mult,
                reads=[vk, (stk, 2), "ln"], writes=[vk])
            yield
            x1, x1k = x1_r.next()
            S.I("dve", "scalar_tensor_tensor", out=x1[:], in0=v[:], scalar=stt[:, 5:6], in1=lnb, op0=ALU.mult, op1=ALU.add,
                reads=[vk, (stk, 5), "ln"], writes=[x1k])
            S.I("pool", "dma_start", out=dst, in_=x1[:], reads=[x1k], dma=x1k)

        def layer_norm_tile(zps, zk, xres, xres_k, lng, lnb, v_r, st_r, junk, x1_r):
            v, vk = v_r.next()
            for hf in range(2):
                S.I("dve", "scalar_tensor_tensor", out=v[:, hf * 512:(hf + 1) * 512], in0=xres[:, hf * 512:(hf + 1) * 512], scalar=ALPHA, in1=zps[hf][:],
                    op0=ALU.mult, op1=ALU.add, reads=[xres_k, zk[hf]], writes=[vk])
            stt, stk = st_r.next()
            S.I("dve", "tensor_reduce", out=stt[:, 0:1], in_=v[:], axis=AX.X, op=ALU.add, reads=[vk], writes=[(stk, 0)])
            S.I("act", "activation", out=junk[:], in_=v[:], func=AF.Square, accum_out=stt[:, 1:2], reads=[vk], writes=["junk", (stk, 1)])
            S.I("dve", "tensor_scalar", out=stt[:, 2:3], in0=stt[:, 0:1], scalar1=1.0 / 1024, scalar2=None, op0=ALU.mult, reads=[(stk, 0)], writes=[(stk, 2)])
            S.I("dve", "tensor_tensor", out=stt[:, 3:4], in0=stt[:, 2:3], in1=stt[:, 2:3], op=ALU.mult, reads=[(stk, 2)], writes=[(stk, 3)])
            S.I("dve", "scalar_tensor_tensor", out=stt[:, 4:5], in0=stt[:, 1:2], scalar=1.0 / 1024, in1=stt[:, 3:4], op0=ALU.mult, op1=ALU.subtract,
                reads=[(stk, 1), (stk, 3)], writes=[(stk, 4)])
            S.I("dve", "tensor_scalar", out=stt[:, 4:5], in0=stt[:, 4:5], scalar1=LN_EPS, scalar2=None, op0=ALU.add, reads=[(stk, 4)], writes=[(stk, 4)])
            S.I("act", "activation", out=stt[:, 6:7], in_=stt[:, 4:5], func=AF.Sqrt, reads=[(stk, 4)], writes=[(stk, 6)])
            S.I("dve", "reciprocal", out=stt[:, 5:6], in_=stt[:, 6:7], reads=[(stk, 6)], writes=[(stk, 5)])
            S.I("dve", "scalar_tensor_tensor", out=v[:], in0=v[:], scalar=stt[:, 2:3], in1=lng, op0=ALU.subtract, op1=ALU.mult,
                reads=[vk, (stk, 2), "ln"], writes=[vk])
            x1, x1k = x1_r.next()
            S.I("dve", "scalar_tensor_tensor", out=x1[:], in0=v[:], scalar=stt[:, 5:6], in1=lnb, op0=ALU.mult, op1=ALU.add,
                reads=[vk, (stk, 5), "ln"], writes=[x1k])
            return x1, x1k

        if stop_after >= 6:
            with ExitStack() as st:
                wo = sb(st, "wo0", [128, 6, 1024], BF16)
                w1 = sb(st, "w1", [128, 8, ODD_IN], BF16)
                with ExitStack() as st2:
                    stage = Ring(nc, st2, "wstg1", 3, [128, 1024], F32)
                    load_weight(wo, w_out_even, 6, 1024, stage)
                    load_weight(w1, w_in_odd, 8, ODD_IN, stage)
                    S.barrier()
                lnt = sb(st, "lnt", [128, 2, 1024], F32)
                S.I("sp", "dma_start", out=lnt[:, 0, :], in_=lng_in[:, 0:1024], writes=["ln"], dma="ln")
                S.I("sp", "dma_start", out=lnt[:, 1, :], in_=lnb_in[:, 0:1024], writes=["ln"], dma="ln")
                esink = sb(st, "esink", [128, 8], F32)
                S.I("sp", "dma_start", out=esink[:], in_=sink_in, writes=["esink"], dma="esink")
                S.I("act", "activation", out=esink[:], in_=esink[:], func=AF.Exp, reads=["esink"], writes=["esink"])
                oa_r = Ring(nc, st, "oa", 3, [128, OA_ROW], F32)
                g_r = Ring(nc, st, "gt", 3, [128, 768], BF16)
                xs_r = Ring(nc, st, "xs2", 3, [128, 1024], F32)
                v_r = Ring(nc, st, "vln", 4, [128, 1024], F32)
                x1_r = Ring(nc, st, "x1t", 2, [128, 1024], F32)
                st_r = Ring(nc, st, "stt", 6, [128, 8], F32)
                sm_r = Ring(nc, st, "smt", 4, [128, 16], F32)
                tmp_r = Ring(nc, st, "tmpc", 2, [128, 768], F32)
                junk = sb(st, "junk", [128, 1024], BF16)
                y_r = Ring(nc, st, "yt", 2, [128, 768], BF16)
                yT_r = Ring(nc, st, "yT", 2, [128, 6, 128], BF16)
                xb_r = Ring(nc, st, "xb2", 2, [128, 1024], BF16)
                xT_r = Ring(nc, st, "xT2", 2, [128, 8, 512], BF16)
                fmb_r = Ring(nc, st, "fmb", 4, [128, 512], BF16)
                fmf_r = Ring(nc, st, "fmf", 4, [128, 512], F32)
                tm_r = Ring(nc, st, "tm1", 2, [128, H1_ROW], BF16)
                pT_r = Ring(nc, st, "pT2", 2, [128, 1024], BF16, psum=True)
                z_r = Ring(nc, st, "zps", 4, [128, 512], F32, psum=True)
                acc_r = Ring(nc, st, "acc2", 2, [128, 512], F32, psum=True)
                for i in range(2):
                    t, k = tm_r.at(i)
                    S.I("pool", "memset", t[:], 1.0, writes=[k])
                fm_cols = [(QC + c * 128, h1Tb, c, fmb_r) for c in range(4)] + [(KC + c * 128, h1Tb, 4 + c, fmb_r) for c in range(4)] + \
                          [(QD + c * 128, h1Tf, c, fmf_r) for c in range(4)] + [(ZF + c * 128, h1Tf, 4 + c, fmf_r) for c in range(4)] + \
                          [(ZB + c * 128, h1Tf, 8 + c, fmf_r) for c in range(4)]
                tm_groups1 = [(VC, 512, H1_VC, 8, None), (IDD, 512, H1_ID, 0, None), (GC, 512, H1_GC, 0, AF.Silu), (GD, 512, H1_GD, 0, AF.Silu)]
                NBLK = TT // 512
                tiles_done = [0] * NBLK
                proj_done = [False] * NBLK
                blk_xT = {}

                def acquire(ring):
                    while not ring.can_next():
                        yield
                    return ring.next_hold()

                def tile_gen2(b, tt):
                    t0 = b * 512
                    r0 = t0 + tt * 128
                    oa, oak, oai = yield from acquire(oa_r)
                    S.I("sp", "dma_start", out=oa[:], in_=oatt[r0:r0 + 128, :], writes=[oak], dma=oak)
                    gt, gk, gi_ = yield from acquire(g_r)
                    S.I("sp", "dma_start", out=gt[:], in_=h0tm[r0:r0 + 128, H0_GA:H0_GA + 768], writes=[gk], dma=gk)
                    xs, xsk, xsi = yield from acquire(xs_r)
                    S.I("sp", "dma_start", out=xs[:], in_=x_in[r0:r0 + 128, :], writes=[xsk], dma=xsk)
                    yield
                    sm, smk, smi = yield from acquire(sm_r)
                    oa3 = oa[:, 0:520].rearrange("p (h d) -> p h d", d=65)
                    S.I("dve", "tensor_tensor", out=sm[:, 0:8], in0=oa3[:, :, 64], in1=esink[:], op=ALU.add, reads=[oak, "esink"], writes=[(smk, 0)])
                    S.I("dve", "tensor_tensor", out=oa[:, 520:780], in0=oa[:, 520:780], in1=oa[:, 780:1040], op=ALU.add, reads=[oak], writes=[oak])
                    yield
                    S.I("dve", "reciprocal", out=sm[:, 0:8], in_=sm[:, 0:8], reads=[(smk, 0)], writes=[(smk, 0)])
                    S.I("dve", "tensor_tensor", out=oa[:, 520:780], in0=oa[:, 520:780], in1=oa[:, 1040:1300], op=ALU.add, reads=[oak], writes=[oak])
                    yield
                    tmp, tmpk = tmp_r.next()
                    y, yk = y_r.next()
                    ob3 = oa[:, 520:780].rearrange("p (h d) -> p h d", d=65)
                    S.I("dve", "tensor_tensor", out=tmp[:, 0:512].rearrange("p (h d) -> p h d", d=64), in0=oa3[:, :, 0:64],
                        in1=sm[:, 0:8].unsqueeze(2).broadcast_to([128, 8, 64]), op=ALU.mult, reads=[oak, (smk, 0)], writes=[(tmpk, 0)])
                    S.I("dve", "reciprocal", out=sm[:, 8:12], in_=ob3[:, :, 64], reads=[oak], writes=[(smk, 1)])
                    yield
                    S.I("dve", "tensor_tensor", out=tmp[:, 512:768].rearrange("p (h d) -> p h d", d=64), in0=ob3[:, :, 0:64],
                        in1=sm[:, 8:12].unsqueeze(2).broadcast_to([128, 4, 64]), op=ALU.mult, reads=[oak, (smk, 1)], writes=[(tmpk, 1)])
                    S.I("dve", "tensor_tensor", out=y[:], in0=tmp[:], in1=gt[:], op=ALU.mult, reads=[(tmpk, 0), (tmpk, 1), gk], writes=[yk])
                    oa_r.release(oai)
                    g_r.release(gi_)
                    sm_r.release(smi)
                    pT, pTk = pT_r.next()
                    for c in range(6):
                        S.I("pe", "transpose", out=pT[:, c * 128:(c + 1) * 128], in_=y[:, c * 128:(c + 1) * 128], identity=ident[:], reads=[yk, "ident"], writes=[pTk])
                    yT, yTk = yT_r.next()
                    copy_op("act", yT[:], pT[:, 0:768].rearrange("p (c t) -> p c t", c=6), [pTk], [yTk])
                    zps, zk, zis = [], [], []
                    for hf in range(2):
                        z, zkk, zi = yield from acquire(z_r)
                        for c in range(6):
                            S.I("pe", "matmul", z[:], lhsT=yT[:, c, :], rhs=wo[:, c, hf * 512:(hf + 1) * 512], start=(c == 0), stop=(c == 5), reads=[yTk], writes=[zkk])
                        zps.append(z)
                        zk.append(zkk)
                        zis.append(zi)
                    yield
                    v, vk, vi_ = yield from acquire(v_r)
                    for hf in range(2):
                        S.I("dve", "scalar_tensor_tensor", out=v[:, hf * 512:(hf + 1) * 512], in0=xs[:, hf * 512:(hf + 1) * 512], scalar=ALPHA, in1=zps[hf][:],
                            op0=ALU.mult, op1=ALU.add, reads=[xsk, zk[hf]], writes=[vk])
                    xs_r.release(xsi)
                    for zi in zis:
                        z_r.release(zi)
                    stt, stk, sti = yield from acquire(st_r)
                    S.I("dve", "tensor_reduce", out=stt[:, 0:1], in_=v[:], axis=AX.X, op=ALU.add, reads=[vk], writes=[(stk, 0)])
                    S.I("act", "activation", out=junk[:], in_=v[:], func=AF.Square, accum_out=stt[:, 1:2], reads=[vk], writes=["junk", (stk, 1)])
                    yield
                    S.I("dve", "tensor_scalar", out=stt[:, 2:3], in0=stt[:, 0:1], scalar1=1.0 / 1024, scalar2=None, op0=ALU.mult, reads=[(stk, 0)], writes=[(stk, 2)])
                    S.I("dve", "tensor_tensor", out=stt[:, 3:4], in0=stt[:, 2:3], in1=stt[:, 2:3], op=ALU.mult, reads=[(stk, 2)], writes=[(stk, 3)])
                    S.I("dve", "scalar_tensor_tensor", out=stt[:, 4:5], in0=stt[:, 1:2], scalar=1.0 / 1024, in1=stt[:, 3:4], op0=ALU.mult, op1=ALU.subtract,
                        reads=[(stk, 1), (stk, 3)], writes=[(stk, 4)])
                    S.I("dve", "tensor_scalar", out=stt[:, 4:5], in0=stt[:, 4:5], scalar1=LN_EPS, scalar2=None, op0=ALU.add, reads=[(stk, 4)], writes=[(stk, 4)])
                    S.I("act", "activation", out=stt[:, 6:7], in_=stt[:, 4:5], func=AF.Sqrt, reads=[(stk, 4)], writes=[(stk, 6)])
                    yield
                    S.I("dve", "reciprocal", out=stt[:, 5:6], in_=stt[:, 6:7], reads=[(stk, 6)], writes=[(stk, 5)])
                    S.I("dve", "scalar_tensor_tensor", out=v[:], in0=v[:], scalar=stt[:, 2:3], in1=lnt[:, 0, :], op0=ALU.subtract, op1=ALU.mult,
                        reads=[vk, (stk, 2), "ln"], writes=[vk])
                    yield
                    while b >= 2 and not proj_done[b - 2]:
                        yield
                    if b not in blk_xT:
                        blk_xT[b] = xT_r.next()
                    xT, xT_k = blk_xT[b]
                    x1, x1k = x1_r.next()
                    S.I("dve", "scalar_tensor_tensor", out=x1[:], in0=v[:], scalar=stt[:, 5:6], in1=lnt[:, 1, :], op0=ALU.mult, op1=ALU.add,
                        reads=[vk, (stk, 5), "ln"], writes=[x1k])
                    v_r.release(vi_)
                    st_r.release(sti)
                    S.I("pool", "dma_start", out=x1s[r0:r0 + 128, :], in_=x1[:], reads=[x1k], dma=x1k)
                    make_xT(x1, x1k, xb_r, pT_r, xT, xT_k, tt * 128)
                    tiles_done[b] += 1

                def proj_gen(b):
                    while tiles_done[b] < 4:
                        yield
                    t0 = b * 512
                    xT, xT_k = blk_xT[b]
                    fm_list = [(f0, dst[ci * 128:(ci + 1) * 128, t0:t0 + 512], ring) for (f0, dst, ci, ring) in fm_cols]
                    for i0 in range(0, len(fm_list), 5):
                        project(xT, xT_k, w1, acc_r, None, fm_list[i0:i0 + 5], tm_r, tm_groups1, t0, h1tm, ntt=4, tts=[])
                        yield
                    for tt in range(4):
                        project(xT, xT_k, w1, acc_r, None, [], tm_r, tm_groups1, t0, h1tm, ntt=4, tts=[tt])
                        yield
                    proj_done[b] = True

                gens = []
                for b in range(NBLK):
                    for tt in range(4):
                        gens.append(tile_gen2(b, tt))
                    gens.append(proj_gen(b))
                run_skewed(gens)
            S.barrier()

        h1tm_h = h1tm.tensor
        if stop_after >= 7:
            with ExitStack() as st:
                EN = sb(st, "EN", [128, 4, 2304], BF16)
                with ExitStack() as st2:
                    tstage = Ring(nc, st2, "nstg", 2, [128, 2304], F32)
                    for hp in range(4):
                        stg, sk = tstage.next()
                        S.I("sp", "dma_start", out=stg[:], in_=tabn_in[hp], writes=[sk], dma=sk)
                        S.I("act", "activation", out=EN[:, hp, :], in_=stg[:], func=AF.Exp, reads=[sk], writes=["EN"])
                    S.barrier()
                W = 512
                Qr = Ring(nc, st, "NQ", 2, [128, 4, W], BF16)
                Kr = Ring(nc, st, "NK", 4, [128, 8, W], BF16)
                Vr = Ring(nc, st, "NV", 4, [128, 4, 520], BF16)
                psr = Ring(nc, st, "Nps", 2, [128, 1536], F32, psum=True)
                por = Ring(nc, st, "Npo", 1, [128, 1024], F32, psum=True)
                esr = Ring(nc, st, "Nes", 2, [128, 1280], BF16)
                ptr = Ring(nc, st, "Npt", 2, [128, 1280], BF16)
                osr = Ring(nc, st, "Nos", 2, [128, 520], F32)
                for i in range(4):
                    Kw, Kk = Kr.at(i)
                    S.I("pool", "memset", Kw[:], 0.0, writes=[Kk])
                for si, T in enumerate(seq_lens):
                    so = seq_off[si]
                    NW = T // W
                    U = T // 128

                    def load_kv_n(wn):
                        Kw, Kk = Kr.at(wn)
                        Vw, Vk = Vr.at(wn)
                        c0 = so + wn * W
                        for h in range(8):
                            half = h % 2
                            S.I("sp", "dma_start", out=Kw[half * 64:(half + 1) * 64, h, :], in_=h1Tb[512 + h * 64:512 + (h + 1) * 64, c0:c0 + W], writes=[Kk], dma=Kk)
                        src = bass.AP(h1tm_h, c0 * H1_ROW + H1_VC, [[H1_ROW, 128], [128 * H1_ROW, 4], [1, 520]])
                        S.I("sp", "dma_start", out=Vw[:], in_=src, writes=[Vk], dma=Vk)

                    def na_stage2(u, hp, neigh, pt, ptk, po, pok, so=so):
                        NJ = len(neigh)
                        for hl in range(2):
                            h = 2 * hp + hl
                            pc = (h // 4) * 512 + (h % 4) * 65
                            for j, (m, blk) in enumerate(neigh):
                                Vw, Vk = Vr.at(m // 4)
                                o0 = (j * 2 + hl) * 128
                                S.I("pe", "matmul", po[:, pc:pc + 65], lhsT=pt[:, o0:o0 + 128], rhs=Vw[:, m % 4, h * 65:(h + 1) * 65], start=(j == 0), stop=(j == NJ - 1),
                                    reads=[ptk, Vk], writes=[pok])
                        if hp == 3:
                            osb, osk = osr.next()
                            copy_op("act", osb[:, 0:260], po[:, 0:260], [pok], [osk])
                            copy_op("dve", osb[:, 260:520], po[:, 512:772], [pok], [osk])
                            r0 = so + u * 128
                            S.I("pool", "dma_start", out=ocs[r0:r0 + 128, :], in_=osb[:], reads=[osk], dma=osk)

                    load_kv_n(0)
                    for wn in range(NW):
                        Qw, Qk = Qr.at(wn)
                        c0 = so + wn * W
                        S.I("sp", "dma_start", out=Qw[:], in_=h1Tb[0:512, c0:c0 + W].rearrange("(c p) t -> p c t", p=128), writes=[Qk], dma=Qk)
                        if wn + 1 < NW:
                            load_kv_n(wn + 1)
                        pending = None
                        for tt in range(4):
                            u = wn * 4 + tt
                            neigh = na_neighbors(u, U)
                            NJ = len(neigh)
                            po, pok = por.next()
                            for hp in range(4):
                                ps, psk = psr.next()
                                for j, (m, blk) in enumerate(neigh):
                                    Kw, Kk = Kr.at(m // 4)
                                    for hl in range(2):
                                        h = 2 * hp + hl
                                        o0 = (j * 2 + hl) * 128
                                        S.I("pe", "matmul", ps[:, o0:o0 + 128], lhsT=Kw[:, h, (m % 4) * 128:(m % 4 + 1) * 128], rhs=Qw[:, hp, tt * 128:(tt + 1) * 128],
                                            start=True, stop=True, reads=[Kk, Qk], writes=[psk])
                                es, esk = esr.next()
                                pt, ptk = ptr.next()
                                ncol = NJ * 256
                                for c0_ in range(0, ncol, 512):
                                    c1_ = min(ncol, c0_ + 512)
                                    S.I("act", "activation", out=es[:, c0_:c1_], in_=ps[:, c0_:c1_], func=AF.Exp, scale=0.125, reads=[psk], writes=[esk])
                                for j, (m, blk) in enumerate(neigh):
                                    S.I("dve", "tensor_tensor", out=pt[:, j * 256:(j + 1) * 256], in0=es[:, j * 256:(j + 1) * 256], in1=EN[:, hp, blk * 256:(blk + 1) * 256], op=ALU.mult,
                                        reads=[esk, "EN"], writes=[ptk])
                                unit = (u, hp, neigh, pt, ptk, po, pok)
                                if pending is not None:
                                    na_stage2(*pending)
                                pending = unit
                        if pending is not None:
                            na_stage2(*pending)
                            pending = None
            S.barrier()

        if stop_after >= 8:
            with ExitStack() as st:
                lbt = sb(st, "lbt", [128, 16], F32)
                lbc = sb(st, "lbc", [128, 5, 8], F32)
                S.I("sp", "dma_start", out=lbt[:], in_=lb_in, writes=["lbt"], dma="lbt")
                S.I("act", "activation", out=lbt[:], in_=lbt[:], func=AF.Exp, reads=["lbt"], writes=["lbt"])
                lb4 = lbt[:].rearrange("p (d l h) -> p d l h", d=2, l=2)
                lbv = lambda i: lbc[:, i, :].rearrange("p (d h) -> p d h", d=2)
                S.I("dve", "tensor_tensor", out=lbv(3), in0=lb4[:, :, 0, :], in1=lb4[:, :, 1, :], op=ALU.add, reads=["lbt"], writes=["lbc3"])
                S.I("dve", "reciprocal", out=lbv(3), in_=lbv(3), reads=["lbc3"], writes=["lbc3"])
                S.I("dve", "tensor_tensor", out=lbv(0), in0=lb4[:, :, 1, :], in1=lbv(3), op=ALU.mult, reads=["lbt", "lbc3"], writes=["lbc0"])
                S.I("dve", "tensor_scalar", out=lbc[:, 1, :], in0=lbc[:, 0, :], scalar1=-1.0, scalar2=1.0, op0=ALU.mult, op1=ALU.add, reads=["lbc0"], writes=["lbc1"])
                S.I("dve", "tensor_scalar", out=lbc[:, 2, :], in0=lbc[:, 0, :], scalar1=1.0, scalar2=-1.0, op0=ALU.mult, op1=ALU.add, reads=["lbc0"], writes=["lbc2"])
                S.I("act", "activation", out=lbc[:, 4, :], in_=lbc[:, 1, :], func=AF.Ln, reads=["lbc1"], writes=["lbc4"])
                LBK = ["lbc0", "lbc1", "lbc2", "lbc4"]

                class Dir:
                    pass

                dirs = []
                for dr in range(2):
                    D = Dir()
                    D.dr = dr
                    nm = "hf" if dr == 0 else "hb"
                    D.z_r = Ring(nc, st, nm + "z", 2, [128, 512], F32)
                    D.q_r = Ring(nc, st, nm + "q", 2, [128, 512], F32)
                    D.t_r = [Ring(nc, st, nm + "t%d" % i, 1, [128, 512], F32) for i in range(6)]
                    D.qh = Ring(nc, st, nm + "qh", 2, [128, 4, 512], BF16)
                    D.qt = Ring(nc, st, nm + "qt", 2, [128, 4, 512], BF16)
                    D.kh = Ring(nc, st, nm + "kh", 2, [128, 4, 512], BF16)
                    D.khT = Ring(nc, st, nm + "khT", 2, [128, 4, 8, 128], BF16)
                    D.dec = Ring(nc, st, nm + "dec", 2, [128, 4, 8], F32)
                    D.v = Ring(nc, st, nm + "v", 2, [128, 8, 512], BF16)
                    D.S = sb(st, nm + "S", [128, 4, 128], F32)
                    D.Sbf = sb(st, nm + "Sbf", [128, 4, 128], BF16)
                    D.Am = Ring(nc, st, nm + "Am", 2, [128, 256], BF16)
                    D.ob = Ring(nc, st, nm + "ob", 4, [64, 512], F32)
                    D.psA = Ring(nc, st, nm + "psA", 1, [128, 512], F32, psum=True)
                    D.po = Ring(nc, st, nm + "po", 1, [128, 512], F32, psum=True)
                    D.psS = Ring(nc, st, nm + "psS", 1, [128, 512], F32, psum=True)
                    D.tril = trilf if dr == 0 else trilb
                    D.odst = odf if dr == 0 else odb
                    D.zrow = 512 if dr == 0 else 1024
                    for i in range(2):
                        for ring in (D.khT, D.v, D.Am):
                            t, k = ring.at(i)
                            S.I("pool", "memset", t[:], 0.0, writes=[k])
                    dirs.append(D)
                pTk_r = Ring(nc, st, "hpT", 1, [128, 1024], BF16, psum=True)

                def prep_begin(D, so, b):
                    c0 = so + b * 512
                    qh, qhk = D.qh.next()
                    qt, qtk = D.qt.next()
                    kh, khk = D.kh.next()
                    khT, khTk = D.khT.next()
                    dec, deck = D.dec.next()
                    v, vk = D.v.next()
                    S.I("sp", "dma_start", out=v[0:64, :, :], in_=bass.AP(h1tm_h, c0 * H1_ROW + H1_ID, [[H1_ROW, 64], [64 * H1_ROW, 8], [1, 512]]), writes=[vk], dma=vk)
                    return dict(qh=qh, qhk=[(qhk, h) for h in range(4)], qt=qt, qtk=[(qtk, h) for h in range(4)], kh=kh, khk=[(khk, h) for h in range(4)],
                                khT=khT, khTk=[(khTk, h) for h in range(4)], dec=dec, deck=[(deck, h) for h in range(4)], v=v, vk=vk, c0=c0)

                def prep_head(D, P, h):
                    dr = D.dr
                    c0 = P["c0"]
                    qh, qt, kh, khT, dec = P["qh"], P["qt"], P["kh"], P["khT"], P["dec"]
                    qhk, qtk, khk, khTk, deck = P["qhk"][0][0], P["qtk"][0][0], P["khk"][0][0], P["khTk"][0][0], P["deck"][0][0]
                    if True:
                        z, zk = D.z_r.next()
                        q, qk = D.q_r.next()
                        S.I("sp", "dma_start", out=z[:], in_=h1Tf[D.zrow + h * 128:D.zrow + (h + 1) * 128, c0:c0 + 512], writes=[zk], dma=zk)
                        S.I("sp", "dma_start", out=q[:], in_=h1Tf[h * 128:(h + 1) * 128, c0:c0 + 512], writes=[qk], dma=qk)
                        T_ = [r_.next() for r_ in D.t_r]
                        (t0_, k0), (t1_, k1), (t2_, k2), (t3_, k3), (t4_, k4), (t5_, k5) = T_
                        lb_ap = lbc[:, 0, dr * 4 + h:dr * 4 + h + 1]
                        oml_ap = lbc[:, 1, dr * 4 + h:dr * 4 + h + 1]
                        noml_ap = lbc[:, 2, dr * 4 + h:dr * 4 + h + 1]
                        lnoml_ap = lbc[:, 4, dr * 4 + h:dr * 4 + h + 1]
                        v3 = lambda t: t[:].rearrange("p (c t) -> p c t", t=64)
                        S.I("act", "activation", out=t0_[:], in_=z[:], func=AF.Exp, scale=-1.0, reads=[zk], writes=[k0])
                        S.I("act", "activation", out=t1_[:], in_=t0_[:], func=AF.Ln, scale=1.0, bias=1.0, reads=[k0], writes=[k1])
                        S.I("act", "activation", out=t2_[:], in_=t0_[:], func=AF.Ln, scale=lb_ap, bias=1.0, reads=[k0] + LBK, writes=[k2])
                        S.I("dve", "tensor_tensor", out=t2_[:], in0=t2_[:], in1=t1_[:], op=ALU.subtract, reads=[k2, k1], writes=[k2])
                        S.I("dve", "tensor_tensor_scan", out=t3_[:], data0=scanmask, data1=t2_[:], initial=0.0, op0=ALU.mult, op1=ALU.add, reads=[k2, "cstf"], writes=[k3])
                        tot = v3(t3_)[:, :, 63]
                        totb = v3(t3_)[:, :, 63:64].broadcast_to([128, 8, 64])
                        if dr == 0:
                            S.I("dve", "tensor_tensor", out=v3(t4_), in0=v3(t3_), in1=totb, op=ALU.subtract, reads=[k3], writes=[k4])
                            bq, bqk = t3_, k3
                        else:
                            S.I("dve", "tensor_tensor", out=t4_[:], in0=t2_[:], in1=t3_[:], op=ALU.subtract, reads=[k2, k3], writes=[k4])
                            S.I("dve", "tensor_tensor", out=v3(t5_), in0=v3(t4_), in1=totb, op=ALU.add, reads=[k4, k3], writes=[k5])
                            bq, bqk = t5_, k5
                        S.I("dve", "tensor_tensor", out=t1_[:], in0=t1_[:], in1=z[:], op=ALU.add, reads=[k1, zk], writes=[k1])
                        S.I("dve", "tensor_tensor", out=t1_[:], in0=t1_[:], in1=t4_[:], op=ALU.add, reads=[k1, k4], writes=[k1])
                        S.I("act", "activation", out=dec[:, h, :], in_=tot, func=AF.Exp, reads=[k3], writes=[(deck, h)])
                        S.I("act", "activation", out=kh[:, h, :], in_=t1_[:], func=AF.Exp, scale=-1.0, bias=lnoml_ap, reads=[k1] + LBK, writes=[(khk, h)])
                        S.I("act", "activation", out=t0_[:], in_=t4_[:], func=AF.Exp, reads=[k4], writes=[k0])
                        S.I("dve", "tensor_tensor", out=qh[:, h, :], in0=q[:], in1=t0_[:], op=ALU.mult, reads=[qk, k0], writes=[(qhk, h)])
                        S.I("act", "activation", out=t2_[:], in_=bq[:], func=AF.Exp, reads=[bqk], writes=[k2])
                        S.I("dve", "tensor_tensor", out=qt[:, h, :], in0=q[:], in1=t2_[:], op=ALU.mult, reads=[qk, k2], writes=[(qtk, h)])
                        pT, pTk = pTk_r.next()
                        for c in range(8):
                            S.I("pe", "transpose", out=pT[0:64, c * 128:(c + 1) * 128], in_=kh[:, h, c * 64:(c + 1) * 64], identity=ident[:], reads=[(khk, h), "ident"], writes=[pTk])
                        copy_op("act", khT[0:64, h, :, :], pT[0:64, :].rearrange("p (c k) -> p c k", c=8), [pTk], [(khTk, h)])

                def chunk_a(D, P, so, b, c):
                    cs = slice(c * 64, (c + 1) * 64)
                    psA, psAk = D.psA.next()
                    for h in range(4):
                        S.I("pe", "matmul", psA[0:64, h * 64:(h + 1) * 64], lhsT=P["kh"][:, h, cs], rhs=P["qh"][:, h, cs], start=True, stop=True,
                            reads=[P["khk"][h], P["qhk"][h]], writes=[psAk])
                    Am, Amk = D.Am.next()
                    S.I("dve", "tensor_tensor", out=Am[0:64, :], in0=psA[0:64, 0:256], in1=D.tril, op=ALU.mult, reads=[psAk, "cstf"], writes=[Amk])
                    psS, psSk = D.psS.next()
                    for h in range(4):
                        S.I("pe", "matmul", psS[:, h * 128:(h + 1) * 128], lhsT=P["khT"][:, h, c, :], rhs=P["v"][:, c, h * 128:(h + 1) * 128], start=True, stop=True,
                            reads=[P["khTk"][h], P["vk"]], writes=[psSk])
                    return Am, Amk, psS, psSk

                def chunk_b(D, P, so, b, c, Am, Amk, psS, psSk):
                    nm = "hf" if D.dr == 0 else "hb"
                    cs = slice(c * 64, (c + 1) * 64)
                    po, pok = D.po.next()
                    for h in range(4):
                        S.I("pe", "matmul", po[0:64, h * 128:(h + 1) * 128], lhsT=P["qt"][:, h, cs], rhs=D.Sbf[:, h, :], start=True, stop=False,
                            reads=[P["qtk"][h], nm + "Sbf"], writes=[pok])
                        S.I("pe", "matmul", po[0:64, h * 128:(h + 1) * 128], lhsT=Am[:, h * 64:(h + 1) * 64], rhs=P["v"][:, c, h * 128:(h + 1) * 128], start=False, stop=True,
                            reads=[Amk, P["vk"]], writes=[pok])
                    for h in range(4):
                        S.I("dve", "scalar_tensor_tensor", out=D.S[:, h, :], in0=D.S[:, h, :], scalar=P["dec"][:, h, c:c + 1], in1=psS[:, h * 128:(h + 1) * 128],
                            op0=ALU.mult, op1=ALU.add, reads=[(nm + "S", h), P["deck"][h], psSk], writes=[(nm + "S", h)])
                    S.I("act", "copy", out=D.Sbf[:].rearrange("p h v -> p (h v)"), in_=D.S[:].rearrange("p h v -> p (h v)"), reads=[(nm + "S", h) for h in range(4)], writes=[nm + "Sbf"])
                    ob, obk = D.ob.next()
                    copy_op("act", ob[:], po[0:64, :], [pok], [obk])
                    r0 = so + b * 512 + c * 64
                    S.I("act", "dma_start", out=D.odst[r0:r0 + 64, :], in_=ob[:], reads=[obk], dma=obk)

                def chunk_pair(Pf, Pb, so, bf_, bb_, ci):
                    af = chunk_a(dirs[0], Pf, so, bf_, ci)
                    ab = chunk_a(dirs[1], Pb, so, bb_, 7 - ci)
                    chunk_b(dirs[0], Pf, so, bf_, ci, *af)
                    chunk_b(dirs[1], Pb, so, bb_, 7 - ci, *ab)

                for si, T in enumerate(seq_lens):
                    so = seq_off[si]
                    NB = T // 512
                    for D in dirs:
                        nm = "hf" if D.dr == 0 else "hb"
                        S.I("pool", "memset", D.S[:], 0.0, writes=[(nm + "S", h) for h in range(4)])
                        S.I("pool", "memset", D.Sbf[:], 0.0, writes=[nm + "Sbf"])
                    Pf = prep_begin(dirs[0], so, 0)
                    Pb = prep_begin(dirs[1], so, NB - 1)
                    for h in range(4):
                        prep_head(dirs[0], Pf, h)
                        prep_head(dirs[1], Pb, h)
                    for bi in range(NB):
                        if bi + 1 < NB:
                            Pf2 = prep_begin(dirs[0], so, bi + 1)
                            Pb2 = prep_begin(dirs[1], so, NB - 2 - bi)
                        for ci in range(8):
                            chunk_pair(Pf, Pb, so, bi, NB - 1 - bi, ci)
                            if bi + 1 < NB:
                                if ci % 2 == 0:
                                    prep_head(dirs[0], Pf2, ci // 2)
                                else:
                                    prep_head(dirs[1], Pb2, ci // 2)
                        if bi + 1 < NB:
                            Pf, Pb = Pf2, Pb2
            S.barrier()

        if stop_after >= 9:
            with ExitStack() as st:
                wo = sb(st, "wo1", [128, 8, 1024], BF16)
                with ExitStack() as st2:
                    stage = Ring(nc, st2, "wstg2", 3, [128, 1024], F32)
                    load_weight(wo, w_out_odd, 8, 1024, stage)
                    S.barrier()
                lnt = sb(st, "lnt1", [128, 2, 1024], F32)
                S.I("sp", "dma_start", out=lnt[:, 0, :], in_=lng_in[:, 1024:2048], writes=["ln"], dma="ln")
                S.I("sp", "dma_start", out=lnt[:, 1, :], in_=lnb_in[:, 1024:2048], writes=["ln"], dma="ln")
                gnt = sb(st, "gnt", [128, 512], F32)
                S.I("sp", "dma_start", out=gnt[:], in_=gn_in, writes=["gn"], dma="gn")
                RD4 = {"oc": 4, "of": 5, "obk": 3, "gt1": 5, "xs3": 6, "vln1": 6, "yo": 3, "stt1": 8, "smt1": 6, "tmpc1": 3, "sq1": 3, "yt1": 3, "yT1": 3}
                oc_r = Ring(nc, st, "oc", RD4['oc'], [128, 520], F32)
                of_r = Ring(nc, st, "of", RD4['of'], [128, 512], F32)
                ob_r = Ring(nc, st, "obk", RD4['obk'], [128, 512], F32)
                g_r = Ring(nc, st, "gt1", RD4['gt1'], [128, 1024], BF16)
                xs_r = Ring(nc, st, "xs3", RD4['xs3'], [128, 1024], F32)
                v_r = Ring(nc, st, "vln1", RD4['vln1'], [128, 1024], F32)
                x1_r = Ring(nc, st, "yo", RD4['yo'], [128, 1024], F32)
                st_r = Ring(nc, st, "stt1", RD4['stt1'], [128, 8], F32)
                sm_r = Ring(nc, st, "smt1", RD4['smt1'], [128, 16], F32)
                tmp_r = Ring(nc, st, "tmpc1", RD4['tmpc1'], [128, 1024], F32)
                sq_r = Ring(nc, st, "sq1", RD4['sq1'], [128, 512], F32)
                junk = sb(st, "junk1", [128, 1024], BF16)
                y_r = Ring(nc, st, "yt1", RD4['yt1'], [128, 1024], BF16)
                yT_r = Ring(nc, st, "yT1", RD4['yT1'], [128, 8, 128], BF16)
                pT_r = Ring(nc, st, "pT3", 2, [128, 1024], BF16, psum=True)
                z_r = Ring(nc, st, "zps1", 6, [128, 512], F32, psum=True)
                def tile_gen(ti):
                    r0 = ti * 128
                    oc, ock = oc_r.next()
                    S.I("sp", "dma_start", out=oc[:], in_=ocs[r0:r0 + 128, :], writes=[ock], dma=ock)
                    of, ofk = of_r.next()
                    S.I("sp", "dma_start", out=of[:], in_=odf[r0:r0 + 128, :], writes=[ofk], dma=ofk)
                    ob, obk = ob_r.next()
                    S.I("sp", "dma_start", out=ob[:], in_=odb[r0:r0 + 128, :], writes=[obk], dma=obk)
                    gt, gk = g_r.next()
                    S.I("sp", "dma_start", out=gt[:], in_=h1tm[r0:r0 + 128, H1_GC:H1_GC + 1024], writes=[gk], dma=gk)
                    xs, xsk = xs_r.next()
                    S.I("sp", "dma_start", out=xs[:], in_=x1s[r0:r0 + 128, :], writes=[xsk], dma=xsk)
                    yield
                    sm, smk = sm_r.next()
                    tmp, tmpk = tmp_r.next()
                    y, yk = y_r.next()
                    sq, sqk = sq_r.next()
                    oc3 = oc[:].rearrange("p (h d) -> p h d", d=65)
                    S.I("dve", "reciprocal", out=sm[:, 0:8], in_=oc3[:, :, 64], reads=[ock], writes=[(smk, 0)])
                    S.I("dve", "tensor_tensor", out=of[:], in0=of[:], in1=ob[:], op=ALU.add, reads=[ofk, obk], writes=[ofk])
                    for h in range(4):
                        S.I("act", "activation", out=sq[:, h * 128:(h + 1) * 128], in_=of[:, h * 128:(h + 1) * 128], func=AF.Square, accum_out=sm[:, 8 + h:9 + h],
                            reads=[ofk], writes=[sqk, (smk, 1)])
                    yield
                    S.I("dve", "tensor_tensor", out=tmp[:, 0:512].rearrange("p (h d) -> p h d", d=64), in0=oc3[:, :, 0:64],
                        in1=sm[:, 0:8].unsqueeze(2).broadcast_to([128, 8, 64]), op=ALU.mult, reads=[ock, (smk, 0)], writes=[(tmpk, 0)])
                    S.I("dve", "tensor_scalar", out=sm[:, 8:12], in0=sm[:, 8:12], scalar1=1.0 / 128, scalar2=RMS_EPS, op0=ALU.mult, op1=ALU.add, reads=[(smk, 1)], writes=[(smk, 1)])
                    S.I("act", "activation", out=sm[:, 12:16], in_=sm[:, 8:12], func=AF.Sqrt, reads=[(smk, 1)], writes=[(smk, 2)])
                    yield
                    S.I("dve", "reciprocal", out=sm[:, 12:16], in_=sm[:, 12:16], reads=[(smk, 2)], writes=[(smk, 2)])
                    S.I("dve", "tensor_tensor", out=tmp[:, 512:1024].rearrange("p (h d) -> p h d", d=128), in0=of[:].rearrange("p (h d) -> p h d", d=128),
                        in1=sm[:, 12:16].unsqueeze(2).broadcast_to([128, 4, 128]), op=ALU.mult, reads=[ofk, (smk, 2)], writes=[(tmpk, 1)])
                    S.I("dve", "tensor_tensor", out=tmp[:, 512:1024], in0=tmp[:, 512:1024], in1=gnt[:], op=ALU.mult, reads=[(tmpk, 1), "gn"], writes=[(tmpk, 1)])
                    S.I("dve", "tensor_tensor", out=y[:], in0=tmp[:], in1=gt[:], op=ALU.mult, reads=[(tmpk, 0), (tmpk, 1), gk], writes=[yk])
                    pT, pTk = pT_r.next()
                    for c in range(8):
                        S.I("pe", "transpose", out=pT[:, c * 128:(c + 1) * 128], in_=y[:, c * 128:(c + 1) * 128], identity=ident[:], reads=[yk, "ident"], writes=[pTk])
                    yT, yTk = yT_r.next()
                    copy_op("act", yT[:], pT[:].rearrange("p (c t) -> p c t", c=8), [pTk], [yTk])
                    zps, zk = [], []
                    for hf in range(2):
                        z, zkk = z_r.next()
                        for c in range(8):
                            S.I("pe", "matmul", z[:], lhsT=yT[:, c, :], rhs=wo[:, c, hf * 512:(hf + 1) * 512], start=(c == 0), stop=(c == 7), reads=[yTk], writes=[zkk])
                        zps.append(z)
                        zk.append(zkk)
                    yield
                    for _ in layer_norm_gen(zps, zk, xs, xsk, lnt[:, 0, :], lnt[:, 1, :], v_r, st_r, junk, x1_r, y_out[r0:r0 + 128, :]):
                        yield

                run_skewed([tile_gen(ti) for ti in range(TT // 128)])
            S.barrier()

        fin = Op("sp", None)
        for k in S.dsem:
            fin.deps.append(("dma", k, S.dcnt[k]))
        S.ops["sp"].append(fin)
        with nc.Block() as block:
            S.emit(block)
    return nc


def host_consts(t5_table, sink_a, rpb_c, lb_d, gnorm_d, ln_g, ln_b):
    cst = np.zeros((128, 1280), np.float32)
    cst[:, 0:128] = np.eye(128, dtype=np.float32)
    s = np.arange(64)[:, None]
    t = np.arange(64)[None, :]
    cst[0:64, 128:384] = np.tile((s <= t).astype(np.float32), (1, 4))
    cst[0:64, 384:640] = np.tile((s >= t).astype(np.float32), (1, 4))
    sm = np.ones((512,), np.float32)
    sm[0::64] = 0.0
    cst[:, 640:1152] = sm[None, :]
    lb = np.ascontiguousarray(np.transpose(lb_d.reshape(2, 2, 4, 128), (3, 0, 1, 2))).reshape(128, 16)
    return {
        "tab0": _tables_l0(np.asarray(t5_table, np.float32)),
        "tabn": _tables_na(np.asarray(rpb_c, np.float32)),
        "sink": np.ascontiguousarray(np.broadcast_to(np.asarray(sink_a, np.float32).reshape(1, 8), (128, 8))),
        "lb": np.ascontiguousarray(lb.astype(np.float32)),
        "gnorm": np.ascontiguousarray(np.broadcast_to(np.asarray(gnorm_d, np.float32).reshape(1, 512), (128, 512))),
        "lng": np.ascontiguousarray(np.broadcast_to(np.asarray(ln_g, np.float32).reshape(1, 2048), (128, 2048))),
        "lnb": np.ascontiguousarray(np.broadcast_to(np.asarray(ln_b, np.float32).reshape(1, 2048), (128, 2048))),
        "cst": cst,
    }


N_CORES = 8
_NC_CACHE = {}


def kernel(x_prompt, x_sample, t5_table, w_in_even, sink_a, w_out_even, w_in_odd, rpb_c, lb_d, gnorm_d, w_out_odd, ln_g, ln_b):
    x_prompt = np.asarray(x_prompt, np.float32)
    x_sample = np.asarray(x_sample, np.float32)
    nb, T, D = x_prompt.shape
    ns, Ts, _ = x_sample.shape
    per = nb // N_CORES
    seq_lens = [T] * per + [Ts]
    key = tuple(seq_lens)
    if key not in _NC_CACHE:
        _NC_CACHE[key] = build(seq_lens)
    nc = _NC_CACHE[key]
    consts = host_consts(np.asarray(t5_table), np.asarray(sink_a), np.asarray(rpb_c), np.asarray(lb_d), np.asarray(gnorm_d), np.asarray(ln_g), np.asarray(ln_b))
    shared = dict(consts)
    shared["w_in_even"] = np.ascontiguousarray(np.asarray(w_in_even, np.float32)[0])
    shared["w_out_even"] = np.ascontiguousarray(np.asarray(w_out_even, np.float32)[0])
    shared["w_in_odd"] = np.ascontiguousarray(np.asarray(w_in_odd, np.float32)[0])
    shared["w_out_odd"] = np.ascontiguousarray(np.asarray(w_out_odd, np.float32)[0])
    in_maps = []
    for c in range(N_CORES):
        xs = [x_prompt[c * per + i] for i in range(per)] + [x_sample[c % ns]]
        m = dict(shared)
        m["x"] = np.ascontiguousarray(np.concatenate(xs, axis=0))
        in_maps.append(m)
    res = run_bass_kernel_spmd(nc, in_maps, core_ids=list(range(N_CORES)))
    y_prompt = np.empty((nb, T, D), np.float32)
    y_sample = np.empty((ns, Ts, D), np.float32)
    for c in range(N_CORES):
        y = np.asarray(res.results[c]["y"], np.float32)
        for i in range(per):
            y_prompt[c * per + i] = y[i * T:(i + 1) * T]
        if c < ns:
            y_sample[c] = y[per * T:per * T + Ts]
    return (y_prompt, y_sample)
```

```python
import math
from contextlib import ExitStack

import numpy as np
import concourse.bass as bass
import concourse.mybir as mybir
from concourse.bass_utils import run_bass_kernel_spmd

F32 = mybir.dt.float32
BF16 = mybir.dt.bfloat16
ALU = mybir.AluOpType
AF = mybir.ActivationFunctionType
AX = mybir.AxisListType

D_MODEL = 1024
DEPTH = 2
ALPHA = (2.0 * DEPTH) ** 0.25
LN_EPS = 1e-5
RMS_EPS = 1e-6
NEG = -30000.0

QA, KA, VA, QB, KB, VB, GA, GB, EVEN_IN = 0, 512, 640, 768, 1536, 2304, 3072, 3584, 3840
QC, KC, VC, QD, IDD, ZF, ZB, GC, GD, ODD_IN = 0, 512, 1024, 1536, 2048, 2560, 3072, 3584, 4096, 4608

H0_VA, H0_VB, H0_GA, H0_GB, H0_ROW = 0, 130, 910, 1422, 1680
H1_VC, H1_ID, H1_GC, H1_GD, H1_ROW = 0, 520, 1032, 1544, 2056
OA_ROW = 1300


class Op:
    __slots__ = ("eng", "fn", "deps", "dma_sem", "need_inc", "val")

    def __init__(self, eng, fn):
        self.eng = eng
        self.fn = fn
        self.deps = []
        self.dma_sem = None
        self.need_inc = False
        self.val = 0


class Sched:
    ENGS = ("pe", "act", "dve", "pool", "sp")
    SAME_ENG_RAW = ("act", "dve", "pool")

    def __init__(self, nc, stack):
        self.nc = nc
        self.stack = stack
        self.ops = {e: [] for e in self.ENGS}
        self.sem = {e: stack.enter_context(nc.semaphore("sem_" + e)) for e in self.ENGS}
        self.lastw = {}
        self.readers = {}
        self.dsem = {}
        self.dcnt = {}
        self.key2idx = {}
        self.next_idx = {}

    def op(self, eng, fn, reads=(), writes=(), dma=None):
        o = Op(eng, fn)
        deps = []
        for k in reads:
            w = self.lastw.get(k)
            if w is not None:
                deps.append((w, True))
        for k in writes:
            w = self.lastw.get(k)
            if w is not None:
                deps.append((w, False))
            for r in self.readers.get(k, ()):
                deps.append((r, False))
        for (d, raw) in deps:
            if d.dma_sem is not None:
                o.deps.append(("dma", d.dma_sem, self.dcnt[d.dma_sem]))
                continue
            if d.eng == eng and dma is None:
                if eng not in self.SAME_ENG_RAW:
                    continue
            d.need_inc = True
            o.deps.append(("op", d))
        if dma is not None:
            if (eng, dma) not in self.key2idx:
                n_ = self.next_idx.get(eng, 0)
                self.next_idx[eng] = n_ + 1
                idx = (eng, n_)
                self.key2idx[(eng, dma)] = idx
                if idx not in self.dsem:
                    self.dsem[idx] = self.stack.enter_context(self.nc.semaphore("dsem_%s%d" % idx))
                    self.dcnt[idx] = 0
            idx = self.key2idx[(eng, dma)]
            self.dcnt[idx] += 16
            o.dma_sem = idx
        for k in reads:
            lst = self.readers.setdefault(k, [])
            if dma is None:
                for idx in range(len(lst)):
                    if lst[idx].dma_sem is None and lst[idx].eng == eng:
                        lst[idx] = o
                        break
                else:
                    lst.append(o)
            else:
                lst.append(o)
        for k in writes:
            self.lastw[k] = o
            self.readers[k] = []
        self.ops[eng].append(o)
        return o

    def I(self, eng, name, *a, reads=(), writes=(), dma=None, **kw):
        return self.op(eng, (name, a, kw), reads=reads, writes=writes, dma=dma)

    def barrier(self):
        lasts = {}
        for e in self.ENGS:
            for o in reversed(self.ops[e]):
                if o.fn is not None and o.dma_sem is None:
                    lasts[e] = o
                    break
        for e in self.ENGS:
            o = Op(e, None)
            for f, l in lasts.items():
                if f != e:
                    l.need_inc = True
                    o.deps.append(("op", l))
            for k in self.dsem:
                o.deps.append(("dma", k, self.dcnt[k]))
            self.ops[e].append(o)
        self.lastw = {}
        self.readers = {}
        self.key2idx = {}
        self.next_idx = {}

    def emit(self, block):
        for e in self.ENGS:
            c = 0
            for o in self.ops[e]:
                if o.need_inc:
                    c += 1
                    o.val = c
        sem = self.sem
        dsem = self.dsem

        def run(engname, engine):
            seen = {}
            for o in self.ops[engname]:
                need = {}
                for d in o.deps:
                    if d[0] == "dma":
                        k = ("d", d[1])
                        v = d[2]
                        s = dsem[d[1]]
                    else:
                        k = ("e", d[1].eng)
                        v = d[1].val
                        s = sem[d[1].eng]
                    if v > seen.get(k, 0) and v > need.get(k, (0, None))[0]:
                        need[k] = (v, s)
                for k, (v, s) in need.items():
                    engine.wait_ge(s, v)
                    seen[k] = v
                if o.fn is None:
                    continue
                name, a, kw = o.fn
                ins = getattr(engine, name)(*a, **kw)
                if o.dma_sem is not None:
                    ins.then_inc(dsem[o.dma_sem], 16)
                elif o.need_inc:
                    ins.then_inc(sem[engname], 1)

        block.tensor(lambda e: run("pe", e))
        block.scalar(lambda e: run("act", e))
        block.vector(lambda e: run("dve", e))
        block.gpsimd(lambda e: run("pool", e))
        block.sync(lambda e: run("sp", e))


class Ring:
    def __init__(self, nc, st, name, n, shape, dtype, psum=False):
        self.name = name
        self.n = n
        self.t = []
        for i in range(n):
            if psum:
                self.t.append(st.enter_context(nc.psum_tensor("%s%d" % (name, i), shape, dtype)))
            else:
                self.t.append(st.enter_context(nc.sbuf_tensor("%s%d" % (name, i), shape, dtype)))
        self.i = 0

    def next(self):
        i = self.i % self.n
        self.i += 1
        return self.t[i], (self.name, i)

    def at(self, i):
        i = i % self.n
        return self.t[i], (self.name, i)

    def can_next(self):
        if not hasattr(self, "busy"):
            self.busy = [False] * self.n
        return not self.busy[self.i % self.n]

    def next_hold(self):
        if not hasattr(self, "busy"):
            self.busy = [False] * self.n
        i = self.i % self.n
        assert not self.busy[i], self.name
        self.busy[i] = True
        self.i += 1
        return self.t[i], (self.name, i), i

    def release(self, i):
        self.busy[i] = False


def _t5_buckets(rel):
    nb = 16
    ret = (rel > 0).astype(np.int64) * nb
    n = np.abs(rel)
    max_exact = nb // 2
    large = max_exact + (np.log(np.maximum(n, 1) / max_exact) / math.log(1024 / max_exact) * (nb - max_exact)).astype(np.int64)
    large = np.minimum(large, nb - 1)
    return ret + np.where(n < max_exact, n, large)


def _tables_l0(t5_table):
    k = np.arange(128)[:, None, None]
    j = np.arange(3)[None, :, None]
    q = np.arange(128)[None, None, :]
    rel = (j - 1) * 128 + k - q
    out = np.full((5, 128, 3, 4, 128), NEG, np.float32)
    specs = [(128, 1, [0, 1, 2, 3]), (128, 1, [4, 5, 6, 7]), (64, 1, [8, 9, 10, 11]), (64, 4, [12, 13, 14, 15]), (64, 16, [16, 17, 18, 19])]
    for s, (hw, d, heads) in enumerate(specs):
        valid = np.abs(rel) <= hw
        bk = _t5_buckets(rel * d)
        for hi, h in enumerate(heads):
            pos = ((hi % 2) * 2 + hi // 2) if s < 2 else hi
            out[s, :, :, pos, :] = np.where(valid, t5_table[bk, h], NEG)
    return out.reshape(5, 128, 1536)


def _tables_na(rpb):
    rpb = rpb.reshape(8, 15, 31)
    k = np.arange(128)[:, None]
    q = np.arange(128)[None, :]
    krl, kc = k // 64, k % 64
    qrl, qc = q // 64, q % 64
    sc = np.clip(qc - 8, 0, 48)
    colmask = (kc >= sc) & (kc < sc + 16)
    colidx = np.clip(kc - qc + 15, 0, 30)
    out = np.full((4, 128, 9, 2, 128), NEG, np.float32)
    for b in range(9):
        off = b - 3 if b < 7 else (-2 if b == 7 else 2)
        dr = 2 * off + krl - qrl
        valid = colmask & (np.abs(dr) <= 7)
        if b >= 7:
            valid = valid & (dr >= -4) & (dr <= 3)
        dri = np.clip(dr + 7, 0, 14)
        for h in range(8):
            vals = rpb[h][dri, colidx]
            out[h // 2, :, b, h % 2, :] = np.where(valid, vals, NEG)
    return out.reshape(4, 128, 9 * 256)


def na_neighbors(u, U):
    if U <= 4:
        return [(m, m - u + 3) for m in range(U)]
    if u == 0 or u == 1:
        return [(m, m - u + 3) for m in range(0, 4)]
    if u == U - 1 or u == U - 2:
        return [(m, m - u + 3) for m in range(U - 4, U)]
    res = []
    for off in (-2, -1, 0, 1, 2):
        m = u + off
        blk = 7 if off == -2 else (8 if off == 2 else off + 3)
        res.append((m, blk))
    return res


def build(seq_lens, debug=False, stop_after=99, lvl=9, skipA=False):
    nc = bass.Bass("TRN2", target_bir_lowering=False)
    TT = sum(seq_lens)
    seq_off = [sum(seq_lens[:i]) for i in range(len(seq_lens))]
    okind = "ExternalOutput" if debug else "Internal"

    def dram(name, shape, dt, kind):
        return nc.dram_tensor(name, shape, dt, kind=kind).ap()

    x_in = dram("x", [TT, D_MODEL], F32, "ExternalInput")
    y_out = dram("y", [TT, D_MODEL], F32, "ExternalOutput")
    w_in_even = dram("w_in_even", [D_MODEL, EVEN_IN], F32, "ExternalInput")
    w_out_even = dram("w_out_even", [768, D_MODEL], F32, "ExternalInput")
    w_in_odd = dram("w_in_odd", [D_MODEL, ODD_IN], F32, "ExternalInput")
    w_out_odd = dram("w_out_odd", [1024, D_MODEL], F32, "ExternalInput")
    tab0_in = dram("tab0", [5, 128, 1536], F32, "ExternalInput")
    tabn_in = dram("tabn", [4, 128, 2304], F32, "ExternalInput")
    sink_in = dram("sink", [128, 8], F32, "ExternalInput")
    lb_in = dram("lb", [128, 16], F32, "ExternalInput")
    gn_in = dram("gnorm", [128, 512], F32, "ExternalInput")
    lng_in = dram("lng", [128, 2048], F32, "ExternalInput")
    lnb_in = dram("lnb", [128, 2048], F32, "ExternalInput")
    cst_in = dram("cst", [128, 1280], F32, "ExternalInput")

    h0T = dram("h0T", [2176, TT], BF16, okind)
    h0tm = dram("h0tm", [TT, H0_ROW], BF16, okind)
    oatt = dram("oatt", [TT, OA_ROW], F32, okind)
    x1s = dram("x1s", [TT, D_MODEL], F32, okind)
    h1Tb = dram("h1Tb", [1024, TT], BF16, okind)
    h1Tf = dram("h1Tf", [1536, TT], F32, okind)
    h1tm = dram("h1tm", [TT, H1_ROW], BF16, okind)
    ocs = dram("ocs", [TT, 520], F32, okind)
    odf = dram("odf", [TT, 512], F32, okind)
    odb = dram("odb", [TT, 512], F32, okind)

    with ExitStack() as top:
        S = Sched(nc, top)
        sb = lambda st, name, shape, dt: st.enter_context(nc.sbuf_tensor(name, shape, dt))
        cyc = {"i": 0}

        def rr(engs):
            cyc["i"] += 1
            return engs[cyc["i"] % len(engs)]

        def copy_op(eng, out, in_, reads, writes):
            if eng == "act":
                S.I("act", "copy", out=out, in_=in_, reads=reads, writes=writes)
            else:
                S.I(eng, "tensor_copy", out=out, in_=in_, reads=reads, writes=writes)

        cstf = sb(top, "cstf", [128, 1280], F32)
        ident = sb(top, "ident", [128, 128], BF16)
        S.I("sp", "dma_start", out=cstf[:], in_=cst_in, writes=["cstf"], dma="cstf")
        S.I("dve", "tensor_copy", out=ident[:], in_=cstf[:, 0:128], reads=["cstf"], writes=["ident"])
        trilf = cstf[0:64, 128:384]
        trilb = cstf[0:64, 384:640]
        scanmask = cstf[:, 640:1152]

        def load_weight(dst, src, KCn, F, stage_ring, pieces=None):
            if pieces is None:
                pieces = [(c0, min(1024, F - c0), c0) for c0 in range(0, F, 1024)]
            for kc in range(KCn):
                for (c0, cw, d0) in pieces:
                    stg, sk = stage_ring.next()
                    S.I("sp", "dma_start", out=stg[:, 0:cw], in_=src[kc * 128:(kc + 1) * 128, c0:c0 + cw], writes=[sk], dma=sk)
                    copy_op(rr(["act", "dve"]), dst[:, kc, d0:d0 + cw], stg[:, 0:cw], [sk], [("w", id(dst), kc, d0)])

        def make_xT(xs, xs_k, xb_ring, pT_ring, xT, xT_k, tcol):
            xb, xb_k = xb_ring.next()
            copy_op(rr(["dve", "act"]), xb[:], xs[:], [xs_k], [xb_k])
            pT, pT_k = pT_ring.next()
            for c in range(8):
                S.I("pe", "transpose", out=pT[:, c * 128:(c + 1) * 128], in_=xb[:, c * 128:(c + 1) * 128], identity=ident[:],
                    reads=[xb_k, "ident"], writes=[pT_k])
            copy_op(rr(["act", "dve"]), xT[:, :, tcol:tcol + 128], pT[:].rearrange("p (c t) -> p c t", c=8), [pT_k], [xT_k])

        def project(xT, xT_k, wb, acc_r, fm_r, fm_list, tm_r, tm_groups, t0, tm_dst, ntt=4, tts=None):
            ncols = ntt * 128
            for (f0, dst, ring) in fm_list:
                acc, acc_k = acc_r.next()
                for c in range(8):
                    S.I("pe", "matmul", acc[:, 0:ncols], lhsT=wb[:, c, f0:f0 + 128], rhs=xT[:, c, 0:ncols], start=(c == 0), stop=(c == 7),
                        reads=[xT_k], writes=[acc_k])
                fm, fm_k = ring.next()
                copy_op(rr(["act", "dve"]), fm[:, 0:ncols], acc[:, 0:ncols], [acc_k], [fm_k])
                S.I("pool", "dma_start", out=dst, in_=fm[:, 0:ncols], reads=[fm_k], dma=fm_k)
            for tt in (range(ntt) if tts is None else tts):
                tm, tm_k = tm_r.next()
                for grp in tm_groups:
                    if len(grp) == 5:
                        wc, width, sc, nh, func = grp
                        plist = [(0, width, sc, nh, func)]
                    else:
                        wc, width, plist = grp
                    acc, acc_k = acc_r.next()
                    for c in range(8):
                        S.I("pe", "matmul", acc[:, 0:width], lhsT=xT[:, c, tt * 128:(tt + 1) * 128], rhs=wb[:, c, wc:wc + width], start=(c == 0), stop=(c == 7),
                            reads=[xT_k], writes=[acc_k])
                    for (o, w, sc, nh, func) in plist:
                        if nh > 0:
                            copy_op(rr(["act", "dve"]), tm[:, sc:sc + nh * 65].rearrange("p (h d) -> p h d", d=65)[:, :, 0:64],
                                    acc[:, o:o + w].rearrange("p (h d) -> p h d", d=64), [acc_k], [tm_k])
                        elif func is not None:
                            S.I("act", "activation", out=tm[:, sc:sc + w], in_=acc[:, o:o + w], func=func, reads=[acc_k], writes=[tm_k])
                        else:
                            copy_op(rr(["act", "dve"]), tm[:, sc:sc + w], acc[:, o:o + w], [acc_k], [tm_k])
                r0 = t0 + tt * 128
                S.I("pool", "dma_start", out=tm_dst[r0:r0 + 128, :], in_=tm[:], reads=[tm_k], dma=tm_k)

        with ExitStack() as st:
            wb = sb(st, "wb0", [128, 8, EVEN_IN], BF16)
            stage = Ring(nc, st, "wstg", 3, [128, 1024], F32)
            load_weight(wb, w_in_even, 8, EVEN_IN, stage,
                        pieces=[(0, 640, 0), (768, 1024, 640), (1792, 512, 1664), (640, 128, 2176), (2304, 1024, 2304), (3328, 512, 3328)])
            xs_r = Ring(nc, st, "xs", 4, [128, 1024], F32)
            xb_r = Ring(nc, st, "xb", 2, [128, 1024], BF16)
            pT_r = Ring(nc, st, "pT", 2, [128, 1024], BF16, psum=True)
            xT_r = Ring(nc, st, "xT", 2, [128, 8, 512], BF16)
            acc_r = Ring(nc, st, "acc", 4, [128, 512], F32, psum=True)
            fm_r = Ring(nc, st, "fm", 4, [128, 512], BF16)
            tm_r = Ring(nc, st, "tm", 2, [128, H0_ROW], BF16)
            for i in range(2):
                t, k = tm_r.at(i)
                S.I("pool", "memset", t[:], 1.0, writes=[k])
            fm_feats = [i * 128 for i in range(17)]
            tm_groups = [
                (2176, 512, [(0, 128, H0_VA, 2, None), (128, 384, H0_VB, 6, None)]),
                (2688, 512, [(0, 384, H0_VB + 6 * 65, 6, None), (384, 128, H0_GA, 0, AF.Silu)]),
                (3200, 512, [(0, 384, H0_GA + 128, 0, AF.Silu), (384, 128, H0_GB, 0, AF.Silu)]),
                (3712, 128, [(0, 128, H0_GB + 128, 0, AF.Silu)]),
            ]
            for b in range(TT // 512):
                t0 = b * 512
                xT, xT_k = xT_r.next()
                for tt in range(4):
                    xs, xs_k = xs_r.next()
                    r0 = t0 + tt * 128
                    S.I("sp", "dma_start", out=xs[:], in_=x_in[r0:r0 + 128, :], writes=[xs_k], dma=xs_k)
                    make_xT(xs, xs_k, xb_r, pT_r, xT, xT_k, tt * 128)
                fm_list = [(f0, h0T[fi * 128:(fi + 1) * 128, t0:t0 + 512], fm_r) for fi, f0 in enumerate(fm_feats)]
                project(xT, xT_k, wb, acc_r, fm_r, fm_list, tm_r, tm_groups, t0, h0tm)
        S.barrier()

        h0tm_h = h0tm.tensor
        oatt_h = oatt.tensor

        def attn_group(st, E0, gname, d, quadsets, qrow, krow, vcol, VW, ocol, is_a):
            W = max(512, 128 * d)
            tpw = W // (128 * d)
            CQ = 4 if is_a else 2
            nring = 4 if d < 16 else 3
            Qr = Ring(nc, st, gname + "Q", 2, [128, CQ, W], BF16)
            Kr = Ring(nc, st, gname + "K", nring, [128, 4, W], BF16)
            Vr = Ring(nc, st, gname + "V", nring, [128, tpw * d, VW], BF16)
            psr = Ring(nc, st, gname + "ps", 2, [128, 1536], F32, psum=True)
            por = Ring(nc, st, gname + "po", 2, [128, 512], F32, psum=True)
            esr = Ring(nc, st, gname + "es", 2, [128, 1536], BF16)
            ptr = Ring(nc, st, gname + "pt", 2, [128, 1536], BF16)
            osr = Ring(nc, st, gname + "os", 3, [128, 260], F32)
            for i in range(nring):
                Kw, Kk = Kr.at(i)
                S.I("pool", "memset", Kw[:], 0.0, writes=[Kk])
            for si, T in enumerate(seq_lens):
                so = seq_off[si]
                NW = T // W
                NT = T // (128 * d)

                def load_kv(wn):
                    Kw, Kk = Kr.at(wn)
                    Vw, Vk = Vr.at(wn)
                    c0 = so + wn * W
                    for v in range(4):
                        if is_a:
                            g, half = v // 2, v % 2
                            src = h0T[krow + g * 64:krow + (g + 1) * 64, c0:c0 + W]
                        else:
                            half = v % 2
                            src = h0T[krow + v * 64:krow + (v + 1) * 64, c0:c0 + W]
                        S.I("sp", "dma_start", out=Kw[half * 64:(half + 1) * 64, v, :], in_=src, writes=[Kk], dma=Kk)
                    if d == 1:
                        src = bass.AP(h0tm_h, c0 * H0_ROW + vcol, [[H0_ROW, 128], [128 * H0_ROW, tpw], [1, VW]])
                    else:
                        src = bass.AP(h0tm_h, c0 * H0_ROW + vcol, [[d * H0_ROW, 128], [H0_ROW, d], [1, VW]])
                    S.I("sp", "dma_start", out=Vw[:], in_=src, writes=[Vk], dma=Vk)

                def stage2(n, r, js, qi, pt, ptk, tt, so=so, NT=NT):
                    po, pok = por.next()
                    for hh in range(4):
                        for ji, j in enumerate(js):
                            n2 = n + j - 1
                            Vw, Vk = Vr.at(n2 // tpw)
                            t2 = n2 % tpw
                            vi = t2 if d == 1 else r
                            vc0 = (qi * 65) if is_a else hh * 65
                            pos = ((hh % 2) * 2 + hh // 2) if is_a else hh
                            o0 = (j * 4 + pos) * 128
                            S.I("pe", "matmul", po[:, hh * 65:hh * 65 + 65], lhsT=pt[:, o0:o0 + 128], rhs=Vw[:, vi, vc0:vc0 + 65], start=(ji == 0), stop=(ji == len(js) - 1),
                                reads=[ptk, Vk], writes=[pok])
                    osb, osk = osr.next()
                    copy_op("dve", osb[:], po[:, 0:260], [pok], [osk])
                    row0 = so + n * 128 * d + r
                    dst = bass.AP(oatt_h, row0 * OA_ROW + ocol + (qi * 260 if is_a else 0), [[d * OA_ROW, 128], [1, 260]])
                    S.I("pool", "dma_start", out=dst, in_=osb[:], reads=[osk], dma=osk)

                load_kv(0)
                for wn in range(NW):
                    Qw, Qk = Qr.at(wn)
                    c0 = so + wn * W
                    S.I("sp", "dma_start", out=Qw[:], in_=h0T[qrow:qrow + CQ * 128, c0:c0 + W].rearrange("(c p) t -> p c t", p=128), writes=[Qk], dma=Qk)
                    if wn + 1 < NW:
                        load_kv(wn + 1)
                    pending = None
                    for tt in range(tpw):
                        for r in range(d):
                            n = wn * tpw + tt
                            js = [j for j in range(3) if 0 <= n + j - 1 < NT]
                            qcols = slice(tt * 128 * d + r, tt * 128 * d + r + 127 * d + 1, d)
                            for qi, qs in enumerate(quadsets):
                                ps, psk = psr.next()
                                for j in js:
                                    n2 = n + j - 1
                                    Kw, Kk = Kr.at(n2 // tpw)
                                    t2 = n2 % tpw
                                    kcols = slice(t2 * 128 * d + r, t2 * 128 * d + r + 127 * d + 1, d)
                                    if is_a:
                                        for half in range(2):
                                            out_ap = ps[:, (j * 4 + half * 2) * 128:(j * 4 + half * 2 + 2) * 128]
                                            S.I("pe", "matmul", out_ap, lhsT=Kw[:, qi * 2 + half, kcols], rhs=Qw[:, qi * 2:qi * 2 + 2, qcols], start=True, stop=True,
                                                reads=[Kk, Qk], writes=[psk])
                                    else:
                                        for hh in range(4):
                                            o0 = (j * 4 + hh) * 128
                                            S.I("pe", "matmul", ps[:, o0:o0 + 128], lhsT=Kw[:, hh, kcols], rhs=Qw[:, hh // 2, qcols], start=True, stop=True, reads=[Kk, Qk], writes=[psk])
                                es, esk = esr.next()
                                pt, ptk = ptr.next()
                                for j in js:
                                    S.I("act", "activation", out=es[:, j * 512:(j + 1) * 512], in_=ps[:, j * 512:(j + 1) * 512], func=AF.Exp, scale=0.125,
                                        reads=[psk], writes=[esk])
                                for j in js:
                                    S.I("dve", "tensor_tensor", out=pt[:, j * 512:(j + 1) * 512], in0=es[:, j * 512:(j + 1) * 512], in1=E0[:, qs, j * 512:(j + 1) * 512], op=ALU.mult,
                                        reads=[esk, "E0"], writes=[ptk])
                                unit = (n, r, js, qi, pt, ptk, tt)
                                if pending is not None:
                                    stage2(*pending)
                                pending = unit
                    if pending is not None:
                        stage2(*pending)
                        pending = None

        if stop_after >= 2:
            with ExitStack() as st:
                E0 = sb(st, "E0", [128, 5, 1536], BF16)
                tstage = Ring(nc, st, "tstg", 2, [128, 1536], F32)
                for qs in range(5):
                    stg, sk = tstage.next()
                    S.I("sp", "dma_start", out=stg[:], in_=tab0_in[qs], writes=[sk], dma=sk)
                    S.I("act", "activation", out=E0[:, qs, :], in_=stg[:], func=AF.Exp, reads=[sk], writes=["E0"])
                with ExitStack() as st2:
                    if not skipA:
                        attn_group(st2, E0, "A", 1, [0, 1], 0, 512, H0_VA, 130, 0, True)
                S.barrier()
                for gi, d in enumerate((1, 4, 16)):
                    if stop_after < 3 + gi:
                        continue
                    with ExitStack() as st2:
                        attn_group(st2, E0, "B%d" % gi, d, [2 + gi], 640 + gi * 256, 1408 + gi * 256, H0_VB + gi * 260, 260, 520 + gi * 260, False)
                    S.barrier()

        def run_skewed(gens):
            active = []
            it = iter(gens)
            done = False
            while True:
                if not done:
                    try:
                        active.append(next(it))
                    except StopIteration:
                        done = True
                if done and not active:
                    break
                for g in list(active):
                    try:
                        next(g)
                    except StopIteration:
                        active.remove(g)

        def layer_norm_gen(zps, zk, xres, xres_k, lng, lnb, v_r, st_r, junk, x1_r, dst):
            v, vk = v_r.next()
            for hf in range(2):
                S.I("dve", "scalar_tensor_tensor", out=v[:, hf * 512:(hf + 1) * 512], in0=xres[:, hf * 512:(hf + 1) * 512], scalar=ALPHA, in1=zps[hf][:],
                    op0=ALU.mult, op1=ALU.add, reads=[xres_k, zk[hf]], writes=[vk])
            stt, stk = st_r.next()
            S.I("dve", "tensor_reduce", out=stt[:, 0:1], in_=v[:], axis=AX.X, op=ALU.add, reads=[vk], writes=[(stk, 0)])
            S.I("act", "activation", out=junk[:], in_=v[:], func=AF.Square, accum_out=stt[:, 1:2], reads=[vk], writes=["junk", (stk, 1)])
            yield
            S.I("dve", "tensor_scalar", out=stt[:, 2:3], in0=stt[:, 0:1], scalar1=1.0 / 1024, scalar2=None, op0=ALU.mult, reads=[(stk, 0)], writes=[(stk, 2)])
            S.I("dve", "tensor_tensor", out=stt[:, 3:4], in0=stt[:, 2:3], in1=stt[:, 2:3], op=ALU.mult, reads=[(stk, 2)], writes=[(stk, 3)])
            yield
            S.I("dve", "scalar_tensor_tensor", out=stt[:, 4:5], in0=stt[:, 1:2], scalar=1.0 / 1024, in1=stt[:, 3:4], op0=ALU.mult, op1=ALU.subtract,
                reads=[(stk, 1), (stk, 3)], writes=[(stk, 4)])
            yield
            S.I("dve", "tensor_scalar", out=stt[:, 4:5], in0=stt[:, 4:5], scalar1=LN_EPS, scalar2=None, op0=ALU.add, reads=[(stk, 4)], writes=[(stk, 4)])
            S.I("act", "activation", out=stt[:, 6:7], in_=stt[:, 4:5], func=AF.Sqrt, reads=[(stk, 4)], writes=[(stk, 6)])
            yield
            S.I("dve", "reciprocal", out=stt[:, 5:6], in_=stt[:, 6:7], reads=[(stk, 6)], writes=[(stk, 5)])
            S.I("dve", "scalar_tensor_tensor", out=v[:], in0=v[:], scalar=stt[:, 2:3], in1=lng, op0=ALU.subtract, op1=ALU.mult,
                reads=[vk, (stk, 2), "ln"], writes=[vk])
            yield
            x1, x1k = x1_r.next()
            S.I("dve", "scalar_tensor_tensor", out=x1[:], in0=v[:], scalar=stt[:, 5:6], in1=lnb, op0=ALU.mult, op1=ALU.add,
                reads=[vk, (stk, 5), "ln"], writes=[x1k])
            S.I("pool", "dma_start", out=dst, in_=x1[:], reads=[x1k], dma=x1k)

        def layer_norm_tile(zps, zk, xres, xres_k, lng, lnb, v_r, st_r, junk, x1_r):
            v, vk = v_r.next()
            for hf in range(2):
                S.I("dve", "scalar_tensor_tensor", out=v[:, hf * 512:(hf + 1) * 512], in0=xres[:, hf * 512:(hf + 1) * 512], scalar=ALPHA, in1=zps[hf][:],
                    op0=ALU.mult, op1=ALU.add, reads=[xres_k, zk[hf]], writes=[vk])
            stt, stk = st_r.next()
            S.I("dve", "tensor_reduce", out=stt[:, 0:1], in_=v[:], axis=AX.X, op=ALU.add, reads=[vk], writes=[(stk, 0)])
            S.I("act", "activation", out=junk[:], in_=v[:], func=AF.Square, accum_out=stt[:, 1:2], reads=[vk], writes=["junk", (stk, 1)])
            S.I("dve", "tensor_scalar", out=stt[:, 2:3], in0=stt[:, 0:1], scalar1=1.0 / 1024, scalar2=None, op0=ALU.mult, reads=[(stk, 0)], writes=[(stk, 2)])
            S.I("dve", "tensor_tensor", out=stt[:, 3:4], in0=stt[:, 2:3], in1=stt[:, 2:3], op=ALU.mult, reads=[(stk, 2)], writes=[(stk, 3)])
            S.I("dve", "scalar_tensor_tensor", out=stt[:, 4:5], in0=stt[:, 1:2], scalar=1.0 / 1024, in1=stt[:, 3:4], op0=ALU.mult, op1=ALU.subtract,
                reads=[(stk, 1), (stk, 3)], writes=[(stk, 4)])
            S.I("dve", "tensor_scalar", out=stt[:, 4:5], in0=stt[:, 4:5], scalar1=LN_EPS, scalar2=None, op0=ALU.add, reads=[(stk, 4)], writes=[(stk, 4)])
            S.I("act", "activation", out=stt[:, 6:7], in_=stt[:, 4:5], func=AF.Sqrt, reads=[(stk, 4)], writes=[(stk, 6)])
            S.I("dve", "reciprocal", out=stt[:, 5:6], in_=stt[:, 6:7], reads=[(stk, 6)], writes=[(stk, 5)])
            S.I("dve", "scalar_tensor_tensor", out=v[:], in0=v[:], scalar=stt[:, 2:3], in1=lng, op0=ALU.subtract, op1=ALU.mult,
                reads=[vk, (stk, 2), "ln"], writes=[vk])
            x1, x1k = x1_r.next()
            S.I("dve", "scalar_tensor_tensor", out=x1[:], in0=v[:], scalar=stt[:, 5:6], in1=lnb, op0=ALU.mult, op1=ALU.add,
                reads=[vk, (stk, 5), "ln"], writes=[x1k])
            return x1, x1k

        if stop_after >= 6:
            with ExitStack() as st:
                wo = sb(st, "wo0", [128, 6, 1024], BF16)
                w1 = sb(st, "w1", [128, 8, ODD_IN], BF16)
                with ExitStack() as st2:
                    stage = Ring(nc, st2, "wstg1", 3, [128, 1024], F32)
                    load_weight(wo, w_out_even, 6, 1024, stage)
                    load_weight(w1, w_in_odd, 8, ODD_IN, stage)
                    S.barrier()
                lnt = sb(st, "lnt", [128, 2, 1024], F32)
                S.I("sp", "dma_start", out=lnt[:, 0, :], in_=lng_in[:, 0:1024], writes=["ln"], dma="ln")
                S.I("sp", "dma_start", out=lnt[:, 1, :], in_=lnb_in[:, 0:1024], writes=["ln"], dma="ln")
                esink = sb(st, "esink", [128, 8], F32)
                S.I("sp", "dma_start", out=esink[:], in_=sink_in, writes=["esink"], dma="esink")
                S.I("act", "activation", out=esink[:], in_=esink[:], func=AF.Exp, reads=["esink"], writes=["esink"])
                oa_r = Ring(nc, st, "oa", 3, [128, OA_ROW], F32)
                g_r = Ring(nc, st, "gt", 3, [128, 768], BF16)
                xs_r = Ring(nc, st, "xs2", 3, [128, 1024], F32)
                v_r = Ring(nc, st, "vln", 4, [128, 1024], F32)
                x1_r = Ring(nc, st, "x1t", 2, [128, 1024], F32)
                st_r = Ring(nc, st, "stt", 6, [128, 8], F32)
                sm_r = Ring(nc, st, "smt", 4, [128, 16], F32)
                tmp_r = Ring(nc, st, "tmpc", 2, [128, 768], F32)
                junk = sb(st, "junk", [128, 1024], BF16)
                y_r = Ring(nc, st, "yt", 2, [128, 768], BF16)
                yT_r = Ring(nc, st, "yT", 2, [128, 6, 128], BF16)
                xb_r = Ring(nc, st, "xb2", 2, [128, 1024], BF16)
                xT_r = Ring(nc, st, "xT2", 2, [128, 8, 512], BF16)
                fmb_r = Ring(nc, st, "fmb", 4, [128, 512], BF16)
                fmf_r = Ring(nc, st, "fmf", 4, [128, 512], F32)
                tm_r = Ring(nc, st, "tm1", 2, [128, H1_ROW], BF16)
                pT_r = Ring(nc, st, "pT2", 2, [128, 1024], BF16, psum=True)
                z_r = Ring(nc, st, "zps", 4, [128, 512], F32, psum=True)
                acc_r = Ring(nc, st, "acc2", 2, [128, 512], F32, psum=True)
                for i in range(2):
                    t, k = tm_r.at(i)
                    S.I("pool", "memset", t[:], 1.0, writes=[k])
                fm_cols = [(QC + c * 128, h1Tb, c, fmb_r) for c in range(4)] + [(KC + c * 128, h1Tb, 4 + c, fmb_r) for c in range(4)] + \
                          [(QD + c * 128, h1Tf, c, fmf_r) for c in range(4)] + [(ZF + c * 128, h1Tf, 4 + c, fmf_r) for c in range(4)] + \
                          [(ZB + c * 128, h1Tf, 8 + c, fmf_r) for c in range(4)]
                tm_groups1 = [(VC, 512, H1_VC, 8, None), (IDD, 512, H1_ID, 0, None), (GC, 512, H1_GC, 0, AF.Silu), (GD, 512, H1_GD, 0, AF.Silu)]
                NBLK = TT // 512
                tiles_done = [0] * NBLK
                proj_done = [False] * NBLK
                blk_xT = {}

                def acquire(ring):
                    while not ring.can_next():
                        yield
                    return ring.next_hold()

                def tile_gen2(b, tt):
                    t0 = b * 512
                    r0 = t0 + tt * 128
                    oa, oak, oai = yield from acquire(oa_r)
                    S.I("sp", "dma_start", out=oa[:], in_=oatt[r0:r0 + 128, :], writes=[oak], dma=oak)
                    gt, gk, gi_ = yield from acquire(g_r)
                    S.I("sp", "dma_start", out=gt[:], in_=h0tm[r0:r0 + 128, H0_GA:H0_GA + 768], writes=[gk], dma=gk)
                    xs, xsk, xsi = yield from acquire(xs_r)
                    S.I("sp", "dma_start", out=xs[:], in_=x_in[r0:r0 + 128, :], writes=[xsk], dma=xsk)
                    yield
                    sm, smk, smi = yield from acquire(sm_r)
                    oa3 = oa[:, 0:520].rearrange("p (h d) -> p h d", d=65)
                    S.I("dve", "tensor_tensor", out=sm[:, 0:8], in0=oa3[:, :, 64], in1=esink[:], op=ALU.add, reads=[oak, "esink"], writes=[(smk, 0)])
                    S.I("dve", "tensor_tensor", out=oa[:, 520:780], in0=oa[:, 520:780], in1=oa[:, 780:1040], op=ALU.add, reads=[oak], writes=[oak])
                    yield
                    S.I("dve", "reciprocal", out=sm[:, 0:8], in_=sm[:, 0:8], reads=[(smk, 0)], writes=[(smk, 0)])
                    S.I("dve", "tensor_tensor", out=oa[:, 520:780], in0=oa[:, 520:780], in1=oa[:, 1040:1300], op=ALU.add, reads=[oak], writes=[oak])
                    yield
                    tmp, tmpk = tmp_r.next()
                    y, yk = y_r.next()
                    ob3 = oa[:, 520:780].rearrange("p (h d) -> p h d", d=65)
                    S.I("dve", "tensor_tensor", out=tmp[:, 0:512].rearrange("p (h d) -> p h d", d=64), in0=oa3[:, :, 0:64],
                        in1=sm[:, 0:8].unsqueeze(2).broadcast_to([128, 8, 64]), op=ALU.mult, reads=[oak, (smk, 0)], writes=[(tmpk, 0)])
                    S.I("dve", "reciprocal", out=sm[:, 8:12], in_=ob3[:, :, 64], reads=[oak], writes=[(smk, 1)])
                    yield
                    S.I("dve", "tensor_tensor", out=tmp[:, 512:768].rearrange("p (h d) -> p h d", d=64), in0=ob3[:, :, 0:64],
                        in1=sm[:, 8:12].unsqueeze(2).broadcast_to([128, 4, 64]), op=ALU.mult, reads=[oak, (smk, 1)], writes=[(tmpk, 1)])
                    S.I("dve", "tensor_tensor", out=y[:], in0=tmp[:], in1=gt[:], op=ALU.mult, reads=[(tmpk, 0), (tmpk, 1), gk], writes=[yk])
                    oa_r.release(oai)
                    g_r.release(gi_)
                    sm_r.release(smi)
                    pT, pTk = pT_r.next()
                    for c in range(6):
                        S.I("pe", "transpose", out=pT[:, c * 128:(c + 1) * 128], in_=y[:, c * 128:(c + 1) * 128], identity=ident[:], reads=[yk, "ident"], writes=[pTk])
                    yT, yTk = yT_r.next()
                    copy_op("act", yT[:], pT[:, 0:768].rearrange("p (c t) -> p c t", c=6), [pTk], [yTk])
                    zps, zk, zis = [], [], []
                    for hf in range(2):
                        z, zkk, zi = yield from acquire(z_r)
                        for c in range(6):
                            S.I("pe", "matmul", z[:], lhsT=yT[:, c, :], rhs=wo[:, c, hf * 512:(hf + 1) * 512], start=(c == 0), stop=(c == 5), reads=[yTk], writes=[zkk])
                        zps.append(z)
                        zk.append(zkk)
                        zis.append(zi)
                    yield
                    v, vk, vi_ = yield from acquire(v_r)
                    for hf in range(2):
                        S.I("dve", "scalar_tensor_tensor", out=v[:, hf * 512:(hf + 1) * 512], in0=xs[:, hf * 512:(hf + 1) * 512], scalar=ALPHA, in1=zps[hf][:],
                            op0=ALU.mult, op1=ALU.add, reads=[xsk, zk[hf]], writes=[vk])
                    xs_r.release(xsi)
                    for zi in zis:
                        z_r.release(zi)
                    stt, stk, sti = yield from acquire(st_r)
                    S.I("dve", "tensor_reduce", out=stt[:, 0:1], in_=v[:], axis=AX.X, op=ALU.add, reads=[vk], writes=[(stk, 0)])
                    S.I("act", "activation", out=junk[:], in_=v[:], func=AF.Square, accum_out=stt[:, 1:2], reads=[vk], writes=["junk", (stk, 1)])
                    yield
                    S.I("dve", "tensor_scalar", out=stt[:, 2:3], in0=stt[:, 0:1], scalar1=1.0 / 1024, scalar2=None, op0=ALU.mult, reads=[(stk, 0)], writes=[(stk, 2)])
                    S.I("dve", "tensor_tensor", out=stt[:, 3:4], in0=stt[:, 2:3], in1=stt[:, 2:3], op=ALU.mult, reads=[(stk, 2)], writes=[(stk, 3)])
                    S.I("dve", "scalar_tensor_tensor", out=stt[:, 4:5], in0=stt[:, 1:2], scalar=1.0 / 1024, in1=stt[:, 3:4], op0=ALU.mult, op1=ALU.subtract,
                        reads=[(stk, 1), (stk, 3)], writes=[(stk, 4)])
                    S.I("dve", "tensor_scalar", out=stt[:, 4:5], in0=stt[:, 4:5], scalar1=LN_EPS, scalar2=None, op0=ALU.add, reads=[(stk, 4)], writes=[(stk, 4)])
                    S.I("act", "activation", out=stt[:, 6:7], in_=stt[:, 4:5], func=AF.Sqrt, reads=[(stk, 4)], writes=[(stk, 6)])
                    yield
                    S.I("dve", "reciprocal", out=stt[:, 5:6], in_=stt[:, 6:7], reads=[(stk, 6)], writes=[(stk, 5)])
                    S.I("dve", "scalar_tensor_tensor", out=v[:], in0=v[:], scalar=stt[:, 2:3], in1=lnt[:, 0, :], op0=ALU.subtract, op1=ALU.mult,
                        reads=[vk, (stk, 2), "ln"], writes=[vk])
                    yield
                    while b >= 2 and not proj_done[b - 2]:
                        yield
                    if b not in blk_xT:
                        blk_xT[b] = xT_r.next()
                    xT, xT_k = blk_xT[b]
                    x1, x1k = x1_r.next()
                    S.I("dve", "scalar_tensor_tensor", out=x1[:], in0=v[:], scalar=stt[:, 5:6], in1=lnt[:, 1, :], op0=ALU.mult, op1=ALU.add,
                        reads=[vk, (stk, 5), "ln"], writes=[x1k])
                    v_r.release(vi_)
                    st_r.release(sti)
                    S.I("pool", "dma_start", out=x1s[r0:r0 + 128, :], in_=x1[:], reads=[x1k], dma=x1k)
                    make_xT(x1, x1k, xb_r, pT_r, xT, xT_k, tt * 128)
                    tiles_done[b] += 1

                def proj_gen(b):
                    while tiles_done[b] < 4:
                        yield
                    t0 = b * 512
                    xT, xT_k = blk_xT[b]
                    fm_list = [(f0, dst[ci * 128:(ci + 1) * 128, t0:t0 + 512], ring) for (f0, dst, ci, ring) in fm_cols]
                    for i0 in range(0, len(fm_list), 5):
                        project(xT, xT_k, w1, acc_r, None, fm_list[i0:i0 + 5], tm_r, tm_groups1, t0, h1tm, ntt=4, tts=[])
                        yield
                    for tt in range(4):
                        project(xT, xT_k, w1, acc_r, None, [], tm_r, tm_groups1, t0, h1tm, ntt=4, tts=[tt])
                        yield
                    proj_done[b] = True

                gens = []
                for b in range(NBLK):
                    for tt in range(4):
                        gens.append(tile_gen2(b, tt))
                    gens.append(proj_gen(b))
                run_skewed(gens)
            S.barrier()

        h1tm_h = h1tm.tensor
        if stop_after >= 7:
            with ExitStack() as st:
                EN = sb(st, "EN", [128, 4, 2304], BF16)
                with ExitStack() as st2:
                    tstage = Ring(nc, st2, "nstg", 2, [128, 2304], F32)
                    for hp in range(4):
                        stg, sk = tstage.next()
                        S.I("sp", "dma_start", out=stg[:], in_=tabn_in[hp], writes=[sk], dma=sk)
                        S.I("act", "activation", out=EN[:, hp, :], in_=stg[:], func=AF.Exp, reads=[sk], writes=["EN"])
                    S.barrier()
                W = 512
                Qr = Ring(nc, st, "NQ", 2, [128, 4, W], BF16)
                Kr = Ring(nc, st, "NK", 4, [128, 8, W], BF16)
                Vr = Ring(nc, st, "NV", 4, [128, 4, 520], BF16)
                psr = Ring(nc, st, "Nps", 2, [128, 1536], F32, psum=True)
                por = Ring(nc, st, "Npo", 1, [128, 1024], F32, psum=True)
                esr = Ring(nc, st, "Nes", 2, [128, 1280], BF16)
                ptr = Ring(nc, st, "Npt", 2, [128, 1280], BF16)
                osr = Ring(nc, st, "Nos", 2, [128, 520], F32)
                for i in range(4):
                    Kw, Kk = Kr.at(i)
                    S.I("pool", "memset", Kw[:], 0.0, writes=[Kk])
                for si, T in enumerate(seq_lens):
                    so = seq_off[si]
                    NW = T // W
                    U = T // 128

                    def load_kv_n(wn):
                        Kw, Kk = Kr.at(wn)
                        Vw, Vk = Vr.at(wn)
                        c0 = so + wn * W
                        for h in range(8):
                            half = h % 2
                            S.I("sp", "dma_start", out=Kw[half * 64:(half + 1) * 64, h, :], in_=h1Tb[512 + h * 64:512 + (h + 1) * 64, c0:c0 + W], writes=[Kk], dma=Kk)
                        src = bass.AP(h1tm_h, c0 * H1_ROW + H1_VC, [[H1_ROW, 128], [128 * H1_ROW, 4], [1, 520]])
                        S.I("sp", "dma_start", out=Vw[:], in_=src, writes=[Vk], dma=Vk)

                    def na_stage2(u, hp, neigh, pt, ptk, po, pok, so=so):
                        NJ = len(neigh)
                        for hl in range(2):
                            h = 2 * hp + hl
                            pc = (h // 4) * 512 + (h % 4) * 65
                            for j, (m, blk) in enumerate(neigh):
                                Vw, Vk = Vr.at(m // 4)
                                o0 = (j * 2 + hl) * 128
                                S.I("pe", "matmul", po[:, pc:pc + 65], lhsT=pt[:, o0:o0 + 128], rhs=Vw[:, m % 4, h * 65:(h + 1) * 65], start=(j == 0), stop=(j == NJ - 1),
                                    reads=[ptk, Vk], writes=[pok])
                        if hp == 3:
                            osb, osk = osr.next()
                            copy_op("act", osb[:, 0:260], po[:, 0:260], [pok], [osk])
                            copy_op("dve", osb[:, 260:520], po[:, 512:772], [pok], [osk])
                            r0 = so + u * 128
                            S.I("pool", "dma_start", out=ocs[r0:r0 + 128, :], in_=osb[:], reads=[osk], dma=osk)

                    load_kv_n(0)
                    for wn in range(NW):
                        Qw, Qk = Qr.at(wn)
                        c0 = so + wn * W
                        S.I("sp", "dma_start", out=Qw[:], in_=h1Tb[0:512, c0:c0 + W].rearrange("(c p) t -> p c t", p=128), writes=[Qk], dma=Qk)
                        if wn + 1 < NW:
                            load_kv_n(wn + 1)
                        pending = None
                        for tt in range(4):
                            u = wn * 4 + tt
                            neigh = na_neighbors(u, U)
                            NJ = len(neigh)
                            po, pok = por.next()
                            for hp in range(4):
                                ps, psk = psr.next()
                                for j, (m, blk) in enumerate(neigh):
                                    Kw, Kk = Kr.at(m // 4)
                                    for hl in range(2):
                                        h = 2 * hp + hl
                                        o0 = (j * 2 + hl) * 128
                                        S.I("pe", "matmul", ps[:, o0:o0 + 128], lhsT=Kw[:, h, (m % 4) * 128:(m % 4 + 1) * 128], rhs=Qw[:, hp, tt * 128:(tt + 1) * 128],
                                            start=True, stop=True, reads=[Kk, Qk], writes=[psk])
                                es, esk = esr.next()
                                pt, ptk = ptr.next()
                                ncol = NJ * 256
                                for c0_ in range(0, ncol, 512):
                                    c1_ = min(ncol, c0_ + 512)
                                    S.I("act", "activation", out=es[:, c0_:c1_], in_=ps[:, c0_:c1_], func=AF.Exp, scale=0.125, reads=[psk], writes=[esk])
                                for j, (m, blk) in enumerate(neigh):
                                    S.I("dve", "tensor_tensor", out=pt[:, j * 256:(j + 1) * 256], in0=es[:, j * 256:(j + 1) * 256], in1=EN[:, hp, blk * 256:(blk + 1) * 256], op=ALU.mult,
                                        reads=[esk, "EN"], writes=[ptk])
                                unit = (u, hp, neigh, pt, ptk, po, pok)
                                if pending is not None:
                                    na_stage2(*pending)
                                pending = unit
                        if pending is not None:
                            na_stage2(*pending)
                            pending = None
            S.barrier()

        if stop_after >= 8:
            with ExitStack() as st:
                lbt = sb(st, "lbt", [128, 16], F32)
                lbc = sb(st, "lbc", [128, 5, 8], F32)
                S.I("sp", "dma_start", out=lbt[:], in_=lb_in, writes=["lbt"], dma="lbt")
                S.I("act", "activation", out=lbt[:], in_=lbt[:], func=AF.Exp, reads=["lbt"], writes=["lbt"])
                lb4 = lbt[:].rearrange("p (d l h) -> p d l h", d=2, l=2)
                lbv = lambda i: lbc[:, i, :].rearrange("p (d h) -> p d h", d=2)
                S.I("dve", "tensor_tensor", out=lbv(3), in0=lb4[:, :, 0, :], in1=lb4[:, :, 1, :], op=ALU.add, reads=["lbt"], writes=["lbc3"])
                S.I("dve", "reciprocal", out=lbv(3), in_=lbv(3), reads=["lbc3"], writes=["lbc3"])
                S.I("dve", "tensor_tensor", out=lbv(0), in0=lb4[:, :, 1, :], in1=lbv(3), op=ALU.mult, reads=["lbt", "lbc3"], writes=["lbc0"])
                S.I("dve", "tensor_scalar", out=lbc[:, 1, :], in0=lbc[:, 0, :], scalar1=-1.0, scalar2=1.0, op0=ALU.mult, op1=ALU.add, reads=["lbc0"], writes=["lbc1"])
                S.I("dve", "tensor_scalar", out=lbc[:, 2, :], in0=lbc[:, 0, :], scalar1=1.0, scalar2=-1.0, op0=ALU.mult, op1=ALU.add, reads=["lbc0"], writes=["lbc2"])
                S.I("act", "activation", out=lbc[:, 4, :], in_=lbc[:, 1, :], func=AF.Ln, reads=["lbc1"], writes=["lbc4"])
                LBK = ["lbc0", "lbc1", "lbc2", "lbc4"]

                class Dir:
                    pass

                dirs = []
                for dr in range(2):
                    D = Dir()
                    D.dr = dr
                    nm = "hf" if dr == 0 else "hb"
                    D.z_r = Ring(nc, st, nm + "z", 2, [128, 512], F32)
                    D.q_r = Ring(nc, st, nm + "q", 2, [128, 512], F32)
                    D.t_r = [Ring(nc, st, nm + "t%d" % i, 2, [128, 512], F32) for i in range(6)]
                    D.qh = Ring(nc, st, nm + "qh", 2, [128, 4, 512], BF16)
                    D.qt = Ring(nc, st, nm + "qt", 2, [128, 4, 512], BF16)
                    D.kh = Ring(nc, st, nm + "kh", 2, [128, 4, 512], BF16)
                    D.khT = Ring(nc, st, nm + "khT", 2, [128, 4, 8, 128], BF16)
                    D.dec = Ring(nc, st, nm + "dec", 2, [128, 4, 8], F32)
                    D.v = Ring(nc, st, nm + "v", 2, [128, 8, 512], BF16)
                    D.S = sb(st, nm + "S", [128, 4, 128], F32)
                    D.Sbf = sb(st, nm + "Sbf", [128, 4, 128], BF16)
                    D.Am = Ring(nc, st, nm + "Am", 2, [128, 256], BF16)
                    D.ob = Ring(nc, st, nm + "ob", 2, [64, 512], F32)
                    D.psA = Ring(nc, st, nm + "psA", 1, [128, 512], F32, psum=True)
                    D.po = Ring(nc, st, nm + "po", 1, [128, 512], F32, psum=True)
                    D.psS = Ring(nc, st, nm + "psS", 1, [128, 512], F32, psum=True)
                    D.tril = trilf if dr == 0 else trilb
                    D.odst = odf if dr == 0 else odb
                    D.zrow = 512 if dr == 0 else 1024
                    for i in range(2):
                        for ring in (D.khT, D.v, D.Am):
                            t, k = ring.at(i)
                            S.I("pool", "memset", t[:], 0.0, writes=[k])
                    dirs.append(D)
                pTk_r = Ring(nc, st, "hpT", 1, [128, 1024], BF16, psum=True)

                def prep_begin(D, so, b):
                    c0 = so + b * 512
                    qh, qhk = D.qh.next()
                    qt, qtk = D.qt.next()
                    kh, khk = D.kh.next()
                    khT, khTk = D.khT.next()
                    dec, deck = D.dec.next()
                    v, vk = D.v.next()
                    S.I("sp", "dma_start", out=v[0:64, :, :], in_=bass.AP(h1tm_h, c0 * H1_ROW + H1_ID, [[H1_ROW, 64], [64 * H1_ROW, 8], [1, 512]]), writes=[vk], dma=vk)
                    return dict(qh=qh, qhk=[(qhk, h) for h in range(4)], qt=qt, qtk=[(qtk, h) for h in range(4)], kh=kh, khk=[(khk, h) for h in range(4)],
                                khT=khT, khTk=[(khTk, h) for h in range(4)], dec=dec, deck=[(deck, h) for h in range(4)], v=v, vk=vk, c0=c0)

                def prep_head(D, getP, h, c0):
                    dr = D.dr
                    z, zk = D.z_r.next()
                    q, qk = D.q_r.next()
                    S.I("sp", "dma_start", out=z[:], in_=h1Tf[D.zrow + h * 128:D.zrow + (h + 1) * 128, c0:c0 + 512], writes=[zk], dma=zk)
                    S.I("sp", "dma_start", out=q[:], in_=h1Tf[h * 128:(h + 1) * 128, c0:c0 + 512], writes=[qk], dma=qk)
                    T_ = [r_.next() for r_ in D.t_r]
                    (t0_, k0), (t1_, k1), (t2_, k2), (t3_, k3), (t4_, k4), (t5_, k5) = T_
                    lb_ap = lbc[:, 0, dr * 4 + h:dr * 4 + h + 1]
                    lnoml_ap = lbc[:, 4, dr * 4 + h:dr * 4 + h + 1]
                    v3 = lambda t: t[:].rearrange("p (c t) -> p c t", t=64)
                    S.I("act", "activation", out=t0_[:], in_=z[:], func=AF.Exp, scale=-1.0, reads=[zk], writes=[k0])
                    S.I("act", "activation", out=t1_[:], in_=t0_[:], func=AF.Ln, scale=1.0, bias=1.0, reads=[k0], writes=[k1])
                    S.I("act", "activation", out=t2_[:], in_=t0_[:], func=AF.Ln, scale=lb_ap, bias=1.0, reads=[k0] + LBK, writes=[k2])
                    yield
                    S.I("dve", "tensor_tensor", out=t2_[:], in0=t2_[:], in1=t1_[:], op=ALU.subtract, reads=[k2, k1], writes=[k2])
                    S.I("dve", "tensor_tensor_scan", out=t3_[:], data0=scanmask, data1=t2_[:], initial=0.0, op0=ALU.mult, op1=ALU.add, reads=[k2, "cstf"], writes=[k3])
                    tot = v3(t3_)[:, :, 63]
                    totb = v3(t3_)[:, :, 63:64].broadcast_to([128, 8, 64])
                    if dr == 0:
                        S.I("dve", "tensor_tensor", out=v3(t4_), in0=v3(t3_), in1=totb, op=ALU.subtract, reads=[k3], writes=[k4])
                        bq, bqk = t3_, k3
                    else:
                        S.I("dve", "tensor_tensor", out=t4_[:], in0=t2_[:], in1=t3_[:], op=ALU.subtract, reads=[k2, k3], writes=[k4])
                        S.I("dve", "tensor_tensor", out=v3(t5_), in0=v3(t4_), in1=totb, op=ALU.add, reads=[k4, k3], writes=[k5])
                        bq, bqk = t5_, k5
                    S.I("dve", "tensor_tensor", out=t1_[:], in0=t1_[:], in1=z[:], op=ALU.add, reads=[k1, zk], writes=[k1])
                    S.I("dve", "tensor_tensor", out=t1_[:], in0=t1_[:], in1=t4_[:], op=ALU.add, reads=[k1, k4], writes=[k1])
                    yield
                    P = getP()
                    qh, qt, kh, khT, dec = P["qh"], P["qt"], P["kh"], P["khT"], P["dec"]
                    qhk, qtk, khk, khTk, deck = P["qhk"][0][0], P["qtk"][0][0], P["khk"][0][0], P["khTk"][0][0], P["deck"][0][0]
                    S.I("act", "activation", out=dec[:, h, :], in_=tot, func=AF.Exp, reads=[k3], writes=[(deck, h)])
                    S.I("act", "activation", out=kh[:, h, :], in_=t1_[:], func=AF.Exp, scale=-1.0, bias=lnoml_ap, reads=[k1] + LBK, writes=[(khk, h)])
                    S.I("act", "activation", out=t0_[:], in_=t4_[:], func=AF.Exp, reads=[k4], writes=[k0])
                    S.I("act", "activation", out=t2_[:], in_=bq[:], func=AF.Exp, reads=[bqk], writes=[k2])
                    yield
                    S.I("dve", "tensor_tensor", out=qh[:, h, :], in0=q[:], in1=t0_[:], op=ALU.mult, reads=[qk, k0], writes=[(qhk, h)])
                    S.I("dve", "tensor_tensor", out=qt[:, h, :], in0=q[:], in1=t2_[:], op=ALU.mult, reads=[qk, k2], writes=[(qtk, h)])
                    pT, pTk = pTk_r.next()
                    for c in range(8):
                        S.I("pe", "transpose", out=pT[0:64, c * 128:(c + 1) * 128], in_=kh[:, h, c * 64:(c + 1) * 64], identity=ident[:], reads=[(khk, h), "ident"], writes=[pTk])
                    copy_op("act", khT[0:64, h, :, :], pT[0:64, :].rearrange("p (c k) -> p c k", c=8), [pTk], [(khTk, h)])

                def chunk_a(D, P, so, b, c):
                    cs = slice(c * 64, (c + 1) * 64)
                    psA, psAk = D.psA.next()
                    for h in range(4):
                        S.I("pe", "matmul", psA[0:64, h * 64:(h + 1) * 64], lhsT=P["kh"][:, h, cs], rhs=P["qh"][:, h, cs], start=True, stop=True,
                            reads=[P["khk"][h], P["qhk"][h]], writes=[psAk])
                    Am, Amk = D.Am.next()
                    S.I("dve", "tensor_tensor", out=Am[0:64, :], in0=psA[0:64, 0:256], in1=D.tril, op=ALU.mult, reads=[psAk, "cstf"], writes=[Amk])
                    psS, psSk = D.psS.next()
                    for h in range(4):
                        S.I("pe", "matmul", psS[:, h * 128:(h + 1) * 128], lhsT=P["khT"][:, h, c, :], rhs=P["v"][:, c, h * 128:(h + 1) * 128], start=True, stop=True,
                            reads=[P["khTk"][h], P["vk"]], writes=[psSk])
                    return Am, Amk, psS, psSk

                def chunk_b(D, P, so, b, c, Am, Amk, psS, psSk):
                    nm = "hf" if D.dr == 0 else "hb"
                    cs = slice(c * 64, (c + 1) * 64)
                    po, pok = D.po.next()
                    for h in range(4):
                        S.I("pe", "matmul", po[0:64, h * 128:(h + 1) * 128], lhsT=P["qt"][:, h, cs], rhs=D.Sbf[:, h, :], start=True, stop=False,
                            reads=[P["qtk"][h], nm + "Sbf"], writes=[pok])
                        S.I("pe", "matmul", po[0:64, h * 128:(h + 1) * 128], lhsT=Am[:, h * 64:(h + 1) * 64], rhs=P["v"][:, c, h * 128:(h + 1) * 128], start=False, stop=True,
                            reads=[Amk, P["vk"]], writes=[pok])
                    for h in range(4):
                        S.I("dve", "scalar_tensor_tensor", out=D.S[:, h, :], in0=D.S[:, h, :], scalar=P["dec"][:, h, c:c + 1], in1=psS[:, h * 128:(h + 1) * 128],
                            op0=ALU.mult, op1=ALU.add, reads=[(nm + "S", h), P["deck"][h], psSk], writes=[(nm + "S", h)])
                    S.I("act", "copy", out=D.Sbf[:].rearrange("p h v -> p (h v)"), in_=D.S[:].rearrange("p h v -> p (h v)"), reads=[(nm + "S", h) for h in range(4)], writes=[nm + "Sbf"])
                    ob, obk = D.ob.next()
                    copy_op("act", ob[:], po[0:64, :], [pok], [obk])
                    r0 = so + b * 512 + c * 64
                    S.I("act", "dma_start", out=D.odst[r0:r0 + 64, :], in_=ob[:], reads=[obk], dma=obk)

                def chunk_pair(Pf, Pb, so, bf_, bb_, ci):
                    af = chunk_a(dirs[0], Pf, so, bf_, ci)
                    ab = chunk_a(dirs[1], Pb, so, bb_, 7 - ci)
                    chunk_b(dirs[0], Pf, so, bf_, ci, *af)
                    chunk_b(dirs[1], Pb, so, bb_, 7 - ci, *ab)

                for si, T in enumerate(seq_lens):
                    so = seq_off[si]
                    NB = T // 512
                    for D in dirs:
                        nm = "hf" if D.dr == 0 else "hb"
                        S.I("pool", "memset", D.S[:], 0.0, writes=[(nm + "S", h) for h in range(4)])
                        S.I("pool", "memset", D.Sbf[:], 0.0, writes=[nm + "Sbf"])
                    Pst = {0: (prep_begin(dirs[0], so, 0), prep_begin(dirs[1], so, NB - 1))}
                    for h in range(4):
                        for dr in range(2):
                            blk_tok = 0 if dr == 0 else NB - 1
                            for _ in prep_head(dirs[dr], (lambda dr=dr: Pst[0][dr]), h, so + blk_tok * 512):
                                pass
                    off = [-2, -1, 0, 1, 2, 3, 4, 4]
                    starts = {}
                    for k in range(1, NB):
                        for i in range(8):
                            starts.setdefault(8 * (k - 1) + off[i], []).append((k, i))
                    active = []

                    def start_heads(P):
                        for (k, i) in starts.get(P, []):
                            dr, h = i % 2, i // 2
                            blk_tok = k if dr == 0 else NB - 1 - k
                            active.append(prep_head(dirs[dr], (lambda k=k, dr=dr: Pst[k][dr]), h, so + blk_tok * 512))

                    def advance():
                        for g in list(active):
                            try:
                                next(g)
                            except StopIteration:
                                active.remove(g)

                    for P in (-2, -1):
                        start_heads(P)
                        advance()
                    for k in range(NB):
                        if k + 1 < NB:
                            Pst[k + 1] = (prep_begin(dirs[0], so, k + 1), prep_begin(dirs[1], so, NB - 2 - k))
                        Pf, Pb = Pst[k]
                        for ci in range(8):
                            chunk_pair(Pf, Pb, so, k, NB - 1 - k, ci)
                            start_heads(8 * k + ci)
                            advance()
                    while active:
                        advance()
            S.barrier()

        if stop_after >= 9:
            with ExitStack() as st:
                wo = sb(st, "wo1", [128, 8, 1024], BF16)
                with ExitStack() as st2:
                    stage = Ring(nc, st2, "wstg2", 3, [128, 1024], F32)
                    load_weight(wo, w_out_odd, 8, 1024, stage)
                    S.barrier()
                lnt = sb(st, "lnt1", [128, 2, 1024], F32)
                S.I("sp", "dma_start", out=lnt[:, 0, :], in_=lng_in[:, 1024:2048], writes=["ln"], dma="ln")
                S.I("sp", "dma_start", out=lnt[:, 1, :], in_=lnb_in[:, 1024:2048], writes=["ln"], dma="ln")
                gnt = sb(st, "gnt", [128, 512], F32)
                S.I("sp", "dma_start", out=gnt[:], in_=gn_in, writes=["gn"], dma="gn")
                RD4 = {"oc": 4, "of": 5, "obk": 3, "gt1": 5, "xs3": 6, "vln1": 6, "yo": 3, "stt1": 8, "smt1": 6, "tmpc1": 3, "sq1": 3, "yt1": 3, "yT1": 3}
                oc_r = Ring(nc, st, "oc", RD4['oc'], [128, 520], F32)
                of_r = Ring(nc, st, "of", RD4['of'], [128, 512], F32)
                ob_r = Ring(nc, st, "obk", RD4['obk'], [128, 512], F32)
                g_r = Ring(nc, st, "gt1", RD4['gt1'], [128, 1024], BF16)
                xs_r = Ring(nc, st, "xs3", RD4['xs3'], [128, 1024], F32)
                v_r = Ring(nc, st, "vln1", RD4['vln1'], [128, 1024], F32)
                x1_r = Ring(nc, st, "yo", RD4['yo'], [128, 1024], F32)
                st_r = Ring(nc, st, "stt1", RD4['stt1'], [128, 8], F32)
                sm_r = Ring(nc, st, "smt1", RD4['smt1'], [128, 16], F32)
                tmp_r = Ring(nc, st, "tmpc1", RD4['tmpc1'], [128, 1024], F32)
                sq_r = Ring(nc, st, "sq1", RD4['sq1'], [128, 512], F32)
                junk = sb(st, "junk1", [128, 1024], BF16)
                y_r = Ring(nc, st, "yt1", RD4['yt1'], [128, 1024], BF16)
                yT_r = Ring(nc, st, "yT1", RD4['yT1'], [128, 8, 128], BF16)
                pT_r = Ring(nc, st, "pT3", 2, [128, 1024], BF16, psum=True)
                z_r = Ring(nc, st, "zps1", 6, [128, 512], F32, psum=True)
                def tile_gen(ti):
                    r0 = ti * 128
                    oc, ock = oc_r.next()
                    S.I("sp", "dma_start", out=oc[:], in_=ocs[r0:r0 + 128, :], writes=[ock], dma=ock)
                    of, ofk = of_r.next()
                    S.I("sp", "dma_start", out=of[:], in_=odf[r0:r0 + 128, :], writes=[ofk], dma=ofk)
                    ob, obk = ob_r.next()
                    S.I("sp", "dma_start", out=ob[:], in_=odb[r0:r0 + 128, :], writes=[obk], dma=obk)
                    gt, gk = g_r.next()
                    S.I("sp", "dma_start", out=gt[:], in_=h1tm[r0:r0 + 128, H1_GC:H1_GC + 1024], writes=[gk], dma=gk)
                    xs, xsk = xs_r.next()
                    S.I("sp", "dma_start", out=xs[:], in_=x1s[r0:r0 + 128, :], writes=[xsk], dma=xsk)
                    yield
                    sm, smk = sm_r.next()
                    tmp, tmpk = tmp_r.next()
                    y, yk = y_r.next()
                    sq, sqk = sq_r.next()
                    oc3 = oc[:].rearrange("p (h d) -> p h d", d=65)
                    S.I("dve", "reciprocal", out=sm[:, 0:8], in_=oc3[:, :, 64], reads=[ock], writes=[(smk, 0)])
                    S.I("dve", "tensor_tensor", out=of[:], in0=of[:], in1=ob[:], op=ALU.add, reads=[ofk, obk], writes=[ofk])
                    for h in range(4):
                        S.I("act", "activation", out=sq[:, h * 128:(h + 1) * 128], in_=of[:, h * 128:(h + 1) * 128], func=AF.Square, accum_out=sm[:, 8 + h:9 + h],
                            reads=[ofk], writes=[sqk, (smk, 1)])
                    yield
                    S.I("dve", "tensor_tensor", out=tmp[:, 0:512].rearrange("p (h d) -> p h d", d=64), in0=oc3[:, :, 0:64],
                        in1=sm[:, 0:8].unsqueeze(2).broadcast_to([128, 8, 64]), op=ALU.mult, reads=[ock, (smk, 0)], writes=[(tmpk, 0)])
                    S.I("dve", "tensor_scalar", out=sm[:, 8:12], in0=sm[:, 8:12], scalar1=1.0 / 128, scalar2=RMS_EPS, op0=ALU.mult, op1=ALU.add, reads=[(smk, 1)], writes=[(smk, 1)])
                    S.I("act", "activation", out=sm[:, 12:16], in_=sm[:, 8:12], func=AF.Sqrt, reads=[(smk, 1)], writes=[(smk, 2)])
                    yield
                    S.I("dve", "reciprocal", out=sm[:, 12:16], in_=sm[:, 12:16], reads=[(smk, 2)], writes=[(smk, 2)])
                    S.I("dve", "tensor_tensor", out=tmp[:, 512:1024].rearrange("p (h d) -> p h d", d=128), in0=of[:].rearrange("p (h d) -> p h d", d=128),
                        in1=sm[:, 12:16].unsqueeze(2).broadcast_to([128, 4, 128]), op=ALU.mult, reads=[ofk, (smk, 2)], writes=[(tmpk, 1)])
                    S.I("dve", "tensor_tensor", out=tmp[:, 512:1024], in0=tmp[:, 512:1024], in1=gnt[:], op=ALU.mult, reads=[(tmpk, 1), "gn"], writes=[(tmpk, 1)])
                    S.I("dve", "tensor_tensor", out=y[:], in0=tmp[:], in1=gt[:], op=ALU.mult, reads=[(tmpk, 0), (tmpk, 1), gk], writes=[yk])
                    pT, pTk = pT_r.next()
                    for c in range(8):
                        S.I("pe", "transpose", out=pT[:, c * 128:(c + 1) * 128], in_=y[:, c * 128:(c + 1) * 128], identity=ident[:], reads=[yk, "ident"], writes=[pTk])
                    yT, yTk = yT_r.next()
                    copy_op("act", yT[:], pT[:].rearrange("p (c t) -> p c t", c=8), [pTk], [yTk])
                    zps, zk = [], []
                    for hf in range(2):
                        z, zkk = z_r.next()
                        for c in range(8):
                            S.I("pe", "matmul", z[:], lhsT=yT[:, c, :], rhs=wo[:, c, hf * 512:(hf + 1) * 512], start=(c == 0), stop=(c == 7), reads=[yTk], writes=[zkk])
                        zps.append(z)
                        zk.append(zkk)
                    yield
                    for _ in layer_norm_gen(zps, zk, xs, xsk, lnt[:, 0, :], lnt[:, 1, :], v_r, st_r, junk, x1_r, y_out[r0:r0 + 128, :]):
                        yield

                run_skewed([tile_gen(ti) for ti in range(TT // 128)])
            S.barrier()

        fin = Op("sp", None)
        for k in S.dsem:
            fin.deps.append(("dma", k, S.dcnt[k]))
        S.ops["sp"].append(fin)
        with nc.Block() as block:
            S.emit(block)
    return nc


def host_consts(t5_table, sink_a, rpb_c, lb_d, gnorm_d, ln_g, ln_b):
    cst = np.zeros((128, 1280), np.float32)
    cst[:, 0:128] = np.eye(128, dtype=np.float32)
    s = np.arange(64)[:, None]
    t = np.arange(64)[None, :]
    cst[0:64, 128:384] = np.tile((s <= t).astype(np.float32), (1, 4))
    cst[0:64, 384:640] = np.tile((s >= t).astype(np.float32), (1, 4))
    sm = np.ones((512,), np.float32)
    sm[0::64] = 0.0
    cst[:, 640:1152] = sm[None, :]
    lb = np.ascontiguousarray(np.transpose(lb_d.reshape(2, 2, 4, 128), (3, 0, 1, 2))).reshape(128, 16)
    return {
        "tab0": _tables_l0(np.asarray(t5_table, np.float32)),
        "tabn": _tables_na(np.asarray(rpb_c, np.float32)),
        "sink": np.ascontiguousarray(np.broadcast_to(np.asarray(sink_a, np.float32).reshape(1, 8), (128, 8))),
        "lb": np.ascontiguousarray(lb.astype(np.float32)),
        "gnorm": np.ascontiguousarray(np.broadcast_to(np.asarray(gnorm_d, np.float32).reshape(1, 512), (128, 512))),
        "lng": np.ascontiguousarray(np.broadcast_to(np.asarray(ln_g, np.float32).reshape(1, 2048), (128, 2048))),
        "lnb": np.ascontiguousarray(np.broadcast_to(np.asarray(ln_b, np.float32).reshape(1, 2048), (128, 2048))),
        "cst": cst,
    }


N_CORES = 8
_NC_CACHE = {}


def kernel(x_prompt, x_sample, t5_table, w_in_even, sink_a, w_out_even, w_in_odd, rpb_c, lb_d, gnorm_d, w_out_odd, ln_g, ln_b):
    x_prompt = np.asarray(x_prompt, np.float32)
    x_sample = np.asarray(x_sample, np.float32)
    nb, T, D = x_prompt.shape
    ns, Ts, _ = x_sample.shape
    per = nb // N_CORES
    seq_lens = [T] * per + [Ts]
    key = tuple(seq_lens)
    if key not in _NC_CACHE:
        _NC_CACHE[key] = build(seq_lens)
    nc = _NC_CACHE[key]
    consts = host_consts(np.asarray(t5_table), np.asarray(sink_a), np.asarray(rpb_c), np.asarray(lb_d), np.asarray(gnorm_d), np.asarray(ln_g), np.asarray(ln_b))
    shared = dict(consts)
    shared["w_in_even"] = np.ascontiguousarray(np.asarray(w_in_even, np.float32)[0])
    shared["w_out_even"] = np.ascontiguousarray(np.asarray(w_out_even, np.float32)[0])
    shared["w_in_odd"] = np.ascontiguousarray(np.asarray(w_in_odd, np.float32)[0])
    shared["w_out_odd"] = np.ascontiguousarray(np.asarray(w_out_odd, np.float32)[0])
    in_maps = []
    for c in range(N_CORES):
        xs = [x_prompt[c * per + i] for i in range(per)] + [x_sample[c % ns]]
        m = dict(shared)
        m["x"] = np.ascontiguousarray(np.concatenate(xs, axis=0))
        in_maps.append(m)
    res = run_bass_kernel_spmd(nc, in_maps, core_ids=list(range(N_CORES)))
    y_prompt = np.empty((nb, T, D), np.float32)
    y_sample = np.empty((ns, Ts, D), np.float32)
    for c in range(N_CORES):
        y = np.asarray(res.results[c]["y"], np.float32)
        for i in range(per):
            y_prompt[c * per + i] = y[i * T:(i + 1) * T]
        if c < ns:
            y_sample[c] = y[per * T:per * T + Ts]
    return (y_prompt, y_sample)
```

```python
import math
from contextlib import ExitStack

import numpy as np
import concourse.bass as bass
import concourse.mybir as mybir
from concourse.bass_utils import run_bass_kernel_spmd

F32 = mybir.dt.float32
BF16 = mybir.dt.bfloat16
ALU = mybir.AluOpType
AF = mybir.ActivationFunctionType
AX = mybir.AxisListType

D_MODEL = 1024
DEPTH = 2
ALPHA = (2.0 * DEPTH) ** 0.25
LN_EPS = 1e-5
RMS_EPS = 1e-6
NEG = -30000.0

QA, KA, VA, QB, KB, VB, GA, GB, EVEN_IN = 0, 512, 640, 768, 1536, 2304, 3072, 3584, 3840
QC, KC, VC, QD, IDD, ZF, ZB, GC, GD, ODD_IN = 0, 512, 1024, 1536, 2048, 2560, 3072, 3584, 4096, 4608

H0_VA, H0_VB, H0_GA, H0_GB, H0_ROW = 0, 130, 910, 1422, 1680
H1_VC, H1_ID, H1_GC, H1_GD, H1_ROW = 0, 520, 1032, 1544, 2056
OA_ROW = 1300


class Op:
    __slots__ = ("eng", "fn", "deps", "dma_sem", "need_inc", "val")

    def __init__(self, eng, fn):
        self.eng = eng
        self.fn = fn
        self.deps = []
        self.dma_sem = None
        self.need_inc = False
        self.val = 0


class Sched:
    ENGS = ("pe", "act", "dve", "pool", "sp")
    SAME_ENG_RAW = ("act", "dve", "pool")

    def __init__(self, nc, stack):
        self.nc = nc
        self.stack = stack
        self.ops = {e: [] for e in self.ENGS}
        self.sem = {e: stack.enter_context(nc.semaphore("sem_" + e)) for e in self.ENGS}
        self.lastw = {}
        self.readers = {}
        self.dsem = {}
        self.dcnt = {}
        self.key2idx = {}
        self.next_idx = {}

    def op(self, eng, fn, reads=(), writes=(), dma=None):
        o = Op(eng, fn)
        deps = []
        for k in reads:
            w = self.lastw.get(k)
            if w is not None:
                deps.append((w, True))
        for k in writes:
            w = self.lastw.get(k)
            if w is not None:
                deps.append((w, False))
            for r in self.readers.get(k, ()):
                deps.append((r, False))
        for (d, raw) in deps:
            if d.dma_sem is not None:
                o.deps.append(("dma", d.dma_sem, self.dcnt[d.dma_sem]))
                continue
            if d.eng == eng and dma is None:
                if eng not in self.SAME_ENG_RAW:
                    continue
            d.need_inc = True
            o.deps.append(("op", d))
        if dma is not None:
            if (eng, dma) not in self.key2idx:
                n_ = self.next_idx.get(eng, 0)
                self.next_idx[eng] = n_ + 1
                idx = (eng, n_)
                self.key2idx[(eng, dma)] = idx
                if idx not in self.dsem:
                    self.dsem[idx] = self.stack.enter_context(self.nc.semaphore("dsem_%s%d" % idx))
                    self.dcnt[idx] = 0
            idx = self.key2idx[(eng, dma)]
            self.dcnt[idx] += 16
            o.dma_sem = idx
        for k in reads:
            lst = self.readers.setdefault(k, [])
            if dma is None:
                for idx in range(len(lst)):
                    if lst[idx].dma_sem is None and lst[idx].eng == eng:
                        lst[idx] = o
                        break
                else:
                    lst.append(o)
            else:
                lst.append(o)
        for k in writes:
            self.lastw[k] = o
            self.readers[k] = []
        self.ops[eng].append(o)
        return o

    def I(self, eng, name, *a, reads=(), writes=(), dma=None, **kw):
        return self.op(eng, (name, a, kw), reads=reads, writes=writes, dma=dma)

    def barrier(self):
        lasts = {}
        for e in self.ENGS:
            for o in reversed(self.ops[e]):
                if o.fn is not None and o.dma_sem is None:
                    lasts[e] = o
                    break
        for e in self.ENGS:
            o = Op(e, None)
            for f, l in lasts.items():
                if f != e:
                    l.need_inc = True
                    o.deps.append(("op", l))
            for k in self.dsem:
                o.deps.append(("dma", k, self.dcnt[k]))
            self.ops[e].append(o)
        self.lastw = {}
        self.readers = {}
        self.key2idx = {}
        self.next_idx = {}

    def emit(self, block):
        for e in self.ENGS:
            c = 0
            for o in self.ops[e]:
                if o.need_inc:
                    c += 1
                    o.val = c
        sem = self.sem
        dsem = self.dsem

        def run(engname, engine):
            seen = {}
            for o in self.ops[engname]:
                need = {}
                for d in o.deps:
                    if d[0] == "dma":
                        k = ("d", d[1])
                        v = d[2]
                        s = dsem[d[1]]
                    else:
                        k = ("e", d[1].eng)
                        v = d[1].val
                        s = sem[d[1].eng]
                    if v > seen.get(k, 0) and v > need.get(k, (0, None))[0]:
                        need[k] = (v, s)
                for k, (v, s) in need.items():
                    engine.wait_ge(s, v)
                    seen[k] = v
                if o.fn is None:
                    continue
                name, a, kw = o.fn
                ins = getattr(engine, name)(*a, **kw)
                if o.dma_sem is not None:
                    ins.then_inc(dsem[o.dma_sem], 16)
                elif o.need_inc:
                    ins.then_inc(sem[engname], 1)

        block.tensor(lambda e: run("pe", e))
        block.scalar(lambda e: run("act", e))
        block.vector(lambda e: run("dve", e))
        block.gpsimd(lambda e: run("pool", e))
        block.sync(lambda e: run("sp", e))


class Ring:
    def __init__(self, nc, st, name, n, shape, dtype, psum=False):
        self.name = name
        self.n = n
        self.t = []
        for i in range(n):
            if psum:
                self.t.append(st.enter_context(nc.psum_tensor("%s%d" % (name, i), shape, dtype)))
            else:
                self.t.append(st.enter_context(nc.sbuf_tensor("%s%d" % (name, i), shape, dtype)))
        self.i = 0

    def next(self):
        i = self.i % self.n
        self.i += 1
        return self.t[i], (self.name, i)

    def at(self, i):
        i = i % self.n
        return self.t[i], (self.name, i)

    def can_next(self):
        if not hasattr(self, "busy"):
            self.busy = [False] * self.n
        return not self.busy[self.i % self.n]

    def next_hold(self):
        if not hasattr(self, "busy"):
            self.busy = [False] * self.n
        i = self.i % self.n
        assert not self.busy[i], self.name
        self.busy[i] = True
        self.i += 1
        return self.t[i], (self.name, i), i

    def release(self, i):
        self.busy[i] = False


def _t5_buckets(rel):
    nb = 16
    ret = (rel > 0).astype(np.int64) * nb
    n = np.abs(rel)
    max_exact = nb // 2
    large = max_exact + (np.log(np.maximum(n, 1) / max_exact) / math.log(1024 / max_exact) * (nb - max_exact)).astype(np.int64)
    large = np.minimum(large, nb - 1)
    return ret + np.where(n < max_exact, n, large)


def _tables_l0(t5_table):
    k = np.arange(128)[:, None, None]
    j = np.arange(3)[None, :, None]
    q = np.arange(128)[None, None, :]
    rel = (j - 1) * 128 + k - q
    out = np.full((5, 128, 3, 4, 128), NEG, np.float32)
    specs = [(128, 1, [0, 1, 2, 3]), (128, 1, [4, 5, 6, 7]), (64, 1, [8, 9, 10, 11]), (64, 4, [12, 13, 14, 15]), (64, 16, [16, 17, 18, 19])]
    for s, (hw, d, heads) in enumerate(specs):
        valid = np.abs(rel) <= hw
        bk = _t5_buckets(rel * d)
        for hi, h in enumerate(heads):
            pos = ((hi % 2) * 2 + hi // 2) if s < 2 else hi
            out[s, :, :, pos, :] = np.where(valid, t5_table[bk, h], NEG)
    return out.reshape(5, 128, 1536)


def _tables_na(rpb):
    rpb = rpb.reshape(8, 15, 31)
    k = np.arange(128)[:, None]
    q = np.arange(128)[None, :]
    krl, kc = k // 64, k % 64
    qrl, qc = q // 64, q % 64
    sc = np.clip(qc - 8, 0, 48)
    colmask = (kc >= sc) & (kc < sc + 16)
    colidx = np.clip(kc - qc + 15, 0, 30)
    out = np.full((4, 128, 9, 2, 128), NEG, np.float32)
    for b in range(9):
        off = b - 3 if b < 7 else (-2 if b == 7 else 2)
        dr = 2 * off + krl - qrl
        valid = colmask & (np.abs(dr) <= 7)
        if b >= 7:
            valid = valid & (dr >= -4) & (dr <= 3)
        dri = np.clip(dr + 7, 0, 14)
        for h in range(8):
            vals = rpb[h][dri, colidx]
            out[h // 2, :, b, h % 2, :] = np.where(valid, vals, NEG)
    return out.reshape(4, 128, 9 * 256)


def na_neighbors(u, U):
    if U <= 4:
        return [(m, m - u + 3) for m in range(U)]
    if u == 0 or u == 1:
        return [(m, m - u + 3) for m in range(0, 4)]
    if u == U - 1 or u == U - 2:
        return [(m, m - u + 3) for m in range(U - 4, U)]
    res = []
    for off in (-2, -1, 0, 1, 2):
        m = u + off
        blk = 7 if off == -2 else (8 if off == 2 else off + 3)
        res.append((m, blk))
    return res


def build(seq_lens, debug=False, stop_after=99, lvl=9, skipA=False):
    nc = bass.Bass("TRN2", target_bir_lowering=False)
    TT = sum(seq_lens)
    seq_off = [sum(seq_lens[:i]) for i in range(len(seq_lens))]
    okind = "ExternalOutput" if debug else "Internal"

    def dram(name, shape, dt, kind):
        return nc.dram_tensor(name, shape, dt, kind=kind).ap()

    x_in = dram("x", [TT, D_MODEL], F32, "ExternalInput")
    y_out = dram("y", [TT, D_MODEL], F32, "ExternalOutput")
    w_in_even = dram("w_in_even", [D_MODEL, EVEN_IN], F32, "ExternalInput")
    w_out_even = dram("w_out_even", [768, D_MODEL], F32, "ExternalInput")
    w_in_odd = dram("w_in_odd", [D_MODEL, ODD_IN], F32, "ExternalInput")
    w_out_odd = dram("w_out_odd", [1024, D_MODEL], F32, "ExternalInput")
    tab0_in = dram("tab0", [5, 128, 1536], F32, "ExternalInput")
    tabn_in = dram("tabn", [4, 128, 2304], F32, "ExternalInput")
    sink_in = dram("sink", [128, 8], F32, "ExternalInput")
    lb_in = dram("lb", [128, 16], F32, "ExternalInput")
    gn_in = dram("gnorm", [128, 512], F32, "ExternalInput")
    lng_in = dram("lng", [128, 2048], F32, "ExternalInput")
    lnb_in = dram("lnb", [128, 2048], F32, "ExternalInput")
    cst_in = dram("cst", [128, 1280], F32, "ExternalInput")

    h0T = dram("h0T", [2176, TT], BF16, okind)
    h0tm = dram("h0tm", [TT, H0_ROW], BF16, okind)
    oatt = dram("oatt", [TT, OA_ROW], F32, okind)
    x1s = dram("x1s", [TT, D_MODEL], F32, okind)
    h1Tb = dram("h1Tb", [1024, TT], BF16, okind)
    h1Tf = dram("h1Tf", [1536, TT], F32, okind)
    h1tm = dram("h1tm", [TT, H1_ROW], BF16, okind)
    ocs = dram("ocs", [TT, 520], F32, okind)
    odf = dram("odf", [TT, 512], F32, okind)
    odb = dram("odb", [TT, 512], F32, okind)

    with ExitStack() as top:
        S = Sched(nc, top)
        sb = lambda st, name, shape, dt: st.enter_context(nc.sbuf_tensor(name, shape, dt))
        cyc = {"i": 0}

        def rr(engs):
            cyc["i"] += 1
            return engs[cyc["i"] % len(engs)]

        def copy_op(eng, out, in_, reads, writes):
            if eng == "act":
                S.I("act", "copy", out=out, in_=in_, reads=reads, writes=writes)
            else:
                S.I(eng, "tensor_copy", out=out, in_=in_, reads=reads, writes=writes)

        cstf = sb(top, "cstf", [128, 1280], F32)
        ident = sb(top, "ident", [128, 128], BF16)
        S.I("sp", "dma_start", out=cstf[:], in_=cst_in, writes=["cstf"], dma="cstf")
        S.I("dve", "tensor_copy", out=ident[:], in_=cstf[:, 0:128], reads=["cstf"], writes=["ident"])
        trilf = cstf[0:64, 128:384]
        trilb = cstf[0:64, 384:640]
        scanmask = cstf[:, 640:1152]

        def load_weight(dst, src, KCn, F, stage_ring, pieces=None):
            if pieces is None:
                pieces = [(c0, min(1024, F - c0), c0) for c0 in range(0, F, 1024)]
            for kc in range(KCn):
                for (c0, cw, d0) in pieces:
                    stg, sk = stage_ring.next()
                    S.I("sp", "dma_start", out=stg[:, 0:cw], in_=src[kc * 128:(kc + 1) * 128, c0:c0 + cw], writes=[sk], dma=sk)
                    copy_op(rr(["act", "dve"]), dst[:, kc, d0:d0 + cw], stg[:, 0:cw], [sk], [("w", id(dst), kc, d0)])

        def make_xT(xs, xs_k, xb_ring, pT_ring, xT, xT_k, tcol, cast_e=("dve", "act"), copy_e=("act", "dve")):
            xb, xb_k = xb_ring.next()
            copy_op(rr(list(cast_e)), xb[:], xs[:], [xs_k], [xb_k])
            pT, pT_k = pT_ring.next()
            for c in range(8):
                S.I("pe", "transpose", out=pT[:, c * 128:(c + 1) * 128], in_=xb[:, c * 128:(c + 1) * 128], identity=ident[:],
                    reads=[xb_k, "ident"], writes=[pT_k])
            copy_op(rr(list(copy_e)), xT[:, :, tcol:tcol + 128], pT[:].rearrange("p (c t) -> p c t", c=8), [pT_k], [xT_k])

        def project(xT, xT_k, wb, acc_r, fm_r, fm_list, tm_r, tm_groups, t0, tm_dst, ntt=4, tts=None, ev=("act", "dve")):
            ncols = ntt * 128
            for (f0, dst, ring) in fm_list:
                acc, acc_k = acc_r.next()
                for c in range(8):
                    S.I("pe", "matmul", acc[:, 0:ncols], lhsT=wb[:, c, f0:f0 + 128], rhs=xT[:, c, 0:ncols], start=(c == 0), stop=(c == 7),
                        reads=[xT_k], writes=[acc_k])
                fm, fm_k = ring.next()
                copy_op(rr(list(ev)), fm[:, 0:ncols], acc[:, 0:ncols], [acc_k], [fm_k])
                S.I("pool", "dma_start", out=dst, in_=fm[:, 0:ncols], reads=[fm_k], dma=fm_k)
            for tt in (range(ntt) if tts is None else tts):
                tm, tm_k = tm_r.next()
                for grp in tm_groups:
                    if len(grp) == 5:
                        wc, width, sc, nh, func = grp
                        plist = [(0, width, sc, nh, func)]
                    else:
                        wc, width, plist = grp
                    acc, acc_k = acc_r.next()
                    for c in range(8):
                        S.I("pe", "matmul", acc[:, 0:width], lhsT=xT[:, c, tt * 128:(tt + 1) * 128], rhs=wb[:, c, wc:wc + width], start=(c == 0), stop=(c == 7),
                            reads=[xT_k], writes=[acc_k])
                    for (o, w, sc, nh, func) in plist:
                        if nh > 0:
                            copy_op(rr(list(ev)), tm[:, sc:sc + nh * 65].rearrange("p (h d) -> p h d", d=65)[:, :, 0:64],
                                    acc[:, o:o + w].rearrange("p (h d) -> p h d", d=64), [acc_k], [tm_k])
                        elif func is not None:
                            S.I("act", "activation", out=tm[:, sc:sc + w], in_=acc[:, o:o + w], func=func, reads=[acc_k], writes=[tm_k])
                        else:
                            copy_op(rr(list(ev)), tm[:, sc:sc + w], acc[:, o:o + w], [acc_k], [tm_k])
                r0 = t0 + tt * 128
                S.I("pool", "dma_start", out=tm_dst[r0:r0 + 128, :], in_=tm[:], reads=[tm_k], dma=tm_k)

        with ExitStack() as st:
            wb = sb(st, "wb0", [128, 8, EVEN_IN], BF16)
            stage = Ring(nc, st, "wstg", 3, [128, 1024], F32)
            load_weight(wb, w_in_even, 8, EVEN_IN, stage,
                        pieces=[(0, 640, 0), (768, 1024, 640), (1792, 512, 1664), (640, 128, 2176), (2304, 1024, 2304), (3328, 512, 3328)])
            xs_r = Ring(nc, st, "xs", 4, [128, 1024], F32)
            xb_r = Ring(nc, st, "xb", 2, [128, 1024], BF16)
            pT_r = Ring(nc, st, "pT", 2, [128, 1024], BF16, psum=True)
            xT_r = Ring(nc, st, "xT", 2, [128, 8, 512], BF16)
            acc_r = Ring(nc, st, "acc", 4, [128, 512], F32, psum=True)
            fm_r = Ring(nc, st, "fm", 4, [128, 512], BF16)
            tm_r = Ring(nc, st, "tm", 2, [128, H0_ROW], BF16)
            for i in range(2):
                t, k = tm_r.at(i)
                S.I("pool", "memset", t[:], 1.0, writes=[k])
            fm_feats = [i * 128 for i in range(17)]
            tm_groups = [
                (2176, 512, [(0, 128, H0_VA, 2, None), (128, 384, H0_VB, 6, None)]),
                (2688, 512, [(0, 384, H0_VB + 6 * 65, 6, None), (384, 128, H0_GA, 0, AF.Silu)]),
                (3200, 512, [(0, 384, H0_GA + 128, 0, AF.Silu), (384, 128, H0_GB, 0, AF.Silu)]),
                (3712, 128, [(0, 128, H0_GB + 128, 0, AF.Silu)]),
            ]
            for b in range(TT // 512):
                t0 = b * 512
                xT, xT_k = xT_r.next()
                for tt in range(4):
                    xs, xs_k = xs_r.next()
                    r0 = t0 + tt * 128
                    S.I("sp", "dma_start", out=xs[:], in_=x_in[r0:r0 + 128, :], writes=[xs_k], dma=xs_k)
                    make_xT(xs, xs_k, xb_r, pT_r, xT, xT_k, tt * 128)
                fm_list = [(f0, h0T[fi * 128:(fi + 1) * 128, t0:t0 + 512], fm_r) for fi, f0 in enumerate(fm_feats)]
                project(xT, xT_k, wb, acc_r, fm_r, fm_list, tm_r, tm_groups, t0, h0tm)
        S.barrier()

        h0tm_h = h0tm.tensor
        oatt_h = oatt.tensor

        def attn_group(st, E0, gname, d, quadsets, qrow, krow, vcol, VW, ocol, is_a):
            W = max(512, 128 * d)
            tpw = W // (128 * d)
            CQ = 4 if is_a else 2
            nring = 4 if d < 16 else 3
            Qr = Ring(nc, st, gname + "Q", 2, [128, CQ, W], BF16)
            Kr = Ring(nc, st, gname + "K", nring, [128, 4, W], BF16)
            Vr = Ring(nc, st, gname + "V", nring, [128, tpw * d, VW], BF16)
            psr = Ring(nc, st, gname + "ps", 2, [128, 1536], F32, psum=True)
            por = Ring(nc, st, gname + "po", 2, [128, 512], F32, psum=True)
            esr = Ring(nc, st, gname + "es", 2, [128, 1536], BF16)
            ptr = Ring(nc, st, gname + "pt", 2, [128, 1536], BF16)
            osr = Ring(nc, st, gname + "os", 3, [128, 260], F32)
            for i in range(nring):
                Kw, Kk = Kr.at(i)
                S.I("pool", "memset", Kw[:], 0.0, writes=[Kk])
            for si, T in enumerate(seq_lens):
                so = seq_off[si]
                NW = T // W
                NT = T // (128 * d)

                def load_kv(wn):
                    Kw, Kk = Kr.at(wn)
                    Vw, Vk = Vr.at(wn)
                    c0 = so + wn * W
                    for v in range(4):
                        if is_a:
                            g, half = v // 2, v % 2
                            src = h0T[krow + g * 64:krow + (g + 1) * 64, c0:c0 + W]
                        else:
                            half = v % 2
                            src = h0T[krow + v * 64:krow + (v + 1) * 64, c0:c0 + W]
                        S.I("sp", "dma_start", out=Kw[half * 64:(half + 1) * 64, v, :], in_=src, writes=[Kk], dma=Kk)
                    if d == 1:
                        src = bass.AP(h0tm_h, c0 * H0_ROW + vcol, [[H0_ROW, 128], [128 * H0_ROW, tpw], [1, VW]])
                    else:
                        src = bass.AP(h0tm_h, c0 * H0_ROW + vcol, [[d * H0_ROW, 128], [H0_ROW, d], [1, VW]])
                    S.I("sp", "dma_start", out=Vw[:], in_=src, writes=[Vk], dma=Vk)

                def stage2(n, r, js, qi, pt, ptk, tt, so=so, NT=NT):
                    po, pok = por.next()
                    for hh in range(4):
                        for ji, j in enumerate(js):
                            n2 = n + j - 1
                            Vw, Vk = Vr.at(n2 // tpw)
                            t2 = n2 % tpw
                            vi = t2 if d == 1 else r
                            vc0 = (qi * 65) if is_a else hh * 65
                            pos = ((hh % 2) * 2 + hh // 2) if is_a else hh
                            o0 = (j * 4 + pos) * 128
                            S.I("pe", "matmul", po[:, hh * 65:hh * 65 + 65], lhsT=pt[:, o0:o0 + 128], rhs=Vw[:, vi, vc0:vc0 + 65], start=(ji == 0), stop=(ji == len(js) - 1),
                                reads=[ptk, Vk], writes=[pok])
                    osb, osk = osr.next()
                    copy_op("dve", osb[:], po[:, 0:260], [pok], [osk])
                    row0 = so + n * 128 * d + r
                    dst = bass.AP(oatt_h, row0 * OA_ROW + ocol + (qi * 260 if is_a else 0), [[d * OA_ROW, 128], [1, 260]])
                    S.I("pool", "dma_start", out=dst, in_=osb[:], reads=[osk], dma=osk)

                load_kv(0)
                for wn in range(NW):
                    Qw, Qk = Qr.at(wn)
                    c0 = so + wn * W
                    S.I("sp", "dma_start", out=Qw[:], in_=h0T[qrow:qrow + CQ * 128, c0:c0 + W].rearrange("(c p) t -> p c t", p=128), writes=[Qk], dma=Qk)
                    if wn + 1 < NW:
                        load_kv(wn + 1)
                    pending = None
                    for tt in range(tpw):
                        for r in range(d):
                            n = wn * tpw + tt
                            js = [j for j in range(3) if 0 <= n + j - 1 < NT]
                            qcols = slice(tt * 128 * d + r, tt * 128 * d + r + 127 * d + 1, d)
                            for qi, qs in enumerate(quadsets):
                                ps, psk = psr.next()
                                for j in js:
                                    n2 = n + j - 1
                                    Kw, Kk = Kr.at(n2 // tpw)
                                    t2 = n2 % tpw
                                    kcols = slice(t2 * 128 * d + r, t2 * 128 * d + r + 127 * d + 1, d)
                                    if is_a:
                                        for half in range(2):
                                            out_ap = ps[:, (j * 4 + half * 2) * 128:(j * 4 + half * 2 + 2) * 128]
                                            S.I("pe", "matmul", out_ap, lhsT=Kw[:, qi * 2 + half, kcols], rhs=Qw[:, qi * 2:qi * 2 + 2, qcols], start=True, stop=True,
                                                reads=[Kk, Qk], writes=[psk])
                                    else:
                                        for hh in range(4):
                                            o0 = (j * 4 + hh) * 128
                                            S.I("pe", "matmul", ps[:, o0:o0 + 128], lhsT=Kw[:, hh, kcols], rhs=Qw[:, hh // 2, qcols], start=True, stop=True, reads=[Kk, Qk], writes=[psk])
                                es, esk = esr.next()
                                pt, ptk = ptr.next()
                                for j in js:
                                    S.I("act", "activation", out=es[:, j * 512:(j + 1) * 512], in_=ps[:, j * 512:(j + 1) * 512], func=AF.Exp, scale=0.125,
                                        reads=[psk], writes=[esk])
                                for j in js:
                                    S.I("dve", "tensor_tensor", out=pt[:, j * 512:(j + 1) * 512], in0=es[:, j * 512:(j + 1) * 512], in1=E0[:, qs, j * 512:(j + 1) * 512], op=ALU.mult,
                                        reads=[esk, "E0"], writes=[ptk])
                                unit = (n, r, js, qi, pt, ptk, tt)
                                if pending is not None:
                                    stage2(*pending)
                                pending = unit
                    if pending is not None:
                        stage2(*pending)
                        pending = None

        if stop_after >= 2:
            with ExitStack() as st:
                E0 = sb(st, "E0", [128, 5, 1536], BF16)
                tstage = Ring(nc, st, "tstg", 2, [128, 1536], F32)
                for qs in range(5):
                    stg, sk = tstage.next()
                    S.I("sp", "dma_start", out=stg[:], in_=tab0_in[qs], writes=[sk], dma=sk)
                    S.I("act", "activation", out=E0[:, qs, :], in_=stg[:], func=AF.Exp, reads=[sk], writes=["E0"])
                with ExitStack() as st2:
                    if not skipA:
                        attn_group(st2, E0, "A", 1, [0, 1], 0, 512, H0_VA, 130, 0, True)
                S.barrier()
                for gi, d in enumerate((1, 4, 16)):
                    if stop_after < 3 + gi:
                        continue
                    with ExitStack() as st2:
                        attn_group(st2, E0, "B%d" % gi, d, [2 + gi], 640 + gi * 256, 1408 + gi * 256, H0_VB + gi * 260, 260, 520 + gi * 260, False)
                    S.barrier()

        def run_skewed(gens):
            active = []
            it = iter(gens)
            done = False
            while True:
                if not done:
                    try:
                        active.append(next(it))
                    except StopIteration:
                        done = True
                if done and not active:
                    break
                for g in list(active):
                    try:
                        next(g)
                    except StopIteration:
                        active.remove(g)

        def layer_norm_gen(zps, zk, xres, xres_k, lng, lnb, v_r, st_r, junk, x1_r, dst):
            v, vk = v_r.next()
            for hf in range(2):
                S.I("dve", "scalar_tensor_tensor", out=v[:, hf * 512:(hf + 1) * 512], in0=xres[:, hf * 512:(hf + 1) * 512], scalar=ALPHA, in1=zps[hf][:],
                    op0=ALU.mult, op1=ALU.add, reads=[xres_k, zk[hf]], writes=[vk])
            stt, stk = st_r.next()
            S.I("dve", "tensor_reduce", out=stt[:, 0:1], in_=v[:], axis=AX.X, op=ALU.add, reads=[vk], writes=[(stk, 0)])
            S.I("act", "activation", out=junk[:], in_=v[:], func=AF.Square, accum_out=stt[:, 1:2], reads=[vk], writes=["junk", (stk, 1)])
            yield
            S.I("dve", "tensor_scalar", out=stt[:, 2:3], in0=stt[:, 0:1], scalar1=1.0 / 1024, scalar2=None, op0=ALU.mult, reads=[(stk, 0)], writes=[(stk, 2)])
            S.I("dve", "tensor_tensor", out=stt[:, 3:4], in0=stt[:, 2:3], in1=stt[:, 2:3], op=ALU.mult, reads=[(stk, 2)], writes=[(stk, 3)])
            yield
            S.I("dve", "scalar_tensor_tensor", out=stt[:, 4:5], in0=stt[:, 1:2], scalar=1.0 / 1024, in1=stt[:, 3:4], op0=ALU.mult, op1=ALU.subtract,
                reads=[(stk, 1), (stk, 3)], writes=[(stk, 4)])
            yield
            S.I("dve", "tensor_scalar", out=stt[:, 4:5], in0=stt[:, 4:5], scalar1=LN_EPS, scalar2=None, op0=ALU.add, reads=[(stk, 4)], writes=[(stk, 4)])
            S.I("act", "activation", out=stt[:, 6:7], in_=stt[:, 4:5], func=AF.Sqrt, reads=[(stk, 4)], writes=[(stk, 6)])
            yield
            S.I("dve", "reciprocal", out=stt[:, 5:6], in_=stt[:, 6:7], reads=[(stk, 6)], writes=[(stk, 5)])
            S.I("dve", "scalar_tensor_tensor", out=v[:], in0=v[:], scalar=stt[:, 2:3], in1=lng, op0=ALU.subtract, op1=ALU.mult,
                reads=[vk, (stk, 2), "ln"], writes=[vk])
            yield
            x1, x1k = x1_r.next()
            S.I("dve", "scalar_tensor_tensor", out=x1[:], in0=v[:], scalar=stt[:, 5:6], in1=lnb, op0=ALU.mult, op1=ALU.add,
                reads=[vk, (stk, 5), "ln"], writes=[x1k])
            S.I("pool", "dma_start", out=dst, in_=x1[:], reads=[x1k], dma=x1k)

        def layer_norm_tile(zps, zk, xres, xres_k, lng, lnb, v_r, st_r, junk, x1_r):
            v, vk = v_r.next()
            for hf in range(2):
                S.I("dve", "scalar_tensor_tensor", out=v[:, hf * 512:(hf + 1) * 512], in0=xres[:, hf * 512:(hf + 1) * 512], scalar=ALPHA, in1=zps[hf][:],
                    op0=ALU.mult, op1=ALU.add, reads=[xres_k, zk[hf]], writes=[vk])
            stt, stk = st_r.next()
            S.I("dve", "tensor_reduce", out=stt[:, 0:1], in_=v[:], axis=AX.X, op=ALU.add, reads=[vk], writes=[(stk, 0)])
            S.I("act", "activation", out=junk[:], in_=v[:], func=AF.Square, accum_out=stt[:, 1:2], reads=[vk], writes=["junk", (stk, 1)])
            S.I("dve", "tensor_scalar", out=stt[:, 2:3], in0=stt[:, 0:1], scalar1=1.0 / 1024, scalar2=None, op0=ALU.mult, reads=[(stk, 0)], writes=[(stk, 2)])
            S.I("dve", "tensor_tensor", out=stt[:, 3:4], in0=stt[:, 2:3], in1=stt[:, 2:3], op=ALU.mult, reads=[(stk, 2)], writes=[(stk, 3)])
            S.I("dve", "scalar_tensor_tensor", out=stt[:, 4:5], in0=stt[:, 1:2], scalar=1.0 / 1024, in1=stt[:, 3:4], op0=ALU.mult, op1=ALU.subtract,
                reads=[(stk, 1), (stk, 3)], writes=[(stk, 4)])
            S.I("dve", "tensor_scalar", out=stt[:, 4:5], in0=stt[:, 4:5], scalar1=LN_EPS, scalar2=None, op0=ALU.add, reads=[(stk, 4)], writes=[(stk, 4)])
            S.I("act", "activation", out=stt[:, 6:7], in_=stt[:, 4:5], func=AF.Sqrt, reads=[(stk, 4)], writes=[(stk, 6)])
            S.I("dve", "reciprocal", out=stt[:, 5:6], in_=stt[:, 6:7], reads=[(stk, 6)], writes=[(stk, 5)])
            S.I("dve", "scalar_tensor_tensor", out=v[:], in0=v[:], scalar=stt[:, 2:3], in1=lng, op0=ALU.subtract, op1=ALU.mult,
                reads=[vk, (stk, 2), "ln"], writes=[vk])
            x1, x1k = x1_r.next()
            S.I("dve", "scalar_tensor_tensor", out=x1[:], in0=v[:], scalar=stt[:, 5:6], in1=lnb, op0=ALU.mult, op1=ALU.add,
                reads=[vk, (stk, 5), "ln"], writes=[x1k])
            return x1, x1k

        if stop_after >= 6:
            with ExitStack() as st:
                wo = sb(st, "wo0", [128, 6, 1024], BF16)
                w1 = sb(st, "w1", [128, 8, ODD_IN], BF16)
                with ExitStack() as st2:
                    stage = Ring(nc, st2, "wstg1", 3, [128, 1024], F32)
                    load_weight(wo, w_out_even, 6, 1024, stage)
                    load_weight(w1, w_in_odd, 8, ODD_IN, stage)
                    S.barrier()
                lnt = sb(st, "lnt", [128, 2, 1024], F32)
                S.I("sp", "dma_start", out=lnt[:, 0, :], in_=lng_in[:, 0:1024], writes=["ln"], dma="ln")
                S.I("sp", "dma_start", out=lnt[:, 1, :], in_=lnb_in[:, 0:1024], writes=["ln"], dma="ln")
                esink = sb(st, "esink", [128, 8], F32)
                S.I("sp", "dma_start", out=esink[:], in_=sink_in, writes=["esink"], dma="esink")
                S.I("act", "activation", out=esink[:], in_=esink[:], func=AF.Exp, reads=["esink"], writes=["esink"])
                oa_r = Ring(nc, st, "oa", 3, [128, OA_ROW], F32)
                g_r = Ring(nc, st, "gt", 3, [128, 768], BF16)
                xs_r = Ring(nc, st, "xs2", 3, [128, 1024], F32)
                v_r = Ring(nc, st, "vln", 4, [128, 1024], F32)
                x1_r = Ring(nc, st, "x1t", 2, [128, 1024], F32)
                st_r = Ring(nc, st, "stt", 6, [128, 8], F32)
                sm_r = Ring(nc, st, "smt", 4, [128, 16], F32)
                tmp_r = Ring(nc, st, "tmpc", 2, [128, 768], F32)
                junk = sb(st, "junk", [128, 1024], BF16)
                y_r = Ring(nc, st, "yt", 2, [128, 768], BF16)
                yT_r = Ring(nc, st, "yT", 2, [128, 6, 128], BF16)
                xb_r = Ring(nc, st, "xb2", 2, [128, 1024], BF16)
                xT_r = Ring(nc, st, "xT2", 2, [128, 8, 512], BF16)
                fmb_r = Ring(nc, st, "fmb", 4, [128, 512], BF16)
                fmf_r = Ring(nc, st, "fmf", 4, [128, 512], F32)
                tm_r = Ring(nc, st, "tm1", 2, [128, H1_ROW], BF16)
                pT_r = Ring(nc, st, "pT2", 2, [128, 1024], BF16, psum=True)
                z_r = Ring(nc, st, "zps", 4, [128, 512], F32, psum=True)
                acc_r = Ring(nc, st, "acc2", 2, [128, 512], F32, psum=True)
                for i in range(2):
                    t, k = tm_r.at(i)
                    S.I("pool", "memset", t[:], 1.0, writes=[k])
                fm_cols = [(QC + c * 128, h1Tb, c, fmb_r) for c in range(4)] + [(KC + c * 128, h1Tb, 4 + c, fmb_r) for c in range(4)] + \
                          [(QD + c * 128, h1Tf, c, fmf_r) for c in range(4)] + [(ZF + c * 128, h1Tf, 4 + c, fmf_r) for c in range(4)] + \
                          [(ZB + c * 128, h1Tf, 8 + c, fmf_r) for c in range(4)]
                tm_groups1 = [(VC, 512, H1_VC, 8, None), (IDD, 512, H1_ID, 0, None), (GC, 512, H1_GC, 0, AF.Silu), (GD, 512, H1_GD, 0, AF.Silu)]
                NBLK = TT // 512
                tiles_done = [0] * NBLK
                proj_done = [False] * NBLK
                blk_xT = {}

                def acquire(ring):
                    while not ring.can_next():
                        yield
                    return ring.next_hold()

                def tile_gen2(b, tt):
                    t0 = b * 512
                    r0 = t0 + tt * 128
                    oa, oak, oai = yield from acquire(oa_r)
                    S.I("sp", "dma_start", out=oa[:], in_=oatt[r0:r0 + 128, :], writes=[oak], dma=oak)
                    gt, gk, gi_ = yield from acquire(g_r)
                    S.I("sp", "dma_start", out=gt[:], in_=h0tm[r0:r0 + 128, H0_GA:H0_GA + 768], writes=[gk], dma=gk)
                    xs, xsk, xsi = yield from acquire(xs_r)
                    S.I("sp", "dma_start", out=xs[:], in_=x_in[r0:r0 + 128, :], writes=[xsk], dma=xsk)
                    yield
                    sm, smk, smi = yield from acquire(sm_r)
                    oa3 = oa[:, 0:520].rearrange("p (h d) -> p h d", d=65)
                    S.I("dve", "tensor_tensor", out=sm[:, 0:8], in0=oa3[:, :, 64], in1=esink[:], op=ALU.add, reads=[oak, "esink"], writes=[(smk, 0)])
                    S.I("dve", "tensor_tensor", out=oa[:, 520:780], in0=oa[:, 520:780], in1=oa[:, 780:1040], op=ALU.add, reads=[oak], writes=[oak])
                    yield
                    S.I("dve", "reciprocal", out=sm[:, 0:8], in_=sm[:, 0:8], reads=[(smk, 0)], writes=[(smk, 0)])
                    S.I("dve", "tensor_tensor", out=oa[:, 520:780], in0=oa[:, 520:780], in1=oa[:, 1040:1300], op=ALU.add, reads=[oak], writes=[oak])
                    yield
                    tmp, tmpk = tmp_r.next()
                    y, yk = y_r.next()
                    ob3 = oa[:, 520:780].rearrange("p (h d) -> p h d", d=65)
                    S.I("dve", "tensor_tensor", out=tmp[:, 0:512].rearrange("p (h d) -> p h d", d=64), in0=oa3[:, :, 0:64],
                        in1=sm[:, 0:8].unsqueeze(2).broadcast_to([128, 8, 64]), op=ALU.mult, reads=[oak, (smk, 0)], writes=[(tmpk, 0)])
                    S.I("dve", "reciprocal", out=sm[:, 8:12], in_=ob3[:, :, 64], reads=[oak], writes=[(smk, 1)])
                    yield
                    S.I("dve", "tensor_tensor", out=tmp[:, 512:768].rearrange("p (h d) -> p h d", d=64), in0=ob3[:, :, 0:64],
                        in1=sm[:, 8:12].unsqueeze(2).broadcast_to([128, 4, 64]), op=ALU.mult, reads=[oak, (smk, 1)], writes=[(tmpk, 1)])
                    S.I("dve", "tensor_tensor", out=y[:], in0=tmp[:], in1=gt[:], op=ALU.mult, reads=[(tmpk, 0), (tmpk, 1), gk], writes=[yk])
                    oa_r.release(oai)
                    g_r.release(gi_)
                    sm_r.release(smi)
                    pT, pTk = pT_r.next()
                    for c in range(6):
                        S.I("pe", "transpose", out=pT[:, c * 128:(c + 1) * 128], in_=y[:, c * 128:(c + 1) * 128], identity=ident[:], reads=[yk, "ident"], writes=[pTk])
                    yT, yTk = yT_r.next()
                    copy_op("act", yT[:], pT[:, 0:768].rearrange("p (c t) -> p c t", c=6), [pTk], [yTk])
                    zps, zk, zis = [], [], []
                    for hf in range(2):
                        z, zkk, zi = yield from acquire(z_r)
                        for c in range(6):
                            S.I("pe", "matmul", z[:], lhsT=yT[:, c, :], rhs=wo[:, c, hf * 512:(hf + 1) * 512], start=(c == 0), stop=(c == 5), reads=[yTk], writes=[zkk])
                        zps.append(z)
                        zk.append(zkk)
                        zis.append(zi)
                    yield
                    v, vk, vi_ = yield from acquire(v_r)
                    for hf in range(2):
                        S.I("dve", "scalar_tensor_tensor", out=v[:, hf * 512:(hf + 1) * 512], in0=xs[:, hf * 512:(hf + 1) * 512], scalar=ALPHA, in1=zps[hf][:],
                            op0=ALU.mult, op1=ALU.add, reads=[xsk, zk[hf]], writes=[vk])
                    xs_r.release(xsi)
                    for zi in zis:
                        z_r.release(zi)
                    stt, stk, sti = yield from acquire(st_r)
                    S.I("dve", "tensor_reduce", out=stt[:, 0:1], in_=v[:], axis=AX.X, op=ALU.add, reads=[vk], writes=[(stk, 0)])
                    S.I("act", "activation", out=junk[:], in_=v[:], func=AF.Square, accum_out=stt[:, 1:2], reads=[vk], writes=["junk", (stk, 1)])
                    yield
                    S.I("dve", "tensor_scalar", out=stt[:, 2:3], in0=stt[:, 0:1], scalar1=1.0 / 1024, scalar2=None, op0=ALU.mult, reads=[(stk, 0)], writes=[(stk, 2)])
                    S.I("dve", "tensor_tensor", out=stt[:, 3:4], in0=stt[:, 2:3], in1=stt[:, 2:3], op=ALU.mult, reads=[(stk, 2)], writes=[(stk, 3)])
                    S.I("dve", "scalar_tensor_tensor", out=stt[:, 4:5], in0=stt[:, 1:2], scalar=1.0 / 1024, in1=stt[:, 3:4], op0=ALU.mult, op1=ALU.subtract,
                        reads=[(stk, 1), (stk, 3)], writes=[(stk, 4)])
                    S.I("dve", "tensor_scalar", out=stt[:, 4:5], in0=stt[:, 4:5], scalar1=LN_EPS, scalar2=None, op0=ALU.add, reads=[(stk, 4)], writes=[(stk, 4)])
                    S.I("act", "activation", out=stt[:, 6:7], in_=stt[:, 4:5], func=AF.Sqrt, reads=[(stk, 4)], writes=[(stk, 6)])
                    yield
                    S.I("dve", "reciprocal", out=stt[:, 5:6], in_=stt[:, 6:7], reads=[(stk, 6)], writes=[(stk, 5)])
                    S.I("dve", "scalar_tensor_tensor", out=v[:], in0=v[:], scalar=stt[:, 2:3], in1=lnt[:, 0, :], op0=ALU.subtract, op1=ALU.mult,
                        reads=[vk, (stk, 2), "ln"], writes=[vk])
                    yield
                    while b >= 2 and not proj_done[b - 2]:
                        yield
                    if b not in blk_xT:
                        blk_xT[b] = xT_r.next()
                    xT, xT_k = blk_xT[b]
                    x1, x1k = x1_r.next()
                    S.I("dve", "scalar_tensor_tensor", out=x1[:], in0=v[:], scalar=stt[:, 5:6], in1=lnt[:, 1, :], op0=ALU.mult, op1=ALU.add,
                        reads=[vk, (stk, 5), "ln"], writes=[x1k])
                    v_r.release(vi_)
                    st_r.release(sti)
                    S.I("pool", "dma_start", out=x1s[r0:r0 + 128, :], in_=x1[:], reads=[x1k], dma=x1k)
                    make_xT(x1, x1k, xb_r, pT_r, xT, xT_k, tt * 128, cast_e=("dve",), copy_e=("act",))
                    tiles_done[b] += 1

                def proj_gen(b):
                    while tiles_done[b] < 4:
                        yield
                    t0 = b * 512
                    xT, xT_k = blk_xT[b]
                    fm_list = [(f0, dst[ci * 128:(ci + 1) * 128, t0:t0 + 512], ring) for (f0, dst, ci, ring) in fm_cols]
                    for i0 in range(0, len(fm_list), 5):
                        project(xT, xT_k, w1, acc_r, None, fm_list[i0:i0 + 5], tm_r, tm_groups1, t0, h1tm, ntt=4, tts=[], ev=("act",))
                        yield
                    for tt in range(4):
                        project(xT, xT_k, w1, acc_r, None, [], tm_r, tm_groups1, t0, h1tm, ntt=4, tts=[tt], ev=("act",))
                        yield
                    proj_done[b] = True

                gens = []
                for b in range(NBLK):
                    for tt in range(4):
                        gens.append(tile_gen2(b, tt))
                    gens.append(proj_gen(b))
                run_skewed(gens)
            S.barrier()

        h1tm_h = h1tm.tensor
        if stop_after >= 7:
            with ExitStack() as st:
                EN = sb(st, "EN", [128, 4, 2304], BF16)
                with ExitStack() as st2:
                    tstage = Ring(nc, st2, "nstg", 2, [128, 2304], F32)
                    for hp in range(4):
                        stg, sk = tstage.next()
                        S.I("sp", "dma_start", out=stg[:], in_=tabn_in[hp], writes=[sk], dma=sk)
                        S.I("act", "activation", out=EN[:, hp, :], in_=stg[:], func=AF.Exp, reads=[sk], writes=["EN"])
                    S.barrier()
                W = 512
                Qr = Ring(nc, st, "NQ", 2, [128, 4, W], BF16)
                Kr = Ring(nc, st, "NK", 4, [128, 8, W], BF16)
                Vr = Ring(nc, st, "NV", 4, [128, 4, 520], BF16)
                psr = Ring(nc, st, "Nps", 2, [128, 1536], F32, psum=True)
                por = Ring(nc, st, "Npo", 1, [128, 1024], F32, psum=True)
                esr = Ring(nc, st, "Nes", 2, [128, 1280], BF16)
                ptr = Ring(nc, st, "Npt", 2, [128, 1280], BF16)
                osr = Ring(nc, st, "Nos", 2, [128, 520], F32)
                for i in range(4):
                    Kw, Kk = Kr.at(i)
                    S.I("pool", "memset", Kw[:], 0.0, writes=[Kk])
                for si, T in enumerate(seq_lens):
                    so = seq_off[si]
                    NW = T // W
                    U = T // 128

                    def load_kv_n(wn):
                        Kw, Kk = Kr.at(wn)
                        Vw, Vk = Vr.at(wn)
                        c0 = so + wn * W
                        for h in range(8):
                            half = h % 2
                            S.I("sp", "dma_start", out=Kw[half * 64:(half + 1) * 64, h, :], in_=h1Tb[512 + h * 64:512 + (h + 1) * 64, c0:c0 + W], writes=[Kk], dma=Kk)
                        src = bass.AP(h1tm_h, c0 * H1_ROW + H1_VC, [[H1_ROW, 128], [128 * H1_ROW, 4], [1, 520]])
                        S.I("sp", "dma_start", out=Vw[:], in_=src, writes=[Vk], dma=Vk)

                    def na_stage2(u, hp, neigh, pt, ptk, po, pok, so=so):
                        NJ = len(neigh)
                        for hl in range(2):
                            h = 2 * hp + hl
                            pc = (h // 4) * 512 + (h % 4) * 65
                            for j, (m, blk) in enumerate(neigh):
                                Vw, Vk = Vr.at(m // 4)
                                o0 = (j * 2 + hl) * 128
                                S.I("pe", "matmul", po[:, pc:pc + 65], lhsT=pt[:, o0:o0 + 128], rhs=Vw[:, m % 4, h * 65:(h + 1) * 65], start=(j == 0), stop=(j == NJ - 1),
                                    reads=[ptk, Vk], writes=[pok])
                        if hp == 3:
                            osb, osk = osr.next()
                            copy_op("act", osb[:, 0:260], po[:, 0:260], [pok], [osk])
                            copy_op("dve", osb[:, 260:520], po[:, 512:772], [pok], [osk])
                            r0 = so + u * 128
                            S.I("pool", "dma_start", out=ocs[r0:r0 + 128, :], in_=osb[:], reads=[osk], dma=osk)

                    load_kv_n(0)
                    for wn in range(NW):
                        Qw, Qk = Qr.at(wn)
                        c0 = so + wn * W
                        S.I("sp", "dma_start", out=Qw[:], in_=h1Tb[0:512, c0:c0 + W].rearrange("(c p) t -> p c t", p=128), writes=[Qk], dma=Qk)
                        if wn + 1 < NW:
                            load_kv_n(wn + 1)
                        pending = None
                        for tt in range(4):
                            u = wn * 4 + tt
                            neigh = na_neighbors(u, U)
                            NJ = len(neigh)
                            po, pok = por.next()
                            for hp in range(4):
                                ps, psk = psr.next()
                                for j, (m, blk) in enumerate(neigh):
                                    Kw, Kk = Kr.at(m // 4)
                                    for hl in range(2):
                                        h = 2 * hp + hl
                                        o0 = (j * 2 + hl) * 128
                                        S.I("pe", "matmul", ps[:, o0:o0 + 128], lhsT=Kw[:, h, (m % 4) * 128:(m % 4 + 1) * 128], rhs=Qw[:, hp, tt * 128:(tt + 1) * 128],
                                            start=True, stop=True, reads=[Kk, Qk], writes=[psk])
                                es, esk = esr.next()
                                pt, ptk = ptr.next()
                                ncol = NJ * 256
                                for c0_ in range(0, ncol, 512):
                                    c1_ = min(ncol, c0_ + 512)
                                    S.I("act", "activation", out=es[:, c0_:c1_], in_=ps[:, c0_:c1_], func=AF.Exp, scale=0.125, reads=[psk], writes=[esk])
                                for j, (m, blk) in enumerate(neigh):
                                    S.I("dve", "tensor_tensor", out=pt[:, j * 256:(j + 1) * 256], in0=es[:, j * 256:(j + 1) * 256], in1=EN[:, hp, blk * 256:(blk + 1) * 256], op=ALU.mult,
                                        reads=[esk, "EN"], writes=[ptk])
                                unit = (u, hp, neigh, pt, ptk, po, pok)
                                if pending is not None:
                                    na_stage2(*pending)
                                pending = unit
                        if pending is not None:
                            na_stage2(*pending)
                            pending = None
            S.barrier()

        if stop_after >= 8:
            with ExitStack() as st:
                lbt = sb(st, "lbt", [128, 16], F32)
                lbc = sb(st, "lbc", [128, 5, 8], F32)
                S.I("sp", "dma_start", out=lbt[:], in_=lb_in, writes=["lbt"], dma="lbt")
                S.I("act", "activation", out=lbt[:], in_=lbt[:], func=AF.Exp, reads=["lbt"], writes=["lbt"])
                lb4 = lbt[:].rearrange("p (d l h) -> p d l h", d=2, l=2)
                lbv = lambda i: lbc[:, i, :].rearrange("p (d h) -> p d h", d=2)
                S.I("dve", "tensor_tensor", out=lbv(3), in0=lb4[:, :, 0, :], in1=lb4[:, :, 1, :], op=ALU.add, reads=["lbt"], writes=["lbc3"])
                S.I("dve", "reciprocal", out=lbv(3), in_=lbv(3), reads=["lbc3"], writes=["lbc3"])
                S.I("dve", "tensor_tensor", out=lbv(0), in0=lb4[:, :, 1, :], in1=lbv(3), op=ALU.mult, reads=["lbt", "lbc3"], writes=["lbc0"])
                S.I("dve", "tensor_scalar", out=lbc[:, 1, :], in0=lbc[:, 0, :], scalar1=-1.0, scalar2=1.0, op0=ALU.mult, op1=ALU.add, reads=["lbc0"], writes=["lbc1"])
                S.I("dve", "tensor_scalar", out=lbc[:, 2, :], in0=lbc[:, 0, :], scalar1=1.0, scalar2=-1.0, op0=ALU.mult, op1=ALU.add, reads=["lbc0"], writes=["lbc2"])
                S.I("act", "activation", out=lbc[:, 4, :], in_=lbc[:, 1, :], func=AF.Ln, reads=["lbc1"], writes=["lbc4"])
                LBK = ["lbc0", "lbc1", "lbc2", "lbc4"]

                class Dir:
                    pass

                dirs = []
                for dr in range(2):
                    D = Dir()
                    D.dr = dr
                    nm = "hf" if dr == 0 else "hb"
                    D.z_r = Ring(nc, st, nm + "z", 2, [128, 512], F32)
                    D.q_r = Ring(nc, st, nm + "q", 2, [128, 512], F32)
                    D.t_r = [Ring(nc, st, nm + "t%d" % i, 2, [128, 512], F32) for i in range(6)]
                    D.qh = Ring(nc, st, nm + "qh", 2, [128, 4, 512], BF16)
                    D.qt = Ring(nc, st, nm + "qt", 2, [128, 4, 512], BF16)
                    D.kh = Ring(nc, st, nm + "kh", 2, [128, 4, 512], BF16)
                    D.khT = Ring(nc, st, nm + "khT", 2, [128, 4, 8, 128], BF16)
                    D.dec = Ring(nc, st, nm + "dec", 2, [128, 4, 8], F32)
                    D.v = Ring(nc, st, nm + "v", 2, [128, 8, 512], BF16)
                    D.S = sb(st, nm + "S", [128, 4, 128], F32)
                    D.Sbf = sb(st, nm + "Sbf", [128, 4, 128], BF16)
                    D.Am = Ring(nc, st, nm + "Am", 2, [128, 256], BF16)
                    D.ob = Ring(nc, st, nm + "ob", 2, [64, 512], F32)
                    D.psA = Ring(nc, st, nm + "psA", 1, [128, 512], F32, psum=True)
                    D.po = Ring(nc, st, nm + "po", 1, [128, 512], F32, psum=True)
                    D.psS = Ring(nc, st, nm + "psS", 1, [128, 512], F32, psum=True)
                    D.tril = trilf if dr == 0 else trilb
                    D.odst = odf if dr == 0 else odb
                    D.zrow = 512 if dr == 0 else 1024
                    for i in range(2):
                        for ring in (D.khT, D.v, D.Am):
                            t, k = ring.at(i)
                            S.I("pool", "memset", t[:], 0.0, writes=[k])
                    dirs.append(D)
                pTk_r = Ring(nc, st, "hpT", 1, [128, 1024], BF16, psum=True)

                def prep_begin(D, so, b):
                    c0 = so + b * 512
                    qh, qhk = D.qh.next()
                    qt, qtk = D.qt.next()
                    kh, khk = D.kh.next()
                    khT, khTk = D.khT.next()
                    dec, deck = D.dec.next()
                    v, vk = D.v.next()
                    S.I("sp", "dma_start", out=v[0:64, :, :], in_=bass.AP(h1tm_h, c0 * H1_ROW + H1_ID, [[H1_ROW, 64], [64 * H1_ROW, 8], [1, 512]]), writes=[vk], dma=vk)
                    return dict(qh=qh, qhk=[(qhk, h) for h in range(4)], qt=qt, qtk=[(qtk, h) for h in range(4)], kh=kh, khk=[(khk, h) for h in range(4)],
                                khT=khT, khTk=[(khTk, h) for h in range(4)], dec=dec, deck=[(deck, h) for h in range(4)], v=v, vk=vk, c0=c0)

                def prep_head(D, getP, h, c0):
                    dr = D.dr
                    z, zk = D.z_r.next()
                    q, qk = D.q_r.next()
                    S.I("sp", "dma_start", out=z[:], in_=h1Tf[D.zrow + h * 128:D.zrow + (h + 1) * 128, c0:c0 + 512], writes=[zk], dma=zk)
                    S.I("sp", "dma_start", out=q[:], in_=h1Tf[h * 128:(h + 1) * 128, c0:c0 + 512], writes=[qk], dma=qk)
                    T_ = [r_.next() for r_ in D.t_r]
                    (t0_, k0), (t1_, k1), (t2_, k2), (t3_, k3), (t4_, k4), (t5_, k5) = T_
                    lb_ap = lbc[:, 0, dr * 4 + h:dr * 4 + h + 1]
                    lnoml_ap = lbc[:, 4, dr * 4 + h:dr * 4 + h + 1]
                    v3 = lambda t: t[:].rearrange("p (c t) -> p c t", t=64)
                    S.I("act", "activation", out=t0_[:], in_=z[:], func=AF.Exp, scale=-1.0, reads=[zk], writes=[k0])
                    S.I("act", "activation", out=t1_[:], in_=t0_[:], func=AF.Ln, scale=1.0, bias=1.0, reads=[k0], writes=[k1])
                    S.I("act", "activation", out=t2_[:], in_=t0_[:], func=AF.Ln, scale=lb_ap, bias=1.0, reads=[k0] + LBK, writes=[k2])
                    yield
                    S.I("dve", "tensor_tensor", out=t2_[:], in0=t2_[:], in1=t1_[:], op=ALU.subtract, reads=[k2, k1], writes=[k2])
                    S.I("dve", "tensor_tensor_scan", out=t3_[:], data0=scanmask, data1=t2_[:], initial=0.0, op0=ALU.mult, op1=ALU.add, reads=[k2, "cstf"], writes=[k3])
                    tot = v3(t3_)[:, :, 63]
                    totb = v3(t3_)[:, :, 63:64].broadcast_to([128, 8, 64])
                    if dr == 0:
                        S.I("dve", "tensor_tensor", out=v3(t4_), in0=v3(t3_), in1=totb, op=ALU.subtract, reads=[k3], writes=[k4])
                        bq, bqk = t3_, k3
                    else:
                        S.I("dve", "tensor_tensor", out=t4_[:], in0=t2_[:], in1=t3_[:], op=ALU.subtract, reads=[k2, k3], writes=[k4])
                        S.I("dve", "tensor_tensor", out=v3(t5_), in0=v3(t4_), in1=totb, op=ALU.add, reads=[k4, k3], writes=[k5])
                        bq, bqk = t5_, k5
                    S.I("dve", "tensor_tensor", out=t1_[:], in0=t1_[:], in1=z[:], op=ALU.add, reads=[k1, zk], writes=[k1])
                    S.I("dve", "tensor_tensor", out=t1_[:], in0=t1_[:], in1=t4_[:], op=ALU.add, reads=[k1, k4], writes=[k1])
                    yield
                    P = getP()
                    qh, qt, kh, khT, dec = P["qh"], P["qt"], P["kh"], P["khT"], P["dec"]
                    qhk, qtk, khk, khTk, deck = P["qhk"][0][0], P["qtk"][0][0], P["khk"][0][0], P["khTk"][0][0], P["deck"][0][0]
                    S.I("act", "activation", out=dec[:, h, :], in_=tot, func=AF.Exp, reads=[k3], writes=[(deck, h)])
                    S.I("act", "activation", out=kh[:, h, :], in_=t1_[:], func=AF.Exp, scale=-1.0, bias=lnoml_ap, reads=[k1] + LBK, writes=[(khk, h)])
                    S.I("act", "activation", out=t0_[:], in_=t4_[:], func=AF.Exp, reads=[k4], writes=[k0])
                    S.I("act", "activation", out=t2_[:], in_=bq[:], func=AF.Exp, reads=[bqk], writes=[k2])
                    yield
                    S.I("dve", "tensor_tensor", out=qh[:, h, :], in0=q[:], in1=t0_[:], op=ALU.mult, reads=[qk, k0], writes=[(qhk, h)])
                    S.I("dve", "tensor_tensor", out=qt[:, h, :], in0=q[:], in1=t2_[:], op=ALU.mult, reads=[qk, k2], writes=[(qtk, h)])
                    pT, pTk = pTk_r.next()
                    for c in range(8):
                        S.I("pe", "transpose", out=pT[0:64, c * 128:(c + 1) * 128], in_=kh[:, h, c * 64:(c + 1) * 64], identity=ident[:], reads=[(khk, h), "ident"], writes=[pTk])
                    copy_op("act", khT[0:64, h, :, :], pT[0:64, :].rearrange("p (c k) -> p c k", c=8), [pTk], [(khTk, h)])

                def chunk_a(D, P, so, b, c):
                    cs = slice(c * 64, (c + 1) * 64)
                    psA, psAk = D.psA.next()
                    for h in range(4):
                        S.I("pe", "matmul", psA[0:64, h * 64:(h + 1) * 64], lhsT=P["kh"][:, h, cs], rhs=P["qh"][:, h, cs], start=True, stop=True,
                            reads=[P["khk"][h], P["qhk"][h]], writes=[psAk])
                    Am, Amk = D.Am.next()
                    S.I("dve", "tensor_tensor", out=Am[0:64, :], in0=psA[0:64, 0:256], in1=D.tril, op=ALU.mult, reads=[psAk, "cstf"], writes=[Amk])
                    psS, psSk = D.psS.next()
                    for h in range(4):
                        S.I("pe", "matmul", psS[:, h * 128:(h + 1) * 128], lhsT=P["khT"][:, h, c, :], rhs=P["v"][:, c, h * 128:(h + 1) * 128], start=True, stop=True,
                            reads=[P["khTk"][h], P["vk"]], writes=[psSk])
                    return Am, Amk, psS, psSk

                def chunk_b(D, P, so, b, c, Am, Amk, psS, psSk):
                    nm = "hf" if D.dr == 0 else "hb"
                    cs = slice(c * 64, (c + 1) * 64)
                    po, pok = D.po.next()
                    for h in range(4):
                        S.I("pe", "matmul", po[0:64, h * 128:(h + 1) * 128], lhsT=P["qt"][:, h, cs], rhs=D.Sbf[:, h, :], start=True, stop=False,
                            reads=[P["qtk"][h], nm + "Sbf"], writes=[pok])
                        S.I("pe", "matmul", po[0:64, h * 128:(h + 1) * 128], lhsT=Am[:, h * 64:(h + 1) * 64], rhs=P["v"][:, c, h * 128:(h + 1) * 128], start=False, stop=True,
                            reads=[Amk, P["vk"]], writes=[pok])
                    for h in range(4):
                        S.I("dve", "scalar_tensor_tensor", out=D.S[:, h, :], in0=D.S[:, h, :], scalar=P["dec"][:, h, c:c + 1], in1=psS[:, h * 128:(h + 1) * 128],
                            op0=ALU.mult, op1=ALU.add, reads=[(nm + "S", h), P["deck"][h], psSk], writes=[(nm + "S", h)])
                    S.I("act", "copy", out=D.Sbf[:].rearrange("p h v -> p (h v)"), in_=D.S[:].rearrange("p h v -> p (h v)"), reads=[(nm + "S", h) for h in range(4)], writes=[nm + "Sbf"])
                    ob, obk = D.ob.next()
                    copy_op("act", ob[:], po[0:64, :], [pok], [obk])
                    r0 = so + b * 512 + c * 64
                    S.I("act", "dma_start", out=D.odst[r0:r0 + 64, :], in_=ob[:], reads=[obk], dma=obk)

                def chunk_pair(Pf, Pb, so, bf_, bb_, ci):
                    af = chunk_a(dirs[0], Pf, so, bf_, ci)
                    ab = chunk_a(dirs[1], Pb, so, bb_, 7 - ci)
                    chunk_b(dirs[0], Pf, so, bf_, ci, *af)
                    chunk_b(dirs[1], Pb, so, bb_, 7 - ci, *ab)

                for si, T in enumerate(seq_lens):
                    so = seq_off[si]
                    NB = T // 512
                    for D in dirs:
                        nm = "hf" if D.dr == 0 else "hb"
                        S.I("pool", "memset", D.S[:], 0.0, writes=[(nm + "S", h) for h in range(4)])
                        S.I("pool", "memset", D.Sbf[:], 0.0, writes=[nm + "Sbf"])
                    Pst = {0: (prep_begin(dirs[0], so, 0), prep_begin(dirs[1], so, NB - 1))}
                    for h in range(4):
                        for dr in range(2):
                            blk_tok = 0 if dr == 0 else NB - 1
                            for _ in prep_head(dirs[dr], (lambda dr=dr: Pst[0][dr]), h, so + blk_tok * 512):
                                pass
                    off = [-2, -1, 0, 1, 2, 3, 4, 4]
                    starts = {}
                    for k in range(1, NB):
                        for i in range(8):
                            starts.setdefault(8 * (k - 1) + off[i], []).append((k, i))
                    active = []

                    def start_heads(P):
                        for (k, i) in starts.get(P, []):
                            dr, h = i % 2, i // 2
                            blk_tok = k if dr == 0 else NB - 1 - k
                            active.append(prep_head(dirs[dr], (lambda k=k, dr=dr: Pst[k][dr]), h, so + blk_tok * 512))

                    def advance():
                        for g in list(active):
                            try:
                                next(g)
                            except StopIteration:
                                active.remove(g)

                    for P in (-2, -1):
                        start_heads(P)
                        advance()
                    for k in range(NB):
                        if k + 1 < NB:
                            Pst[k + 1] = (prep_begin(dirs[0], so, k + 1), prep_begin(dirs[1], so, NB - 2 - k))
                        Pf, Pb = Pst[k]
                        for ci in range(8):
                            chunk_pair(Pf, Pb, so, k, NB - 1 - k, ci)
                            start_heads(8 * k + ci)
                            advance()
                    while active:
                        advance()
            S.barrier()

        if stop_after >= 9:
            with ExitStack() as st:
                wo = sb(st, "wo1", [128, 8, 1024], BF16)
                with ExitStack() as st2:
                    stage = Ring(nc, st2, "wstg2", 3, [128, 1024], F32)
                    load_weight(wo, w_out_odd, 8, 1024, stage)
                    S.barrier()
                lnt = sb(st, "lnt1", [128, 2, 1024], F32)
                S.I("sp", "dma_start", out=lnt[:, 0, :], in_=lng_in[:, 1024:2048], writes=["ln"], dma="ln")
                S.I("sp", "dma_start", out=lnt[:, 1, :], in_=lnb_in[:, 1024:2048], writes=["ln"], dma="ln")
                gnt = sb(st, "gnt", [128, 512], F32)
                S.I("sp", "dma_start", out=gnt[:], in_=gn_in, writes=["gn"], dma="gn")
                RD4 = {"oc": 4, "of": 5, "obk": 3, "gt1": 5, "xs3": 6, "vln1": 6, "yo": 3, "stt1": 8, "smt1": 6, "tmpc1": 3, "sq1": 3, "yt1": 3, "yT1": 3}
                oc_r = Ring(nc, st, "oc", RD4['oc'], [128, 520], F32)
                of_r = Ring(nc, st, "of", RD4['of'], [128, 512], F32)
                ob_r = Ring(nc, st, "obk", RD4['obk'], [128, 512], F32)
                g_r = Ring(nc, st, "gt1", RD4['gt1'], [128, 1024], BF16)
                xs_r = Ring(nc, st, "xs3", RD4['xs3'], [128, 1024], F32)
                v_r = Ring(nc, st, "vln1", RD4['vln1'], [128, 1024], F32)
                x1_r = Ring(nc, st, "yo", RD4['yo'], [128, 1024], F32)
                st_r = Ring(nc, st, "stt1", RD4['stt1'], [128, 8], F32)
                sm_r = Ring(nc, st, "smt1", RD4['smt1'], [128, 16], F32)
                tmp_r = Ring(nc, st, "tmpc1", RD4['tmpc1'], [128, 1024], F32)
                sq_r = Ring(nc, st, "sq1", RD4['sq1'], [128, 512], F32)
                junk = sb(st, "junk1", [128, 1024], BF16)
                y_r = Ring(nc, st, "yt1", RD4['yt1'], [128, 1024], BF16)
                yT_r = Ring(nc, st, "yT1", RD4['yT1'], [128, 8, 128], BF16)
                pT_r = Ring(nc, st, "pT3", 2, [128, 1024], BF16, psum=True)
                z_r = Ring(nc, st, "zps1", 6, [128, 512], F32, psum=True)
                def tile_gen(ti):
                    r0 = ti * 128
                    oc, ock = oc_r.next()
                    S.I("sp", "dma_start", out=oc[:], in_=ocs[r0:r0 + 128, :], writes=[ock], dma=ock)
                    of, ofk = of_r.next()
                    S.I("sp", "dma_start", out=of[:], in_=odf[r0:r0 + 128, :], writes=[ofk], dma=ofk)
                    ob, obk = ob_r.next()
                    S.I("sp", "dma_start", out=ob[:], in_=odb[r0:r0 + 128, :], writes=[obk], dma=obk)
                    gt, gk = g_r.next()
                    S.I("sp", "dma_start", out=gt[:], in_=h1tm[r0:r0 + 128, H1_GC:H1_GC + 1024], writes=[gk], dma=gk)
                    xs, xsk = xs_r.next()
                    S.I("sp", "dma_start", out=xs[:], in_=x1s[r0:r0 + 128, :], writes=[xsk], dma=xsk)
                    yield
                    sm, smk = sm_r.next()
                    tmp, tmpk = tmp_r.next()
                    y, yk = y_r.next()
                    sq, sqk = sq_r.next()
                    oc3 = oc[:].rearrange("p (h d) -> p h d", d=65)
                    S.I("dve", "reciprocal", out=sm[:, 0:8], in_=oc3[:, :, 64], reads=[ock], writes=[(smk, 0)])
                    S.I("dve", "tensor_tensor", out=of[:], in0=of[:], in1=ob[:], op=ALU.add, reads=[ofk, obk], writes=[ofk])
                    for h in range(4):
                        S.I("act", "activation", out=sq[:, h * 128:(h + 1) * 128], in_=of[:, h * 128:(h + 1) * 128], func=AF.Square, accum_out=sm[:, 8 + h:9 + h],
                            reads=[ofk], writes=[sqk, (smk, 1)])
                    yield
                    S.I("dve", "tensor_tensor", out=tmp[:, 0:512].rearrange("p (h d) -> p h d", d=64), in0=oc3[:, :, 0:64],
                        in1=sm[:, 0:8].unsqueeze(2).broadcast_to([128, 8, 64]), op=ALU.mult, reads=[ock, (smk, 0)], writes=[(tmpk, 0)])
                    S.I("dve", "tensor_scalar", out=sm[:, 8:12], in0=sm[:, 8:12], scalar1=1.0 / 128, scalar2=RMS_EPS, op0=ALU.mult, op1=ALU.add, reads=[(smk, 1)], writes=[(smk, 1)])
                    S.I("act", "activation", out=sm[:, 12:16], in_=sm[:, 8:12], func=AF.Sqrt, reads=[(smk, 1)], writes=[(smk, 2)])
                    yield
                    S.I("dve", "reciprocal", out=sm[:, 12:16], in_=sm[:, 12:16], reads=[(smk, 2)], writes=[(smk, 2)])
                    S.I("dve", "tensor_tensor", out=tmp[:, 512:1024].rearrange("p (h d) -> p h d", d=128), in0=of[:].rearrange("p (h d) -> p h d", d=128),
                        in1=sm[:, 12:16].unsqueeze(2).broadcast_to([128, 4, 128]), op=ALU.mult, reads=[ofk, (smk, 2)], writes=[(tmpk, 1)])
                    S.I("dve", "tensor_tensor", out=tmp[:, 512:1024], in0=tmp[:, 512:1024], in1=gnt[:], op=ALU.mult, reads=[(tmpk, 1), "gn"], writes=[(tmpk, 1)])
                    S.I("dve", "tensor_tensor", out=y[:], in0=tmp[:], in1=gt[:], op=ALU.mult, reads=[(tmpk, 0), (tmpk, 1), gk], writes=[yk])
                    pT, pTk = pT_r.next()
                    for c in range(8):
                        S.I("pe", "transpose", out=pT[:, c * 128:(c + 1) * 128], in_=y[:, c * 128:(c + 1) * 128], identity=ident[:], reads=[yk, "ident"], writes=[pTk])
                    yT, yTk = yT_r.next()
                    copy_op("act", yT[:], pT[:].rearrange("p (c t) -> p c t", c=8), [pTk], [yTk])
                    zps, zk = [], []
                    for hf in range(2):
                        z, zkk = z_r.next()
                        for c in range(8):
                            S.I("pe", "matmul", z[:], lhsT=yT[:, c, :], rhs=wo[:, c, hf * 512:(hf + 1) * 512], start=(c == 0), stop=(c == 7), reads=[yTk], writes=[zkk])
                        zps.append(z)
                        zk.append(zkk)
                    yield
                    for _ in layer_norm_gen(zps, zk, xs, xsk, lnt[:, 0, :], lnt[:, 1, :], v_r, st_r, junk, x1_r, y_out[r0:r0 + 128, :]):
                        yield

                run_skewed([tile_gen(ti) for ti in range(TT // 128)])
            S.barrier()

        fin = Op("sp", None)
        for k in S.dsem:
            fin.deps.append(("dma", k, S.dcnt[k]))
        S.ops["sp"].append(fin)
        with nc.Block() as block:
            S.emit(block)
    return nc


def host_consts(t5_table, sink_a, rpb_c, lb_d, gnorm_d, ln_g, ln_b):
    cst = np.zeros((128, 1280), np.float32)
    cst[:, 0:128] = np.eye(128, dtype=np.float32)
    s = np.arange(64)[:, None]
    t = np.arange(64)[None, :]
    cst[0:64, 128:384] = np.tile((s <= t).astype(np.float32), (1, 4))
    cst[0:64, 384:640] = np.tile((s >= t).astype(np.float32), (1, 4))
    sm = np.ones((512,), np.float32)
    sm[0::64] = 0.0
    cst[:, 640:1152] = sm[None, :]
    lb = np.ascontiguousarray(np.transpose(lb_d.reshape(2, 2, 4, 128), (3, 0, 1, 2))).reshape(128, 16)
    return {
        "tab0": _tables_l0(np.asarray(t5_table, np.float32)),
        "tabn": _tables_na(np.asarray(rpb_c, np.float32)),
        "sink": np.ascontiguousarray(np.broadcast_to(np.asarray(sink_a, np.float32).reshape(1, 8), (128, 8))),
        "lb": np.ascontiguousarray(lb.astype(np.float32)),
        "gnorm": np.ascontiguousarray(np.broadcast_to(np.asarray(gnorm_d, np.float32).reshape(1, 512), (128, 512))),
        "lng": np.ascontiguousarray(np.broadcast_to(np.asarray(ln_g, np.float32).reshape(1, 2048), (128, 2048))),
        "lnb": np.ascontiguousarray(np.broadcast_to(np.asarray(ln_b, np.float32).reshape(1, 2048), (128, 2048))),
        "cst": cst,
    }


N_CORES = 8
_NC_CACHE = {}


def kernel(x_prompt, x_sample, t5_table, w_in_even, sink_a, w_out_even, w_in_odd, rpb_c, lb_d, gnorm_d, w_out_odd, ln_g, ln_b):
    x_prompt = np.asarray(x_prompt, np.float32)
    x_sample = np.asarray(x_sample, np.float32)
    nb, T, D = x_prompt.shape
    ns, Ts, _ = x_sample.shape
    per = nb // N_CORES
    seq_lens = [T] * per + [Ts]
    key = tuple(seq_lens)
    if key not in _NC_CACHE:
        _NC_CACHE[key] = build(seq_lens)
    nc = _NC_CACHE[key]
    consts = host_consts(np.asarray(t5_table), np.asarray(sink_a), np.asarray(rpb_c), np.asarray(lb_d), np.asarray(gnorm_d), np.asarray(ln_g), np.asarray(ln_b))
    shared = dict(consts)
    shared["w_in_even"] = np.ascontiguousarray(np.asarray(w_in_even, np.float32)[0])
    shared["w_out_even"] = np.ascontiguousarray(np.asarray(w_out_even, np.float32)[0])
    shared["w_in_odd"] = np.ascontiguousarray(np.asarray(w_in_odd, np.float32)[0])
    shared["w_out_odd"] = np.ascontiguousarray(np.asarray(w_out_odd, np.float32)[0])
    in_maps = []
    for c in range(N_CORES):
        xs = [x_prompt[c * per + i] for i in range(per)] + [x_sample[c % ns]]
        m = dict(shared)
        m["x"] = np.ascontiguousarray(np.concatenate(xs, axis=0))
        in_maps.append(m)
    res = run_bass_kernel_spmd(nc, in_maps, core_ids=list(range(N_CORES)))
    y_prompt = np.empty((nb, T, D), np.float32)
    y_sample = np.empty((ns, Ts, D), np.float32)
    for c in range(N_CORES):
        y = np.asarray(res.results[c]["y"], np.float32)
        for i in range(per):
            y_prompt[c * per + i] = y[i * T:(i + 1) * T]
        if c < ns:
            y_sample[c] = y[per * T:per * T + Ts]
    return (y_prompt, y_sample)
```

```python
import math
from contextlib import ExitStack

import numpy as np
import concourse.bass as bass
import concourse.mybir as mybir
from concourse.bass_utils import run_bass_kernel_spmd

F32 = mybir.dt.float32
BF16 = mybir.dt.bfloat16
ALU = mybir.AluOpType
AF = mybir.ActivationFunctionType
AX = mybir.AxisListType

D_MODEL = 1024
DEPTH = 2
ALPHA = (2.0 * DEPTH) ** 0.25
LN_EPS = 1e-5
RMS_EPS = 1e-6
NEG = -30000.0

QA, KA, VA, QB, KB, VB, GA, GB, EVEN_IN = 0, 512, 640, 768, 1536, 2304, 3072, 3584, 3840
QC, KC, VC, QD, IDD, ZF, ZB, GC, GD, ODD_IN = 0, 512, 1024, 1536, 2048, 2560, 3072, 3584, 4096, 4608

H0_VA, H0_VB, H0_GA, H0_GB, H0_ROW = 0, 130, 910, 1422, 1680
H1_VC, H1_ID, H1_GC, H1_GD, H1_ROW = 0, 520, 1032, 1544, 2056
OA_ROW = 1300


class Op:
    __slots__ = ("eng", "fn", "deps", "dma_sem", "need_inc", "val")

    def __init__(self, eng, fn):
        self.eng = eng
        self.fn = fn
        self.deps = []
        self.dma_sem = None
        self.need_inc = False
        self.val = 0


class Sched:
    ENGS = ("pe", "act", "dve", "pool", "sp")
    SAME_ENG_RAW = ("act", "dve", "pool")

    def __init__(self, nc, stack):
        self.nc = nc
        self.stack = stack
        self.ops = {e: [] for e in self.ENGS}
        self.sem = {e: stack.enter_context(nc.semaphore("sem_" + e)) for e in self.ENGS}
        self.lastw = {}
        self.readers = {}
        self.dsem = {}
        self.dcnt = {}
        self.key2idx = {}
        self.next_idx = {}

    def op(self, eng, fn, reads=(), writes=(), dma=None):
        o = Op(eng, fn)
        deps = []
        for k in reads:
            w = self.lastw.get(k)
            if w is not None:
                deps.append((w, True))
        for k in writes:
            w = self.lastw.get(k)
            if w is not None:
                deps.append((w, False))
            for r in self.readers.get(k, ()):
                deps.append((r, False))
        for (d, raw) in deps:
            if d.dma_sem is not None:
                o.deps.append(("dma", d.dma_sem, self.dcnt[d.dma_sem]))
                continue
            if d.eng == eng and dma is None:
                if eng not in self.SAME_ENG_RAW:
                    continue
            d.need_inc = True
            o.deps.append(("op", d))
        if dma is not None:
            if (eng, dma) not in self.key2idx:
                n_ = self.next_idx.get(eng, 0)
                self.next_idx[eng] = n_ + 1
                idx = (eng, n_)
                self.key2idx[(eng, dma)] = idx
                if idx not in self.dsem:
                    self.dsem[idx] = self.stack.enter_context(self.nc.semaphore("dsem_%s%d" % idx))
                    self.dcnt[idx] = 0
            idx = self.key2idx[(eng, dma)]
            self.dcnt[idx] += 16
            o.dma_sem = idx
        for k in reads:
            lst = self.readers.setdefault(k, [])
            if dma is None:
                for idx in range(len(lst)):
                    if lst[idx].dma_sem is None and lst[idx].eng == eng:
                        lst[idx] = o
                        break
                else:
                    lst.append(o)
            else:
                lst.append(o)
        for k in writes:
            self.lastw[k] = o
            self.readers[k] = []
        self.ops[eng].append(o)
        return o

    def I(self, eng, name, *a, reads=(), writes=(), dma=None, **kw):
        return self.op(eng, (name, a, kw), reads=reads, writes=writes, dma=dma)

    def barrier(self):
        lasts = {}
        for e in self.ENGS:
            for o in reversed(self.ops[e]):
                if o.fn is not None and o.dma_sem is None:
                    lasts[e] = o
                    break
        for e in self.ENGS:
            o = Op(e, None)
            for f, l in lasts.items():
                if f != e:
                    l.need_inc = True
                    o.deps.append(("op", l))
            for k in self.dsem:
                o.deps.append(("dma", k, self.dcnt[k]))
            self.ops[e].append(o)
        self.lastw = {}
        self.readers = {}
        self.key2idx = {}
        self.next_idx = {}

    def emit(self, block):
        for e in self.ENGS:
            c = 0
            for o in self.ops[e]:
                if o.need_inc:
                    c += 1
                    o.val = c
        sem = self.sem
        dsem = self.dsem

        def run(engname, engine):
            seen = {}
            for o in self.ops[engname]:
                need = {}
                for d in o.deps:
                    if d[0] == "dma":
                        k = ("d", d[1])
                        v = d[2]
                        s = dsem[d[1]]
                    else:
                        k = ("e", d[1].eng)
                        v = d[1].val
                        s = sem[d[1].eng]
                    if v > seen.get(k, 0) and v > need.get(k, (0, None))[0]:
                        need[k] = (v, s)
                for k, (v, s) in need.items():
                    engine.wait_ge(s, v)
                    seen[k] = v
                if o.fn is None:
                    continue
                name, a, kw = o.fn
                ins = getattr(engine, name)(*a, **kw)
                if o.dma_sem is not None:
                    ins.then_inc(dsem[o.dma_sem], 16)
                elif o.need_inc:
                    ins.then_inc(sem[engname], 1)

        block.tensor(lambda e: run("pe", e))
        block.scalar(lambda e: run("act", e))
        block.vector(lambda e: run("dve", e))
        block.gpsimd(lambda e: run("pool", e))
        block.sync(lambda e: run("sp", e))


class Ring:
    def __init__(self, nc, st, name, n, shape, dtype, psum=False):
        self.name = name
        self.n = n
        self.t = []
        for i in range(n):
            if psum:
                self.t.append(st.enter_context(nc.psum_tensor("%s%d" % (name, i), shape, dtype)))
            else:
                self.t.append(st.enter_context(nc.sbuf_tensor("%s%d" % (name, i), shape, dtype)))
        self.i = 0

    def next(self):
        i = self.i % self.n
        self.i += 1
        return self.t[i], (self.name, i)

    def at(self, i):
        i = i % self.n
        return self.t[i], (self.name, i)

    def can_next(self):
        if not hasattr(self, "busy"):
            self.busy = [False] * self.n
        return not self.busy[self.i % self.n]

    def next_hold(self):
        if not hasattr(self, "busy"):
            self.busy = [False] * self.n
        i = self.i % self.n
        assert not self.busy[i], self.name
        self.busy[i] = True
        self.i += 1
        return self.t[i], (self.name, i), i

    def release(self, i):
        self.busy[i] = False


def _t5_buckets(rel):
    nb = 16
    ret = (rel > 0).astype(np.int64) * nb
    n = np.abs(rel)
    max_exact = nb // 2
    large = max_exact + (np.log(np.maximum(n, 1) / max_exact) / math.log(1024 / max_exact) * (nb - max_exact)).astype(np.int64)
    large = np.minimum(large, nb - 1)
    return ret + np.where(n < max_exact, n, large)


def _tables_l0(t5_table):
    k = np.arange(128)[:, None, None]
    j = np.arange(3)[None, :, None]
    q = np.arange(128)[None, None, :]
    rel = (j - 1) * 128 + k - q
    out = np.full((5, 128, 3, 4, 128), NEG, np.float32)
    specs = [(128, 1, [0, 1, 2, 3]), (128, 1, [4, 5, 6, 7]), (64, 1, [8, 9, 10, 11]), (64, 4, [12, 13, 14, 15]), (64, 16, [16, 17, 18, 19])]
    for s, (hw, d, heads) in enumerate(specs):
        valid = np.abs(rel) <= hw
        bk = _t5_buckets(rel * d)
        for hi, h in enumerate(heads):
            pos = ((hi % 2) * 2 + hi // 2) if s < 2 else hi
            out[s, :, :, pos, :] = np.where(valid, t5_table[bk, h], NEG)
    return out.reshape(5, 128, 1536)


def _tables_na(rpb):
    rpb = rpb.reshape(8, 15, 31)
    k = np.arange(128)[:, None]
    q = np.arange(128)[None, :]
    krl, kc = k // 64, k % 64
    qrl, qc = q // 64, q % 64
    sc = np.clip(qc - 8, 0, 48)
    colmask = (kc >= sc) & (kc < sc + 16)
    colidx = np.clip(kc - qc + 15, 0, 30)
    out = np.full((4, 128, 9, 2, 128), NEG, np.float32)
    for b in range(9):
        off = b - 3 if b < 7 else (-2 if b == 7 else 2)
        dr = 2 * off + krl - qrl
        valid = colmask & (np.abs(dr) <= 7)
        if b >= 7:
            valid = valid & (dr >= -4) & (dr <= 3)
        dri = np.clip(dr + 7, 0, 14)
        for h in range(8):
            vals = rpb[h][dri, colidx]
            out[h // 2, :, b, h % 2, :] = np.where(valid, vals, NEG)
    return out.reshape(4, 128, 9 * 256)


def na_neighbors(u, U):
    if U <= 4:
        return [(m, m - u + 3) for m in range(U)]
    if u == 0 or u == 1:
        return [(m, m - u + 3) for m in range(0, 4)]
    if u == U - 1 or u == U - 2:
        return [(m, m - u + 3) for m in range(U - 4, U)]
    res = []
    for off in (-2, -1, 0, 1, 2):
        m = u + off
        blk = 7 if off == -2 else (8 if off == 2 else off + 3)
        res.append((m, blk))
    return res


def build(seq_lens, debug=False, stop_after=99, lvl=9, skipA=False):
    nc = bass.Bass("TRN2", target_bir_lowering=False)
    TT = sum(seq_lens)
    seq_off = [sum(seq_lens[:i]) for i in range(len(seq_lens))]
    okind = "ExternalOutput" if debug else "Internal"

    def dram(name, shape, dt, kind):
        return nc.dram_tensor(name, shape, dt, kind=kind).ap()

    x_in = dram("x", [TT, D_MODEL], F32, "ExternalInput")
    y_out = dram("y", [TT, D_MODEL], F32, "ExternalOutput")
    w_in_even = dram("w_in_even", [D_MODEL, EVEN_IN], F32, "ExternalInput")
    w_out_even = dram("w_out_even", [768, D_MODEL], F32, "ExternalInput")
    w_in_odd = dram("w_in_odd", [D_MODEL, ODD_IN], F32, "ExternalInput")
    w_out_odd = dram("w_out_odd", [1024, D_MODEL], F32, "ExternalInput")
    tab0_in = dram("tab0", [5, 128, 1536], F32, "ExternalInput")
    tabn_in = dram("tabn", [4, 128, 2304], F32, "ExternalInput")
    sink_in = dram("sink", [128, 8], F32, "ExternalInput")
    lb_in = dram("lb", [128, 16], F32, "ExternalInput")
    gn_in = dram("gnorm", [128, 512], F32, "ExternalInput")
    lng_in = dram("lng", [128, 2048], F32, "ExternalInput")
    lnb_in = dram("lnb", [128, 2048], F32, "ExternalInput")
    cst_in = dram("cst", [128, 1280], F32, "ExternalInput")

    h0T = dram("h0T", [2176, TT], BF16, okind)
    h0tm = dram("h0tm", [TT, H0_ROW], BF16, okind)
    oatt = dram("oatt", [TT, OA_ROW], F32, okind)
    x1s = dram("x1s", [TT, D_MODEL], F32, okind)
    h1Tb = dram("h1Tb", [1024, TT], BF16, okind)
    h1Tf = dram("h1Tf", [1536, TT], F32, okind)
    h1tm = dram("h1tm", [TT, H1_ROW], BF16, okind)
    ocs = dram("ocs", [TT, 520], F32, okind)
    odf = dram("odf", [TT, 512], F32, okind)
    odb = dram("odb", [TT, 512], F32, okind)

    with ExitStack() as top:
        S = Sched(nc, top)
        sb = lambda st, name, shape, dt: st.enter_context(nc.sbuf_tensor(name, shape, dt))
        cyc = {"i": 0}

        def rr(engs):
            cyc["i"] += 1
            return engs[cyc["i"] % len(engs)]

        def copy_op(eng, out, in_, reads, writes):
            if eng == "act":
                S.I("act", "copy", out=out, in_=in_, reads=reads, writes=writes)
            else:
                S.I(eng, "tensor_copy", out=out, in_=in_, reads=reads, writes=writes)

        cstf = sb(top, "cstf", [128, 1280], F32)
        ident = sb(top, "ident", [128, 128], BF16)
        S.I("sp", "dma_start", out=cstf[:], in_=cst_in, writes=["cstf"], dma="cstf")
        S.I("dve", "tensor_copy", out=ident[:], in_=cstf[:, 0:128], reads=["cstf"], writes=["ident"])
        trilf = cstf[0:64, 128:384]
        trilb = cstf[0:64, 384:640]
        scanmask = cstf[:, 640:1152]

        def load_weight(dst, src, KCn, F, stage_ring, pieces=None):
            if pieces is None:
                pieces = [(c0, min(1024, F - c0), c0) for c0 in range(0, F, 1024)]
            for kc in range(KCn):
                for (c0, cw, d0) in pieces:
                    stg, sk = stage_ring.next()
                    S.I("sp", "dma_start", out=stg[:, 0:cw], in_=src[kc * 128:(kc + 1) * 128, c0:c0 + cw], writes=[sk], dma=sk)
                    copy_op(rr(["act", "dve"]), dst[:, kc, d0:d0 + cw], stg[:, 0:cw], [sk], [("w", id(dst), kc, d0)])

        def make_xT(xs, xs_k, xb_ring, pT_ring, xT, xT_k, tcol, cast_e=("dve", "act"), copy_e=("act", "dve")):
            xb, xb_k = xb_ring.next()
            copy_op(rr(list(cast_e)), xb[:], xs[:], [xs_k], [xb_k])
            pT, pT_k = pT_ring.next()
            for c in range(8):
                S.I("pe", "transpose", out=pT[:, c * 128:(c + 1) * 128], in_=xb[:, c * 128:(c + 1) * 128], identity=ident[:],
                    reads=[xb_k, "ident"], writes=[pT_k])
            copy_op(rr(list(copy_e)), xT[:, :, tcol:tcol + 128], pT[:].rearrange("p (c t) -> p c t", c=8), [pT_k], [xT_k])

        def project(xT, xT_k, wb, acc_r, fm_r, fm_list, tm_r, tm_groups, t0, tm_dst, ntt=4, tts=None, ev=("act", "dve")):
            ncols = ntt * 128
            for (f0, dst, ring) in fm_list:
                acc, acc_k = acc_r.next()
                for c in range(8):
                    S.I("pe", "matmul", acc[:, 0:ncols], lhsT=wb[:, c, f0:f0 + 128], rhs=xT[:, c, 0:ncols], start=(c == 0), stop=(c == 7),
                        reads=[xT_k], writes=[acc_k])
                fm, fm_k = ring.next()
                copy_op(rr(list(ev)), fm[:, 0:ncols], acc[:, 0:ncols], [acc_k], [fm_k])
                S.I("pool", "dma_start", out=dst, in_=fm[:, 0:ncols], reads=[fm_k], dma=fm_k)
            for tt in (range(ntt) if tts is None else tts):
                tm, tm_k = tm_r.next()
                for grp in tm_groups:
                    if len(grp) == 5:
                        wc, width, sc, nh, func = grp
                        plist = [(0, width, sc, nh, func)]
                    else:
                        wc, width, plist = grp
                    acc, acc_k = acc_r.next()
                    for c in range(8):
                        S.I("pe", "matmul", acc[:, 0:width], lhsT=xT[:, c, tt * 128:(tt + 1) * 128], rhs=wb[:, c, wc:wc + width], start=(c == 0), stop=(c == 7),
                            reads=[xT_k], writes=[acc_k])
                    for (o, w, sc, nh, func) in plist:
                        if nh > 0:
                            copy_op(rr(list(ev)), tm[:, sc:sc + nh * 65].rearrange("p (h d) -> p h d", d=65)[:, :, 0:64],
                                    acc[:, o:o + w].rearrange("p (h d) -> p h d", d=64), [acc_k], [tm_k])
                        elif func is not None:
                            S.I("act", "activation", out=tm[:, sc:sc + w], in_=acc[:, o:o + w], func=func, reads=[acc_k], writes=[tm_k])
                        else:
                            copy_op(rr(list(ev)), tm[:, sc:sc + w], acc[:, o:o + w], [acc_k], [tm_k])
                r0 = t0 + tt * 128
                S.I("pool", "dma_start", out=tm_dst[r0:r0 + 128, :], in_=tm[:], reads=[tm_k], dma=tm_k)

        with ExitStack() as st:
            wb = sb(st, "wb0", [128, 8, EVEN_IN], BF16)
            stage = Ring(nc, st, "wstg", 3, [128, 1024], F32)
            load_weight(wb, w_in_even, 8, EVEN_IN, stage,
                        pieces=[(0, 640, 0), (768, 1024, 640), (1792, 512, 1664), (640, 128, 2176), (2304, 1024, 2304), (3328, 512, 3328)])
            xs_r = Ring(nc, st, "xs", 4, [128, 1024], F32)
            xb_r = Ring(nc, st, "xb", 2, [128, 1024], BF16)
            pT_r = Ring(nc, st, "pT", 2, [128, 1024], BF16, psum=True)
            xT_r = Ring(nc, st, "xT", 2, [128, 8, 512], BF16)
            acc_r = Ring(nc, st, "acc", 4, [128, 512], F32, psum=True)
            fm_r = Ring(nc, st, "fm", 4, [128, 512], BF16)
            tm_r = Ring(nc, st, "tm", 2, [128, H0_ROW], BF16)
            for i in range(2):
                t, k = tm_r.at(i)
                S.I("pool", "memset", t[:], 1.0, writes=[k])
            fm_feats = [i * 128 for i in range(17)]
            tm_groups = [
                (2176, 512, [(0, 128, H0_VA, 2, None), (128, 384, H0_VB, 6, None)]),
                (2688, 512, [(0, 384, H0_VB + 6 * 65, 6, None), (384, 128, H0_GA, 0, AF.Silu)]),
                (3200, 512, [(0, 384, H0_GA + 128, 0, AF.Silu), (384, 128, H0_GB, 0, AF.Silu)]),
                (3712, 128, [(0, 128, H0_GB + 128, 0, AF.Silu)]),
            ]
            for b in range(TT // 512):
                t0 = b * 512
                xT, xT_k = xT_r.next()
                for tt in range(4):
                    xs, xs_k = xs_r.next()
                    r0 = t0 + tt * 128
                    S.I("sp", "dma_start", out=xs[:], in_=x_in[r0:r0 + 128, :], writes=[xs_k], dma=xs_k)
                    make_xT(xs, xs_k, xb_r, pT_r, xT, xT_k, tt * 128, cast_e=("dve",), copy_e=("act",))
                fm_list = [(f0, h0T[fi * 128:(fi + 1) * 128, t0:t0 + 512], fm_r) for fi, f0 in enumerate(fm_feats)]
                project(xT, xT_k, wb, acc_r, fm_r, fm_list, tm_r, tm_groups, t0, h0tm, ev=("act",))
        S.barrier()

        h0tm_h = h0tm.tensor
        oatt_h = oatt.tensor

        def attn_group(st, E0, gname, d, quadsets, qrow, krow, vcol, VW, ocol, is_a):
            W = max(512, 128 * d)
            tpw = W // (128 * d)
            CQ = 4 if is_a else 2
            nring = 4 if d < 16 else 3
            Qr = Ring(nc, st, gname + "Q", 2, [128, CQ, W], BF16)
            Kr = Ring(nc, st, gname + "K", nring, [128, 4, W], BF16)
            Vr = Ring(nc, st, gname + "V", nring, [128, tpw * d, VW], BF16)
            psr = Ring(nc, st, gname + "ps", 2, [128, 1536], F32, psum=True)
            por = Ring(nc, st, gname + "po", 2, [128, 512], F32, psum=True)
            esr = Ring(nc, st, gname + "es", 2, [128, 1536], BF16)
            ptr = Ring(nc, st, gname + "pt", 2, [128, 1536], BF16)
            osr = Ring(nc, st, gname + "os", 3, [128, 260], F32)
            for i in range(nring):
                Kw, Kk = Kr.at(i)
                S.I("pool", "memset", Kw[:], 0.0, writes=[Kk])
            for si, T in enumerate(seq_lens):
                so = seq_off[si]
                NW = T // W
                NT = T // (128 * d)

                def load_kv(wn):
                    Kw, Kk = Kr.at(wn)
                    Vw, Vk = Vr.at(wn)
                    c0 = so + wn * W
                    for v in range(4):
                        if is_a:
                            g, half = v // 2, v % 2
                            src = h0T[krow + g * 64:krow + (g + 1) * 64, c0:c0 + W]
                        else:
                            half = v % 2
                            src = h0T[krow + v * 64:krow + (v + 1) * 64, c0:c0 + W]
                        S.I("sp", "dma_start", out=Kw[half * 64:(half + 1) * 64, v, :], in_=src, writes=[Kk], dma=Kk)
                    if d == 1:
                        src = bass.AP(h0tm_h, c0 * H0_ROW + vcol, [[H0_ROW, 128], [128 * H0_ROW, tpw], [1, VW]])
                    else:
                        src = bass.AP(h0tm_h, c0 * H0_ROW + vcol, [[d * H0_ROW, 128], [H0_ROW, d], [1, VW]])
                    S.I("sp", "dma_start", out=Vw[:], in_=src, writes=[Vk], dma=Vk)

                def stage2(n, r, js, qi, pt, ptk, tt, so=so, NT=NT):
                    po, pok = por.next()
                    for hh in range(4):
                        for ji, j in enumerate(js):
                            n2 = n + j - 1
                            Vw, Vk = Vr.at(n2 // tpw)
                            t2 = n2 % tpw
                            vi = t2 if d == 1 else r
                            vc0 = (qi * 65) if is_a else hh * 65
                            pos = ((hh % 2) * 2 + hh // 2) if is_a else hh
                            o0 = (j * 4 + pos) * 128
                            S.I("pe", "matmul", po[:, hh * 65:hh * 65 + 65], lhsT=pt[:, o0:o0 + 128], rhs=Vw[:, vi, vc0:vc0 + 65], start=(ji == 0), stop=(ji == len(js) - 1),
                                reads=[ptk, Vk], writes=[pok])
                    osb, osk = osr.next()
                    copy_op("dve", osb[:], po[:, 0:260], [pok], [osk])
                    row0 = so + n * 128 * d + r
                    dst = bass.AP(oatt_h, row0 * OA_ROW + ocol + (qi * 260 if is_a else 0), [[d * OA_ROW, 128], [1, 260]])
                    S.I("pool", "dma_start", out=dst, in_=osb[:], reads=[osk], dma=osk)

                load_kv(0)
                for wn in range(NW):
                    Qw, Qk = Qr.at(wn)
                    c0 = so + wn * W
                    S.I("sp", "dma_start", out=Qw[:], in_=h0T[qrow:qrow + CQ * 128, c0:c0 + W].rearrange("(c p) t -> p c t", p=128), writes=[Qk], dma=Qk)
                    if wn + 1 < NW:
                        load_kv(wn + 1)
                    pending = None
                    for tt in range(tpw):
                        for r in range(d):
                            n = wn * tpw + tt
                            js = [j for j in range(3) if 0 <= n + j - 1 < NT]
                            qcols = slice(tt * 128 * d + r, tt * 128 * d + r + 127 * d + 1, d)
                            for qi, qs in enumerate(quadsets):
                                ps, psk = psr.next()
                                for j in js:
                                    n2 = n + j - 1
                                    Kw, Kk = Kr.at(n2 // tpw)
                                    t2 = n2 % tpw
                                    kcols = slice(t2 * 128 * d + r, t2 * 128 * d + r + 127 * d + 1, d)
                                    if is_a:
                                        for half in range(2):
                                            out_ap = ps[:, (j * 4 + half * 2) * 128:(j * 4 + half * 2 + 2) * 128]
                                            S.I("pe", "matmul", out_ap, lhsT=Kw[:, qi * 2 + half, kcols], rhs=Qw[:, qi * 2:qi * 2 + 2, qcols], start=True, stop=True,
                                                reads=[Kk, Qk], writes=[psk])
                                    else:
                                        for hh in range(4):
                                            o0 = (j * 4 + hh) * 128
                                            S.I("pe", "matmul", ps[:, o0:o0 + 128], lhsT=Kw[:, hh, kcols], rhs=Qw[:, hh // 2, qcols], start=True, stop=True, reads=[Kk, Qk], writes=[psk])
                                es, esk = esr.next()
                                pt, ptk = ptr.next()
                                for j in js:
                                    S.I("act", "activation", out=es[:, j * 512:(j + 1) * 512], in_=ps[:, j * 512:(j + 1) * 512], func=AF.Exp, scale=0.125,
                                        reads=[psk], writes=[esk])
                                for j in js:
                                    S.I("dve", "tensor_tensor", out=pt[:, j * 512:(j + 1) * 512], in0=es[:, j * 512:(j + 1) * 512], in1=E0[:, qs, j * 512:(j + 1) * 512], op=ALU.mult,
                                        reads=[esk, "E0"], writes=[ptk])
                                unit = (n, r, js, qi, pt, ptk, tt)
                                if pending is not None:
                                    stage2(*pending)
                                pending = unit
                    if pending is not None:
                        stage2(*pending)
                        pending = None

        if stop_after >= 2:
            with ExitStack() as st:
                E0 = sb(st, "E0", [128, 5, 1536], BF16)
                tstage = Ring(nc, st, "tstg", 2, [128, 1536], F32)
                for qs in range(5):
                    stg, sk = tstage.next()
                    S.I("sp", "dma_start", out=stg[:], in_=tab0_in[qs], writes=[sk], dma=sk)
                    S.I("act", "activation", out=E0[:, qs, :], in_=stg[:], func=AF.Exp, reads=[sk], writes=["E0"])
                with ExitStack() as st2:
                    if not skipA:
                        attn_group(st2, E0, "A", 1, [0, 1], 0, 512, H0_VA, 130, 0, True)
                S.barrier()
                for gi, d in enumerate((1, 4, 16)):
                    if stop_after < 3 + gi:
                        continue
                    with ExitStack() as st2:
                        attn_group(st2, E0, "B%d" % gi, d, [2 + gi], 640 + gi * 256, 1408 + gi * 256, H0_VB + gi * 260, 260, 520 + gi * 260, False)
                    S.barrier()

        def run_skewed(gens):
            active = []
            it = iter(gens)
            done = False
            while True:
                if not done:
                    try:
                        active.append(next(it))
                    except StopIteration:
                        done = True
                if done and not active:
                    break
                for g in list(active):
                    try:
                        next(g)
                    except StopIteration:
                        active.remove(g)

        def layer_norm_gen(zps, zk, xres, xres_k, lng, lnb, v_r, st_r, junk, x1_r, dst):
            v, vk = v_r.next()
            for hf in range(2):
                S.I("dve", "scalar_tensor_tensor", out=v[:, hf * 512:(hf + 1) * 512], in0=xres[:, hf * 512:(hf + 1) * 512], scalar=ALPHA, in1=zps[hf][:],
                    op0=ALU.mult, op1=ALU.add, reads=[xres_k, zk[hf]], writes=[vk])
            stt, stk = st_r.next()
            S.I("dve", "tensor_reduce", out=stt[:, 0:1], in_=v[:], axis=AX.X, op=ALU.add, reads=[vk], writes=[(stk, 0)])
            S.I("act", "activation", out=junk[:], in_=v[:], func=AF.Square, accum_out=stt[:, 1:2], reads=[vk], writes=["junk", (stk, 1)])
            yield
            S.I("dve", "tensor_scalar", out=stt[:, 2:3], in0=stt[:, 0:1], scalar1=1.0 / 1024, scalar2=None, op0=ALU.mult, reads=[(stk, 0)], writes=[(stk, 2)])
            S.I("dve", "tensor_tensor", out=stt[:, 3:4], in0=stt[:, 2:3], in1=stt[:, 2:3], op=ALU.mult, reads=[(stk, 2)], writes=[(stk, 3)])
            yield
            S.I("dve", "scalar_tensor_tensor", out=stt[:, 4:5], in0=stt[:, 1:2], scalar=1.0 / 1024, in1=stt[:, 3:4], op0=ALU.mult, op1=ALU.subtract,
                reads=[(stk, 1), (stk, 3)], writes=[(stk, 4)])
            yield
            S.I("dve", "tensor_scalar", out=stt[:, 4:5], in0=stt[:, 4:5], scalar1=LN_EPS, scalar2=None, op0=ALU.add, reads=[(stk, 4)], writes=[(stk, 4)])
            S.I("act", "activation", out=stt[:, 6:7], in_=stt[:, 4:5], func=AF.Sqrt, reads=[(stk, 4)], writes=[(stk, 6)])
            yield
            S.I("dve", "reciprocal", out=stt[:, 5:6], in_=stt[:, 6:7], reads=[(stk, 6)], writes=[(stk, 5)])
            S.I("dve", "scalar_tensor_tensor", out=v[:], in0=v[:], scalar=stt[:, 2:3], in1=lng, op0=ALU.subtract, op1=ALU.mult,
                reads=[vk, (stk, 2), "ln"], writes=[vk])
            yield
            x1, x1k = x1_r.next()
            S.I("dve", "scalar_tensor_tensor", out=x1[:], in0=v[:], scalar=stt[:, 5:6], in1=lnb, op0=ALU.mult, op1=ALU.add,
                reads=[vk, (stk, 5), "ln"], writes=[x1k])
            S.I("pool", "dma_start", out=dst, in_=x1[:], reads=[x1k], dma=x1k)

        def layer_norm_tile(zps, zk, xres, xres_k, lng, lnb, v_r, st_r, junk, x1_r):
            v, vk = v_r.next()
            for hf in range(2):
                S.I("dve", "scalar_tensor_tensor", out=v[:, hf * 512:(hf + 1) * 512], in0=xres[:, hf * 512:(hf + 1) * 512], scalar=ALPHA, in1=zps[hf][:],
                    op0=ALU.mult, op1=ALU.add, reads=[xres_k, zk[hf]], writes=[vk])
            stt, stk = st_r.next()
            S.I("dve", "tensor_reduce", out=stt[:, 0:1], in_=v[:], axis=AX.X, op=ALU.add, reads=[vk], writes=[(stk, 0)])
            S.I("act", "activation", out=junk[:], in_=v[:], func=AF.Square, accum_out=stt[:, 1:2], reads=[vk], writes=["junk", (stk, 1)])
            S.I("dve", "tensor_scalar", out=stt[:, 2:3], in0=stt[:, 0:1], scalar1=1.0 / 1024, scalar2=None, op0=ALU.mult, reads=[(stk, 0)], writes=[(stk, 2)])
            S.I("dve", "tensor_tensor", out=stt[:, 3:4], in0=stt[:, 2:3], in1=stt[:, 2:3], op=ALU.mult, reads=[(stk, 2)], writes=[(stk, 3)])
            S.I("dve", "scalar_tensor_tensor", out=stt[:, 4:5], in0=stt[:, 1:2], scalar=1.0 / 1024, in1=stt[:, 3:4], op0=ALU.mult, op1=ALU.subtract,
                reads=[(stk, 1), (stk, 3)], writes=[(stk, 4)])
            S.I("dve", "tensor_scalar", out=stt[:, 4:5], in0=stt[:, 4:5], scalar1=LN_EPS, scalar2=None, op0=ALU.add, reads=[(stk, 4)], writes=[(stk, 4)])
            S.I("act", "activation", out=stt[:, 6:7], in_=stt[:, 4:5], func=AF.Sqrt, reads=[(stk, 4)], writes=[(stk, 6)])
            S.I("dve", "reciprocal", out=stt[:, 5:6], in_=stt[:, 6:7], reads=[(stk, 6)], writes=[(stk, 5)])
            S.I("dve", "scalar_tensor_tensor", out=v[:], in0=v[:], scalar=stt[:, 2:3], in1=lng, op0=ALU.subtract, op1=ALU.mult,
                reads=[vk, (stk, 2), "ln"], writes=[vk])
            x1, x1k = x1_r.next()
            S.I("dve", "scalar_tensor_tensor", out=x1[:], in0=v[:], scalar=stt[:, 5:6], in1=lnb, op0=ALU.mult, op1=ALU.add,
                reads=[vk, (stk, 5), "ln"], writes=[x1k])
            return x1, x1k

        if stop_after >= 6:
            with ExitStack() as st:
                wo = sb(st, "wo0", [128, 6, 1024], BF16)
                w1 = sb(st, "w1", [128, 8, ODD_IN], BF16)
                with ExitStack() as st2:
                    stage = Ring(nc, st2, "wstg1", 3, [128, 1024], F32)
                    load_weight(wo, w_out_even, 6, 1024, stage)
                    load_weight(w1, w_in_odd, 8, ODD_IN, stage)
                    S.barrier()
                lnt = sb(st, "lnt", [128, 2, 1024], F32)
                S.I("sp", "dma_start", out=lnt[:, 0, :], in_=lng_in[:, 0:1024], writes=["ln"], dma="ln")
                S.I("sp", "dma_start", out=lnt[:, 1, :], in_=lnb_in[:, 0:1024], writes=["ln"], dma="ln")
                esink = sb(st, "esink", [128, 8], F32)
                S.I("sp", "dma_start", out=esink[:], in_=sink_in, writes=["esink"], dma="esink")
                S.I("act", "activation", out=esink[:], in_=esink[:], func=AF.Exp, reads=["esink"], writes=["esink"])
                oa_r = Ring(nc, st, "oa", 3, [128, OA_ROW], F32)
                g_r = Ring(nc, st, "gt", 3, [128, 768], BF16)
                xs_r = Ring(nc, st, "xs2", 3, [128, 1024], F32)
                v_r = Ring(nc, st, "vln", 4, [128, 1024], F32)
                x1_r = Ring(nc, st, "x1t", 2, [128, 1024], F32)
                st_r = Ring(nc, st, "stt", 6, [128, 8], F32)
                sm_r = Ring(nc, st, "smt", 4, [128, 16], F32)
                tmp_r = Ring(nc, st, "tmpc", 2, [128, 768], F32)
                junk = sb(st, "junk", [128, 1024], BF16)
                y_r = Ring(nc, st, "yt", 2, [128, 768], BF16)
                yT_r = Ring(nc, st, "yT", 2, [128, 6, 128], BF16)
                xb_r = Ring(nc, st, "xb2", 2, [128, 1024], BF16)
                xT_r = Ring(nc, st, "xT2", 2, [128, 8, 512], BF16)
                fmb_r = Ring(nc, st, "fmb", 4, [128, 512], BF16)
                fmf_r = Ring(nc, st, "fmf", 4, [128, 512], F32)
                tm_r = Ring(nc, st, "tm1", 2, [128, H1_ROW], BF16)
                pT_r = Ring(nc, st, "pT2", 2, [128, 1024], BF16, psum=True)
                z_r = Ring(nc, st, "zps", 4, [128, 512], F32, psum=True)
                acc_r = Ring(nc, st, "acc2", 2, [128, 512], F32, psum=True)
                for i in range(2):
                    t, k = tm_r.at(i)
                    S.I("pool", "memset", t[:], 1.0, writes=[k])
                fm_cols = [(QC + c * 128, h1Tb, c, fmb_r) for c in range(4)] + [(KC + c * 128, h1Tb, 4 + c, fmb_r) for c in range(4)] + \
                          [(QD + c * 128, h1Tf, c, fmf_r) for c in range(4)] + [(ZF + c * 128, h1Tf, 4 + c, fmf_r) for c in range(4)] + \
                          [(ZB + c * 128, h1Tf, 8 + c, fmf_r) for c in range(4)]
                tm_groups1 = [(VC, 512, H1_VC, 8, None), (IDD, 512, H1_ID, 0, None), (GC, 512, H1_GC, 0, AF.Silu), (GD, 512, H1_GD, 0, AF.Silu)]
                NBLK = TT // 512
                tiles_done = [0] * NBLK
                proj_done = [False] * NBLK
                blk_xT = {}

                def acquire(ring):
                    while not ring.can_next():
                        yield
                    return ring.next_hold()

                def tile_gen2(b, tt):
                    t0 = b * 512
                    r0 = t0 + tt * 128
                    oa, oak, oai = yield from acquire(oa_r)
                    S.I("sp", "dma_start", out=oa[:], in_=oatt[r0:r0 + 128, :], writes=[oak], dma=oak)
                    gt, gk, gi_ = yield from acquire(g_r)
                    S.I("sp", "dma_start", out=gt[:], in_=h0tm[r0:r0 + 128, H0_GA:H0_GA + 768], writes=[gk], dma=gk)
                    xs, xsk, xsi = yield from acquire(xs_r)
                    S.I("sp", "dma_start", out=xs[:], in_=x_in[r0:r0 + 128, :], writes=[xsk], dma=xsk)
                    yield
                    sm, smk, smi = yield from acquire(sm_r)
                    oa3 = oa[:, 0:520].rearrange("p (h d) -> p h d", d=65)
                    S.I("dve", "tensor_tensor", out=sm[:, 0:8], in0=oa3[:, :, 64], in1=esink[:], op=ALU.add, reads=[oak, "esink"], writes=[(smk, 0)])
                    S.I("dve", "tensor_tensor", out=oa[:, 520:780], in0=oa[:, 520:780], in1=oa[:, 780:1040], op=ALU.add, reads=[oak], writes=[oak])
                    yield
                    S.I("dve", "reciprocal", out=sm[:, 0:8], in_=sm[:, 0:8], reads=[(smk, 0)], writes=[(smk, 0)])
                    S.I("dve", "tensor_tensor", out=oa[:, 520:780], in0=oa[:, 520:780], in1=oa[:, 1040:1300], op=ALU.add, reads=[oak], writes=[oak])
                    yield
                    tmp, tmpk = tmp_r.next()
                    y, yk = y_r.next()
                    ob3 = oa[:, 520:780].rearrange("p (h d) -> p h d", d=65)
                    S.I("dve", "tensor_tensor", out=tmp[:, 0:512].rearrange("p (h d) -> p h d", d=64), in0=oa3[:, :, 0:64],
                        in1=sm[:, 0:8].unsqueeze(2).broadcast_to([128, 8, 64]), op=ALU.mult, reads=[oak, (smk, 0)], writes=[(tmpk, 0)])
                    S.I("dve", "reciprocal", out=sm[:, 8:12], in_=ob3[:, :, 64], reads=[oak], writes=[(smk, 1)])
                    yield
                    S.I("dve", "tensor_tensor", out=tmp[:, 512:768].rearrange("p (h d) -> p h d", d=64), in0=ob3[:, :, 0:64],
                        in1=sm[:, 8:12].unsqueeze(2).broadcast_to([128, 4, 64]), op=ALU.mult, reads=[oak, (smk, 1)], writes=[(tmpk, 1)])
                    S.I("dve", "tensor_tensor", out=y[:], in0=tmp[:], in1=gt[:], op=ALU.mult, reads=[(tmpk, 0), (tmpk, 1), gk], writes=[yk])
                    oa_r.release(oai)
                    g_r.release(gi_)
                    sm_r.release(smi)
                    pT, pTk = pT_r.next()
                    for c in range(6):
                        S.I("pe", "transpose", out=pT[:, c * 128:(c + 1) * 128], in_=y[:, c * 128:(c + 1) * 128], identity=ident[:], reads=[yk, "ident"], writes=[pTk])
                    yT, yTk = yT_r.next()
                    copy_op("act", yT[:], pT[:, 0:768].rearrange("p (c t) -> p c t", c=6), [pTk], [yTk])
                    zps, zk, zis = [], [], []
                    for hf in range(2):
                        z, zkk, zi = yield from acquire(z_r)
                        for c in range(6):
                            S.I("pe", "matmul", z[:], lhsT=yT[:, c, :], rhs=wo[:, c, hf * 512:(hf + 1) * 512], start=(c == 0), stop=(c == 5), reads=[yTk], writes=[zkk])
                        zps.append(z)
                        zk.append(zkk)
                        zis.append(zi)
                    yield
                    v, vk, vi_ = yield from acquire(v_r)
                    for hf in range(2):
                        S.I("dve", "scalar_tensor_tensor", out=v[:, hf * 512:(hf + 1) * 512], in0=xs[:, hf * 512:(hf + 1) * 512], scalar=ALPHA, in1=zps[hf][:],
                            op0=ALU.mult, op1=ALU.add, reads=[xsk, zk[hf]], writes=[vk])
                    xs_r.release(xsi)
                    for zi in zis:
                        z_r.release(zi)
                    stt, stk, sti = yield from acquire(st_r)
                    S.I("dve", "tensor_reduce", out=stt[:, 0:1], in_=v[:], axis=AX.X, op=ALU.add, reads=[vk], writes=[(stk, 0)])
                    S.I("act", "activation", out=junk[:], in_=v[:], func=AF.Square, accum_out=stt[:, 1:2], reads=[vk], writes=["junk", (stk, 1)])
                    yield
                    S.I("dve", "tensor_scalar", out=stt[:, 2:3], in0=stt[:, 0:1], scalar1=1.0 / 1024, scalar2=None, op0=ALU.mult, reads=[(stk, 0)], writes=[(stk, 2)])
                    S.I("dve", "tensor_tensor", out=stt[:, 3:4], in0=stt[:, 2:3], in1=stt[:, 2:3], op=ALU.mult, reads=[(stk, 2)], writes=[(stk, 3)])
                    S.I("dve", "scalar_tensor_tensor", out=stt[:, 4:5], in0=stt[:, 1:2], scalar=1.0 / 1024, in1=stt[:, 3:4], op0=ALU.mult, op1=ALU.subtract,
                        reads=[(stk, 1), (stk, 3)], writes=[(stk, 4)])
                    S.I("dve", "tensor_scalar", out=stt[:, 4:5], in0=stt[:, 4:5], scalar1=LN_EPS, scalar2=None, op0=ALU.add, reads=[(stk, 4)], writes=[(stk, 4)])
                    S.I("act", "activation", out=stt[:, 6:7], in_=stt[:, 4:5], func=AF.Sqrt, reads=[(stk, 4)], writes=[(stk, 6)])
                    yield
                    S.I("dve", "reciprocal", out=stt[:, 5:6], in_=stt[:, 6:7], reads=[(stk, 6)], writes=[(stk, 5)])
                    S.I("dve", "scalar_tensor_tensor", out=v[:], in0=v[:], scalar=stt[:, 2:3], in1=lnt[:, 0, :], op0=ALU.subtract, op1=ALU.mult,
                        reads=[vk, (stk, 2), "ln"], writes=[vk])
                    yield
                    while b >= 2 and not proj_done[b - 2]:
                        yield
                    if b not in blk_xT:
                        blk_xT[b] = xT_r.next()
                    xT, xT_k = blk_xT[b]
                    x1, x1k = x1_r.next()
                    S.I("dve", "scalar_tensor_tensor", out=x1[:], in0=v[:], scalar=stt[:, 5:6], in1=lnt[:, 1, :], op0=ALU.mult, op1=ALU.add,
                        reads=[vk, (stk, 5), "ln"], writes=[x1k])
                    v_r.release(vi_)
                    st_r.release(sti)
                    S.I("pool", "dma_start", out=x1s[r0:r0 + 128, :], in_=x1[:], reads=[x1k], dma=x1k)
                    make_xT(x1, x1k, xb_r, pT_r, xT, xT_k, tt * 128, cast_e=("dve",), copy_e=("act",))
                    tiles_done[b] += 1

                def proj_gen(b):
                    while tiles_done[b] < 4:
                        yield
                    t0 = b * 512
                    xT, xT_k = blk_xT[b]
                    fm_list = [(f0, dst[ci * 128:(ci + 1) * 128, t0:t0 + 512], ring) for (f0, dst, ci, ring) in fm_cols]
                    for i0 in range(0, len(fm_list), 5):
                        project(xT, xT_k, w1, acc_r, None, fm_list[i0:i0 + 5], tm_r, tm_groups1, t0, h1tm, ntt=4, tts=[], ev=("act",))
                        yield
                    for tt in range(4):
                        project(xT, xT_k, w1, acc_r, None, [], tm_r, tm_groups1, t0, h1tm, ntt=4, tts=[tt], ev=("act",))
                        yield
                    proj_done[b] = True

                gens = []
                for b in range(NBLK):
                    for tt in range(4):
                        gens.append(tile_gen2(b, tt))
                    gens.append(proj_gen(b))
                run_skewed(gens)
            S.barrier()

        h1tm_h = h1tm.tensor
        if stop_after >= 7:
            with ExitStack() as st:
                EN = sb(st, "EN", [128, 4, 2304], BF16)
                with ExitStack() as st2:
                    tstage = Ring(nc, st2, "nstg", 2, [128, 2304], F32)
                    for hp in range(4):
                        stg, sk = tstage.next()
                        S.I("sp", "dma_start", out=stg[:], in_=tabn_in[hp], writes=[sk], dma=sk)
                        S.I("act", "activation", out=EN[:, hp, :], in_=stg[:], func=AF.Exp, reads=[sk], writes=["EN"])
                    S.barrier()
                W = 512
                Qr = Ring(nc, st, "NQ", 2, [128, 4, W], BF16)
                Kr = Ring(nc, st, "NK", 4, [128, 8, W], BF16)
                Vr = Ring(nc, st, "NV", 4, [128, 4, 520], BF16)
                psr = Ring(nc, st, "Nps", 2, [128, 1536], F32, psum=True)
                por = Ring(nc, st, "Npo", 1, [128, 1024], F32, psum=True)
                esr = Ring(nc, st, "Nes", 2, [128, 1280], BF16)
                ptr = Ring(nc, st, "Npt", 2, [128, 1280], BF16)
                osr = Ring(nc, st, "Nos", 2, [128, 520], F32)
                for i in range(4):
                    Kw, Kk = Kr.at(i)
                    S.I("pool", "memset", Kw[:], 0.0, writes=[Kk])
                for si, T in enumerate(seq_lens):
                    so = seq_off[si]
                    NW = T // W
                    U = T // 128

                    def load_kv_n(wn):
                        Kw, Kk = Kr.at(wn)
                        Vw, Vk = Vr.at(wn)
                        c0 = so + wn * W
                        for h in range(8):
                            half = h % 2
                            S.I("sp", "dma_start", out=Kw[half * 64:(half + 1) * 64, h, :], in_=h1Tb[512 + h * 64:512 + (h + 1) * 64, c0:c0 + W], writes=[Kk], dma=Kk)
                        src = bass.AP(h1tm_h, c0 * H1_ROW + H1_VC, [[H1_ROW, 128], [128 * H1_ROW, 4], [1, 520]])
                        S.I("sp", "dma_start", out=Vw[:], in_=src, writes=[Vk], dma=Vk)

                    def na_stage2(u, hp, neigh, pt, ptk, po, pok, so=so):
                        NJ = len(neigh)
                        for hl in range(2):
                            h = 2 * hp + hl
                            pc = (h // 4) * 512 + (h % 4) * 65
                            for j, (m, blk) in enumerate(neigh):
                                Vw, Vk = Vr.at(m // 4)
                                o0 = (j * 2 + hl) * 128
                                S.I("pe", "matmul", po[:, pc:pc + 65], lhsT=pt[:, o0:o0 + 128], rhs=Vw[:, m % 4, h * 65:(h + 1) * 65], start=(j == 0), stop=(j == NJ - 1),
                                    reads=[ptk, Vk], writes=[pok])
                        if hp == 3:
                            osb, osk = osr.next()
                            copy_op("act", osb[:, 0:260], po[:, 0:260], [pok], [osk])
                            copy_op("dve", osb[:, 260:520], po[:, 512:772], [pok], [osk])
                            r0 = so + u * 128
                            S.I("pool", "dma_start", out=ocs[r0:r0 + 128, :], in_=osb[:], reads=[osk], dma=osk)

                    load_kv_n(0)
                    for wn in range(NW):
                        Qw, Qk = Qr.at(wn)
                        c0 = so + wn * W
                        S.I("sp", "dma_start", out=Qw[:], in_=h1Tb[0:512, c0:c0 + W].rearrange("(c p) t -> p c t", p=128), writes=[Qk], dma=Qk)
                        if wn + 1 < NW:
                            load_kv_n(wn + 1)
                        pending = None
                        for tt in range(4):
                            u = wn * 4 + tt
                            neigh = na_neighbors(u, U)
                            NJ = len(neigh)
                            po, pok = por.next()
                            for hp in range(4):
                                ps, psk = psr.next()
                                for j, (m, blk) in enumerate(neigh):
                                    Kw, Kk = Kr.at(m // 4)
                                    for hl in range(2):
                                        h = 2 * hp + hl
                                        o0 = (j * 2 + hl) * 128
                                        S.I("pe", "matmul", ps[:, o0:o0 + 128], lhsT=Kw[:, h, (m % 4) * 128:(m % 4 + 1) * 128], rhs=Qw[:, hp, tt * 128:(tt + 1) * 128],
                                            start=True, stop=True, reads=[Kk, Qk], writes=[psk])
                                es, esk = esr.next()
                                pt, ptk = ptr.next()
                                ncol = NJ * 256
                                for c0_ in range(0, ncol, 512):
                                    c1_ = min(ncol, c0_ + 512)
                                    S.I("act", "activation", out=es[:, c0_:c1_], in_=ps[:, c0_:c1_], func=AF.Exp, scale=0.125, reads=[psk], writes=[esk])
                                for j, (m, blk) in enumerate(neigh):
                                    S.I("dve", "tensor_tensor", out=pt[:, j * 256:(j + 1) * 256], in0=es[:, j * 256:(j + 1) * 256], in1=EN[:, hp, blk * 256:(blk + 1) * 256], op=ALU.mult,
                                        reads=[esk, "EN"], writes=[ptk])
                                unit = (u, hp, neigh, pt, ptk, po, pok)
                                if pending is not None:
                                    na_stage2(*pending)
                                pending = unit
                        if pending is not None:
                            na_stage2(*pending)
                            pending = None
            S.barrier()

        if stop_after >= 8:
            with ExitStack() as st:
                lbt = sb(st, "lbt", [128, 16], F32)
                lbc = sb(st, "lbc", [128, 5, 8], F32)
                S.I("sp", "dma_start", out=lbt[:], in_=lb_in, writes=["lbt"], dma="lbt")
                S.I("act", "activation", out=lbt[:], in_=lbt[:], func=AF.Exp, reads=["lbt"], writes=["lbt"])
                lb4 = lbt[:].rearrange("p (d l h) -> p d l h", d=2, l=2)
                lbv = lambda i: lbc[:, i, :].rearrange("p (d h) -> p d h", d=2)
                S.I("dve", "tensor_tensor", out=lbv(3), in0=lb4[:, :, 0, :], in1=lb4[:, :, 1, :], op=ALU.add, reads=["lbt"], writes=["lbc3"])
                S.I("dve", "reciprocal", out=lbv(3), in_=lbv(3), reads=["lbc3"], writes=["lbc3"])
                S.I("dve", "tensor_tensor", out=lbv(0), in0=lb4[:, :, 1, :], in1=lbv(3), op=ALU.mult, reads=["lbt", "lbc3"], writes=["lbc0"])
                S.I("dve", "tensor_scalar", out=lbc[:, 1, :], in0=lbc[:, 0, :], scalar1=-1.0, scalar2=1.0, op0=ALU.mult, op1=ALU.add, reads=["lbc0"], writes=["lbc1"])
                S.I("dve", "tensor_scalar", out=lbc[:, 2, :], in0=lbc[:, 0, :], scalar1=1.0, scalar2=-1.0, op0=ALU.mult, op1=ALU.add, reads=["lbc0"], writes=["lbc2"])
                S.I("act", "activation", out=lbc[:, 4, :], in_=lbc[:, 1, :], func=AF.Ln, reads=["lbc1"], writes=["lbc4"])
                LBK = ["lbc0", "lbc1", "lbc2", "lbc4"]

                class Dir:
                    pass

                dirs = []
                for dr in range(2):
                    D = Dir()
                    D.dr = dr
                    nm = "hf" if dr == 0 else "hb"
                    D.z_r = Ring(nc, st, nm + "z", 2, [128, 512], F32)
                    D.q_r = Ring(nc, st, nm + "q", 2, [128, 512], F32)
                    D.t_r = [Ring(nc, st, nm + "t%d" % i, 2, [128, 512], F32) for i in range(6)]
                    D.qh = Ring(nc, st, nm + "qh", 2, [128, 4, 512], BF16)
                    D.qt = Ring(nc, st, nm + "qt", 2, [128, 4, 512], BF16)
                    D.kh = Ring(nc, st, nm + "kh", 2, [128, 4, 512], BF16)
                    D.khT = Ring(nc, st, nm + "khT", 2, [128, 4, 8, 128], BF16)
                    D.dec = Ring(nc, st, nm + "dec", 2, [128, 4, 8], F32)
                    D.v = Ring(nc, st, nm + "v", 2, [128, 8, 512], BF16)
                    D.S = sb(st, nm + "S", [128, 4, 128], F32)
                    D.Sbf = sb(st, nm + "Sbf", [128, 4, 128], BF16)
                    D.Am = Ring(nc, st, nm + "Am", 2, [128, 256], BF16)
                    D.ob = Ring(nc, st, nm + "ob", 2, [64, 512], F32)
                    D.psA = Ring(nc, st, nm + "psA", 1, [128, 512], F32, psum=True)
                    D.po = Ring(nc, st, nm + "po", 1, [128, 512], F32, psum=True)
                    D.psS = Ring(nc, st, nm + "psS", 1, [128, 512], F32, psum=True)
                    D.tril = trilf if dr == 0 else trilb
                    D.odst = odf if dr == 0 else odb
                    D.zrow = 512 if dr == 0 else 1024
                    for i in range(2):
                        for ring in (D.khT, D.v, D.Am):
                            t, k = ring.at(i)
                            S.I("pool", "memset", t[:], 0.0, writes=[k])
                    dirs.append(D)
                pTk_r = Ring(nc, st, "hpT", 1, [128, 1024], BF16, psum=True)

                def prep_begin(D, so, b):
                    c0 = so + b * 512
                    qh, qhk = D.qh.next()
                    qt, qtk = D.qt.next()
                    kh, khk = D.kh.next()
                    khT, khTk = D.khT.next()
                    dec, deck = D.dec.next()
                    v, vk = D.v.next()
                    S.I("sp", "dma_start", out=v[0:64, :, :], in_=bass.AP(h1tm_h, c0 * H1_ROW + H1_ID, [[H1_ROW, 64], [64 * H1_ROW, 8], [1, 512]]), writes=[vk], dma=vk)
                    return dict(qh=qh, qhk=[(qhk, h) for h in range(4)], qt=qt, qtk=[(qtk, h) for h in range(4)], kh=kh, khk=[(khk, h) for h in range(4)],
                                khT=khT, khTk=[(khTk, h) for h in range(4)], dec=dec, deck=[(deck, h) for h in range(4)], v=v, vk=vk, c0=c0)

                def prep_head(D, getP, h, c0):
                    dr = D.dr
                    z, zk = D.z_r.next()
                    q, qk = D.q_r.next()
                    S.I("sp", "dma_start", out=z[:], in_=h1Tf[D.zrow + h * 128:D.zrow + (h + 1) * 128, c0:c0 + 512], writes=[zk], dma=zk)
                    S.I("sp", "dma_start", out=q[:], in_=h1Tf[h * 128:(h + 1) * 128, c0:c0 + 512], writes=[qk], dma=qk)
                    T_ = [r_.next() for r_ in D.t_r]
                    (t0_, k0), (t1_, k1), (t2_, k2), (t3_, k3), (t4_, k4), (t5_, k5) = T_
                    lb_ap = lbc[:, 0, dr * 4 + h:dr * 4 + h + 1]
                    lnoml_ap = lbc[:, 4, dr * 4 + h:dr * 4 + h + 1]
                    v3 = lambda t: t[:].rearrange("p (c t) -> p c t", t=64)
                    S.I("act", "activation", out=t0_[:], in_=z[:], func=AF.Exp, scale=-1.0, reads=[zk], writes=[k0])
                    S.I("act", "activation", out=t1_[:], in_=t0_[:], func=AF.Ln, scale=1.0, bias=1.0, reads=[k0], writes=[k1])
                    S.I("act", "activation", out=t2_[:], in_=t0_[:], func=AF.Ln, scale=lb_ap, bias=1.0, reads=[k0] + LBK, writes=[k2])
                    yield
                    S.I("dve", "tensor_tensor", out=t2_[:], in0=t2_[:], in1=t1_[:], op=ALU.subtract, reads=[k2, k1], writes=[k2])
                    S.I("dve", "tensor_tensor_scan", out=t3_[:], data0=scanmask, data1=t2_[:], initial=0.0, op0=ALU.mult, op1=ALU.add, reads=[k2, "cstf"], writes=[k3])
                    tot = v3(t3_)[:, :, 63]
                    totb = v3(t3_)[:, :, 63:64].broadcast_to([128, 8, 64])
                    if dr == 0:
                        S.I("dve", "tensor_tensor", out=v3(t4_), in0=v3(t3_), in1=totb, op=ALU.subtract, reads=[k3], writes=[k4])
                        bq, bqk = t3_, k3
                    else:
                        S.I("dve", "tensor_tensor", out=t4_[:], in0=t2_[:], in1=t3_[:], op=ALU.subtract, reads=[k2, k3], writes=[k4])
                        S.I("dve", "tensor_tensor", out=v3(t5_), in0=v3(t4_), in1=totb, op=ALU.add, reads=[k4, k3], writes=[k5])
                        bq, bqk = t5_, k5
                    S.I("dve", "tensor_tensor", out=t1_[:], in0=t1_[:], in1=z[:], op=ALU.add, reads=[k1, zk], writes=[k1])
                    S.I("dve", "tensor_tensor", out=t1_[:], in0=t1_[:], in1=t4_[:], op=ALU.add, reads=[k1, k4], writes=[k1])
                    yield
                    P = getP()
                    qh, qt, kh, khT, dec = P["qh"], P["qt"], P["kh"], P["khT"], P["dec"]
                    qhk, qtk, khk, khTk, deck = P["qhk"][0][0], P["qtk"][0][0], P["khk"][0][0], P["khTk"][0][0], P["deck"][0][0]
                    S.I("act", "activation", out=dec[:, h, :], in_=tot, func=AF.Exp, reads=[k3], writes=[(deck, h)])
                    S.I("act", "activation", out=kh[:, h, :], in_=t1_[:], func=AF.Exp, scale=-1.0, bias=lnoml_ap, reads=[k1] + LBK, writes=[(khk, h)])
                    S.I("act", "activation", out=t0_[:], in_=t4_[:], func=AF.Exp, reads=[k4], writes=[k0])
                    S.I("act", "activation", out=t2_[:], in_=bq[:], func=AF.Exp, reads=[bqk], writes=[k2])
                    yield
                    S.I("dve", "tensor_tensor", out=qh[:, h, :], in0=q[:], in1=t0_[:], op=ALU.mult, reads=[qk, k0], writes=[(qhk, h)])
                    S.I("dve", "tensor_tensor", out=qt[:, h, :], in0=q[:], in1=t2_[:], op=ALU.mult, reads=[qk, k2], writes=[(qtk, h)])
                    pT, pTk = pTk_r.next()
                    for c in range(8):
                        S.I("pe", "transpose", out=pT[0:64, c * 128:(c + 1) * 128], in_=kh[:, h, c * 64:(c + 1) * 64], identity=ident[:], reads=[(khk, h), "ident"], writes=[pTk])
                    copy_op("act", khT[0:64, h, :, :], pT[0:64, :].rearrange("p (c k) -> p c k", c=8), [pTk], [(khTk, h)])

                def chunk_a(D, P, so, b, c):
                    cs = slice(c * 64, (c + 1) * 64)
                    psA, psAk = D.psA.next()
                    for h in range(4):
                        S.I("pe", "matmul", psA[0:64, h * 64:(h + 1) * 64], lhsT=P["kh"][:, h, cs], rhs=P["qh"][:, h, cs], start=True, stop=True,
                            reads=[P["khk"][h], P["qhk"][h]], writes=[psAk])
                    Am, Amk = D.Am.next()
                    S.I("dve", "tensor_tensor", out=Am[0:64, :], in0=psA[0:64, 0:256], in1=D.tril, op=ALU.mult, reads=[psAk, "cstf"], writes=[Amk])
                    psS, psSk = D.psS.next()
                    for h in range(4):
                        S.I("pe", "matmul", psS[:, h * 128:(h + 1) * 128], lhsT=P["khT"][:, h, c, :], rhs=P["v"][:, c, h * 128:(h + 1) * 128], start=True, stop=True,
                            reads=[P["khTk"][h], P["vk"]], writes=[psSk])
                    return Am, Amk, psS, psSk

                def chunk_b(D, P, so, b, c, Am, Amk, psS, psSk):
                    nm = "hf" if D.dr == 0 else "hb"
                    cs = slice(c * 64, (c + 1) * 64)
                    po, pok = D.po.next()
                    for h in range(4):
                        S.I("pe", "matmul", po[0:64, h * 128:(h + 1) * 128], lhsT=P["qt"][:, h, cs], rhs=D.Sbf[:, h, :], start=True, stop=False,
                            reads=[P["qtk"][h], nm + "Sbf"], writes=[pok])
                        S.I("pe", "matmul", po[0:64, h * 128:(h + 1) * 128], lhsT=Am[:, h * 64:(h + 1) * 64], rhs=P["v"][:, c, h * 128:(h + 1) * 128], start=False, stop=True,
                            reads=[Amk, P["vk"]], writes=[pok])
                    for h in range(4):
                        S.I("dve", "scalar_tensor_tensor", out=D.S[:, h, :], in0=D.S[:, h, :], scalar=P["dec"][:, h, c:c + 1], in1=psS[:, h * 128:(h + 1) * 128],
                            op0=ALU.mult, op1=ALU.add, reads=[(nm + "S", h), P["deck"][h], psSk], writes=[(nm + "S", h)])
                    S.I("act", "copy", out=D.Sbf[:].rearrange("p h v -> p (h v)"), in_=D.S[:].rearrange("p h v -> p (h v)"), reads=[(nm + "S", h) for h in range(4)], writes=[nm + "Sbf"])
                    ob, obk = D.ob.next()
                    copy_op("act", ob[:], po[0:64, :], [pok], [obk])
                    r0 = so + b * 512 + c * 64
                    S.I("act", "dma_start", out=D.odst[r0:r0 + 64, :], in_=ob[:], reads=[obk], dma=obk)

                def chunk_pair(Pf, Pb, so, bf_, bb_, ci):
                    af = chunk_a(dirs[0], Pf, so, bf_, ci)
                    ab = chunk_a(dirs[1], Pb, so, bb_, 7 - ci)
                    chunk_b(dirs[0], Pf, so, bf_, ci, *af)
                    chunk_b(dirs[1], Pb, so, bb_, 7 - ci, *ab)

                for si, T in enumerate(seq_lens):
                    so = seq_off[si]
                    NB = T // 512
                    for D in dirs:
                        nm = "hf" if D.dr == 0 else "hb"
                        S.I("pool", "memset", D.S[:], 0.0, writes=[(nm + "S", h) for h in range(4)])
                        S.I("pool", "memset", D.Sbf[:], 0.0, writes=[nm + "Sbf"])
                    Pst = {0: (prep_begin(dirs[0], so, 0), prep_begin(dirs[1], so, NB - 1))}
                    for h in range(4):
                        for dr in range(2):
                            blk_tok = 0 if dr == 0 else NB - 1
                            for _ in prep_head(dirs[dr], (lambda dr=dr: Pst[0][dr]), h, so + blk_tok * 512):
                                pass
                    off = [-2, -1, 0, 1, 2, 3, 4, 4]
                    starts = {}
                    for k in range(1, NB):
                        for i in range(8):
                            starts.setdefault(8 * (k - 1) + off[i], []).append((k, i))
                    active = []

                    def start_heads(P):
                        for (k, i) in starts.get(P, []):
                            dr, h = i % 2, i // 2
                            blk_tok = k if dr == 0 else NB - 1 - k
                            active.append(prep_head(dirs[dr], (lambda k=k, dr=dr: Pst[k][dr]), h, so + blk_tok * 512))

                    def advance():
                        for g in list(active):
                            try:
                                next(g)
                            except StopIteration:
                                active.remove(g)

                    for P in (-2, -1):
                        start_heads(P)
                        advance()
                    for k in range(NB):
                        if k + 1 < NB:
                            Pst[k + 1] = (prep_begin(dirs[0], so, k + 1), prep_begin(dirs[1], so, NB - 2 - k))
                        Pf, Pb = Pst[k]
                        for ci in range(8):
                            chunk_pair(Pf, Pb, so, k, NB - 1 - k, ci)
                            start_heads(8 * k + ci)
                            advance()
                    while active:
                        advance()
            S.barrier()

        if stop_after >= 9:
            with ExitStack() as st:
                wo = sb(st, "wo1", [128, 8, 1024], BF16)
                with ExitStack() as st2:
                    stage = Ring(nc, st2, "wstg2", 3, [128, 1024], F32)
                    load_weight(wo, w_out_odd, 8, 1024, stage)
                    S.barrier()
                lnt = sb(st, "lnt1", [128, 2, 1024], F32)
                S.I("sp", "dma_start", out=lnt[:, 0, :], in_=lng_in[:, 1024:2048], writes=["ln"], dma="ln")
                S.I("sp", "dma_start", out=lnt[:, 1, :], in_=lnb_in[:, 1024:2048], writes=["ln"], dma="ln")
                gnt = sb(st, "gnt", [128, 512], F32)
                S.I("sp", "dma_start", out=gnt[:], in_=gn_in, writes=["gn"], dma="gn")
                RD4 = {"oc": 4, "of": 5, "obk": 3, "gt1": 5, "xs3": 6, "vln1": 6, "yo": 3, "stt1": 8, "smt1": 6, "tmpc1": 3, "sq1": 3, "yt1": 3, "yT1": 3}
                oc_r = Ring(nc, st, "oc", RD4['oc'], [128, 520], F32)
                of_r = Ring(nc, st, "of", RD4['of'], [128, 512], F32)
                ob_r = Ring(nc, st, "obk", RD4['obk'], [128, 512], F32)
                g_r = Ring(nc, st, "gt1", RD4['gt1'], [128, 1024], BF16)
                xs_r = Ring(nc, st, "xs3", RD4['xs3'], [128, 1024], F32)
                v_r = Ring(nc, st, "vln1", RD4['vln1'], [128, 1024], F32)
                x1_r = Ring(nc, st, "yo", RD4['yo'], [128, 1024], F32)
                st_r = Ring(nc, st, "stt1", RD4['stt1'], [128, 8], F32)
                sm_r = Ring(nc, st, "smt1", RD4['smt1'], [128, 16], F32)
                tmp_r = Ring(nc, st, "tmpc1", RD4['tmpc1'], [128, 1024], F32)
                sq_r = Ring(nc, st, "sq1", RD4['sq1'], [128, 512], F32)
                junk = sb(st, "junk1", [128, 1024], BF16)
                y_r = Ring(nc, st, "yt1", RD4['yt1'], [128, 1024], BF16)
                yT_r = Ring(nc, st, "yT1", RD4['yT1'], [128, 8, 128], BF16)
                pT_r = Ring(nc, st, "pT3", 2, [128, 1024], BF16, psum=True)
                z_r = Ring(nc, st, "zps1", 6, [128, 512], F32, psum=True)
                def tile_gen(ti):
                    r0 = ti * 128
                    oc, ock = oc_r.next()
                    S.I("sp", "dma_start", out=oc[:], in_=ocs[r0:r0 + 128, :], writes=[ock], dma=ock)
                    of, ofk = of_r.next()
                    S.I("sp", "dma_start", out=of[:], in_=odf[r0:r0 + 128, :], writes=[ofk], dma=ofk)
                    ob, obk = ob_r.next()
                    S.I("sp", "dma_start", out=ob[:], in_=odb[r0:r0 + 128, :], writes=[obk], dma=obk)
                    gt, gk = g_r.next()
                    S.I("sp", "dma_start", out=gt[:], in_=h1tm[r0:r0 + 128, H1_GC:H1_GC + 1024], writes=[gk], dma=gk)
                    xs, xsk = xs_r.next()
                    S.I("sp", "dma_start", out=xs[:], in_=x1s[r0:r0 + 128, :], writes=[xsk], dma=xsk)
                    yield
                    sm, smk = sm_r.next()
                    tmp, tmpk = tmp_r.next()
                    y, yk = y_r.next()
                    sq, sqk = sq_r.next()
                    oc3 = oc[:].rearrange("p (h d) -> p h d", d=65)
                    S.I("dve", "reciprocal", out=sm[:, 0:8], in_=oc3[:, :, 64], reads=[ock], writes=[(smk, 0)])
                    S.I("dve", "tensor_tensor", out=of[:], in0=of[:], in1=ob[:], op=ALU.add, reads=[ofk, obk], writes=[ofk])
                    for h in range(4):
                        S.I("act", "activation", out=sq[:, h * 128:(h + 1) * 128], in_=of[:, h * 128:(h + 1) * 128], func=AF.Square, accum_out=sm[:, 8 + h:9 + h],
                            reads=[ofk], writes=[sqk, (smk, 1)])
                    yield
                    S.I("dve", "tensor_tensor", out=tmp[:, 0:512].rearrange("p (h d) -> p h d", d=64), in0=oc3[:, :, 0:64],
                        in1=sm[:, 0:8].unsqueeze(2).broadcast_to([128, 8, 64]), op=ALU.mult, reads=[ock, (smk, 0)], writes=[(tmpk, 0)])
                    S.I("dve", "tensor_scalar", out=sm[:, 8:12], in0=sm[:, 8:12], scalar1=1.0 / 128, scalar2=RMS_EPS, op0=ALU.mult, op1=ALU.add, reads=[(smk, 1)], writes=[(smk, 1)])
                    S.I("act", "activation", out=sm[:, 12:16], in_=sm[:, 8:12], func=AF.Sqrt, reads=[(smk, 1)], writes=[(smk, 2)])
                    yield
                    S.I("dve", "reciprocal", out=sm[:, 12:16], in_=sm[:, 12:16], reads=[(smk, 2)], writes=[(smk, 2)])
                    S.I("dve", "tensor_tensor", out=tmp[:, 512:1024].rearrange("p (h d) -> p h d", d=128), in0=of[:].rearrange("p (h d) -> p h d", d=128),
                        in1=sm[:, 12:16].unsqueeze(2).broadcast_to([128, 4, 128]), op=ALU.mult, reads=[ofk, (smk, 2)], writes=[(tmpk, 1)])
                    S.I("dve", "tensor_tensor", out=tmp[:, 512:1024], in0=tmp[:, 512:1024], in1=gnt[:], op=ALU.mult, reads=[(tmpk, 1), "gn"], writes=[(tmpk, 1)])
                    S.I("dve", "tensor_tensor", out=y[:], in0=tmp[:], in1=gt[:], op=ALU.mult, reads=[(tmpk, 0), (tmpk, 1), gk], writes=[yk])
                    pT, pTk = pT_r.next()
                    for c in range(8):
                        S.I("pe", "transpose", out=pT[:, c * 128:(c + 1) * 128], in_=y[:, c * 128:(c + 1) * 128], identity=ident[:], reads=[yk, "ident"], writes=[pTk])
                    yT, yTk = yT_r.next()
                    copy_op("act", yT[:], pT[:].rearrange("p (c t) -> p c t", c=8), [pTk], [yTk])
                    zps, zk = [], []
                    for hf in range(2):
                        z, zkk = z_r.next()
                        for c in range(8):
                            S.I("pe", "matmul", z[:], lhsT=yT[:, c, :], rhs=wo[:, c, hf * 512:(hf + 1) * 512], start=(c == 0), stop=(c == 7), reads=[yTk], writes=[zkk])
                        zps.append(z)
                        zk.append(zkk)
                    yield
                    for _ in layer_norm_gen(zps, zk, xs, xsk, lnt[:, 0, :], lnt[:, 1, :], v_r, st_r, junk, x1_r, y_out[r0:r0 + 128, :]):
                        yield

                run_skewed([tile_gen(ti) for ti in range(TT // 128)])
            S.barrier()

        fin = Op("sp", None)
        for k in S.dsem:
            fin.deps.append(("dma", k, S.dcnt[k]))
        S.ops["sp"].append(fin)
        with nc.Block() as block:
            S.emit(block)
    return nc


def host_consts(t5_table, sink_a, rpb_c, lb_d, gnorm_d, ln_g, ln_b):
    cst = np.zeros((128, 1280), np.float32)
    cst[:, 0:128] = np.eye(128, dtype=np.float32)
    s = np.arange(64)[:, None]
    t = np.arange(64)[None, :]
    cst[0:64, 128:384] = np.tile((s <= t).astype(np.float32), (1, 4))
    cst[0:64, 384:640] = np.tile((s >= t).astype(np.float32), (1, 4))
    sm = np.ones((512,), np.float32)
    sm[0::64] = 0.0
    cst[:, 640:1152] = sm[None, :]
    lb = np.ascontiguousarray(np.transpose(lb_d.reshape(2, 2, 4, 128), (3, 0, 1, 2))).reshape(128, 16)
    return {
        "tab0": _tables_l0(np.asarray(t5_table, np.float32)),
        "tabn": _tables_na(np.asarray(rpb_c, np.float32)),
        "sink": np.ascontiguousarray(np.broadcast_to(np.asarray(sink_a, np.float32).reshape(1, 8), (128, 8))),
        "lb": np.ascontiguousarray(lb.astype(np.float32)),
        "gnorm": np.ascontiguousarray(np.broadcast_to(np.asarray(gnorm_d, np.float32).reshape(1, 512), (128, 512))),
        "lng": np.ascontiguousarray(np.broadcast_to(np.asarray(ln_g, np.float32).reshape(1, 2048), (128, 2048))),
        "lnb": np.ascontiguousarray(np.broadcast_to(np.asarray(ln_b, np.float32).reshape(1, 2048), (128, 2048))),
        "cst": cst,
    }


N_CORES = 8
_NC_CACHE = {}


def kernel(x_prompt, x_sample, t5_table, w_in_even, sink_a, w_out_even, w_in_odd, rpb_c, lb_d, gnorm_d, w_out_odd, ln_g, ln_b):
    x_prompt = np.asarray(x_prompt, np.float32)
    x_sample = np.asarray(x_sample, np.float32)
    nb, T, D = x_prompt.shape
    ns, Ts, _ = x_sample.shape
    per = nb // N_CORES
    seq_lens = [T] * per + [Ts]
    key = tuple(seq_lens)
    if key not in _NC_CACHE:
        _NC_CACHE[key] = build(seq_lens)
    nc = _NC_CACHE[key]
    consts = host_consts(np.asarray(t5_table), np.asarray(sink_a), np.asarray(rpb_c), np.asarray(lb_d), np.asarray(gnorm_d), np.asarray(ln_g), np.asarray(ln_b))
    shared = dict(consts)
    shared["w_in_even"] = np.ascontiguousarray(np.asarray(w_in_even, np.float32)[0])
    shared["w_out_even"] = np.ascontiguousarray(np.asarray(w_out_even, np.float32)[0])
    shared["w_in_odd"] = np.ascontiguousarray(np.asarray(w_in_odd, np.float32)[0])
    shared["w_out_odd"] = np.ascontiguousarray(np.asarray(w_out_odd, np.float32)[0])
    in_maps = []
    for c in range(N_CORES):
        xs = [x_prompt[c * per + i] for i in range(per)] + [x_sample[c % ns]]
        m = dict(shared)
        m["x"] = np.ascontiguousarray(np.concatenate(xs, axis=0))
        in_maps.append(m)
    res = run_bass_kernel_spmd(nc, in_maps, core_ids=list(range(N_CORES)))
    y_prompt = np.empty((nb, T, D), np.float32)
    y_sample = np.empty((ns, Ts, D), np.float32)
    for c in range(N_CORES):
        y = np.asarray(res.results[c]["y"], np.float32)
        for i in range(per):
            y_prompt[c * per + i] = y[i * T:(i + 1) * T]
        if c < ns:
            y_sample[c] = y[per * T:per * T + Ts]
    return (y_prompt, y_sample)
```

```python
import math
from contextlib import ExitStack

import numpy as np
import concourse.bass as bass
import concourse.mybir as mybir
from concourse.bass_utils import run_bass_kernel_spmd

F32 = mybir.dt.float32
BF16 = mybir.dt.bfloat16
ALU = mybir.AluOpType
AF = mybir.ActivationFunctionType
AX = mybir.AxisListType

D_MODEL = 1024
DEPTH = 2
ALPHA = (2.0 * DEPTH) ** 0.25
LN_EPS = 1e-5
RMS_EPS = 1e-6
NEG = -30000.0

QA, KA, VA, QB, KB, VB, GA, GB, EVEN_IN = 0, 512, 640, 768, 1536, 2304, 3072, 3584, 3840
QC, KC, VC, QD, IDD, ZF, ZB, GC, GD, ODD_IN = 0, 512, 1024, 1536, 2048, 2560, 3072, 3584, 4096, 4608

H0_VA, H0_VB, H0_GA, H0_GB, H0_ROW = 0, 130, 910, 1422, 1680
H1_VC, H1_ID, H1_GC, H1_GD, H1_ROW = 0, 520, 1032, 1544, 2056
OA_ROW = 1300


class Op:
    __slots__ = ("eng", "fn", "deps", "dma_sem", "need_inc", "val")

    def __init__(self, eng, fn):
        self.eng = eng
        self.fn = fn
        self.deps = []
        self.dma_sem = None
        self.need_inc = False
        self.val = 0


class Sched:
    ENGS = ("pe", "act", "dve", "pool", "sp")
    SAME_ENG_RAW = ("act", "dve", "pool")

    def __init__(self, nc, stack):
        self.nc = nc
        self.stack = stack
        self.ops = {e: [] for e in self.ENGS}
        self.sem = {e: stack.enter_context(nc.semaphore("sem_" + e)) for e in self.ENGS}
        self.lastw = {}
        self.readers = {}
        self.dsem = {}
        self.dcnt = {}
        self.key2idx = {}
        self.next_idx = {}

    def op(self, eng, fn, reads=(), writes=(), dma=None):
        o = Op(eng, fn)
        deps = []
        for k in reads:
            w = self.lastw.get(k)
            if w is not None:
                deps.append((w, True))
        for k in writes:
            w = self.lastw.get(k)
            if w is not None:
                deps.append((w, False))
            for r in self.readers.get(k, ()):
                deps.append((r, False))
        for (d, raw) in deps:
            if d.dma_sem is not None:
                o.deps.append(("dma", d.dma_sem, self.dcnt[d.dma_sem]))
                continue
            if d.eng == eng and dma is None:
                if eng not in self.SAME_ENG_RAW:
                    continue
            d.need_inc = True
            o.deps.append(("op", d))
        if dma is not None:
            if (eng, dma) not in self.key2idx:
                n_ = self.next_idx.get(eng, 0)
                self.next_idx[eng] = n_ + 1
                idx = (eng, n_)
                self.key2idx[(eng, dma)] = idx
                if idx not in self.dsem:
                    self.dsem[idx] = self.stack.enter_context(self.nc.semaphore("dsem_%s%d" % idx))
                    self.dcnt[idx] = 0
            idx = self.key2idx[(eng, dma)]
            self.dcnt[idx] += 16
            o.dma_sem = idx
        for k in reads:
            lst = self.readers.setdefault(k, [])
            if dma is None:
                for idx in range(len(lst)):
                    if lst[idx].dma_sem is None and lst[idx].eng == eng:
                        lst[idx] = o
                        break
                else:
                    lst.append(o)
            else:
                lst.append(o)
        for k in writes:
            self.lastw[k] = o
            self.readers[k] = []
        self.ops[eng].append(o)
        return o

    def I(self, eng, name, *a, reads=(), writes=(), dma=None, **kw):
        return self.op(eng, (name, a, kw), reads=reads, writes=writes, dma=dma)

    def barrier(self):
        lasts = {}
        for e in self.ENGS:
            for o in reversed(self.ops[e]):
                if o.fn is not None and o.dma_sem is None:
                    lasts[e] = o
                    break
        for e in self.ENGS:
            o = Op(e, None)
            for f, l in lasts.items():
                if f != e:
                    l.need_inc = True
                    o.deps.append(("op", l))
            for k in self.dsem:
                o.deps.append(("dma", k, self.dcnt[k]))
            self.ops[e].append(o)
        self.lastw = {}
        self.readers = {}
        self.key2idx = {}
        self.next_idx = {}

    def emit(self, block):
        for e in self.ENGS:
            c = 0
            for o in self.ops[e]:
                if o.need_inc:
                    c += 1
                    o.val = c
        sem = self.sem
        dsem = self.dsem

        def run(engname, engine):
            seen = {}
            for o in self.ops[engname]:
                need = {}
                for d in o.deps:
                    if d[0] == "dma":
                        k = ("d", d[1])
                        v = d[2]
                        s = dsem[d[1]]
                    else:
                        k = ("e", d[1].eng)
                        v = d[1].val
                        s = sem[d[1].eng]
                    if v > seen.get(k, 0) and v > need.get(k, (0, None))[0]:
                        need[k] = (v, s)
                for k, (v, s) in need.items():
                    engine.wait_ge(s, v)
                    seen[k] = v
                if o.fn is None:
                    continue
                name, a, kw = o.fn
                ins = getattr(engine, name)(*a, **kw)
                if o.dma_sem is not None:
                    ins.then_inc(dsem[o.dma_sem], 16)
                elif o.need_inc:
                    ins.then_inc(sem[engname], 1)

        block.tensor(lambda e: run("pe", e))
        block.scalar(lambda e: run("act", e))
        block.vector(lambda e: run("dve", e))
        block.gpsimd(lambda e: run("pool", e))
        block.sync(lambda e: run("sp", e))


class Ring:
    def __init__(self, nc, st, name, n, shape, dtype, psum=False):
        self.name = name
        self.n = n
        self.t = []
        for i in range(n):
            if psum:
                self.t.append(st.enter_context(nc.psum_tensor("%s%d" % (name, i), shape, dtype)))
            else:
                self.t.append(st.enter_context(nc.sbuf_tensor("%s%d" % (name, i), shape, dtype)))
        self.i = 0

    def next(self):
        i = self.i % self.n
        self.i += 1
        return self.t[i], (self.name, i)

    def at(self, i):
        i = i % self.n
        return self.t[i], (self.name, i)

    def can_next(self):
        if not hasattr(self, "busy"):
            self.busy = [False] * self.n
        return not self.busy[self.i % self.n]

    def next_hold(self):
        if not hasattr(self, "busy"):
            self.busy = [False] * self.n
        i = self.i % self.n
        assert not self.busy[i], self.name
        self.busy[i] = True
        self.i += 1
        return self.t[i], (self.name, i), i

    def release(self, i):
        self.busy[i] = False


def _t5_buckets(rel):
    nb = 16
    ret = (rel > 0).astype(np.int64) * nb
    n = np.abs(rel)
    max_exact = nb // 2
    large = max_exact + (np.log(np.maximum(n, 1) / max_exact) / math.log(1024 / max_exact) * (nb - max_exact)).astype(np.int64)
    large = np.minimum(large, nb - 1)
    return ret + np.where(n < max_exact, n, large)


def _tables_l0(t5_table):
    k = np.arange(128)[:, None, None]
    j = np.arange(3)[None, :, None]
    q = np.arange(128)[None, None, :]
    rel = (j - 1) * 128 + k - q
    out = np.full((5, 128, 3, 4, 128), NEG, np.float32)
    specs = [(128, 1, [0, 1, 2, 3]), (128, 1, [4, 5, 6, 7]), (64, 1, [8, 9, 10, 11]), (64, 4, [12, 13, 14, 15]), (64, 16, [16, 17, 18, 19])]
    for s, (hw, d, heads) in enumerate(specs):
        valid = np.abs(rel) <= hw
        bk = _t5_buckets(rel * d)
        for hi, h in enumerate(heads):
            pos = ((hi % 2) * 2 + hi // 2) if s < 2 else hi
            out[s, :, :, pos, :] = np.where(valid, t5_table[bk, h], NEG)
    return out.reshape(5, 128, 1536)


def _tables_na(rpb):
    rpb = rpb.reshape(8, 15, 31)
    k = np.arange(128)[:, None]
    q = np.arange(128)[None, :]
    krl, kc = k // 64, k % 64
    qrl, qc = q // 64, q % 64
    sc = np.clip(qc - 8, 0, 48)
    colmask = (kc >= sc) & (kc < sc + 16)
    colidx = np.clip(kc - qc + 15, 0, 30)
    out = np.full((4, 128, 9, 2, 128), NEG, np.float32)
    for b in range(9):
        off = b - 3 if b < 7 else (-2 if b == 7 else 2)
        dr = 2 * off + krl - qrl
        valid = colmask & (np.abs(dr) <= 7)
        if b >= 7:
            valid = valid & (dr >= -4) & (dr <= 3)
        dri = np.clip(dr + 7, 0, 14)
        for h in range(8):
            vals = rpb[h][dri, colidx]
            out[h // 2, :, b, h % 2, :] = np.where(valid, vals, NEG)
    return out.reshape(4, 128, 9 * 256)


def na_neighbors(u, U):
    if U <= 4:
        return [(m, m - u + 3) for m in range(U)]
    if u == 0 or u == 1:
        return [(m, m - u + 3) for m in range(0, 4)]
    if u == U - 1 or u == U - 2:
        return [(m, m - u + 3) for m in range(U - 4, U)]
    res = []
    for off in (-2, -1, 0, 1, 2):
        m = u + off
        blk = 7 if off == -2 else (8 if off == 2 else off + 3)
        res.append((m, blk))
    return res


def build(seq_lens, debug=False, stop_after=99, lvl=9, skipA=False):
    nc = bass.Bass("TRN2", target_bir_lowering=False)
    TT = sum(seq_lens)
    seq_off = [sum(seq_lens[:i]) for i in range(len(seq_lens))]
    okind = "ExternalOutput" if debug else "Internal"

    def dram(name, shape, dt, kind):
        return nc.dram_tensor(name, shape, dt, kind=kind).ap()

    x_in = dram("x", [TT, D_MODEL], F32, "ExternalInput")
    y_out = dram("y", [TT, D_MODEL], F32, "ExternalOutput")
    w_in_even = dram("w_in_even", [D_MODEL, EVEN_IN], F32, "ExternalInput")
    w_out_even = dram("w_out_even", [768, D_MODEL], F32, "ExternalInput")
    w_in_odd = dram("w_in_odd", [D_MODEL, ODD_IN], F32, "ExternalInput")
    w_out_odd = dram("w_out_odd", [1024, D_MODEL], F32, "ExternalInput")
    tab0_in = dram("tab0", [5, 128, 1536], F32, "ExternalInput")
    tabn_in = dram("tabn", [4, 128, 2304], F32, "ExternalInput")
    sink_in = dram("sink", [128, 8], F32, "ExternalInput")
    lb_in = dram("lb", [128, 16], F32, "ExternalInput")
    gn_in = dram("gnorm", [128, 512], F32, "ExternalInput")
    lng_in = dram("lng", [128, 2048], F32, "ExternalInput")
    lnb_in = dram("lnb", [128, 2048], F32, "ExternalInput")
    cst_in = dram("cst", [128, 1280], F32, "ExternalInput")

    h0T = dram("h0T", [2176, TT], BF16, okind)
    h0tm = dram("h0tm", [TT, H0_ROW], BF16, okind)
    oatt = dram("oatt", [TT, OA_ROW], F32, okind)
    x1s = dram("x1s", [TT, D_MODEL], F32, okind)
    h1Tb = dram("h1Tb", [1024, TT], BF16, okind)
    h1Tf = dram("h1Tf", [1536, TT], F32, okind)
    h1tm = dram("h1tm", [TT, H1_ROW], BF16, okind)
    ocs = dram("ocs", [TT, 520], F32, okind)
    odf = dram("odf", [TT, 512], F32, okind)
    odb = dram("odb", [TT, 512], F32, okind)

    with ExitStack() as top:
        S = Sched(nc, top)
        sb = lambda st, name, shape, dt: st.enter_context(nc.sbuf_tensor(name, shape, dt))
        cyc = {"i": 0}

        def rr(engs):
            cyc["i"] += 1
            return engs[cyc["i"] % len(engs)]

        def copy_op(eng, out, in_, reads, writes):
            if eng == "act":
                S.I("act", "copy", out=out, in_=in_, reads=reads, writes=writes)
            else:
                S.I(eng, "tensor_copy", out=out, in_=in_, reads=reads, writes=writes)

        cstf = sb(top, "cstf", [128, 1280], F32)
        ident = sb(top, "ident", [128, 128], BF16)
        S.I("sp", "dma_start", out=cstf[:], in_=cst_in, writes=["cstf"], dma="cstf")
        S.I("dve", "tensor_copy", out=ident[:], in_=cstf[:, 0:128], reads=["cstf"], writes=["ident"])
        trilf = cstf[0:64, 128:384]
        trilb = cstf[0:64, 384:640]
        scanmask = cstf[:, 640:1152]

        def load_weight(dst, src, KCn, F, stage_ring, pieces=None):
            if pieces is None:
                pieces = [(c0, min(1024, F - c0), c0) for c0 in range(0, F, 1024)]
            for kc in range(KCn):
                for (c0, cw, d0) in pieces:
                    stg, sk = stage_ring.next()
                    S.I("sp", "dma_start", out=stg[:, 0:cw], in_=src[kc * 128:(kc + 1) * 128, c0:c0 + cw], writes=[sk], dma=sk)
                    copy_op(rr(["act", "dve"]), dst[:, kc, d0:d0 + cw], stg[:, 0:cw], [sk], [("w", id(dst), kc, d0)])

        def make_xT(xs, xs_k, xb_ring, pT_ring, xT, xT_k, tcol, cast_e=("dve", "act"), copy_e=("act", "dve")):
            xb, xb_k = xb_ring.next()
            copy_op(rr(list(cast_e)), xb[:], xs[:], [xs_k], [xb_k])
            pT, pT_k = pT_ring.next()
            for c in range(8):
                S.I("pe", "transpose", out=pT[:, c * 128:(c + 1) * 128], in_=xb[:, c * 128:(c + 1) * 128], identity=ident[:],
                    reads=[xb_k, "ident"], writes=[pT_k])
            copy_op(rr(list(copy_e)), xT[:, :, tcol:tcol + 128], pT[:].rearrange("p (c t) -> p c t", c=8), [pT_k], [xT_k])

        def project(xT, xT_k, wb, acc_r, fm_r, fm_list, tm_r, tm_groups, t0, tm_dst, ntt=4, tts=None, ev=("act", "dve")):
            ncols = ntt * 128
            for (f0, dst, ring) in fm_list:
                acc, acc_k = acc_r.next()
                for c in range(8):
                    S.I("pe", "matmul", acc[:, 0:ncols], lhsT=wb[:, c, f0:f0 + 128], rhs=xT[:, c, 0:ncols], start=(c == 0), stop=(c == 7),
                        reads=[xT_k], writes=[acc_k])
                fm, fm_k = ring.next()
                copy_op(rr(list(ev)), fm[:, 0:ncols], acc[:, 0:ncols], [acc_k], [fm_k])
                S.I("pool", "dma_start", out=dst, in_=fm[:, 0:ncols], reads=[fm_k], dma=fm_k)
            for tt in (range(ntt) if tts is None else tts):
                tm, tm_k = tm_r.next()
                for grp in tm_groups:
                    if len(grp) == 5:
                        wc, width, sc, nh, func = grp
                        plist = [(0, width, sc, nh, func)]
                    else:
                        wc, width, plist = grp
                    acc, acc_k = acc_r.next()
                    for c in range(8):
                        S.I("pe", "matmul", acc[:, 0:width], lhsT=xT[:, c, tt * 128:(tt + 1) * 128], rhs=wb[:, c, wc:wc + width], start=(c == 0), stop=(c == 7),
                            reads=[xT_k], writes=[acc_k])
                    for (o, w, sc, nh, func) in plist:
                        if nh > 0:
                            copy_op(rr(list(ev)), tm[:, sc:sc + nh * 65].rearrange("p (h d) -> p h d", d=65)[:, :, 0:64],
                                    acc[:, o:o + w].rearrange("p (h d) -> p h d", d=64), [acc_k], [tm_k])
                        elif func is not None:
                            S.I("act", "activation", out=tm[:, sc:sc + w], in_=acc[:, o:o + w], func=func, reads=[acc_k], writes=[tm_k])
                        else:
                            copy_op(rr(list(ev)), tm[:, sc:sc + w], acc[:, o:o + w], [acc_k], [tm_k])
                r0 = t0 + tt * 128
                S.I("pool", "dma_start", out=tm_dst[r0:r0 + 128, :], in_=tm[:], reads=[tm_k], dma=tm_k)

        with ExitStack() as st:
            wb = sb(st, "wb0", [128, 8, EVEN_IN], BF16)
            stage = Ring(nc, st, "wstg", 3, [128, 1024], F32)
            load_weight(wb, w_in_even, 8, EVEN_IN, stage,
                        pieces=[(0, 640, 0), (768, 1024, 640), (1792, 512, 1664), (640, 128, 2176), (2304, 1024, 2304), (3328, 512, 3328)])
            xs_r = Ring(nc, st, "xs", 4, [128, 1024], F32)
            xb_r = Ring(nc, st, "xb", 2, [128, 1024], BF16)
            pT_r = Ring(nc, st, "pT", 2, [128, 1024], BF16, psum=True)
            xT_r = Ring(nc, st, "xT", 2, [128, 8, 512], BF16)
            acc_r = Ring(nc, st, "acc", 4, [128, 512], F32, psum=True)
            fm_r = Ring(nc, st, "fm", 4, [128, 512], BF16)
            tm_r = Ring(nc, st, "tm", 2, [128, H0_ROW], BF16)
            for i in range(2):
                t, k = tm_r.at(i)
                S.I("pool", "memset", t[:], 1.0, writes=[k])
            fm_feats = [i * 128 for i in range(17)]
            tm_groups = [
                (2176, 512, [(0, 128, H0_VA, 2, None), (128, 384, H0_VB, 6, None)]),
                (2688, 512, [(0, 384, H0_VB + 6 * 65, 6, None), (384, 128, H0_GA, 0, AF.Silu)]),
                (3200, 512, [(0, 384, H0_GA + 128, 0, AF.Silu), (384, 128, H0_GB, 0, AF.Silu)]),
                (3712, 128, [(0, 128, H0_GB + 128, 0, AF.Silu)]),
            ]
            for b in range(TT // 512):
                t0 = b * 512
                xT, xT_k = xT_r.next()
                for tt in range(4):
                    xs, xs_k = xs_r.next()
                    r0 = t0 + tt * 128
                    S.I("sp", "dma_start", out=xs[:], in_=x_in[r0:r0 + 128, :], writes=[xs_k], dma=xs_k)
                    make_xT(xs, xs_k, xb_r, pT_r, xT, xT_k, tt * 128, cast_e=("dve",), copy_e=("act",))
                fm_list = [(f0, h0T[fi * 128:(fi + 1) * 128, t0:t0 + 512], fm_r) for fi, f0 in enumerate(fm_feats)]
                project(xT, xT_k, wb, acc_r, fm_r, fm_list, tm_r, tm_groups, t0, h0tm, ev=("act",))
        S.barrier()

        h0tm_h = h0tm.tensor
        oatt_h = oatt.tensor

        def attn_group(st, E0, gname, d, quadsets, qrow, krow, vcol, VW, ocol, is_a):
            W = max(512, 128 * d)
            tpw = W // (128 * d)
            CQ = 4 if is_a else 2
            nring = 4 if d < 16 else 3
            Qr = Ring(nc, st, gname + "Q", 2, [128, CQ, W], BF16)
            Kr = Ring(nc, st, gname + "K", nring, [128, 4, W], BF16)
            Vr = Ring(nc, st, gname + "V", nring, [128, tpw * d, VW], BF16)
            psr = Ring(nc, st, gname + "ps", 2, [128, 1536], F32, psum=True)
            por = Ring(nc, st, gname + "po", 2, [128, 512], F32, psum=True)
            esr = Ring(nc, st, gname + "es", 2, [128, 1536], BF16)
            ptr = Ring(nc, st, gname + "pt", 2, [128, 1536], BF16)
            osr = Ring(nc, st, gname + "os", 3, [128, 260], F32)
            for i in range(nring):
                Kw, Kk = Kr.at(i)
                S.I("pool", "memset", Kw[:], 0.0, writes=[Kk])
            for si, T in enumerate(seq_lens):
                so = seq_off[si]
                NW = T // W
                NT = T // (128 * d)

                def load_kv(wn):
                    Kw, Kk = Kr.at(wn)
                    Vw, Vk = Vr.at(wn)
                    c0 = so + wn * W
                    for v in range(4):
                        if is_a:
                            g, half = v // 2, v % 2
                            src = h0T[krow + g * 64:krow + (g + 1) * 64, c0:c0 + W]
                        else:
                            half = v % 2
                            src = h0T[krow + v * 64:krow + (v + 1) * 64, c0:c0 + W]
                        S.I("sp", "dma_start", out=Kw[half * 64:(half + 1) * 64, v, :], in_=src, writes=[Kk], dma=Kk)
                    if d == 1:
                        src = bass.AP(h0tm_h, c0 * H0_ROW + vcol, [[H0_ROW, 128], [128 * H0_ROW, tpw], [1, VW]])
                    else:
                        src = bass.AP(h0tm_h, c0 * H0_ROW + vcol, [[d * H0_ROW, 128], [H0_ROW, d], [1, VW]])
                    S.I("sp", "dma_start", out=Vw[:], in_=src, writes=[Vk], dma=Vk)

                def stage2(n, r, js, qi, pt, ptk, tt, so=so, NT=NT):
                    po, pok = por.next()
                    for hh in range(4):
                        for ji, j in enumerate(js):
                            n2 = n + j - 1
                            Vw, Vk = Vr.at(n2 // tpw)
                            t2 = n2 % tpw
                            vi = t2 if d == 1 else r
                            vc0 = (qi * 65) if is_a else hh * 65
                            pos = ((hh % 2) * 2 + hh // 2) if is_a else hh
                            o0 = (j * 4 + pos) * 128
                            S.I("pe", "matmul", po[:, hh * 65:hh * 65 + 65], lhsT=pt[:, o0:o0 + 128], rhs=Vw[:, vi, vc0:vc0 + 65], start=(ji == 0), stop=(ji == len(js) - 1),
                                reads=[ptk, Vk], writes=[pok])
                    osb, osk = osr.next()
                    copy_op("dve", osb[:], po[:, 0:260], [pok], [osk])
                    row0 = so + n * 128 * d + r
                    dst = bass.AP(oatt_h, row0 * OA_ROW + ocol + (qi * 260 if is_a else 0), [[d * OA_ROW, 128], [1, 260]])
                    S.I("pool", "dma_start", out=dst, in_=osb[:], reads=[osk], dma=osk)

                load_kv(0)
                for wn in range(NW):
                    Qw, Qk = Qr.at(wn)
                    c0 = so + wn * W
                    S.I("sp", "dma_start", out=Qw[:], in_=h0T[qrow:qrow + CQ * 128, c0:c0 + W].rearrange("(c p) t -> p c t", p=128), writes=[Qk], dma=Qk)
                    if wn + 1 < NW:
                        load_kv(wn + 1)
                    pending = None
                    for tt in range(tpw):
                        for r in range(d):
                            n = wn * tpw + tt
                            js = [j for j in range(3) if 0 <= n + j - 1 < NT]
                            qcols = slice(tt * 128 * d + r, tt * 128 * d + r + 127 * d + 1, d)
                            for qi, qs in enumerate(quadsets):
                                ps, psk = psr.next()
                                for j in js:
                                    n2 = n + j - 1
                                    Kw, Kk = Kr.at(n2 // tpw)
                                    t2 = n2 % tpw
                                    kcols = slice(t2 * 128 * d + r, t2 * 128 * d + r + 127 * d + 1, d)
                                    if is_a:
                                        for half in range(2):
                                            out_ap = ps[:, (j * 4 + half * 2) * 128:(j * 4 + half * 2 + 2) * 128]
                                            S.I("pe", "matmul", out_ap, lhsT=Kw[:, qi * 2 + half, kcols], rhs=Qw[:, qi * 2:qi * 2 + 2, qcols], start=True, stop=True,
                                                reads=[Kk, Qk], writes=[psk])
                                    else:
                                        for hh in range(4):
                                            o0 = (j * 4 + hh) * 128
                                            S.I("pe", "matmul", ps[:, o0:o0 + 128], lhsT=Kw[:, hh, kcols], rhs=Qw[:, hh // 2, qcols], start=True, stop=True, reads=[Kk, Qk], writes=[psk])
                                es, esk = esr.next()
                                pt, ptk = ptr.next()
                                for j in js:
                                    S.I("act", "activation", out=es[:, j * 512:(j + 1) * 512], in_=ps[:, j * 512:(j + 1) * 512], func=AF.Exp, scale=0.125,
                                        reads=[psk], writes=[esk])
                                for j in js:
                                    S.I("dve", "tensor_tensor", out=pt[:, j * 512:(j + 1) * 512], in0=es[:, j * 512:(j + 1) * 512], in1=E0[:, qs, j * 512:(j + 1) * 512], op=ALU.mult,
                                        reads=[esk, "E0"], writes=[ptk])
                                unit = (n, r, js, qi, pt, ptk, tt)
                                if pending is not None:
                                    stage2(*pending)
                                pending = unit
                    if pending is not None:
                        stage2(*pending)
                        pending = None

        if stop_after >= 2:
            with ExitStack() as st:
                E0 = sb(st, "E0", [128, 5, 1536], BF16)
                tstage = Ring(nc, st, "tstg", 2, [128, 1536], F32)
                for qs in range(5):
                    stg, sk = tstage.next()
                    S.I("sp", "dma_start", out=stg[:], in_=tab0_in[qs], writes=[sk], dma=sk)
                    S.I("act", "activation", out=E0[:, qs, :], in_=stg[:], func=AF.Exp, reads=[sk], writes=["E0"])
                with ExitStack() as st2:
                    if not skipA:
                        attn_group(st2, E0, "A", 1, [0, 1], 0, 512, H0_VA, 130, 0, True)
                S.barrier()
                for gi, d in enumerate((1, 4, 16)):
                    if stop_after < 3 + gi:
                        continue
                    with ExitStack() as st2:
                        attn_group(st2, E0, "B%d" % gi, d, [2 + gi], 640 + gi * 256, 1408 + gi * 256, H0_VB + gi * 260, 260, 520 + gi * 260, False)
                    S.barrier()

        def run_skewed(gens):
            active = []
            it = iter(gens)
            done = False
            while True:
                if not done:
                    try:
                        active.append(next(it))
                    except StopIteration:
                        done = True
                if done and not active:
                    break
                for g in list(active):
                    try:
                        next(g)
                    except StopIteration:
                        active.remove(g)

        def layer_norm_gen(zps, zk, xres, xres_k, lng, lnb, v_r, st_r, junk, x1_r, dst):
            v, vk = v_r.next()
            for hf in range(2):
                S.I("dve", "scalar_tensor_tensor", out=v[:, hf * 512:(hf + 1) * 512], in0=xres[:, hf * 512:(hf + 1) * 512], scalar=ALPHA, in1=zps[hf][:],
                    op0=ALU.mult, op1=ALU.add, reads=[xres_k, zk[hf]], writes=[vk])
            stt, stk = st_r.next()
            S.I("dve", "tensor_reduce", out=stt[:, 0:1], in_=v[:], axis=AX.X, op=ALU.add, reads=[vk], writes=[(stk, 0)])
            S.I("act", "activation", out=junk[:], in_=v[:], func=AF.Square, accum_out=stt[:, 1:2], reads=[vk], writes=["junk", (stk, 1)])
            yield
            S.I("dve", "tensor_scalar", out=stt[:, 2:3], in0=stt[:, 0:1], scalar1=1.0 / 1024, scalar2=None, op0=ALU.mult, reads=[(stk, 0)], writes=[(stk, 2)])
            S.I("dve", "tensor_tensor", out=stt[:, 3:4], in0=stt[:, 2:3], in1=stt[:, 2:3], op=ALU.mult, reads=[(stk, 2)], writes=[(stk, 3)])
            yield
            S.I("dve", "scalar_tensor_tensor", out=stt[:, 4:5], in0=stt[:, 1:2], scalar=1.0 / 1024, in1=stt[:, 3:4], op0=ALU.mult, op1=ALU.subtract,
                reads=[(stk, 1), (stk, 3)], writes=[(stk, 4)])
            yield
            S.I("dve", "tensor_scalar", out=stt[:, 4:5], in0=stt[:, 4:5], scalar1=LN_EPS, scalar2=None, op0=ALU.add, reads=[(stk, 4)], writes=[(stk, 4)])
            S.I("act", "activation", out=stt[:, 6:7], in_=stt[:, 4:5], func=AF.Sqrt, reads=[(stk, 4)], writes=[(stk, 6)])
            yield
            S.I("dve", "reciprocal", out=stt[:, 5:6], in_=stt[:, 6:7], reads=[(stk, 6)], writes=[(stk, 5)])
            S.I("dve", "scalar_tensor_tensor", out=v[:], in0=v[:], scalar=stt[:, 2:3], in1=lng, op0=ALU.subtract, op1=ALU.mult,
                reads=[vk, (stk, 2), "ln"], writes=[vk])
            yield
            x1, x1k = x1_r.next()
            S.I("dve", "scalar_tensor_tensor", out=x1[:], in0=v[:], scalar=stt[:, 5:6], in1=lnb, op0=ALU.mult, op1=ALU.add,
                reads=[vk, (stk, 5), "ln"], writes=[x1k])
            S.I("pool", "dma_start", out=dst, in_=x1[:], reads=[x1k], dma=x1k)

        def layer_norm_tile(zps, zk, xres, xres_k, lng, lnb, v_r, st_r, junk, x1_r):
            v, vk = v_r.next()
            for hf in range(2):
                S.I("dve", "scalar_tensor_tensor", out=v[:, hf * 512:(hf + 1) * 512], in0=xres[:, hf * 512:(hf + 1) * 512], scalar=ALPHA, in1=zps[hf][:],
                    op0=ALU.mult, op1=ALU.add, reads=[xres_k, zk[hf]], writes=[vk])
            stt, stk = st_r.next()
            S.I("dve", "tensor_reduce", out=stt[:, 0:1], in_=v[:], axis=AX.X, op=ALU.add, reads=[vk], writes=[(stk, 0)])
            S.I("act", "activation", out=junk[:], in_=v[:], func=AF.Square, accum_out=stt[:, 1:2], reads=[vk], writes=["junk", (stk, 1)])
            S.I("dve", "tensor_scalar", out=stt[:, 2:3], in0=stt[:, 0:1], scalar1=1.0 / 1024, scalar2=None, op0=ALU.mult, reads=[(stk, 0)], writes=[(stk, 2)])
            S.I("dve", "tensor_tensor", out=stt[:, 3:4], in0=stt[:, 2:3], in1=stt[:, 2:3], op=ALU.mult, reads=[(stk, 2)], writes=[(stk, 3)])
            S.I("dve", "scalar_tensor_tensor", out=stt[:, 4:5], in0=stt[:, 1:2], scalar=1.0 / 1024, in1=stt[:, 3:4], op0=ALU.mult, op1=ALU.subtract,
                reads=[(stk, 1), (stk, 3)], writes=[(stk, 4)])
            S.I("dve", "tensor_scalar", out=stt[:, 4:5], in0=stt[:, 4:5], scalar1=LN_EPS, scalar2=None, op0=ALU.add, reads=[(stk, 4)], writes=[(stk, 4)])
            S.I("act", "activation", out=stt[:, 6:7], in_=stt[:, 4:5], func=AF.Sqrt, reads=[(stk, 4)], writes=[(stk, 6)])
            S.I("dve", "reciprocal", out=stt[:, 5:6], in_=stt[:, 6:7], reads=[(stk, 6)], writes=[(stk, 5)])
            S.I("dve", "scalar_tensor_tensor", out=v[:], in0=v[:], scalar=stt[:, 2:3], in1=lng, op0=ALU.subtract, op1=ALU.mult,
                reads=[vk, (stk, 2), "ln"], writes=[vk])
            x1, x1k = x1_r.next()
            S.I("dve", "scalar_tensor_tensor", out=x1[:], in0=v[:], scalar=stt[:, 5:6], in1=lnb, op0=ALU.mult, op1=ALU.add,
                reads=[vk, (stk, 5), "ln"], writes=[x1k])
            return x1, x1k

        if stop_after >= 6:
            with ExitStack() as st:
                wo = sb(st, "wo0", [128, 6, 1024], BF16)
                w1 = sb(st, "w1", [128, 8, ODD_IN], BF16)
                with ExitStack() as st2:
                    stage = Ring(nc, st2, "wstg1", 3, [128, 1024], F32)
                    load_weight(wo, w_out_even, 6, 1024, stage)
                    load_weight(w1, w_in_odd, 8, ODD_IN, stage)
                    S.barrier()
                lnt = sb(st, "lnt", [128, 2, 1024], F32)
                S.I("sp", "dma_start", out=lnt[:, 0, :], in_=lng_in[:, 0:1024], writes=["ln"], dma="ln")
                S.I("sp", "dma_start", out=lnt[:, 1, :], in_=lnb_in[:, 0:1024], writes=["ln"], dma="ln")
                esink = sb(st, "esink", [128, 8], F32)
                S.I("sp", "dma_start", out=esink[:], in_=sink_in, writes=["esink"], dma="esink")
                S.I("act", "activation", out=esink[:], in_=esink[:], func=AF.Exp, reads=["esink"], writes=["esink"])
                oa_r = Ring(nc, st, "oa", 3, [128, OA_ROW], F32)
                g_r = Ring(nc, st, "gt", 3, [128, 768], BF16)
                xs_r = Ring(nc, st, "xs2", 3, [128, 1024], F32)
                v_r = Ring(nc, st, "vln", 4, [128, 1024], F32)
                x1_r = Ring(nc, st, "x1t", 2, [128, 1024], F32)
                st_r = Ring(nc, st, "stt", 6, [128, 8], F32)
                sm_r = Ring(nc, st, "smt", 4, [128, 16], F32)
                tmp_r = Ring(nc, st, "tmpc", 2, [128, 768], F32)
                junk = sb(st, "junk", [128, 1024], BF16)
                y_r = Ring(nc, st, "yt", 2, [128, 768], BF16)
                yT_r = Ring(nc, st, "yT", 2, [128, 6, 128], BF16)
                xb_r = Ring(nc, st, "xb2", 2, [128, 1024], BF16)
                xT_r = Ring(nc, st, "xT2", 2, [128, 8, 512], BF16)
                fmb_r = Ring(nc, st, "fmb", 4, [128, 512], BF16)
                fmf_r = Ring(nc, st, "fmf", 4, [128, 512], F32)
                tm_r = Ring(nc, st, "tm1", 2, [128, H1_ROW], BF16)
                pT_r = Ring(nc, st, "pT2", 2, [128, 1024], BF16, psum=True)
                z_r = Ring(nc, st, "zps", 4, [128, 512], F32, psum=True)
                acc_r = Ring(nc, st, "acc2", 2, [128, 512], F32, psum=True)
                for i in range(2):
                    t, k = tm_r.at(i)
                    S.I("pool", "memset", t[:], 1.0, writes=[k])
                fm_cols = [(QC + c * 128, h1Tb, c, fmb_r) for c in range(4)] + [(KC + c * 128, h1Tb, 4 + c, fmb_r) for c in range(4)] + \
                          [(QD + c * 128, h1Tf, c, fmf_r) for c in range(4)] + [(ZF + c * 128, h1Tf, 4 + c, fmf_r) for c in range(4)] + \
                          [(ZB + c * 128, h1Tf, 8 + c, fmf_r) for c in range(4)]
                tm_groups1 = [(VC, 512, H1_VC, 8, None), (IDD, 512, H1_ID, 0, None), (GC, 512, H1_GC, 0, AF.Silu), (GD, 512, H1_GD, 0, AF.Silu)]
                NBLK = TT // 512
                tiles_done = [0] * NBLK
                proj_done = [False] * NBLK
                blk_xT = {}

                def acquire(ring):
                    while not ring.can_next():
                        yield
                    return ring.next_hold()

                def tile_gen2(b, tt):
                    t0 = b * 512
                    r0 = t0 + tt * 128
                    oa, oak, oai = yield from acquire(oa_r)
                    S.I("sp", "dma_start", out=oa[:], in_=oatt[r0:r0 + 128, :], writes=[oak], dma=oak)
                    gt, gk, gi_ = yield from acquire(g_r)
                    S.I("sp", "dma_start", out=gt[:], in_=h0tm[r0:r0 + 128, H0_GA:H0_GA + 768], writes=[gk], dma=gk)
                    xs, xsk, xsi = yield from acquire(xs_r)
                    S.I("sp", "dma_start", out=xs[:], in_=x_in[r0:r0 + 128, :], writes=[xsk], dma=xsk)
                    yield
                    sm, smk, smi = yield from acquire(sm_r)
                    oa3 = oa[:, 0:520].rearrange("p (h d) -> p h d", d=65)
                    S.I("dve", "tensor_tensor", out=sm[:, 0:8], in0=oa3[:, :, 64], in1=esink[:], op=ALU.add, reads=[oak, "esink"], writes=[(smk, 0)])
                    S.I("dve", "tensor_tensor", out=oa[:, 520:780], in0=oa[:, 520:780], in1=oa[:, 780:1040], op=ALU.add, reads=[oak], writes=[oak])
                    yield
                    S.I("dve", "reciprocal", out=sm[:, 0:8], in_=sm[:, 0:8], reads=[(smk, 0)], writes=[(smk, 0)])
                    S.I("dve", "tensor_tensor", out=oa[:, 520:780], in0=oa[:, 520:780], in1=oa[:, 1040:1300], op=ALU.add, reads=[oak], writes=[oak])
                    yield
                    tmp, tmpk = tmp_r.next()
                    y, yk = y_r.next()
                    ob3 = oa[:, 520:780].rearrange("p (h d) -> p h d", d=65)
                    S.I("dve", "tensor_tensor", out=tmp[:, 0:512].rearrange("p (h d) -> p h d", d=64), in0=oa3[:, :, 0:64],
                        in1=sm[:, 0:8].unsqueeze(2).broadcast_to([128, 8, 64]), op=ALU.mult, reads=[oak, (smk, 0)], writes=[(tmpk, 0)])
                    S.I("dve", "reciprocal", out=sm[:, 8:12], in_=ob3[:, :, 64], reads=[oak], writes=[(smk, 1)])
                    yield
                    S.I("dve", "tensor_tensor", out=tmp[:, 512:768].rearrange("p (h d) -> p h d", d=64), in0=ob3[:, :, 0:64],
                        in1=sm[:, 8:12].unsqueeze(2).broadcast_to([128, 4, 64]), op=ALU.mult, reads=[oak, (smk, 1)], writes=[(tmpk, 1)])
                    S.I("dve", "tensor_tensor", out=y[:], in0=tmp[:], in1=gt[:], op=ALU.mult, reads=[(tmpk, 0), (tmpk, 1), gk], writes=[yk])
                    oa_r.release(oai)
                    g_r.release(gi_)
                    sm_r.release(smi)
                    pT, pTk = pT_r.next()
                    for c in range(6):
                        S.I("pe", "transpose", out=pT[:, c * 128:(c + 1) * 128], in_=y[:, c * 128:(c + 1) * 128], identity=ident[:], reads=[yk, "ident"], writes=[pTk])
                    yT, yTk = yT_r.next()
                    copy_op("act", yT[:], pT[:, 0:768].rearrange("p (c t) -> p c t", c=6), [pTk], [yTk])
                    zps, zk, zis = [], [], []
                    for hf in range(2):
                        z, zkk, zi = yield from acquire(z_r)
                        for c in range(6):
                            S.I("pe", "matmul", z[:], lhsT=yT[:, c, :], rhs=wo[:, c, hf * 512:(hf + 1) * 512], start=(c == 0), stop=(c == 5), reads=[yTk], writes=[zkk])
                        zps.append(z)
                        zk.append(zkk)
                        zis.append(zi)
                    yield
                    v, vk, vi_ = yield from acquire(v_r)
                    for hf in range(2):
                        S.I("dve", "scalar_tensor_tensor", out=v[:, hf * 512:(hf + 1) * 512], in0=xs[:, hf * 512:(hf + 1) * 512], scalar=ALPHA, in1=zps[hf][:],
                            op0=ALU.mult, op1=ALU.add, reads=[xsk, zk[hf]], writes=[vk])
                    xs_r.release(xsi)
                    for zi in zis:
                        z_r.release(zi)
                    stt, stk, sti = yield from acquire(st_r)
                    S.I("dve", "tensor_reduce", out=stt[:, 0:1], in_=v[:], axis=AX.X, op=ALU.add, reads=[vk], writes=[(stk, 0)])
                    S.I("act", "activation", out=junk[:], in_=v[:], func=AF.Square, accum_out=stt[:, 1:2], reads=[vk], writes=["junk", (stk, 1)])
                    yield
                    S.I("dve", "tensor_scalar", out=stt[:, 2:3], in0=stt[:, 0:1], scalar1=1.0 / 1024, scalar2=None, op0=ALU.mult, reads=[(stk, 0)], writes=[(stk, 2)])
                    S.I("dve", "tensor_tensor", out=stt[:, 3:4], in0=stt[:, 2:3], in1=stt[:, 2:3], op=ALU.mult, reads=[(stk, 2)], writes=[(stk, 3)])
                    S.I("dve", "scalar_tensor_tensor", out=stt[:, 4:5], in0=stt[:, 1:2], scalar=1.0 / 1024, in1=stt[:, 3:4], op0=ALU.mult, op1=ALU.subtract,
                        reads=[(stk, 1), (stk, 3)], writes=[(stk, 4)])
                    S.I("dve", "tensor_scalar", out=stt[:, 4:5], in0=stt[:, 4:5], scalar1=LN_EPS, scalar2=None, op0=ALU.add, reads=[(stk, 4)], writes=[(stk, 4)])
                    S.I("act", "activation", out=stt[:, 6:7], in_=stt[:, 4:5], func=AF.Sqrt, reads=[(stk, 4)], writes=[(stk, 6)])
                    yield
                    S.I("dve", "reciprocal", out=stt[:, 5:6], in_=stt[:, 6:7], reads=[(stk, 6)], writes=[(stk, 5)])
                    S.I("dve", "scalar_tensor_tensor", out=v[:], in0=v[:], scalar=stt[:, 2:3], in1=lnt[:, 0, :], op0=ALU.subtract, op1=ALU.mult,
                        reads=[vk, (stk, 2), "ln"], writes=[vk])
                    yield
                    while b >= 2 and not proj_done[b - 2]:
                        yield
                    if b not in blk_xT:
                        blk_xT[b] = xT_r.next()
                    xT, xT_k = blk_xT[b]
                    x1, x1k = x1_r.next()
                    S.I("dve", "scalar_tensor_tensor", out=x1[:], in0=v[:], scalar=stt[:, 5:6], in1=lnt[:, 1, :], op0=ALU.mult, op1=ALU.add,
                        reads=[vk, (stk, 5), "ln"], writes=[x1k])
                    v_r.release(vi_)
                    st_r.release(sti)
                    S.I("pool", "dma_start", out=x1s[r0:r0 + 128, :], in_=x1[:], reads=[x1k], dma=x1k)
                    make_xT(x1, x1k, xb_r, pT_r, xT, xT_k, tt * 128, cast_e=("dve",), copy_e=("act",))
                    tiles_done[b] += 1

                def proj_gen(b):
                    while tiles_done[b] < 4:
                        yield
                    t0 = b * 512
                    xT, xT_k = blk_xT[b]
                    fm_list = [(f0, dst[ci * 128:(ci + 1) * 128, t0:t0 + 512], ring) for (f0, dst, ci, ring) in fm_cols]
                    for i0 in range(0, len(fm_list), 5):
                        project(xT, xT_k, w1, acc_r, None, fm_list[i0:i0 + 5], tm_r, tm_groups1, t0, h1tm, ntt=4, tts=[], ev=("act",))
                        yield
                    for tt in range(4):
                        project(xT, xT_k, w1, acc_r, None, [], tm_r, tm_groups1, t0, h1tm, ntt=4, tts=[tt], ev=("act",))
                        yield
                    proj_done[b] = True

                gens = []
                for b in range(NBLK):
                    for tt in range(4):
                        gens.append(tile_gen2(b, tt))
                    gens.append(proj_gen(b))
                run_skewed(gens)
            S.barrier()

        h1tm_h = h1tm.tensor
        if stop_after >= 7:
            with ExitStack() as st:
                EN = sb(st, "EN", [128, 4, 2304], BF16)
                with ExitStack() as st2:
                    tstage = Ring(nc, st2, "nstg", 2, [128, 2304], F32)
                    for hp in range(4):
                        stg, sk = tstage.next()
                        S.I("sp", "dma_start", out=stg[:], in_=tabn_in[hp], writes=[sk], dma=sk)
                        S.I("act", "activation", out=EN[:, hp, :], in_=stg[:], func=AF.Exp, reads=[sk], writes=["EN"])
                    S.barrier()
                W = 512
                Qr = Ring(nc, st, "NQ", 2, [128, 4, W], BF16)
                Kr = Ring(nc, st, "NK", 4, [128, 8, W], BF16)
                Vr = Ring(nc, st, "NV", 4, [128, 4, 520], BF16)
                psr = Ring(nc, st, "Nps", 2, [128, 1536], F32, psum=True)
                por = Ring(nc, st, "Npo", 1, [128, 1024], F32, psum=True)
                esr = Ring(nc, st, "Nes", 2, [128, 1280], BF16)
                ptr = Ring(nc, st, "Npt", 2, [128, 1280], BF16)
                osr = Ring(nc, st, "Nos", 2, [128, 520], F32)
                for i in range(4):
                    Kw, Kk = Kr.at(i)
                    S.I("pool", "memset", Kw[:], 0.0, writes=[Kk])
                for si, T in enumerate(seq_lens):
                    so = seq_off[si]
                    NW = T // W
                    U = T // 128

                    def load_kv_n(wn):
                        Kw, Kk = Kr.at(wn)
                        Vw, Vk = Vr.at(wn)
                        c0 = so + wn * W
                        for h in range(8):
                            half = h % 2
                            S.I("sp", "dma_start", out=Kw[half * 64:(half + 1) * 64, h, :], in_=h1Tb[512 + h * 64:512 + (h + 1) * 64, c0:c0 + W], writes=[Kk], dma=Kk)
                        src = bass.AP(h1tm_h, c0 * H1_ROW + H1_VC, [[H1_ROW, 128], [128 * H1_ROW, 4], [1, 520]])
                        S.I("sp", "dma_start", out=Vw[:], in_=src, writes=[Vk], dma=Vk)

                    def na_stage2(u, hp, neigh, pt, ptk, po, pok, so=so):
                        NJ = len(neigh)
                        for hl in range(2):
                            h = 2 * hp + hl
                            pc = (h // 4) * 512 + (h % 4) * 65
                            for j, (m, blk) in enumerate(neigh):
                                Vw, Vk = Vr.at(m // 4)
                                o0 = (j * 2 + hl) * 128
                                S.I("pe", "matmul", po[:, pc:pc + 65], lhsT=pt[:, o0:o0 + 128], rhs=Vw[:, m % 4, h * 65:(h + 1) * 65], start=(j == 0), stop=(j == NJ - 1),
                                    reads=[ptk, Vk], writes=[pok])
                        if hp == 3:
                            osb, osk = osr.next()
                            copy_op("act", osb[:, 0:260], po[:, 0:260], [pok], [osk])
                            copy_op("dve", osb[:, 260:520], po[:, 512:772], [pok], [osk])
                            r0 = so + u * 128
                            S.I("pool", "dma_start", out=ocs[r0:r0 + 128, :], in_=osb[:], reads=[osk], dma=osk)

                    load_kv_n(0)
                    for wn in range(NW):
                        Qw, Qk = Qr.at(wn)
                        c0 = so + wn * W
                        S.I("sp", "dma_start", out=Qw[:], in_=h1Tb[0:512, c0:c0 + W].rearrange("(c p) t -> p c t", p=128), writes=[Qk], dma=Qk)
                        if wn + 1 < NW:
                            load_kv_n(wn + 1)
                        pending = None
                        for tt in range(4):
                            u = wn * 4 + tt
                            neigh = na_neighbors(u, U)
                            NJ = len(neigh)
                            po, pok = por.next()
                            for hp in range(4):
                                ps, psk = psr.next()
                                for j, (m, blk) in enumerate(neigh):
                                    Kw, Kk = Kr.at(m // 4)
                                    for hl in range(2):
                                        h = 2 * hp + hl
                                        o0 = (j * 2 + hl) * 128
                                        S.I("pe", "matmul", ps[:, o0:o0 + 128], lhsT=Kw[:, h, (m % 4) * 128:(m % 4 + 1) * 128], rhs=Qw[:, hp, tt * 128:(tt + 1) * 128],
                                            start=True, stop=True, reads=[Kk, Qk], writes=[psk])
                                es, esk = esr.next()
                                pt, ptk = ptr.next()
                                ncol = NJ * 256
                                for c0_ in range(0, ncol, 512):
                                    c1_ = min(ncol, c0_ + 512)
                                    S.I("act", "activation", out=es[:, c0_:c1_], in_=ps[:, c0_:c1_], func=AF.Exp, scale=0.125, reads=[psk], writes=[esk])
                                for j, (m, blk) in enumerate(neigh):
                                    S.I("dve", "tensor_tensor", out=pt[:, j * 256:(j + 1) * 256], in0=es[:, j * 256:(j + 1) * 256], in1=EN[:, hp, blk * 256:(blk + 1) * 256], op=ALU.mult,
                                        reads=[esk, "EN"], writes=[ptk])
                                unit = (u, hp, neigh, pt, ptk, po, pok)
                                if pending is not None:
                                    na_stage2(*pending)
                                pending = unit
                        if pending is not None:
                            na_stage2(*pending)
                            pending = None
            S.barrier()

        if stop_after >= 8:
            with ExitStack() as st:
                lbt = sb(st, "lbt", [128, 16], F32)
                lbc = sb(st, "lbc", [128, 5, 8], F32)
                S.I("sp", "dma_start", out=lbt[:], in_=lb_in, writes=["lbt"], dma="lbt")
                S.I("act", "activation", out=lbt[:], in_=lbt[:], func=AF.Exp, reads=["lbt"], writes=["lbt"])
                lb4 = lbt[:].rearrange("p (d l h) -> p d l h", d=2, l=2)
                lbv = lambda i: lbc[:, i, :].rearrange("p (d h) -> p d h", d=2)
                S.I("dve", "tensor_tensor", out=lbv(3), in0=lb4[:, :, 0, :], in1=lb4[:, :, 1, :], op=ALU.add, reads=["lbt"], writes=["lbc3"])
                S.I("dve", "reciprocal", out=lbv(3), in_=lbv(3), reads=["lbc3"], writes=["lbc3"])
                S.I("dve", "tensor_tensor", out=lbv(0), in0=lb4[:, :, 1, :], in1=lbv(3), op=ALU.mult, reads=["lbt", "lbc3"], writes=["lbc0"])
                S.I("dve", "tensor_scalar", out=lbc[:, 1, :], in0=lbc[:, 0, :], scalar1=-1.0, scalar2=1.0, op0=ALU.mult, op1=ALU.add, reads=["lbc0"], writes=["lbc1"])
                S.I("dve", "tensor_scalar", out=lbc[:, 2, :], in0=lbc[:, 0, :], scalar1=1.0, scalar2=-1.0, op0=ALU.mult, op1=ALU.add, reads=["lbc0"], writes=["lbc2"])
                S.I("act", "activation", out=lbc[:, 4, :], in_=lbc[:, 1, :], func=AF.Ln, reads=["lbc1"], writes=["lbc4"])
                LBK = ["lbc0", "lbc1", "lbc2", "lbc4"]

                class Dir:
                    pass

                dirs = []
                for dr in range(2):
                    D = Dir()
                    D.dr = dr
                    nm = "hf" if dr == 0 else "hb"
                    D.z_r = Ring(nc, st, nm + "z", 2, [128, 512], F32)
                    D.q_r = Ring(nc, st, nm + "q", 2, [128, 512], F32)
                    D.t_r = [Ring(nc, st, nm + "t%d" % i, 2, [128, 512], F32) for i in range(6)]
                    D.qh = Ring(nc, st, nm + "qh", 2, [128, 4, 512], BF16)
                    D.qt = Ring(nc, st, nm + "qt", 2, [128, 4, 512], BF16)
                    D.kh = Ring(nc, st, nm + "kh", 2, [128, 4, 512], BF16)
                    D.khT = Ring(nc, st, nm + "khT", 2, [128, 4, 8, 128], BF16)
                    D.dec = Ring(nc, st, nm + "dec", 2, [128, 4, 8], F32)
                    D.v = Ring(nc, st, nm + "v", 2, [128, 8, 512], BF16)
                    D.S = sb(st, nm + "S", [128, 4, 128], F32)
                    D.Sbf = sb(st, nm + "Sbf", [128, 4, 128], BF16)
                    D.Am = Ring(nc, st, nm + "Am", 2, [128, 256], BF16)
                    D.ob = Ring(nc, st, nm + "ob", 2, [64, 512], F32)
                    D.psA = Ring(nc, st, nm + "psA", 1, [128, 512], F32, psum=True)
                    D.po = Ring(nc, st, nm + "po", 1, [128, 512], F32, psum=True)
                    D.psS = Ring(nc, st, nm + "psS", 1, [128, 512], F32, psum=True)
                    D.tril = trilf if dr == 0 else trilb
                    D.odst = odf if dr == 0 else odb
                    D.zrow = 512 if dr == 0 else 1024
                    for i in range(2):
                        for ring in (D.khT, D.v, D.Am):
                            t, k = ring.at(i)
                            S.I("pool", "memset", t[:], 0.0, writes=[k])
                    dirs.append(D)
                pTk_r = Ring(nc, st, "hpT", 1, [128, 1024], BF16, psum=True)

                def prep_begin(D, so, b):
                    c0 = so + b * 512
                    qh, qhk = D.qh.next()
                    qt, qtk = D.qt.next()
                    kh, khk = D.kh.next()
                    khT, khTk = D.khT.next()
                    dec, deck = D.dec.next()
                    v, vk = D.v.next()
                    S.I("sp", "dma_start", out=v[0:64, :, :], in_=bass.AP(h1tm_h, c0 * H1_ROW + H1_ID, [[H1_ROW, 64], [64 * H1_ROW, 8], [1, 512]]), writes=[vk], dma=vk)
                    return dict(qh=qh, qhk=[(qhk, h) for h in range(4)], qt=qt, qtk=[(qtk, h) for h in range(4)], kh=kh, khk=[(khk, h) for h in range(4)],
                                khT=khT, khTk=[(khTk, h) for h in range(4)], dec=dec, deck=[(deck, h) for h in range(4)], v=v, vk=vk, c0=c0)

                def prep_head(D, getP, h, c0):
                    dr = D.dr
                    z, zk = D.z_r.next()
                    q, qk = D.q_r.next()
                    S.I("sp", "dma_start", out=z[:], in_=h1Tf[D.zrow + h * 128:D.zrow + (h + 1) * 128, c0:c0 + 512], writes=[zk], dma=zk)
                    S.I("sp", "dma_start", out=q[:], in_=h1Tf[h * 128:(h + 1) * 128, c0:c0 + 512], writes=[qk], dma=qk)
                    T_ = [r_.next() for r_ in D.t_r]
                    (t0_, k0), (t1_, k1), (t2_, k2), (t3_, k3), (t4_, k4), (t5_, k5) = T_
                    lb_ap = lbc[:, 0, dr * 4 + h:dr * 4 + h + 1]
                    lnoml_ap = lbc[:, 4, dr * 4 + h:dr * 4 + h + 1]
                    v3 = lambda t: t[:].rearrange("p (c t) -> p c t", t=64)
                    S.I("act", "activation", out=t0_[:], in_=z[:], func=AF.Exp, scale=-1.0, reads=[zk], writes=[k0])
                    S.I("act", "activation", out=t1_[:], in_=t0_[:], func=AF.Ln, scale=1.0, bias=1.0, reads=[k0], writes=[k1])
                    S.I("act", "activation", out=t2_[:], in_=t0_[:], func=AF.Ln, scale=lb_ap, bias=1.0, reads=[k0] + LBK, writes=[k2])
                    yield
                    S.I("dve", "tensor_tensor", out=t2_[:], in0=t2_[:], in1=t1_[:], op=ALU.subtract, reads=[k2, k1], writes=[k2])
                    S.I("dve", "tensor_tensor_scan", out=t3_[:], data0=scanmask, data1=t2_[:], initial=0.0, op0=ALU.mult, op1=ALU.add, reads=[k2, "cstf"], writes=[k3])
                    tot = v3(t3_)[:, :, 63]
                    totb = v3(t3_)[:, :, 63:64].broadcast_to([128, 8, 64])
                    if dr == 0:
                        S.I("dve", "tensor_tensor", out=v3(t4_), in0=v3(t3_), in1=totb, op=ALU.subtract, reads=[k3], writes=[k4])
                        bq, bqk = t3_, k3
                    else:
                        S.I("dve", "tensor_tensor", out=t4_[:], in0=t2_[:], in1=t3_[:], op=ALU.subtract, reads=[k2, k3], writes=[k4])
                        S.I("dve", "tensor_tensor", out=v3(t5_), in0=v3(t4_), in1=totb, op=ALU.add, reads=[k4, k3], writes=[k5])
                        bq, bqk = t5_, k5
                    S.I("dve", "tensor_tensor", out=t1_[:], in0=t1_[:], in1=z[:], op=ALU.add, reads=[k1, zk], writes=[k1])
                    S.I("dve", "tensor_tensor", out=t1_[:], in0=t1_[:], in1=t4_[:], op=ALU.add, reads=[k1, k4], writes=[k1])
                    yield
                    P = getP()
                    qh, qt, kh, khT, dec = P["qh"], P["qt"], P["kh"], P["khT"], P["dec"]
                    qhk, qtk, khk, khTk, deck = P["qhk"][0][0], P["qtk"][0][0], P["khk"][0][0], P["khTk"][0][0], P["deck"][0][0]
                    S.I("act", "activation", out=dec[:, h, :], in_=tot, func=AF.Exp, reads=[k3], writes=[(deck, h)])
                    S.I("act", "activation", out=kh[:, h, :], in_=t1_[:], func=AF.Exp, scale=-1.0, bias=lnoml_ap, reads=[k1] + LBK, writes=[(khk, h)])
                    S.I("act", "activation", out=t0_[:], in_=t4_[:], func=AF.Exp, reads=[k4], writes=[k0])
                    S.I("act", "activation", out=t2_[:], in_=bq[:], func=AF.Exp, reads=[bqk], writes=[k2])
                    yield
                    S.I("dve", "tensor_tensor", out=qh[:, h, :], in0=q[:], in1=t0_[:], op=ALU.mult, reads=[qk, k0], writes=[(qhk, h)])
                    S.I("dve", "tensor_tensor", out=qt[:, h, :], in0=q[:], in1=t2_[:], op=ALU.mult, reads=[qk, k2], writes=[(qtk, h)])
                    pT, pTk = pTk_r.next()
                    for c in range(8):
                        S.I("pe", "transpose", out=pT[0:64, c * 128:(c + 1) * 128], in_=kh[:, h, c * 64:(c + 1) * 64], identity=ident[:], reads=[(khk, h), "ident"], writes=[pTk])
                    copy_op("act", khT[0:64, h, :, :], pT[0:64, :].rearrange("p (c k) -> p c k", c=8), [pTk], [(khTk, h)])

                def chunk_a(D, P, so, b, c):
                    cs = slice(c * 64, (c + 1) * 64)
                    psA, psAk = D.psA.next()
                    for h in range(4):
                        S.I("pe", "matmul", psA[0:64, h * 64:(h + 1) * 64], lhsT=P["kh"][:, h, cs], rhs=P["qh"][:, h, cs], start=True, stop=True,
                            reads=[P["khk"][h], P["qhk"][h]], writes=[psAk])
                    Am, Amk = D.Am.next()
                    S.I("dve", "tensor_tensor", out=Am[0:64, :], in0=psA[0:64, 0:256], in1=D.tril, op=ALU.mult, reads=[psAk, "cstf"], writes=[Amk])
                    psS, psSk = D.psS.next()
                    for h in range(4):
                        S.I("pe", "matmul", psS[:, h * 128:(h + 1) * 128], lhsT=P["khT"][:, h, c, :], rhs=P["v"][:, c, h * 128:(h + 1) * 128], start=True, stop=True,
                            reads=[P["khTk"][h], P["vk"]], writes=[psSk])
                    return Am, Amk, psS, psSk

                def chunk_b(D, P, so, b, c, Am, Amk, psS, psSk):
                    nm = "hf" if D.dr == 0 else "hb"
                    cs = slice(c * 64, (c + 1) * 64)
                    po, pok = D.po.next()
                    for h in range(4):
                        S.I("pe", "matmul", po[0:64, h * 128:(h + 1) * 128], lhsT=P["qt"][:, h, cs], rhs=D.Sbf[:, h, :], start=True, stop=False,
                            reads=[P["qtk"][h], nm + "Sbf"], writes=[pok])
                        S.I("pe", "matmul", po[0:64, h * 128:(h + 1) * 128], lhsT=Am[:, h * 64:(h + 1) * 64], rhs=P["v"][:, c, h * 128:(h + 1) * 128], start=False, stop=True,
                            reads=[Amk, P["vk"]], writes=[pok])
                    for h in range(4):
                        S.I("dve", "scalar_tensor_tensor", out=D.S[:, h, :], in0=D.S[:, h, :], scalar=P["dec"][:, h, c:c + 1], in1=psS[:, h * 128:(h + 1) * 128],
                            op0=ALU.mult, op1=ALU.add, reads=[(nm + "S", h), P["deck"][h], psSk], writes=[(nm + "S", h)])
                    S.I("act", "copy", out=D.Sbf[:].rearrange("p h v -> p (h v)"), in_=D.S[:].rearrange("p h v -> p (h v)"), reads=[(nm + "S", h) for h in range(4)], writes=[nm + "Sbf"])
                    ob, obk = D.ob.next()
                    copy_op("act", ob[:], po[0:64, :], [pok], [obk])
                    r0 = so + b * 512 + c * 64
                    S.I("pool", "dma_start", out=D.odst[r0:r0 + 64, :], in_=ob[:], reads=[obk], dma=obk)

                def chunk_pair(Pf, Pb, so, bf_, bb_, ci):
                    af = chunk_a(dirs[0], Pf, so, bf_, ci)
                    ab = chunk_a(dirs[1], Pb, so, bb_, 7 - ci)
                    chunk_b(dirs[0], Pf, so, bf_, ci, *af)
                    chunk_b(dirs[1], Pb, so, bb_, 7 - ci, *ab)

                for si, T in enumerate(seq_lens):
                    so = seq_off[si]
                    NB = T // 512
                    for D in dirs:
                        nm = "hf" if D.dr == 0 else "hb"
                        S.I("pool", "memset", D.S[:], 0.0, writes=[(nm + "S", h) for h in range(4)])
                        S.I("pool", "memset", D.Sbf[:], 0.0, writes=[nm + "Sbf"])
                    Pst = {0: (prep_begin(dirs[0], so, 0), prep_begin(dirs[1], so, NB - 1))}
                    for h in range(4):
                        for dr in range(2):
                            blk_tok = 0 if dr == 0 else NB - 1
                            for _ in prep_head(dirs[dr], (lambda dr=dr: Pst[0][dr]), h, so + blk_tok * 512):
                                pass
                    off = [-2, -1, 0, 1, 2, 3, 4, 4]
                    starts = {}
                    for k in range(1, NB):
                        for i in range(8):
                            starts.setdefault(8 * (k - 1) + off[i], []).append((k, i))
                    active = []

                    def start_heads(P):
                        for (k, i) in starts.get(P, []):
                            dr, h = i % 2, i // 2
                            blk_tok = k if dr == 0 else NB - 1 - k
                            active.append(prep_head(dirs[dr], (lambda k=k, dr=dr: Pst[k][dr]), h, so + blk_tok * 512))

                    def advance():
                        for g in list(active):
                            try:
                                next(g)
                            except StopIteration:
                                active.remove(g)

                    for P in (-2, -1):
                        start_heads(P)
                        advance()
                    for k in range(NB):
                        if k + 1 < NB:
                            Pst[k + 1] = (prep_begin(dirs[0], so, k + 1), prep_begin(dirs[1], so, NB - 2 - k))
                        Pf, Pb = Pst[k]
                        for ci in range(8):
                            chunk_pair(Pf, Pb, so, k, NB - 1 - k, ci)
                            start_heads(8 * k + ci)
                            advance()
                    while active:
                        advance()
            S.barrier()

        if stop_after >= 9:
            with ExitStack() as st:
                wo = sb(st, "wo1", [128, 8, 1024], BF16)
                with ExitStack() as st2:
                    stage = Ring(nc, st2, "wstg2", 3, [128, 1024], F32)
                    load_weight(wo, w_out_odd, 8, 1024, stage)
                    S.barrier()
                lnt = sb(st, "lnt1", [128, 2, 1024], F32)
                S.I("sp", "dma_start", out=lnt[:, 0, :], in_=lng_in[:, 1024:2048], writes=["ln"], dma="ln")
                S.I("sp", "dma_start", out=lnt[:, 1, :], in_=lnb_in[:, 1024:2048], writes=["ln"], dma="ln")
                gnt = sb(st, "gnt", [128, 512], F32)
                S.I("sp", "dma_start", out=gnt[:], in_=gn_in, writes=["gn"], dma="gn")
                RD4 = {"oc": 4, "of": 5, "obk": 3, "gt1": 5, "xs3": 6, "vln1": 6, "yo": 3, "stt1": 8, "smt1": 6, "tmpc1": 3, "sq1": 3, "yt1": 3, "yT1": 3}
                oc_r = Ring(nc, st, "oc", RD4['oc'], [128, 520], F32)
                of_r = Ring(nc, st, "of", RD4['of'], [128, 512], F32)
                ob_r = Ring(nc, st, "obk", RD4['obk'], [128, 512], F32)
                g_r = Ring(nc, st, "gt1", RD4['gt1'], [128, 1024], BF16)
                xs_r = Ring(nc, st, "xs3", RD4['xs3'], [128, 1024], F32)
                v_r = Ring(nc, st, "vln1", RD4['vln1'], [128, 1024], F32)
                x1_r = Ring(nc, st, "yo", RD4['yo'], [128, 1024], F32)
                st_r = Ring(nc, st, "stt1", RD4['stt1'], [128, 8], F32)
                sm_r = Ring(nc, st, "smt1", RD4['smt1'], [128, 16], F32)
                tmp_r = Ring(nc, st, "tmpc1", RD4['tmpc1'], [128, 1024], F32)
                sq_r = Ring(nc, st, "sq1", RD4['sq1'], [128, 512], F32)
                junk = sb(st, "junk1", [128, 1024], BF16)
                y_r = Ring(nc, st, "yt1", RD4['yt1'], [128, 1024], BF16)
                yT_r = Ring(nc, st, "yT1", RD4['yT1'], [128, 8, 128], BF16)
                pT_r = Ring(nc, st, "pT3", 2, [128, 1024], BF16, psum=True)
                z_r = Ring(nc, st, "zps1", 6, [128, 512], F32, psum=True)
                def tile_gen(ti):
                    r0 = ti * 128
                    oc, ock = oc_r.next()
                    S.I("sp", "dma_start", out=oc[:], in_=ocs[r0:r0 + 128, :], writes=[ock], dma=ock)
                    of, ofk = of_r.next()
                    S.I("sp", "dma_start", out=of[:], in_=odf[r0:r0 + 128, :], writes=[ofk], dma=ofk)
                    ob, obk = ob_r.next()
                    S.I("sp", "dma_start", out=ob[:], in_=odb[r0:r0 + 128, :], writes=[obk], dma=obk)
                    gt, gk = g_r.next()
                    S.I("sp", "dma_start", out=gt[:], in_=h1tm[r0:r0 + 128, H1_GC:H1_GC + 1024], writes=[gk], dma=gk)
                    xs, xsk = xs_r.next()
                    S.I("sp", "dma_start", out=xs[:], in_=x1s[r0:r0 + 128, :], writes=[xsk], dma=xsk)
                    yield
                    sm, smk = sm_r.next()
                    tmp, tmpk = tmp_r.next()
                    y, yk = y_r.next()
                    sq, sqk = sq_r.next()
                    oc3 = oc[:].rearrange("p (h d) -> p h d", d=65)
                    S.I("dve", "reciprocal", out=sm[:, 0:8], in_=oc3[:, :, 64], reads=[ock], writes=[(smk, 0)])
                    S.I("dve", "tensor_tensor", out=of[:], in0=of[:], in1=ob[:], op=ALU.add, reads=[ofk, obk], writes=[ofk])
                    for h in range(4):
                        S.I("act", "activation", out=sq[:, h * 128:(h + 1) * 128], in_=of[:, h * 128:(h + 1) * 128], func=AF.Square, accum_out=sm[:, 8 + h:9 + h],
                            reads=[ofk], writes=[sqk, (smk, 1)])
                    yield
                    S.I("dve", "tensor_tensor", out=tmp[:, 0:512].rearrange("p (h d) -> p h d", d=64), in0=oc3[:, :, 0:64],
                        in1=sm[:, 0:8].unsqueeze(2).broadcast_to([128, 8, 64]), op=ALU.mult, reads=[ock, (smk, 0)], writes=[(tmpk, 0)])
                    S.I("dve", "tensor_scalar", out=sm[:, 8:12], in0=sm[:, 8:12], scalar1=1.0 / 128, scalar2=RMS_EPS, op0=ALU.mult, op1=ALU.add, reads=[(smk, 1)], writes=[(smk, 1)])
                    S.I("act", "activation", out=sm[:, 12:16], in_=sm[:, 8:12], func=AF.Sqrt, reads=[(smk, 1)], writes=[(smk, 2)])
                    yield
                    S.I("dve", "reciprocal", out=sm[:, 12:16], in_=sm[:, 12:16], reads=[(smk, 2)], writes=[(smk, 2)])
                    S.I("dve", "tensor_tensor", out=tmp[:, 512:1024].rearrange("p (h d) -> p h d", d=128), in0=of[:].rearrange("p (h d) -> p h d", d=128),
                        in1=sm[:, 12:16].unsqueeze(2).broadcast_to([128, 4, 128]), op=ALU.mult, reads=[ofk, (smk, 2)], writes=[(tmpk, 1)])
                    S.I("dve", "tensor_tensor", out=tmp[:, 512:1024], in0=tmp[:, 512:1024], in1=gnt[:], op=ALU.mult, reads=[(tmpk, 1), "gn"], writes=[(tmpk, 1)])
                    S.I("dve", "tensor_tensor", out=y[:], in0=tmp[:], in1=gt[:], op=ALU.mult, reads=[(tmpk, 0), (tmpk, 1), gk], writes=[yk])
                    pT, pTk = pT_r.next()
                    for c in range(8):
                        S.I("pe", "transpose", out=pT[:, c * 128:(c + 1) * 128], in_=y[:, c * 128:(c + 1) * 128], identity=ident[:], reads=[yk, "ident"], writes=[pTk])
                    yT, yTk = yT_r.next()
                    copy_op("act", yT[:], pT[:].rearrange("p (c t) -> p c t", c=8), [pTk], [yTk])
                    zps, zk = [], []
                    for hf in range(2):
                        z, zkk = z_r.next()
                        for c in range(8):
                            S.I("pe", "matmul", z[:], lhsT=yT[:, c, :], rhs=wo[:, c, hf * 512:(hf + 1) * 512], start=(c == 0), stop=(c == 7), reads=[yTk], writes=[zkk])
                        zps.append(z)
                        zk.append(zkk)
                    yield
                    for _ in layer_norm_gen(zps, zk, xs, xsk, lnt[:, 0, :], lnt[:, 1, :], v_r, st_r, junk, x1_r, y_out[r0:r0 + 128, :]):
                        yield

                run_skewed([tile_gen(ti) for ti in range(TT // 128)])
            S.barrier()

        fin = Op("sp", None)
        for k in S.dsem:
            fin.deps.append(("dma", k, S.dcnt[k]))
        S.ops["sp"].append(fin)
        with nc.Block() as block:
            S.emit(block)
    return nc


def host_consts(t5_table, sink_a, rpb_c, lb_d, gnorm_d, ln_g, ln_b):
    cst = np.zeros((128, 1280), np.float32)
    cst[:, 0:128] = np.eye(128, dtype=np.float32)
    s = np.arange(64)[:, None]
    t = np.arange(64)[None, :]
    cst[0:64, 128:384] = np.tile((s <= t).astype(np.float32), (1, 4))
    cst[0:64, 384:640] = np.tile((s >= t).astype(np.float32), (1, 4))
    sm = np.ones((512,), np.float32)
    sm[0::64] = 0.0
    cst[:, 640:1152] = sm[None, :]
    lb = np.ascontiguousarray(np.transpose(lb_d.reshape(2, 2, 4, 128), (3, 0, 1, 2))).reshape(128, 16)
    return {
        "tab0": _tables_l0(np.asarray(t5_table, np.float32)),
        "tabn": _tables_na(np.asarray(rpb_c, np.float32)),
        "sink": np.ascontiguousarray(np.broadcast_to(np.asarray(sink_a, np.float32).reshape(1, 8), (128, 8))),
        "lb": np.ascontiguousarray(lb.astype(np.float32)),
        "gnorm": np.ascontiguousarray(np.broadcast_to(np.asarray(gnorm_d, np.float32).reshape(1, 512), (128, 512))),
        "lng": np.ascontiguousarray(np.broadcast_to(np.asarray(ln_g, np.float32).reshape(1, 2048), (128, 2048))),
        "lnb": np.ascontiguousarray(np.broadcast_to(np.asarray(ln_b, np.float32).reshape(1, 2048), (128, 2048))),
        "cst": cst,
    }


N_CORES = 8
_NC_CACHE = {}


def kernel(x_prompt, x_sample, t5_table, w_in_even, sink_a, w_out_even, w_in_odd, rpb_c, lb_d, gnorm_d, w_out_odd, ln_g, ln_b):
    x_prompt = np.asarray(x_prompt, np.float32)
    x_sample = np.asarray(x_sample, np.float32)
    nb, T, D = x_prompt.shape
    ns, Ts, _ = x_sample.shape
    per = nb // N_CORES
    seq_lens = [T] * per + [Ts]
    key = tuple(seq_lens)
    if key not in _NC_CACHE:
        _NC_CACHE[key] = build(seq_lens)
    nc = _NC_CACHE[key]
    consts = host_consts(np.asarray(t5_table), np.asarray(sink_a), np.asarray(rpb_c), np.asarray(lb_d), np.asarray(gnorm_d), np.asarray(ln_g), np.asarray(ln_b))
    shared = dict(consts)
    shared["w_in_even"] = np.ascontiguousarray(np.asarray(w_in_even, np.float32)[0])
    shared["w_out_even"] = np.ascontiguousarray(np.asarray(w_out_even, np.float32)[0])
    shared["w_in_odd"] = np.ascontiguousarray(np.asarray(w_in_odd, np.float32)[0])
    shared["w_out_odd"] = np.ascontiguousarray(np.asarray(w_out_odd, np.float32)[0])
    in_maps = []
    for c in range(N_CORES):
        xs = [x_prompt[c * per + i] for i in range(per)] + [x_sample[c % ns]]
        m = dict(shared)
        m["x"] = np.ascontiguousarray(np.concatenate(xs, axis=0))
        in_maps.append(m)
    res = run_bass_kernel_spmd(nc, in_maps, core_ids=list(range(N_CORES)))
    y_prompt = np.empty((nb, T, D), np.float32)
    y_sample = np.empty((ns, Ts, D), np.float32)
    for c in range(N_CORES):
        y = np.asarray(res.results[c]["y"], np.float32)
        for i in range(per):
            y_prompt[c * per + i] = y[i * T:(i + 1) * T]
        if c < ns:
            y_sample[c] = y[per * T:per * T + Ts]
    return (y_prompt, y_sample)
```

```python
import math
from contextlib import ExitStack

import numpy as np
import concourse.bass as bass
import concourse.mybir as mybir
from concourse.bass_utils import run_bass_kernel_spmd

F32 = mybir.dt.float32
BF16 = mybir.dt.bfloat16
ALU = mybir.AluOpType
AF = mybir.ActivationFunctionType
AX = mybir.AxisListType

D_MODEL = 1024
DEPTH = 2
ALPHA = (2.0 * DEPTH) ** 0.25
LN_EPS = 1e-5
RMS_EPS = 1e-6
NEG = -30000.0

QA, KA, VA, QB, KB, VB, GA, GB, EVEN_IN = 0, 512, 640, 768, 1536, 2304, 3072, 3584, 3840
QC, KC, VC, QD, IDD, ZF, ZB, GC, GD, ODD_IN = 0, 512, 1024, 1536, 2048, 2560, 3072, 3584, 4096, 4608

H0_VA, H0_VB, H0_GA, H0_GB, H0_ROW = 0, 130, 910, 1422, 1680
H1_VC, H1_ID, H1_GC, H1_GD, H1_ROW = 0, 520, 1032, 1544, 2056
OA_ROW = 1300


class Op:
    __slots__ = ("eng", "fn", "deps", "dma_sem", "need_inc", "val")

    def __init__(self, eng, fn):
        self.eng = eng
        self.fn = fn
        self.deps = []
        self.dma_sem = None
        self.need_inc = False
        self.val = 0


class Sched:
    ENGS = ("pe", "act", "dve", "pool", "sp")
    SAME_ENG_RAW = ("act", "dve", "pool")

    def __init__(self, nc, stack):
        self.nc = nc
        self.stack = stack
        self.ops = {e: [] for e in self.ENGS}
        self.sem = {e: stack.enter_context(nc.semaphore("sem_" + e)) for e in self.ENGS}
        self.lastw = {}
        self.readers = {}
        self.dsem = {}
        self.dcnt = {}
        self.key2idx = {}
        self.next_idx = {}

    def op(self, eng, fn, reads=(), writes=(), dma=None):
        o = Op(eng, fn)
        deps = []
        for k in reads:
            w = self.lastw.get(k)
            if w is not None:
                deps.append((w, True))
        for k in writes:
            w = self.lastw.get(k)
            if w is not None:
                deps.append((w, False))
            for r in self.readers.get(k, ()):
                deps.append((r, False))
        for (d, raw) in deps:
            if d.dma_sem is not None:
                o.deps.append(("dma", d.dma_sem, self.dcnt[d.dma_sem]))
                continue
            if d.eng == eng and dma is None:
                if eng not in self.SAME_ENG_RAW:
                    continue
            d.need_inc = True
            o.deps.append(("op", d))
        if dma is not None:
            if (eng, dma) not in self.key2idx:
                n_ = self.next_idx.get(eng, 0)
                self.next_idx[eng] = n_ + 1
                idx = (eng, n_)
                self.key2idx[(eng, dma)] = idx
                if idx not in self.dsem:
                    self.dsem[idx] = self.stack.enter_context(self.nc.semaphore("dsem_%s%d" % idx))
                    self.dcnt[idx] = 0
            idx = self.key2idx[(eng, dma)]
            self.dcnt[idx] += 16
            o.dma_sem = idx
        for k in reads:
            lst = self.readers.setdefault(k, [])
            if dma is None:
                for idx in range(len(lst)):
                    if lst[idx].dma_sem is None and lst[idx].eng == eng:
                        lst[idx] = o
                        break
                else:
                    lst.append(o)
            else:
                lst.append(o)
        for k in writes:
            self.lastw[k] = o
            self.readers[k] = []
        self.ops[eng].append(o)
        return o

    def I(self, eng, name, *a, reads=(), writes=(), dma=None, **kw):
        return self.op(eng, (name, a, kw), reads=reads, writes=writes, dma=dma)

    def barrier(self):
        lasts = {}
        for e in self.ENGS:
            for o in reversed(self.ops[e]):
                if o.fn is not None and o.dma_sem is None:
                    lasts[e] = o
                    break
        for e in self.ENGS:
            o = Op(e, None)
            for f, l in lasts.items():
                if f != e:
                    l.need_inc = True
                    o.deps.append(("op", l))
            for k in self.dsem:
                o.deps.append(("dma", k, self.dcnt[k]))
            self.ops[e].append(o)
        self.lastw = {}
        self.readers = {}
        self.key2idx = {}
        self.next_idx = {}

    def emit(self, block):
        for e in self.ENGS:
            c = 0
            for o in self.ops[e]:
                if o.need_inc:
                    c += 1
                    o.val = c
        sem = self.sem
        dsem = self.dsem

        def run(engname, engine):
            seen = {}
            for o in self.ops[engname]:
                need = {}
                for d in o.deps:
                    if d[0] == "dma":
                        k = ("d", d[1])
                        v = d[2]
                        s = dsem[d[1]]
                    else:
                        k = ("e", d[1].eng)
                        v = d[1].val
                        s = sem[d[1].eng]
                    if v > seen.get(k, 0) and v > need.get(k, (0, None))[0]:
                        need[k] = (v, s)
                for k, (v, s) in need.items():
                    engine.wait_ge(s, v)
                    seen[k] = v
                if o.fn is None:
                    continue
                name, a, kw = o.fn
                ins = getattr(engine, name)(*a, **kw)
                if o.dma_sem is not None:
                    ins.then_inc(dsem[o.dma_sem], 16)
                elif o.need_inc:
                    ins.then_inc(sem[engname], 1)

        block.tensor(lambda e: run("pe", e))
        block.scalar(lambda e: run("act", e))
        block.vector(lambda e: run("dve", e))
        block.gpsimd(lambda e: run("pool", e))
        block.sync(lambda e: run("sp", e))


class Ring:
    def __init__(self, nc, st, name, n, shape, dtype, psum=False):
        self.name = name
        self.n = n
        self.t = []
        for i in range(n):
            if psum:
                self.t.append(st.enter_context(nc.psum_tensor("%s%d" % (name, i), shape, dtype)))
            else:
                self.t.append(st.enter_context(nc.sbuf_tensor("%s%d" % (name, i), shape, dtype)))
        self.i = 0

    def next(self):
        i = self.i % self.n
        self.i += 1
        return self.t[i], (self.name, i)

    def at(self, i):
        i = i % self.n
        return self.t[i], (self.name, i)

    def can_next(self):
        if not hasattr(self, "busy"):
            self.busy = [False] * self.n
        return not self.busy[self.i % self.n]

    def next_hold(self):
        if not hasattr(self, "busy"):
            self.busy = [False] * self.n
        i = self.i % self.n
        assert not self.busy[i], self.name
        self.busy[i] = True
        self.i += 1
        return self.t[i], (self.name, i), i

    def release(self, i):
        self.busy[i] = False


def _t5_buckets(rel):
    nb = 16
    ret = (rel > 0).astype(np.int64) * nb
    n = np.abs(rel)
    max_exact = nb // 2
    large = max_exact + (np.log(np.maximum(n, 1) / max_exact) / math.log(1024 / max_exact) * (nb - max_exact)).astype(np.int64)
    large = np.minimum(large, nb - 1)
    return ret + np.where(n < max_exact, n, large)


def _tables_l0(t5_table):
    k = np.arange(128)[:, None, None]
    j = np.arange(3)[None, :, None]
    q = np.arange(128)[None, None, :]
    rel = (j - 1) * 128 + k - q
    out = np.full((5, 128, 3, 4, 128), NEG, np.float32)
    specs = [(128, 1, [0, 1, 2, 3]), (128, 1, [4, 5, 6, 7]), (64, 1, [8, 9, 10, 11]), (64, 4, [12, 13, 14, 15]), (64, 16, [16, 17, 18, 19])]
    for s, (hw, d, heads) in enumerate(specs):
        valid = np.abs(rel) <= hw
        bk = _t5_buckets(rel * d)
        for hi, h in enumerate(heads):
            pos = ((hi % 2) * 2 + hi // 2) if s < 2 else hi
            out[s, :, :, pos, :] = np.where(valid, t5_table[bk, h], NEG)
    return out.reshape(5, 128, 1536)


def _tables_na(rpb):
    rpb = rpb.reshape(8, 15, 31)
    k = np.arange(128)[:, None]
    q = np.arange(128)[None, :]
    krl, kc = k // 64, k % 64
    qrl, qc = q // 64, q % 64
    sc = np.clip(qc - 8, 0, 48)
    colmask = (kc >= sc) & (kc < sc + 16)
    colidx = np.clip(kc - qc + 15, 0, 30)
    out = np.full((4, 128, 9, 2, 128), NEG, np.float32)
    for b in range(9):
        off = b - 3 if b < 7 else (-2 if b == 7 else 2)
        dr = 2 * off + krl - qrl
        valid = colmask & (np.abs(dr) <= 7)
        if b >= 7:
            valid = valid & (dr >= -4) & (dr <= 3)
        dri = np.clip(dr + 7, 0, 14)
        for h in range(8):
            vals = rpb[h][dri, colidx]
            out[h // 2, :, b, h % 2, :] = np.where(valid, vals, NEG)
    return out.reshape(4, 128, 9 * 256)


def na_neighbors(u, U):
    if U <= 4:
        return [(m, m - u + 3) for m in range(U)]
    if u == 0 or u == 1:
        return [(m, m - u + 3) for m in range(0, 4)]
    if u == U - 1 or u == U - 2:
        return [(m, m - u + 3) for m in range(U - 4, U)]
    res = []
    for off in (-2, -1, 0, 1, 2):
        m = u + off
        blk = 7 if off == -2 else (8 if off == 2 else off + 3)
        res.append((m, blk))
    return res


def build(seq_lens, debug=False, stop_after=99, lvl=9, skipA=False):
    nc = bass.Bass("TRN2", target_bir_lowering=False)
    TT = sum(seq_lens)
    seq_off = [sum(seq_lens[:i]) for i in range(len(seq_lens))]
    okind = "ExternalOutput" if debug else "Internal"

    def dram(name, shape, dt, kind):
        return nc.dram_tensor(name, shape, dt, kind=kind).ap()

    x_in = dram("x", [TT, D_MODEL], F32, "ExternalInput")
    y_out = dram("y", [TT, D_MODEL], F32, "ExternalOutput")
    w_in_even = dram("w_in_even", [D_MODEL, EVEN_IN], F32, "ExternalInput")
    w_out_even = dram("w_out_even", [768, D_MODEL], F32, "ExternalInput")
    w_in_odd = dram("w_in_odd", [D_MODEL, ODD_IN], F32, "ExternalInput")
    w_out_odd = dram("w_out_odd", [1024, D_MODEL], F32, "ExternalInput")
    tab0_in = dram("tab0", [5, 128, 1536], F32, "ExternalInput")
    tabn_in = dram("tabn", [4, 128, 2304], F32, "ExternalInput")
    sink_in = dram("sink", [128, 8], F32, "ExternalInput")
    lb_in = dram("lb", [128, 16], F32, "ExternalInput")
    gn_in = dram("gnorm", [128, 512], F32, "ExternalInput")
    lng_in = dram("lng", [128, 2048], F32, "ExternalInput")
    lnb_in = dram("lnb", [128, 2048], F32, "ExternalInput")
    cst_in = dram("cst", [128, 1280], F32, "ExternalInput")

    h0T = dram("h0T", [2176, TT], BF16, okind)
    h0tm = dram("h0tm", [TT, H0_ROW], BF16, okind)
    oatt = dram("oatt", [TT, OA_ROW], F32, okind)
    x1s = dram("x1s", [TT, D_MODEL], F32, okind)
    h1Tb = dram("h1Tb", [1024, TT], BF16, okind)
    h1Tf = dram("h1Tf", [1536, TT], F32, okind)
    h1tm = dram("h1tm", [TT, H1_ROW], BF16, okind)
    ocs = dram("ocs", [TT, 520], F32, okind)
    odf = dram("odf", [TT, 512], F32, okind)
    odb = dram("odb", [TT, 512], F32, okind)

    with ExitStack() as top:
        S = Sched(nc, top)
        sb = lambda st, name, shape, dt: st.enter_context(nc.sbuf_tensor(name, shape, dt))
        cyc = {"i": 0}

        def rr(engs):
            cyc["i"] += 1
            return engs[cyc["i"] % len(engs)]

        def copy_op(eng, out, in_, reads, writes):
            if eng == "act":
                S.I("act", "copy", out=out, in_=in_, reads=reads, writes=writes)
            else:
                S.I(eng, "tensor_copy", out=out, in_=in_, reads=reads, writes=writes)

        cstf = sb(top, "cstf", [128, 1280], F32)
        ident = sb(top, "ident", [128, 128], BF16)
        S.I("sp", "dma_start", out=cstf[:], in_=cst_in, writes=["cstf"], dma="cstf")
        S.I("dve", "tensor_copy", out=ident[:], in_=cstf[:, 0:128], reads=["cstf"], writes=["ident"])
        trilf = cstf[0:64, 128:384]
        trilb = cstf[0:64, 384:640]
        scanmask = cstf[:, 640:1152]

        def load_weight(dst, src, KCn, F, stage_ring, pieces=None):
            if pieces is None:
                pieces = [(c0, min(1024, F - c0), c0) for c0 in range(0, F, 1024)]
            for kc in range(KCn):
                for (c0, cw, d0) in pieces:
                    stg, sk = stage_ring.next()
                    S.I("sp", "dma_start", out=stg[:, 0:cw], in_=src[kc * 128:(kc + 1) * 128, c0:c0 + cw], writes=[sk], dma=sk)
                    copy_op(rr(["act", "dve"]), dst[:, kc, d0:d0 + cw], stg[:, 0:cw], [sk], [("w", id(dst), kc, d0)])

        def make_xT(xs, xs_k, xb_ring, pT_ring, xT, xT_k, tcol, cast_e=("dve", "act"), copy_e=("act", "dve")):
            xb, xb_k = xb_ring.next()
            copy_op(rr(list(cast_e)), xb[:], xs[:], [xs_k], [xb_k])
            pT, pT_k = pT_ring.next()
            for c in range(8):
                S.I("pe", "transpose", out=pT[:, c * 128:(c + 1) * 128], in_=xb[:, c * 128:(c + 1) * 128], identity=ident[:],
                    reads=[xb_k, "ident"], writes=[pT_k])
            copy_op(rr(list(copy_e)), xT[:, :, tcol:tcol + 128], pT[:].rearrange("p (c t) -> p c t", c=8), [pT_k], [xT_k])

        def project(xT, xT_k, wb, acc_r, fm_r, fm_list, tm_r, tm_groups, t0, tm_dst, ntt=4, tts=None, ev=("act", "dve")):
            ncols = ntt * 128
            for (f0, dst, ring) in fm_list:
                acc, acc_k = acc_r.next()
                for c in range(8):
                    S.I("pe", "matmul", acc[:, 0:ncols], lhsT=wb[:, c, f0:f0 + 128], rhs=xT[:, c, 0:ncols], start=(c == 0), stop=(c == 7),
                        reads=[xT_k], writes=[acc_k])
                fm, fm_k = ring.next()
                copy_op(rr(list(ev)), fm[:, 0:ncols], acc[:, 0:ncols], [acc_k], [fm_k])
                S.I("pool", "dma_start", out=dst, in_=fm[:, 0:ncols], reads=[fm_k], dma=fm_k)
            for tt in (range(ntt) if tts is None else tts):
                tm, tm_k = tm_r.next()
                for grp in tm_groups:
                    if len(grp) == 5:
                        wc, width, sc, nh, func = grp
                        plist = [(0, width, sc, nh, func)]
                    else:
                        wc, width, plist = grp
                    acc, acc_k = acc_r.next()
                    for c in range(8):
                        S.I("pe", "matmul", acc[:, 0:width], lhsT=xT[:, c, tt * 128:(tt + 1) * 128], rhs=wb[:, c, wc:wc + width], start=(c == 0), stop=(c == 7),
                            reads=[xT_k], writes=[acc_k])
                    for (o, w, sc, nh, func) in plist:
                        if nh > 0:
                            copy_op(rr(list(ev)), tm[:, sc:sc + nh * 65].rearrange("p (h d) -> p h d", d=65)[:, :, 0:64],
                                    acc[:, o:o + w].rearrange("p (h d) -> p h d", d=64), [acc_k], [tm_k])
                        elif func is not None:
                            S.I("act", "activation", out=tm[:, sc:sc + w], in_=acc[:, o:o + w], func=func, reads=[acc_k], writes=[tm_k])
                        else:
                            copy_op(rr(list(ev)), tm[:, sc:sc + w], acc[:, o:o + w], [acc_k], [tm_k])
                r0 = t0 + tt * 128
                S.I("pool", "dma_start", out=tm_dst[r0:r0 + 128, :], in_=tm[:], reads=[tm_k], dma=tm_k)

        with ExitStack() as st:
            wb = sb(st, "wb0", [128, 8, EVEN_IN], BF16)
            stage = Ring(nc, st, "wstg", 3, [128, 1024], F32)
            load_weight(wb, w_in_even, 8, EVEN_IN, stage,
                        pieces=[(0, 640, 0), (768, 1024, 640), (1792, 512, 1664), (640, 128, 2176), (2304, 1024, 2304), (3328, 512, 3328)])
            xs_r = Ring(nc, st, "xs", 4, [128, 1024], F32)
            xb_r = Ring(nc, st, "xb", 2, [128, 1024], BF16)
            pT_r = Ring(nc, st, "pT", 2, [128, 1024], BF16, psum=True)
            xT_r = Ring(nc, st, "xT", 2, [128, 8, 512], BF16)
            acc_r = Ring(nc, st, "acc", 4, [128, 512], F32, psum=True)
            fm_r = Ring(nc, st, "fm", 4, [128, 512], BF16)
            tm_r = Ring(nc, st, "tm", 2, [128, H0_ROW], BF16)
            for i in range(2):
                t, k = tm_r.at(i)
                S.I("pool", "memset", t[:], 1.0, writes=[k])
            fm_feats = [i * 128 for i in range(17)]
            tm_groups = [
                (2176, 512, [(0, 128, H0_VA, 2, None), (128, 384, H0_VB, 6, None)]),
                (2688, 512, [(0, 384, H0_VB + 6 * 65, 6, None), (384, 128, H0_GA, 0, AF.Silu)]),
                (3200, 512, [(0, 384, H0_GA + 128, 0, AF.Silu), (384, 128, H0_GB, 0, AF.Silu)]),
                (3712, 128, [(0, 128, H0_GB + 128, 0, AF.Silu)]),
            ]
            for b in range(TT // 512):
                t0 = b * 512
                xT, xT_k = xT_r.next()
                for tt in range(4):
                    xs, xs_k = xs_r.next()
                    r0 = t0 + tt * 128
                    S.I("sp", "dma_start", out=xs[:], in_=x_in[r0:r0 + 128, :], writes=[xs_k], dma=xs_k)
                    make_xT(xs, xs_k, xb_r, pT_r, xT, xT_k, tt * 128, cast_e=("dve",), copy_e=("act",))
                fm_list = [(f0, h0T[fi * 128:(fi + 1) * 128, t0:t0 + 512], fm_r) for fi, f0 in enumerate(fm_feats)]
                project(xT, xT_k, wb, acc_r, fm_r, fm_list, tm_r, tm_groups, t0, h0tm, ev=("act",))
        S.barrier()

        h0tm_h = h0tm.tensor
        oatt_h = oatt.tensor

        def attn_group(st, E0, gname, d, quadsets, qrow, krow, vcol, VW, ocol, is_a):
            W = max(512, 128 * d)
            tpw = W // (128 * d)
            CQ = 4 if is_a else 2
            nring = 4 if d < 16 else 3
            Qr = Ring(nc, st, gname + "Q", 2, [128, CQ, W], BF16)
            Kr = Ring(nc, st, gname + "K", nring, [128, 4, W], BF16)
            Vr = Ring(nc, st, gname + "V", nring, [128, tpw * d, VW], BF16)
            psr = Ring(nc, st, gname + "ps", 2, [128, 1536], F32, psum=True)
            por = Ring(nc, st, gname + "po", 2, [128, 512], F32, psum=True)
            esr = Ring(nc, st, gname + "es", 2, [128, 1536], BF16)
            ptr = Ring(nc, st, gname + "pt", 2, [128, 1536], BF16)
            osr = Ring(nc, st, gname + "os", 3, [128, 260], F32)
            for i in range(nring):
                Kw, Kk = Kr.at(i)
                S.I("pool", "memset", Kw[:], 0.0, writes=[Kk])
            for si, T in enumerate(seq_lens):
                so = seq_off[si]
                NW = T // W
                NT = T // (128 * d)

                def load_kv(wn):
                    Kw, Kk = Kr.at(wn)
                    Vw, Vk = Vr.at(wn)
                    c0 = so + wn * W
                    for v in range(4):
                        if is_a:
                            g, half = v // 2, v % 2
                            src = h0T[krow + g * 64:krow + (g + 1) * 64, c0:c0 + W]
                        else:
                            half = v % 2
                            src = h0T[krow + v * 64:krow + (v + 1) * 64, c0:c0 + W]
                        S.I("sp", "dma_start", out=Kw[half * 64:(half + 1) * 64, v, :], in_=src, writes=[Kk], dma=Kk)
                    if d == 1:
                        src = bass.AP(h0tm_h, c0 * H0_ROW + vcol, [[H0_ROW, 128], [128 * H0_ROW, tpw], [1, VW]])
                    else:
                        src = bass.AP(h0tm_h, c0 * H0_ROW + vcol, [[d * H0_ROW, 128], [H0_ROW, d], [1, VW]])
                    S.I("sp", "dma_start", out=Vw[:], in_=src, writes=[Vk], dma=Vk)

                def stage2(n, r, js, qi, pt, ptk, tt, so=so, NT=NT):
                    po, pok = por.next()
                    for hh in range(4):
                        for ji, j in enumerate(js):
                            n2 = n + j - 1
                            Vw, Vk = Vr.at(n2 // tpw)
                            t2 = n2 % tpw
                            vi = t2 if d == 1 else r
                            vc0 = (qi * 65) if is_a else hh * 65
                            pos = ((hh % 2) * 2 + hh // 2) if is_a else hh
                            o0 = (j * 4 + pos) * 128
                            S.I("pe", "matmul", po[:, hh * 65:hh * 65 + 65], lhsT=pt[:, o0:o0 + 128], rhs=Vw[:, vi, vc0:vc0 + 65], start=(ji == 0), stop=(ji == len(js) - 1),
                                reads=[ptk, Vk], writes=[pok])
                    osb, osk = osr.next()
                    copy_op("dve", osb[:], po[:, 0:260], [pok], [osk])
                    row0 = so + n * 128 * d + r
                    dst = bass.AP(oatt_h, row0 * OA_ROW + ocol + (qi * 260 if is_a else 0), [[d * OA_ROW, 128], [1, 260]])
                    S.I("pool", "dma_start", out=dst, in_=osb[:], reads=[osk], dma=osk)

                load_kv(0)
                for wn in range(NW):
                    Qw, Qk = Qr.at(wn)
                    c0 = so + wn * W
                    S.I("sp", "dma_start", out=Qw[:], in_=h0T[qrow:qrow + CQ * 128, c0:c0 + W].rearrange("(c p) t -> p c t", p=128), writes=[Qk], dma=Qk)
                    if wn + 1 < NW:
                        load_kv(wn + 1)
                    pending = None
                    for tt in range(tpw):
                        for r in range(d):
                            n = wn * tpw + tt
                            js = [j for j in range(3) if 0 <= n + j - 1 < NT]
                            qcols = slice(tt * 128 * d + r, tt * 128 * d + r + 127 * d + 1, d)
                            for qi, qs in enumerate(quadsets):
                                ps, psk = psr.next()
                                for j in js:
                                    n2 = n + j - 1
                                    Kw, Kk = Kr.at(n2 // tpw)
                                    t2 = n2 % tpw
                                    kcols = slice(t2 * 128 * d + r, t2 * 128 * d + r + 127 * d + 1, d)
                                    if is_a:
                                        for half in range(2):
                                            out_ap = ps[:, (j * 4 + half * 2) * 128:(j * 4 + half * 2 + 2) * 128]
                                            S.I("pe", "matmul", out_ap, lhsT=Kw[:, qi * 2 + half, kcols], rhs=Qw[:, qi * 2:qi * 2 + 2, qcols], start=True, stop=True,
                                                reads=[Kk, Qk], writes=[psk])
                                    else:
                                        for hh in range(4):
                                            o0 = (j * 4 + hh) * 128
                                            S.I("pe", "matmul", ps[:, o0:o0 + 128], lhsT=Kw[:, hh, kcols], rhs=Qw[:, hh // 2, qcols], start=True, stop=True, reads=[Kk, Qk], writes=[psk])
                                es, esk = esr.next()
                                pt, ptk = ptr.next()
                                for j in js:
                                    S.I("act", "activation", out=es[:, j * 512:(j + 1) * 512], in_=ps[:, j * 512:(j + 1) * 512], func=AF.Exp, scale=0.125,
                                        reads=[psk], writes=[esk])
                                for j in js:
                                    S.I("dve", "tensor_tensor", out=pt[:, j * 512:(j + 1) * 512], in0=es[:, j * 512:(j + 1) * 512], in1=E0[:, qs, j * 512:(j + 1) * 512], op=ALU.mult,
                                        reads=[esk, "E0"], writes=[ptk])
                                unit = (n, r, js, qi, pt, ptk, tt)
                                if pending is not None:
                                    stage2(*pending)
                                pending = unit
                    if pending is not None:
                        stage2(*pending)
                        pending = None

        if stop_after >= 2:
            with ExitStack() as st:
                E0 = sb(st, "E0", [128, 5, 1536], BF16)
                tstage = Ring(nc, st, "tstg", 2, [128, 1536], F32)
                for qs in range(5):
                    stg, sk = tstage.next()
                    S.I("sp", "dma_start", out=stg[:], in_=tab0_in[qs], writes=[sk], dma=sk)
                    S.I("act", "activation", out=E0[:, qs, :], in_=stg[:], func=AF.Exp, reads=[sk], writes=["E0"])
                with ExitStack() as st2:
                    if not skipA:
                        attn_group(st2, E0, "A", 1, [0, 1], 0, 512, H0_VA, 130, 0, True)
                S.barrier()
                for gi, d in enumerate((1, 4, 16)):
                    if stop_after < 3 + gi:
                        continue
                    with ExitStack() as st2:
                        attn_group(st2, E0, "B%d" % gi, d, [2 + gi], 640 + gi * 256, 1408 + gi * 256, H0_VB + gi * 260, 260, 520 + gi * 260, False)
                    S.barrier()

        def run_skewed(gens):
            active = []
            it = iter(gens)
            done = False
            while True:
                if not done:
                    try:
                        active.append(next(it))
                    except StopIteration:
                        done = True
                if done and not active:
                    break
                for g in list(active):
                    try:
                        next(g)
                    except StopIteration:
                        active.remove(g)

        def layer_norm_gen(zps, zk, xres, xres_k, lng, lnb, v_r, st_r, junk, x1_r, dst):
            v, vk = v_r.next()
            for hf in range(2):
                S.I("dve", "scalar_tensor_tensor", out=v[:, hf * 512:(hf + 1) * 512], in0=xres[:, hf * 512:(hf + 1) * 512], scalar=ALPHA, in1=zps[hf][:],
                    op0=ALU.mult, op1=ALU.add, reads=[xres_k, zk[hf]], writes=[vk])
            stt, stk = st_r.next()
            S.I("act", "activation", out=junk[:], in_=v[:], func=AF.Copy, accum_out=stt[:, 0:1], reads=[vk], writes=["junk", (stk, 0)])
            S.I("act", "activation", out=junk[:], in_=v[:], func=AF.Square, accum_out=stt[:, 1:2], reads=[vk], writes=["junk", (stk, 1)])
            yield
            S.I("dve", "tensor_scalar", out=stt[:, 2:3], in0=stt[:, 0:1], scalar1=1.0 / 1024, scalar2=None, op0=ALU.mult, reads=[(stk, 0)], writes=[(stk, 2)])
            S.I("dve", "tensor_tensor", out=stt[:, 3:4], in0=stt[:, 2:3], in1=stt[:, 2:3], op=ALU.mult, reads=[(stk, 2)], writes=[(stk, 3)])
            yield
            S.I("dve", "scalar_tensor_tensor", out=stt[:, 4:5], in0=stt[:, 1:2], scalar=1.0 / 1024, in1=stt[:, 3:4], op0=ALU.mult, op1=ALU.subtract,
                reads=[(stk, 1), (stk, 3)], writes=[(stk, 4)])
            yield
            S.I("dve", "tensor_scalar", out=stt[:, 4:5], in0=stt[:, 4:5], scalar1=LN_EPS, scalar2=None, op0=ALU.add, reads=[(stk, 4)], writes=[(stk, 4)])
            S.I("act", "activation", out=stt[:, 6:7], in_=stt[:, 4:5], func=AF.Sqrt, reads=[(stk, 4)], writes=[(stk, 6)])
            yield
            S.I("dve", "reciprocal", out=stt[:, 5:6], in_=stt[:, 6:7], reads=[(stk, 6)], writes=[(stk, 5)])
            S.I("dve", "scalar_tensor_tensor", out=v[:], in0=v[:], scalar=stt[:, 2:3], in1=lng, op0=ALU.subtract, op1=ALU.mult,
                reads=[vk, (stk, 2), "ln"], writes=[vk])
            yield
            x1, x1k = x1_r.next()
            S.I("dve", "scalar_tensor_tensor", out=x1[:], in0=v[:], scalar=stt[:, 5:6], in1=lnb, op0=ALU.mult, op1=ALU.add,
                reads=[vk, (stk, 5), "ln"], writes=[x1k])
            S.I("pool", "dma_start", out=dst, in_=x1[:], reads=[x1k], dma=x1k)

        def layer_norm_tile(zps, zk, xres, xres_k, lng, lnb, v_r, st_r, junk, x1_r):
            v, vk = v_r.next()
            for hf in range(2):
                S.I("dve", "scalar_tensor_tensor", out=v[:, hf * 512:(hf + 1) * 512], in0=xres[:, hf * 512:(hf + 1) * 512], scalar=ALPHA, in1=zps[hf][:],
                    op0=ALU.mult, op1=ALU.add, reads=[xres_k, zk[hf]], writes=[vk])
            stt, stk = st_r.next()
            S.I("dve", "tensor_reduce", out=stt[:, 0:1], in_=v[:], axis=AX.X, op=ALU.add, reads=[vk], writes=[(stk, 0)])
            S.I("act", "activation", out=junk[:], in_=v[:], func=AF.Square, accum_out=stt[:, 1:2], reads=[vk], writes=["junk", (stk, 1)])
            S.I("dve", "tensor_scalar", out=stt[:, 2:3], in0=stt[:, 0:1], scalar1=1.0 / 1024, scalar2=None, op0=ALU.mult, reads=[(stk, 0)], writes=[(stk, 2)])
            S.I("dve", "tensor_tensor", out=stt[:, 3:4], in0=stt[:, 2:3], in1=stt[:, 2:3], op=ALU.mult, reads=[(stk, 2)], writes=[(stk, 3)])
            S.I("dve", "scalar_tensor_tensor", out=stt[:, 4:5], in0=stt[:, 1:2], scalar=1.0 / 1024, in1=stt[:, 3:4], op0=ALU.mult, op1=ALU.subtract,
                reads=[(stk, 1), (stk, 3)], writes=[(stk, 4)])
            S.I("dve", "tensor_scalar", out=stt[:, 4:5], in0=stt[:, 4:5], scalar1=LN_EPS, scalar2=None, op0=ALU.add, reads=[(stk, 4)], writes=[(stk, 4)])
            S.I("act", "activation", out=stt[:, 6:7], in_=stt[:, 4:5], func=AF.Sqrt, reads=[(stk, 4)], writes=[(stk, 6)])
            S.I("dve", "reciprocal", out=stt[:, 5:6], in_=stt[:, 6:7], reads=[(stk, 6)], writes=[(stk, 5)])
            S.I("dve", "scalar_tensor_tensor", out=v[:], in0=v[:], scalar=stt[:, 2:3], in1=lng, op0=ALU.subtract, op1=ALU.mult,
                reads=[vk, (stk, 2), "ln"], writes=[vk])
            x1, x1k = x1_r.next()
            S.I("dve", "scalar_tensor_tensor", out=x1[:], in0=v[:], scalar=stt[:, 5:6], in1=lnb, op0=ALU.mult, op1=ALU.add,
                reads=[vk, (stk, 5), "ln"], writes=[x1k])
            return x1, x1k

        if stop_after >= 6:
            with ExitStack() as st:
                wo = sb(st, "wo0", [128, 6, 1024], BF16)
                w1 = sb(st, "w1", [128, 8, ODD_IN], BF16)
                with ExitStack() as st2:
                    stage = Ring(nc, st2, "wstg1", 3, [128, 1024], F32)
                    load_weight(wo, w_out_even, 6, 1024, stage)
                    load_weight(w1, w_in_odd, 8, ODD_IN, stage)
                    S.barrier()
                lnt = sb(st, "lnt", [128, 2, 1024], F32)
                S.I("sp", "dma_start", out=lnt[:, 0, :], in_=lng_in[:, 0:1024], writes=["ln"], dma="ln")
                S.I("sp", "dma_start", out=lnt[:, 1, :], in_=lnb_in[:, 0:1024], writes=["ln"], dma="ln")
                esink = sb(st, "esink", [128, 8], F32)
                S.I("sp", "dma_start", out=esink[:], in_=sink_in, writes=["esink"], dma="esink")
                S.I("act", "activation", out=esink[:], in_=esink[:], func=AF.Exp, reads=["esink"], writes=["esink"])
                oa_r = Ring(nc, st, "oa", 3, [128, OA_ROW], F32)
                g_r = Ring(nc, st, "gt", 3, [128, 768], BF16)
                xs_r = Ring(nc, st, "xs2", 3, [128, 1024], F32)
                v_r = Ring(nc, st, "vln", 4, [128, 1024], F32)
                x1_r = Ring(nc, st, "x1t", 2, [128, 1024], F32)
                st_r = Ring(nc, st, "stt", 6, [128, 8], F32)
                sm_r = Ring(nc, st, "smt", 4, [128, 16], F32)
                tmp_r = Ring(nc, st, "tmpc", 2, [128, 768], F32)
                junk = sb(st, "junk", [128, 1024], BF16)
                y_r = Ring(nc, st, "yt", 2, [128, 768], BF16)
                yT_r = Ring(nc, st, "yT", 2, [128, 6, 128], BF16)
                xb_r = Ring(nc, st, "xb2", 2, [128, 1024], BF16)
                xT_r = Ring(nc, st, "xT2", 2, [128, 8, 512], BF16)
                fmb_r = Ring(nc, st, "fmb", 4, [128, 512], BF16)
                fmf_r = Ring(nc, st, "fmf", 4, [128, 512], F32)
                tm_r = Ring(nc, st, "tm1", 2, [128, H1_ROW], BF16)
                pT_r = Ring(nc, st, "pT2", 2, [128, 1024], BF16, psum=True)
                z_r = Ring(nc, st, "zps", 4, [128, 512], F32, psum=True)
                acc_r = Ring(nc, st, "acc2", 2, [128, 512], F32, psum=True)
                for i in range(2):
                    t, k = tm_r.at(i)
                    S.I("pool", "memset", t[:], 1.0, writes=[k])
                fm_cols = [(QC + c * 128, h1Tb, c, fmb_r) for c in range(4)] + [(KC + c * 128, h1Tb, 4 + c, fmb_r) for c in range(4)] + \
                          [(QD + c * 128, h1Tf, c, fmf_r) for c in range(4)] + [(ZF + c * 128, h1Tf, 4 + c, fmf_r) for c in range(4)] + \
                          [(ZB + c * 128, h1Tf, 8 + c, fmf_r) for c in range(4)]
                tm_groups1 = [(VC, 512, H1_VC, 8, None), (IDD, 512, H1_ID, 0, None), (GC, 512, H1_GC, 0, AF.Silu), (GD, 512, H1_GD, 0, AF.Silu)]
                NBLK = TT // 512
                tiles_done = [0] * NBLK
                proj_done = [False] * NBLK
                blk_xT = {}

                def acquire(ring):
                    while not ring.can_next():
                        yield
                    return ring.next_hold()

                def tile_gen2(b, tt):
                    t0 = b * 512
                    r0 = t0 + tt * 128
                    oa, oak, oai = yield from acquire(oa_r)
                    S.I("sp", "dma_start", out=oa[:], in_=oatt[r0:r0 + 128, :], writes=[oak], dma=oak)
                    gt, gk, gi_ = yield from acquire(g_r)
                    S.I("sp", "dma_start", out=gt[:], in_=h0tm[r0:r0 + 128, H0_GA:H0_GA + 768], writes=[gk], dma=gk)
                    xs, xsk, xsi = yield from acquire(xs_r)
                    S.I("sp", "dma_start", out=xs[:], in_=x_in[r0:r0 + 128, :], writes=[xsk], dma=xsk)
                    yield
                    sm, smk, smi = yield from acquire(sm_r)
                    oa3 = oa[:, 0:520].rearrange("p (h d) -> p h d", d=65)
                    S.I("dve", "tensor_tensor", out=sm[:, 0:8], in0=oa3[:, :, 64], in1=esink[:], op=ALU.add, reads=[oak, "esink"], writes=[(smk, 0)])
                    S.I("dve", "tensor_tensor", out=oa[:, 520:780], in0=oa[:, 520:780], in1=oa[:, 780:1040], op=ALU.add, reads=[oak], writes=[oak])
                    yield
                    S.I("dve", "reciprocal", out=sm[:, 0:8], in_=sm[:, 0:8], reads=[(smk, 0)], writes=[(smk, 0)])
                    S.I("dve", "tensor_tensor", out=oa[:, 520:780], in0=oa[:, 520:780], in1=oa[:, 1040:1300], op=ALU.add, reads=[oak], writes=[oak])
                    yield
                    tmp, tmpk = tmp_r.next()
                    y, yk = y_r.next()
                    ob3 = oa[:, 520:780].rearrange("p (h d) -> p h d", d=65)
                    S.I("dve", "tensor_tensor", out=tmp[:, 0:512].rearrange("p (h d) -> p h d", d=64), in0=oa3[:, :, 0:64],
                        in1=sm[:, 0:8].unsqueeze(2).broadcast_to([128, 8, 64]), op=ALU.mult, reads=[oak, (smk, 0)], writes=[(tmpk, 0)])
                    S.I("dve", "reciprocal", out=sm[:, 8:12], in_=ob3[:, :, 64], reads=[oak], writes=[(smk, 1)])
                    yield
                    S.I("dve", "tensor_tensor", out=tmp[:, 512:768].rearrange("p (h d) -> p h d", d=64), in0=ob3[:, :, 0:64],
                        in1=sm[:, 8:12].unsqueeze(2).broadcast_to([128, 4, 64]), op=ALU.mult, reads=[oak, (smk, 1)], writes=[(tmpk, 1)])
                    S.I("dve", "tensor_tensor", out=y[:], in0=tmp[:], in1=gt[:], op=ALU.mult, reads=[(tmpk, 0), (tmpk, 1), gk], writes=[yk])
                    oa_r.release(oai)
                    g_r.release(gi_)
                    sm_r.release(smi)
                    pT, pTk = pT_r.next()
                    for c in range(6):
                        S.I("pe", "transpose", out=pT[:, c * 128:(c + 1) * 128], in_=y[:, c * 128:(c + 1) * 128], identity=ident[:], reads=[yk, "ident"], writes=[pTk])
                    yT, yTk = yT_r.next()
                    copy_op("act", yT[:], pT[:, 0:768].rearrange("p (c t) -> p c t", c=6), [pTk], [yTk])
                    zps, zk, zis = [], [], []
                    for hf in range(2):
                        z, zkk, zi = yield from acquire(z_r)
                        for c in range(6):
                            S.I("pe", "matmul", z[:], lhsT=yT[:, c, :], rhs=wo[:, c, hf * 512:(hf + 1) * 512], start=(c == 0), stop=(c == 5), reads=[yTk], writes=[zkk])
                        zps.append(z)
                        zk.append(zkk)
                        zis.append(zi)
                    yield
                    v, vk, vi_ = yield from acquire(v_r)
                    for hf in range(2):
                        S.I("dve", "scalar_tensor_tensor", out=v[:, hf * 512:(hf + 1) * 512], in0=xs[:, hf * 512:(hf + 1) * 512], scalar=ALPHA, in1=zps[hf][:],
                            op0=ALU.mult, op1=ALU.add, reads=[xsk, zk[hf]], writes=[vk])
                    xs_r.release(xsi)
                    for zi in zis:
                        z_r.release(zi)
                    stt, stk, sti = yield from acquire(st_r)
                    S.I("dve", "tensor_reduce", out=stt[:, 0:1], in_=v[:], axis=AX.X, op=ALU.add, reads=[vk], writes=[(stk, 0)])
                    S.I("act", "activation", out=junk[:], in_=v[:], func=AF.Square, accum_out=stt[:, 1:2], reads=[vk], writes=["junk", (stk, 1)])
                    yield
                    S.I("dve", "tensor_scalar", out=stt[:, 2:3], in0=stt[:, 0:1], scalar1=1.0 / 1024, scalar2=None, op0=ALU.mult, reads=[(stk, 0)], writes=[(stk, 2)])
                    S.I("dve", "tensor_tensor", out=stt[:, 3:4], in0=stt[:, 2:3], in1=stt[:, 2:3], op=ALU.mult, reads=[(stk, 2)], writes=[(stk, 3)])
                    S.I("dve", "scalar_tensor_tensor", out=stt[:, 4:5], in0=stt[:, 1:2], scalar=1.0 / 1024, in1=stt[:, 3:4], op0=ALU.mult, op1=ALU.subtract,
                        reads=[(stk, 1), (stk, 3)], writes=[(stk, 4)])
                    S.I("dve", "tensor_scalar", out=stt[:, 4:5], in0=stt[:, 4:5], scalar1=LN_EPS, scalar2=None, op0=ALU.add, reads=[(stk, 4)], writes=[(stk, 4)])
                    S.I("act", "activation", out=stt[:, 6:7], in_=stt[:, 4:5], func=AF.Sqrt, reads=[(stk, 4)], writes=[(stk, 6)])
                    yield
                    S.I("dve", "reciprocal", out=stt[:, 5:6], in_=stt[:, 6:7], reads=[(stk, 6)], writes=[(stk, 5)])
                    S.I("dve", "scalar_tensor_tensor", out=v[:], in0=v[:], scalar=stt[:, 2:3], in1=lnt[:, 0, :], op0=ALU.subtract, op1=ALU.mult,
                        reads=[vk, (stk, 2), "ln"], writes=[vk])
                    yield
                    while b >= 2 and not proj_done[b - 2]:
                        yield
                    if b not in blk_xT:
                        blk_xT[b] = xT_r.next()
                    xT, xT_k = blk_xT[b]
                    x1, x1k = x1_r.next()
                    S.I("dve", "scalar_tensor_tensor", out=x1[:], in0=v[:], scalar=stt[:, 5:6], in1=lnt[:, 1, :], op0=ALU.mult, op1=ALU.add,
                        reads=[vk, (stk, 5), "ln"], writes=[x1k])
                    v_r.release(vi_)
                    st_r.release(sti)
                    S.I("pool", "dma_start", out=x1s[r0:r0 + 128, :], in_=x1[:], reads=[x1k], dma=x1k)
                    make_xT(x1, x1k, xb_r, pT_r, xT, xT_k, tt * 128, cast_e=("dve",), copy_e=("act",))
                    tiles_done[b] += 1

                def proj_gen(b):
                    while tiles_done[b] < 4:
                        yield
                    t0 = b * 512
                    xT, xT_k = blk_xT[b]
                    fm_list = [(f0, dst[ci * 128:(ci + 1) * 128, t0:t0 + 512], ring) for (f0, dst, ci, ring) in fm_cols]
                    for i0 in range(0, len(fm_list), 5):
                        project(xT, xT_k, w1, acc_r, None, fm_list[i0:i0 + 5], tm_r, tm_groups1, t0, h1tm, ntt=4, tts=[], ev=("act",))
                        yield
                    for tt in range(4):
                        project(xT, xT_k, w1, acc_r, None, [], tm_r, tm_groups1, t0, h1tm, ntt=4, tts=[tt], ev=("act",))
                        yield
                    proj_done[b] = True

                gens = []
                for b in range(NBLK):
                    for tt in range(4):
                        gens.append(tile_gen2(b, tt))
                    gens.append(proj_gen(b))
                run_skewed(gens)
            S.barrier()

        h1tm_h = h1tm.tensor
        if stop_after >= 7:
            with ExitStack() as st:
                EN = sb(st, "EN", [128, 4, 2304], BF16)
                with ExitStack() as st2:
                    tstage = Ring(nc, st2, "nstg", 2, [128, 2304], F32)
                    for hp in range(4):
                        stg, sk = tstage.next()
                        S.I("sp", "dma_start", out=stg[:], in_=tabn_in[hp], writes=[sk], dma=sk)
                        S.I("act", "activation", out=EN[:, hp, :], in_=stg[:], func=AF.Exp, reads=[sk], writes=["EN"])
                    S.barrier()
                W = 512
                Qr = Ring(nc, st, "NQ", 2, [128, 4, W], BF16)
                Kr = Ring(nc, st, "NK", 4, [128, 8, W], BF16)
                Vr = Ring(nc, st, "NV", 4, [128, 4, 520], BF16)
                psr = Ring(nc, st, "Nps", 2, [128, 1536], F32, psum=True)
                por = Ring(nc, st, "Npo", 1, [128, 1024], F32, psum=True)
                esr = Ring(nc, st, "Nes", 2, [128, 1280], BF16)
                ptr = Ring(nc, st, "Npt", 2, [128, 1280], BF16)
                osr = Ring(nc, st, "Nos", 2, [128, 520], F32)
                for i in range(4):
                    Kw, Kk = Kr.at(i)
                    S.I("pool", "memset", Kw[:], 0.0, writes=[Kk])
                for si, T in enumerate(seq_lens):
                    so = seq_off[si]
                    NW = T // W
                    U = T // 128

                    def load_kv_n(wn):
                        Kw, Kk = Kr.at(wn)
                        Vw, Vk = Vr.at(wn)
                        c0 = so + wn * W
                        for h in range(8):
                            half = h % 2
                            S.I("sp", "dma_start", out=Kw[half * 64:(half + 1) * 64, h, :], in_=h1Tb[512 + h * 64:512 + (h + 1) * 64, c0:c0 + W], writes=[Kk], dma=Kk)
                        src = bass.AP(h1tm_h, c0 * H1_ROW + H1_VC, [[H1_ROW, 128], [128 * H1_ROW, 4], [1, 520]])
                        S.I("sp", "dma_start", out=Vw[:], in_=src, writes=[Vk], dma=Vk)

                    def na_stage2(u, hp, neigh, pt, ptk, po, pok, so=so):
                        NJ = len(neigh)
                        for hl in range(2):
                            h = 2 * hp + hl
                            pc = (h // 4) * 512 + (h % 4) * 65
                            for j, (m, blk) in enumerate(neigh):
                                Vw, Vk = Vr.at(m // 4)
                                o0 = (j * 2 + hl) * 128
                                S.I("pe", "matmul", po[:, pc:pc + 65], lhsT=pt[:, o0:o0 + 128], rhs=Vw[:, m % 4, h * 65:(h + 1) * 65], start=(j == 0), stop=(j == NJ - 1),
                                    reads=[ptk, Vk], writes=[pok])
                        if hp == 3:
                            osb, osk = osr.next()
                            copy_op("act", osb[:, 0:260], po[:, 0:260], [pok], [osk])
                            copy_op("dve", osb[:, 260:520], po[:, 512:772], [pok], [osk])
                            r0 = so + u * 128
                            S.I("pool", "dma_start", out=ocs[r0:r0 + 128, :], in_=osb[:], reads=[osk], dma=osk)

                    load_kv_n(0)
                    for wn in range(NW):
                        Qw, Qk = Qr.at(wn)
                        c0 = so + wn * W
                        S.I("sp", "dma_start", out=Qw[:], in_=h1Tb[0:512, c0:c0 + W].rearrange("(c p) t -> p c t", p=128), writes=[Qk], dma=Qk)
                        if wn + 1 < NW:
                            load_kv_n(wn + 1)
                        pending = None
                        for tt in range(4):
                            u = wn * 4 + tt
                            neigh = na_neighbors(u, U)
                            NJ = len(neigh)
                            po, pok = por.next()
                            for hp in range(4):
                                ps, psk = psr.next()
                                for j, (m, blk) in enumerate(neigh):
                                    Kw, Kk = Kr.at(m // 4)
                                    for hl in range(2):
                                        h = 2 * hp + hl
                                        o0 = (j * 2 + hl) * 128
                                        S.I("pe", "matmul", ps[:, o0:o0 + 128], lhsT=Kw[:, h, (m % 4) * 128:(m % 4 + 1) * 128], rhs=Qw[:, hp, tt * 128:(tt + 1) * 128],
                                            start=True, stop=True, reads=[Kk, Qk], writes=[psk])
                                es, esk = esr.next()
                                pt, ptk = ptr.next()
                                ncol = NJ * 256
                                for c0_ in range(0, ncol, 512):
                                    c1_ = min(ncol, c0_ + 512)
                                    S.I("act", "activation", out=es[:, c0_:c1_], in_=ps[:, c0_:c1_], func=AF.Exp, scale=0.125, reads=[psk], writes=[esk])
                                for j, (m, blk) in enumerate(neigh):
                                    S.I("dve", "tensor_tensor", out=pt[:, j * 256:(j + 1) * 256], in0=es[:, j * 256:(j + 1) * 256], in1=EN[:, hp, blk * 256:(blk + 1) * 256], op=ALU.mult,
                                        reads=[esk, "EN"], writes=[ptk])
                                unit = (u, hp, neigh, pt, ptk, po, pok)
                                if pending is not None:
                                    na_stage2(*pending)
                                pending = unit
                        if pending is not None:
                            na_stage2(*pending)
                            pending = None
            S.barrier()

        if stop_after >= 8:
            with ExitStack() as st:
                lbt = sb(st, "lbt", [128, 16], F32)
                lbc = sb(st, "lbc", [128, 5, 8], F32)
                S.I("sp", "dma_start", out=lbt[:], in_=lb_in, writes=["lbt"], dma="lbt")
                S.I("act", "activation", out=lbt[:], in_=lbt[:], func=AF.Exp, reads=["lbt"], writes=["lbt"])
                lb4 = lbt[:].rearrange("p (d l h) -> p d l h", d=2, l=2)
                lbv = lambda i: lbc[:, i, :].rearrange("p (d h) -> p d h", d=2)
                S.I("dve", "tensor_tensor", out=lbv(3), in0=lb4[:, :, 0, :], in1=lb4[:, :, 1, :], op=ALU.add, reads=["lbt"], writes=["lbc3"])
                S.I("dve", "reciprocal", out=lbv(3), in_=lbv(3), reads=["lbc3"], writes=["lbc3"])
                S.I("dve", "tensor_tensor", out=lbv(0), in0=lb4[:, :, 1, :], in1=lbv(3), op=ALU.mult, reads=["lbt", "lbc3"], writes=["lbc0"])
                S.I("dve", "tensor_scalar", out=lbc[:, 1, :], in0=lbc[:, 0, :], scalar1=-1.0, scalar2=1.0, op0=ALU.mult, op1=ALU.add, reads=["lbc0"], writes=["lbc1"])
                S.I("dve", "tensor_scalar", out=lbc[:, 2, :], in0=lbc[:, 0, :], scalar1=1.0, scalar2=-1.0, op0=ALU.mult, op1=ALU.add, reads=["lbc0"], writes=["lbc2"])
                S.I("act", "activation", out=lbc[:, 4, :], in_=lbc[:, 1, :], func=AF.Ln, reads=["lbc1"], writes=["lbc4"])
                LBK = ["lbc0", "lbc1", "lbc2", "lbc4"]

                class Dir:
                    pass

                dirs = []
                for dr in range(2):
                    D = Dir()
                    D.dr = dr
                    nm = "hf" if dr == 0 else "hb"
                    D.z_r = Ring(nc, st, nm + "z", 2, [128, 512], F32)
                    D.q_r = Ring(nc, st, nm + "q", 2, [128, 512], F32)
                    D.t_r = [Ring(nc, st, nm + "t%d" % i, 2, [128, 512], F32) for i in range(6)]
                    D.qh = Ring(nc, st, nm + "qh", 2, [128, 4, 512], BF16)
                    D.qt = Ring(nc, st, nm + "qt", 2, [128, 4, 512], BF16)
                    D.kh = Ring(nc, st, nm + "kh", 2, [128, 4, 512], BF16)
                    D.khT = Ring(nc, st, nm + "khT", 2, [128, 4, 8, 128], BF16)
                    D.dec = Ring(nc, st, nm + "dec", 2, [128, 4, 8], F32)
                    D.v = Ring(nc, st, nm + "v", 2, [128, 8, 512], BF16)
                    D.S = sb(st, nm + "S", [128, 4, 128], F32)
                    D.Sbf = sb(st, nm + "Sbf", [128, 4, 128], BF16)
                    D.Am = Ring(nc, st, nm + "Am", 2, [128, 256], BF16)
                    D.ob = Ring(nc, st, nm + "ob", 2, [64, 512], F32)
                    D.psA = Ring(nc, st, nm + "psA", 1, [128, 512], F32, psum=True)
                    D.po = Ring(nc, st, nm + "po", 1, [128, 512], F32, psum=True)
                    D.psS = Ring(nc, st, nm + "psS", 1, [128, 512], F32, psum=True)
                    D.tril = trilf if dr == 0 else trilb
                    D.odst = odf if dr == 0 else odb
                    D.zrow = 512 if dr == 0 else 1024
                    for i in range(2):
                        for ring in (D.khT, D.v, D.Am):
                            t, k = ring.at(i)
                            S.I("pool", "memset", t[:], 0.0, writes=[k])
                    dirs.append(D)
                pTk_r = Ring(nc, st, "hpT", 1, [128, 1024], BF16, psum=True)

                def prep_begin(D, so, b):
                    c0 = so + b * 512
                    qh, qhk = D.qh.next()
                    qt, qtk = D.qt.next()
                    kh, khk = D.kh.next()
                    khT, khTk = D.khT.next()
                    dec, deck = D.dec.next()
                    v, vk = D.v.next()
                    S.I("sp", "dma_start", out=v[0:64, :, :], in_=bass.AP(h1tm_h, c0 * H1_ROW + H1_ID, [[H1_ROW, 64], [64 * H1_ROW, 8], [1, 512]]), writes=[vk], dma=vk)
                    return dict(qh=qh, qhk=[(qhk, h) for h in range(4)], qt=qt, qtk=[(qtk, h) for h in range(4)], kh=kh, khk=[(khk, h) for h in range(4)],
                                khT=khT, khTk=[(khTk, h) for h in range(4)], dec=dec, deck=[(deck, h) for h in range(4)], v=v, vk=vk, c0=c0)

                def prep_head(D, getP, h, c0):
                    dr = D.dr
                    z, zk = D.z_r.next()
                    q, qk = D.q_r.next()
                    S.I("sp", "dma_start", out=z[:], in_=h1Tf[D.zrow + h * 128:D.zrow + (h + 1) * 128, c0:c0 + 512], writes=[zk], dma=zk)
                    S.I("sp", "dma_start", out=q[:], in_=h1Tf[h * 128:(h + 1) * 128, c0:c0 + 512], writes=[qk], dma=qk)
                    T_ = [r_.next() for r_ in D.t_r]
                    (t0_, k0), (t1_, k1), (t2_, k2), (t3_, k3), (t4_, k4), (t5_, k5) = T_
                    lb_ap = lbc[:, 0, dr * 4 + h:dr * 4 + h + 1]
                    lnoml_ap = lbc[:, 4, dr * 4 + h:dr * 4 + h + 1]
                    v3 = lambda t: t[:].rearrange("p (c t) -> p c t", t=64)
                    S.I("act", "activation", out=t0_[:], in_=z[:], func=AF.Exp, scale=-1.0, reads=[zk], writes=[k0])
                    S.I("act", "activation", out=t1_[:], in_=t0_[:], func=AF.Ln, scale=1.0, bias=1.0, reads=[k0], writes=[k1])
                    S.I("act", "activation", out=t2_[:], in_=t0_[:], func=AF.Ln, scale=lb_ap, bias=1.0, reads=[k0] + LBK, writes=[k2])
                    yield
                    S.I("dve", "tensor_tensor", out=t2_[:], in0=t2_[:], in1=t1_[:], op=ALU.subtract, reads=[k2, k1], writes=[k2])
                    S.I("dve", "tensor_tensor_scan", out=t3_[:], data0=scanmask, data1=t2_[:], initial=0.0, op0=ALU.mult, op1=ALU.add, reads=[k2, "cstf"], writes=[k3])
                    tot = v3(t3_)[:, :, 63]
                    totb = v3(t3_)[:, :, 63:64].broadcast_to([128, 8, 64])
                    if dr == 0:
                        S.I("dve", "tensor_tensor", out=v3(t4_), in0=v3(t3_), in1=totb, op=ALU.subtract, reads=[k3], writes=[k4])
                        bq, bqk = t3_, k3
                    else:
                        S.I("dve", "tensor_tensor", out=t4_[:], in0=t2_[:], in1=t3_[:], op=ALU.subtract, reads=[k2, k3], writes=[k4])
                        S.I("dve", "tensor_tensor", out=v3(t5_), in0=v3(t4_), in1=totb, op=ALU.add, reads=[k4, k3], writes=[k5])
                        bq, bqk = t5_, k5
                    S.I("dve", "tensor_tensor", out=t1_[:], in0=t1_[:], in1=z[:], op=ALU.add, reads=[k1, zk], writes=[k1])
                    S.I("dve", "tensor_tensor", out=t1_[:], in0=t1_[:], in1=t4_[:], op=ALU.add, reads=[k1, k4], writes=[k1])
                    yield
                    P = getP()
                    qh, qt, kh, khT, dec = P["qh"], P["qt"], P["kh"], P["khT"], P["dec"]
                    qhk, qtk, khk, khTk, deck = P["qhk"][0][0], P["qtk"][0][0], P["khk"][0][0], P["khTk"][0][0], P["deck"][0][0]
                    S.I("act", "activation", out=dec[:, h, :], in_=tot, func=AF.Exp, reads=[k3], writes=[(deck, h)])
                    S.I("act", "activation", out=kh[:, h, :], in_=t1_[:], func=AF.Exp, scale=-1.0, bias=lnoml_ap, reads=[k1] + LBK, writes=[(khk, h)])
                    S.I("act", "activation", out=t0_[:], in_=t4_[:], func=AF.Exp, reads=[k4], writes=[k0])
                    S.I("act", "activation", out=t2_[:], in_=bq[:], func=AF.Exp, reads=[bqk], writes=[k2])
                    yield
                    S.I("dve", "tensor_tensor", out=qh[:, h, :], in0=q[:], in1=t0_[:], op=ALU.mult, reads=[qk, k0], writes=[(qhk, h)])
                    S.I("dve", "tensor_tensor", out=qt[:, h, :], in0=q[:], in1=t2_[:], op=ALU.mult, reads=[qk, k2], writes=[(qtk, h)])
                    pT, pTk = pTk_r.next()
                    for c in range(8):
                        S.I("pe", "transpose", out=pT[0:64, c * 128:(c + 1) * 128], in_=kh[:, h, c * 64:(c + 1) * 64], identity=ident[:], reads=[(khk, h), "ident"], writes=[pTk])
                    copy_op("act", khT[0:64, h, :, :], pT[0:64, :].rearrange("p (c k) -> p c k", c=8), [pTk], [(khTk, h)])

                def chunk_a(D, P, so, b, c):
                    cs = slice(c * 64, (c + 1) * 64)
                    psA, psAk = D.psA.next()
                    for h in range(4):
                        S.I("pe", "matmul", psA[0:64, h * 64:(h + 1) * 64], lhsT=P["kh"][:, h, cs], rhs=P["qh"][:, h, cs], start=True, stop=True,
                            reads=[P["khk"][h], P["qhk"][h]], writes=[psAk])
                    Am, Amk = D.Am.next()
                    S.I("dve", "tensor_tensor", out=Am[0:64, :], in0=psA[0:64, 0:256], in1=D.tril, op=ALU.mult, reads=[psAk, "cstf"], writes=[Amk])
                    psS, psSk = D.psS.next()
                    for h in range(4):
                        S.I("pe", "matmul", psS[:, h * 128:(h + 1) * 128], lhsT=P["khT"][:, h, c, :], rhs=P["v"][:, c, h * 128:(h + 1) * 128], start=True, stop=True,
                            reads=[P["khTk"][h], P["vk"]], writes=[psSk])
                    return Am, Amk, psS, psSk

                def chunk_b(D, P, so, b, c, Am, Amk, psS, psSk):
                    nm = "hf" if D.dr == 0 else "hb"
                    cs = slice(c * 64, (c + 1) * 64)
                    po, pok = D.po.next()
                    for h in range(4):
                        S.I("pe", "matmul", po[0:64, h * 128:(h + 1) * 128], lhsT=P["qt"][:, h, cs], rhs=D.Sbf[:, h, :], start=True, stop=False,
                            reads=[P["qtk"][h], nm + "Sbf"], writes=[pok])
                        S.I("pe", "matmul", po[0:64, h * 128:(h + 1) * 128], lhsT=Am[:, h * 64:(h + 1) * 64], rhs=P["v"][:, c, h * 128:(h + 1) * 128], start=False, stop=True,
                            reads=[Amk, P["vk"]], writes=[pok])
                    for h in range(4):
                        S.I("dve", "scalar_tensor_tensor", out=D.S[:, h, :], in0=D.S[:, h, :], scalar=P["dec"][:, h, c:c + 1], in1=psS[:, h * 128:(h + 1) * 128],
                            op0=ALU.mult, op1=ALU.add, reads=[(nm + "S", h), P["deck"][h], psSk], writes=[(nm + "S", h)])
                    S.I("act", "copy", out=D.Sbf[:].rearrange("p h v -> p (h v)"), in_=D.S[:].rearrange("p h v -> p (h v)"), reads=[(nm + "S", h) for h in range(4)], writes=[nm + "Sbf"])
                    ob, obk = D.ob.next()
                    copy_op("act", ob[:], po[0:64, :], [pok], [obk])
                    r0 = so + b * 512 + c * 64
                    S.I("pool", "dma_start", out=D.odst[r0:r0 + 64, :], in_=ob[:], reads=[obk], dma=obk)

                def chunk_pair(Pf, Pb, so, bf_, bb_, ci):
                    af = chunk_a(dirs[0], Pf, so, bf_, ci)
                    ab = chunk_a(dirs[1], Pb, so, bb_, 7 - ci)
                    chunk_b(dirs[0], Pf, so, bf_, ci, *af)
                    chunk_b(dirs[1], Pb, so, bb_, 7 - ci, *ab)

                for si, T in enumerate(seq_lens):
                    so = seq_off[si]
                    NB = T // 512
                    for D in dirs:
                        nm = "hf" if D.dr == 0 else "hb"
                        S.I("pool", "memset", D.S[:], 0.0, writes=[(nm + "S", h) for h in range(4)])
                        S.I("pool", "memset", D.Sbf[:], 0.0, writes=[nm + "Sbf"])
                    Pst = {0: (prep_begin(dirs[0], so, 0), prep_begin(dirs[1], so, NB - 1))}
                    for h in range(4):
                        for dr in range(2):
                            blk_tok = 0 if dr == 0 else NB - 1
                            for _ in prep_head(dirs[dr], (lambda dr=dr: Pst[0][dr]), h, so + blk_tok * 512):
                                pass
                    off = [-2, -1, 0, 1, 2, 3, 4, 4]
                    starts = {}
                    for k in range(1, NB):
                        for i in range(8):
                            starts.setdefault(8 * (k - 1) + off[i], []).append((k, i))
                    active = []

                    def start_heads(P):
                        for (k, i) in starts.get(P, []):
                            dr, h = i % 2, i // 2
                            blk_tok = k if dr == 0 else NB - 1 - k
                            active.append(prep_head(dirs[dr], (lambda k=k, dr=dr: Pst[k][dr]), h, so + blk_tok * 512))

                    def advance():
                        for g in list(active):
                            try:
                                next(g)
                            except StopIteration:
                                active.remove(g)

                    for P in (-2, -1):
                        start_heads(P)
                        advance()
                    for k in range(NB):
                        if k + 1 < NB:
                            Pst[k + 1] = (prep_begin(dirs[0], so, k + 1), prep_begin(dirs[1], so, NB - 2 - k))
                        Pf, Pb = Pst[k]
                        for ci in range(8):
                            chunk_pair(Pf, Pb, so, k, NB - 1 - k, ci)
                            start_heads(8 * k + ci)
                            advance()
                    while active:
                        advance()
            S.barrier()

        if stop_after >= 9:
            with ExitStack() as st:
                wo = sb(st, "wo1", [128, 8, 1024], BF16)
                with ExitStack() as st2:
                    stage = Ring(nc, st2, "wstg2", 3, [128, 1024], F32)
                    load_weight(wo, w_out_odd, 8, 1024, stage)
                    S.barrier()
                lnt = sb(st, "lnt1", [128, 2, 1024], F32)
                S.I("sp", "dma_start", out=lnt[:, 0, :], in_=lng_in[:, 1024:2048], writes=["ln"], dma="ln")
                S.I("sp", "dma_start", out=lnt[:, 1, :], in_=lnb_in[:, 1024:2048], writes=["ln"], dma="ln")
                gnt = sb(st, "gnt", [128, 512], F32)
                S.I("sp", "dma_start", out=gnt[:], in_=gn_in, writes=["gn"], dma="gn")
                RD4 = {"oc": 4, "of": 5, "obk": 3, "gt1": 5, "xs3": 6, "vln1": 6, "yo": 3, "stt1": 8, "smt1": 6, "tmpc1": 3, "sq1": 3, "yt1": 3, "yT1": 3}
                oc_r = Ring(nc, st, "oc", RD4['oc'], [128, 520], F32)
                of_r = Ring(nc, st, "of", RD4['of'], [128, 512], F32)
                ob_r = Ring(nc, st, "obk", RD4['obk'], [128, 512], F32)
                g_r = Ring(nc, st, "gt1", RD4['gt1'], [128, 1024], BF16)
                xs_r = Ring(nc, st, "xs3", RD4['xs3'], [128, 1024], F32)
                v_r = Ring(nc, st, "vln1", RD4['vln1'], [128, 1024], F32)
                x1_r = Ring(nc, st, "yo", RD4['yo'], [128, 1024], F32)
                st_r = Ring(nc, st, "stt1", RD4['stt1'], [128, 8], F32)
                sm_r = Ring(nc, st, "smt1", RD4['smt1'], [128, 16], F32)
                tmp_r = Ring(nc, st, "tmpc1", RD4['tmpc1'], [128, 1024], F32)
                sq_r = Ring(nc, st, "sq1", RD4['sq1'], [128, 512], F32)
                junk = sb(st, "junk1", [128, 1024], BF16)
                y_r = Ring(nc, st, "yt1", RD4['yt1'], [128, 1024], BF16)
                yT_r = Ring(nc, st, "yT1", RD4['yT1'], [128, 8, 128], BF16)
                pT_r = Ring(nc, st, "pT3", 2, [128, 1024], BF16, psum=True)
                z_r = Ring(nc, st, "zps1", 6, [128, 512], F32, psum=True)
                def tile_gen(ti):
                    r0 = ti * 128
                    oc, ock = oc_r.next()
                    S.I("sp", "dma_start", out=oc[:], in_=ocs[r0:r0 + 128, :], writes=[ock], dma=ock)
                    of, ofk = of_r.next()
                    S.I("sp", "dma_start", out=of[:], in_=odf[r0:r0 + 128, :], writes=[ofk], dma=ofk)
                    ob, obk = ob_r.next()
                    S.I("sp", "dma_start", out=ob[:], in_=odb[r0:r0 + 128, :], writes=[obk], dma=obk)
                    gt, gk = g_r.next()
                    S.I("sp", "dma_start", out=gt[:], in_=h1tm[r0:r0 + 128, H1_GC:H1_GC + 1024], writes=[gk], dma=gk)
                    xs, xsk = xs_r.next()
                    S.I("sp", "dma_start", out=xs[:], in_=x1s[r0:r0 + 128, :], writes=[xsk], dma=xsk)
                    yield
                    sm, smk = sm_r.next()
                    tmp, tmpk = tmp_r.next()
                    y, yk = y_r.next()
                    sq, sqk = sq_r.next()
                    oc3 = oc[:].rearrange("p (h d) -> p h d", d=65)
                    S.I("dve", "reciprocal", out=sm[:, 0:8], in_=oc3[:, :, 64], reads=[ock], writes=[(smk, 0)])
                    S.I("dve", "tensor_tensor", out=of[:], in0=of[:], in1=ob[:], op=ALU.add, reads=[ofk, obk], writes=[ofk])
                    for h in range(4):
                        S.I("act", "activation", out=sq[:, h * 128:(h + 1) * 128], in_=of[:, h * 128:(h + 1) * 128], func=AF.Square, accum_out=sm[:, 8 + h:9 + h],
                            reads=[ofk], writes=[sqk, (smk, 1)])
                    yield
                    S.I("dve", "tensor_tensor", out=tmp[:, 0:512].rearrange("p (h d) -> p h d", d=64), in0=oc3[:, :, 0:64],
                        in1=sm[:, 0:8].unsqueeze(2).broadcast_to([128, 8, 64]), op=ALU.mult, reads=[ock, (smk, 0)], writes=[(tmpk, 0)])
                    S.I("dve", "tensor_scalar", out=sm[:, 8:12], in0=sm[:, 8:12], scalar1=1.0 / 128, scalar2=RMS_EPS, op0=ALU.mult, op1=ALU.add, reads=[(smk, 1)], writes=[(smk, 1)])
                    S.I("act", "activation", out=sm[:, 12:16], in_=sm[:, 8:12], func=AF.Sqrt, reads=[(smk, 1)], writes=[(smk, 2)])
                    yield
                    S.I("dve", "reciprocal", out=sm[:, 12:16], in_=sm[:, 12:16], reads=[(smk, 2)], writes=[(smk, 2)])
                    S.I("dve", "tensor_tensor", out=tmp[:, 512:1024].rearrange("p (h d) -> p h d", d=128), in0=of[:].rearrange("p (h d) -> p h d", d=128),
                        in1=sm[:, 12:16].unsqueeze(2).broadcast_to([128, 4, 128]), op=ALU.mult, reads=[ofk, (smk, 2)], writes=[(tmpk, 1)])
                    S.I("dve", "tensor_tensor", out=tmp[:, 512:1024], in0=tmp[:, 512:1024], in1=gnt[:], op=ALU.mult, reads=[(tmpk, 1), "gn"], writes=[(tmpk, 1)])
                    S.I("dve", "tensor_tensor", out=y[:], in0=tmp[:], in1=gt[:], op=ALU.mult, reads=[(tmpk, 0), (tmpk, 1), gk], writes=[yk])
                    pT, pTk = pT_r.next()
                    for c in range(8):
                        S.I("pe", "transpose", out=pT[:, c * 128:(c + 1) * 128], in_=y[:, c * 128:(c + 1) * 128], identity=ident[:], reads=[yk, "ident"], writes=[pTk])
                    yT, yTk = yT_r.next()
                    copy_op("act", yT[:], pT[:].rearrange("p (c t) -> p c t", c=8), [pTk], [yTk])
                    zps, zk = [], []
                    for hf in range(2):
                        z, zkk = z_r.next()
                        for c in range(8):
                            S.I("pe", "matmul", z[:], lhsT=yT[:, c, :], rhs=wo[:, c, hf * 512:(hf + 1) * 512], start=(c == 0), stop=(c == 7), reads=[yTk], writes=[zkk])
                        zps.append(z)
                        zk.append(zkk)
                    yield
                    for _ in layer_norm_gen(zps, zk, xs, xsk, lnt[:, 0, :], lnt[:, 1, :], v_r, st_r, junk, x1_r, y_out[r0:r0 + 128, :]):
                        yield

                run_skewed([tile_gen(ti) for ti in range(TT // 128)])
            S.barrier()

        fin = Op("sp", None)
        for k in S.dsem:
            fin.deps.append(("dma", k, S.dcnt[k]))
        S.ops["sp"].append(fin)
        with nc.Block() as block:
            S.emit(block)
    return nc


def host_consts(t5_table, sink_a, rpb_c, lb_d, gnorm_d, ln_g, ln_b):
    cst = np.zeros((128, 1280), np.float32)
    cst[:, 0:128] = np.eye(128, dtype=np.float32)
    s = np.arange(64)[:, None]
    t = np.arange(64)[None, :]
    cst[0:64, 128:384] = np.tile((s <= t).astype(np.float32), (1, 4))
    cst[0:64, 384:640] = np.tile((s >= t).astype(np.float32), (1, 4))
    sm = np.ones((512,), np.float32)
    sm[0::64] = 0.0
    cst[:, 640:1152] = sm[None, :]
    lb = np.ascontiguousarray(np.transpose(lb_d.reshape(2, 2, 4, 128), (3, 0, 1, 2))).reshape(128, 16)
    return {
        "tab0": _tables_l0(np.asarray(t5_table, np.float32)),
        "tabn": _tables_na(np.asarray(rpb_c, np.float32)),
        "sink": np.ascontiguousarray(np.broadcast_to(np.asarray(sink_a, np.float32).reshape(1, 8), (128, 8))),
        "lb": np.ascontiguousarray(lb.astype(np.float32)),
        "gnorm": np.ascontiguousarray(np.broadcast_to(np.asarray(gnorm_d, np.float32).reshape(1, 512), (128, 512))),
        "lng": np.ascontiguousarray(np.broadcast_to(np.asarray(ln_g, np.float32).reshape(1, 2048), (128, 2048))),
        "lnb": np.ascontiguousarray(np.broadcast_to(np.asarray(ln_b, np.float32).reshape(1, 2048), (128, 2048))),
        "cst": cst,
    }


N_CORES = 8
_NC_CACHE = {}


def kernel(x_prompt, x_sample, t5_table, w_in_even, sink_a, w_out_even, w_in_odd, rpb_c, lb_d, gnorm_d, w_out_odd, ln_g, ln_b):
    x_prompt = np.asarray(x_prompt, np.float32)
    x_sample = np.asarray(x_sample, np.float32)
    nb, T, D = x_prompt.shape
    ns, Ts, _ = x_sample.shape
    per = nb // N_CORES
    seq_lens = [T] * per + [Ts]
    key = tuple(seq_lens)
    if key not in _NC_CACHE:
        _NC_CACHE[key] = build(seq_lens)
    nc = _NC_CACHE[key]
    consts = host_consts(np.asarray(t5_table), np.asarray(sink_a), np.asarray(rpb_c), np.asarray(lb_d), np.asarray(gnorm_d), np.asarray(ln_g), np.asarray(ln_b))
    shared = dict(consts)
    shared["w_in_even"] = np.ascontiguousarray(np.asarray(w_in_even, np.float32)[0])
    shared["w_out_even"] = np.ascontiguousarray(np.asarray(w_out_even, np.float32)[0])
    shared["w_in_odd"] = np.ascontiguousarray(np.asarray(w_in_odd, np.float32)[0])
    shared["w_out_odd"] = np.ascontiguousarray(np.asarray(w_out_odd, np.float32)[0])
    in_maps = []
    for c in range(N_CORES):
        xs = [x_prompt[c * per + i] for i in range(per)] + [x_sample[c % ns]]
        m = dict(shared)
        m["x"] = np.ascontiguousarray(np.concatenate(xs, axis=0))
        in_maps.append(m)
    res = run_bass_kernel_spmd(nc, in_maps, core_ids=list(range(N_CORES)))
    y_prompt = np.empty((nb, T, D), np.float32)
    y_sample = np.empty((ns, Ts, D), np.float32)
    for c in range(N_CORES):
        y = np.asarray(res.results[c]["y"], np.float32)
        for i in range(per):
            y_prompt[c * per + i] = y[i * T:(i + 1) * T]
        if c < ns:
            y_sample[c] = y[per * T:per * T + Ts]
    return (y_prompt, y_sample)
```
